# Optimizing a Trainium2 kernel written in Bass

```python
import math
import jax, jax.numpy as jnp
from jax import lax
import numpy as np

D_MODEL = 1024
BATCH = 32
SEQ = 256
DEPTH = 4
DEC_BATCH = 2
DEC_SEQ = 2048
PAST_LEN = 256

GRID_W = 64
N_MIXERS = 2
N_A_LAYERS = (DEPTH + 1) // 2
N_B_LAYERS = DEPTH // 2
A_HEAD_DIM = 128
A_HEADS = D_MODEL // A_HEAD_DIM
A_CHUNK = 16
B_HEAD_DIM = 64
B_HEADS = D_MODEL // B_HEAD_DIM
B_LORA_W = max(32, int(round(1.8 * D_MODEL ** 0.5 / 32)) * 32)
B_LORA_A = max(32, int(round(1.8 * D_MODEL ** 0.5 / 32)) * 32)
B_LORA_G = max(32, int(round(0.6 * D_MODEL ** 0.8 / 32)) * 32)
D_FF = 4 * D_MODEL
DN_ALPHA = (2 * DEPTH) ** 0.25
DN_BETA = (8 * DEPTH) ** -0.25
LN_EPS = 1e-5
RMS_EPS = 1e-6
GN_EPS = 64e-5
DECAY_SCALE = 0.606531
POS_BASE = 10000.0
EXP_CLIP = 80.0

kernel_name = 'hgrn2_rwkv7_bidir_diffusion_step'


def layer_norm(x, g, b):
    xf = x.astype(jnp.float32)
    mu = jnp.mean(xf, -1, keepdims=True)
    var = jnp.mean(jnp.square(xf - mu), -1, keepdims=True)
    y = (xf - mu) * lax.rsqrt(var + LN_EPS) * g.astype(jnp.float32) + b.astype(jnp.float32)
    return y.astype(x.dtype)


def grid_pos_embed(n_tokens, dtype):
    rows = n_tokens // GRID_W
    quarter = D_MODEL // 4
    half = D_MODEL // 2
    omega = 1.0 / (POS_BASE ** (jnp.arange(quarter, dtype=jnp.float32) / quarter))
    r = jnp.arange(rows, dtype=jnp.float32)[:, None] * omega
    cc = jnp.arange(GRID_W, dtype=jnp.float32)[:, None] * omega
    row_emb = jnp.concatenate([jnp.sin(r), jnp.cos(r)], -1)
    col_emb = jnp.concatenate([jnp.sin(cc), jnp.cos(cc)], -1)
    emb = jnp.concatenate([
        jnp.broadcast_to(row_emb[:, None, :], (rows, GRID_W, half)),
        jnp.broadcast_to(col_emb[None, :, :], (rows, GRID_W, half))], -1)
    return emb.reshape(rows * GRID_W, D_MODEL).astype(dtype)


def adaln(cond, w, b):
    m = jax.nn.silu(cond) @ w + b
    return jnp.split(m.reshape(-1, 1, 6 * D_MODEL), 6, axis=-1)


def squared_relu_mlp(h, w_up, w_down):
    return jnp.square(jax.nn.relu(h @ w_up)) @ w_down


def hgrn2_mixer(h, w_in, lb, norm_g, w_o, s0):
    f32 = jnp.float32
    bsz, seq_len, _ = h.shape
    H, K, C = A_HEADS, A_HEAD_DIM, A_CHUNK
    n_chunks = seq_len // C
    proj = (h @ w_in).astype(f32)
    qfi = proj[..., :6 * D_MODEL].reshape(bsz, seq_len, 2, 3, H, K)
    gate = proj[..., 6 * D_MODEL:]
    q = jax.nn.silu(qfi[:, :, :, 0])
    z = qfi[:, :, :, 1]
    v = qfi[:, :, :, 2]
    lbh = lb.astype(f32).reshape(2, H, K)
    log_f = jax.nn.log_sigmoid(z) + jnp.log1p(lbh * jnp.exp(jnp.minimum(-z, EXP_CLIP)))
    k = (1.0 - lbh) * jax.nn.sigmoid(-z)

    def to_chunks(t):
        t = jnp.stack([t[:, :, 0], jnp.flip(t[:, :, 1], axis=1)], axis=1)
        return t.reshape(bsz, 2, n_chunks, C, H, K).transpose(0, 1, 4, 2, 3, 5)

    q, k, v, log_f = to_chunks(q), to_chunks(k), to_chunks(v), to_chunks(log_f)
    cum = jnp.cumsum(log_f, axis=-2)
    causal = jnp.tril(jnp.ones((C, C), dtype=bool))[:, :, None]
    rel = cum[..., :, None, :] - cum[..., None, :, :]
    pair_decay = jnp.where(causal, jnp.exp(jnp.where(causal, rel, 0.0)), 0.0)
    scores = jnp.einsum('bzhntk,bzhnsk,bzhntsk->bzhnts', q, k, pair_decay)
    o_intra = jnp.einsum('bzhnts,bzhnsv->bzhntv', scores, v)
    q_in = q * jnp.exp(cum)
    k_out = k * jnp.exp(cum[..., -1:, :] - cum)
    u = jnp.einsum('bzhnsk,bzhnsv->bzhnkv', k_out, v)
    chunk_decay = jnp.exp(cum[..., -1, :])

    def chunk_step(s, inp):
        q_n, d_n, u_n = inp
        o_n = jnp.einsum('bzhtk,bzhkv->bzhtv', q_n, s)
        return d_n[..., None] * s + u_n, o_n

    xs = (jnp.moveaxis(q_in, 3, 0), jnp.moveaxis(chunk_decay, 3, 0), jnp.moveaxis(u, 3, 0))
    s_fin, o_inter = lax.scan(chunk_step, s0.astype(f32), xs)
    o = o_intra + jnp.moveaxis(o_inter, 0, 3)
    o = o.transpose(0, 1, 3, 4, 2, 5).reshape(bsz, 2, seq_len, H, K)
    o = o[:, 0] + jnp.flip(o[:, 1], axis=1)
    o = o * lax.rsqrt(jnp.mean(o * o, -1, keepdims=True) + RMS_EPS)
    o = o.reshape(bsz, seq_len, D_MODEL) * norm_g.astype(f32) * jax.nn.silu(gate)
    return o.astype(h.dtype) @ w_o, s_fin.astype(h.dtype)


def rwkv7_mixer(h, mu, w_rkv, w0, w_la, w_lb, a0, a_la, a_lb, g_la, g_lb,
                k_k, k_a, r_k, gn_g, gn_b, w_o, s0):
    f32 = jnp.float32
    bsz, seq_len, _ = h.shape
    H, N = B_HEADS, B_HEAD_DIM
    zeros = jnp.zeros_like(h[:, :1])
    nbr = 0.5 * (jnp.concatenate([zeros, h[:, :-1]], 1) + jnp.concatenate([h[:, 1:], zeros], 1))
    xx = nbr - h
    xs = h[None] + xx[None] * mu[:, None, None, :]
    rkv = jnp.einsum('nbld,nde->nble', jnp.stack([xs[0], xs[2], xs[3]]), w_rkv)
    rkv = rkv.astype(f32).reshape(3, bsz, seq_len, 2, H, N)
    r, k, v = rkv[0], rkv[1], rkv[2]
    zw = w0 + jnp.einsum('blzr,zre->blze', jnp.tanh(jnp.einsum('bld,dzr->blzr', xs[1], w_la)), w_lb)
    decay = jnp.exp(-DECAY_SCALE * jax.nn.sigmoid(zw.astype(f32))).reshape(bsz, seq_len, 2, H, N)
    za = a0 + jnp.einsum('blzr,zre->blze', jnp.einsum('bld,dzr->blzr', xs[4], a_la), a_lb)
    a = jax.nn.sigmoid(za.astype(f32)).reshape(bsz, seq_len, 2, H, N)
    g = (jax.nn.sigmoid(xs[5] @ g_la) @ g_lb).astype(f32)
    kk = k * k_k.astype(f32).reshape(2, H, N)
    kk = kk / jnp.maximum(jnp.sqrt(jnp.sum(kk * kk, -1, keepdims=True)), 1e-12)
    k = k * (1.0 + (a - 1.0) * k_a.astype(f32).reshape(2, H, N))
    bonus = jnp.sum(r * k * r_k.astype(f32).reshape(2, H, N), -1, keepdims=True) * v

    def to_scan(t):
        return jnp.moveaxis(jnp.stack([t[:, :, 0], jnp.flip(t[:, :, 1], axis=1)], axis=2), 1, 0)

    def step(s, inp):
        r_t, w_t, k_t, v_t, kk_t, a_t = inp
        sa = jnp.einsum('bzhvk,bzhk->bzhv', s, kk_t)
        s = s * w_t[..., None, :] - sa[..., None] * (kk_t * a_t)[..., None, :] + v_t[..., None] * k_t[..., None, :]
        return s, jnp.einsum('bzhvk,bzhk->bzhv', s, r_t)

    s_fin, y = lax.scan(step, s0.astype(f32),
                        (to_scan(r), to_scan(decay), to_scan(k), to_scan(v), to_scan(kk), to_scan(a)))
    y = jnp.moveaxis(y, 0, 1)
    y = jnp.stack([y[:, :, 0], jnp.flip(y[:, :, 1], axis=1)], axis=2)
    m = jnp.mean(y, -1, keepdims=True)
    var = jnp.mean(jnp.square(y - m), -1, keepdims=True)
    y = (y - m) * lax.rsqrt(var + GN_EPS) * gn_g.astype(f32).reshape(2, H, N) + gn_b.astype(f32).reshape(2, H, N)
    y = y + bonus
    y = (y[:, :, 0] + y[:, :, 1]).reshape(bsz, seq_len, D_MODEL) * g
    return y.astype(h.dtype) @ w_o, s_fin.astype(h.dtype)


def trunk_layer(x, mod, mixer, s0, ln_g, ln_b, w_up, w_down):
    sh1, sc1, g1, sh2, sc2, g2 = mod
    y, s_fin = mixer(x * (1.0 + sc1) + sh1, s0)
    x = layer_norm(DN_ALPHA * x + g1 * y, ln_g[0], ln_b[0])
    y = squared_relu_mlp(x * (1.0 + sc2) + sh2, w_up, w_down)
    x = layer_norm(DN_ALPHA * x + g2 * y, ln_g[1], ln_b[1])
    return x, s_fin


def setup_inputs(seed: int = 0) -> dict:
    key = jax.random.key(seed)
    ks = iter(jax.random.split(key, 40))
    D = D_MODEL

    def nrm(shape, scale):
        return jax.random.normal(next(ks), shape, jnp.float32) * scale

    return {
        'x_prompt': nrm((BATCH, SEQ, D), 1.0),
        'x_sample': nrm((DEC_BATCH, DEC_SEQ, D), 1.0),
        'state_hgrn': nrm((DEC_BATCH, N_A_LAYERS, 2, A_HEADS, A_HEAD_DIM, A_HEAD_DIM), 0.5),
        'state_rwkv': nrm((DEC_BATCH, N_B_LAYERS, 2, B_HEADS, B_HEAD_DIM, B_HEAD_DIM), 0.3),
        'c': nrm((DEC_BATCH, D), 1.0),
        'c_ctx': nrm((D,), 1.0),
        'ada_w': nrm((DEPTH, D, 6 * D), D ** -0.5),
        'ada_b': nrm((DEPTH, 6 * D), 0.02),
        'ln_g': 1.0 + nrm((DEPTH, 2, D), 0.02),
        'ln_b': nrm((DEPTH, 2, D), 0.02),
        'ffn_w_up': nrm((DEPTH, D, D_FF), D ** -0.5),
        'ffn_w_down': nrm((DEPTH, D_FF, D), DN_BETA * D_FF ** -0.5),
        'hgrn_w_in': nrm((N_A_LAYERS, D, 7 * D), D ** -0.5),
        'hgrn_lb': nrm((N_A_LAYERS, 2, D), 0.5),
        'hgrn_norm_g': 1.0 + nrm((N_A_LAYERS, D), 0.02),
        'hgrn_w_o': nrm((N_A_LAYERS, D, D), DN_BETA * D ** -0.5),
        'rwkv_mu': jax.random.uniform(next(ks), (N_B_LAYERS, 6, D), jnp.float32),
        'rwkv_w_rkv': nrm((N_B_LAYERS, 3, D, 2 * D), D ** -0.5),
        'rwkv_w0': nrm((N_B_LAYERS, 2, D), 0.5),
        'rwkv_w_la': nrm((N_B_LAYERS, D, 2, B_LORA_W), D ** -0.5),
        'rwkv_w_lb': nrm((N_B_LAYERS, 2, B_LORA_W, D), 0.5 * B_LORA_W ** -0.5),
        'rwkv_a0': nrm((N_B_LAYERS, 2, D), 0.2),
        'rwkv_a_la': nrm((N_B_LAYERS, D, 2, B_LORA_A), D ** -0.5),
        'rwkv_a_lb': nrm((N_B_LAYERS, 2, B_LORA_A, D), 0.5 * B_LORA_A ** -0.5),
        'rwkv_g_la': nrm((N_B_LAYERS, D, B_LORA_G), D ** -0.5),
        'rwkv_g_lb': nrm((N_B_LAYERS, B_LORA_G, D), B_LORA_G ** -0.5),
        'rwkv_k_k': 0.85 + nrm((N_B_LAYERS, 2, D), 0.05),
        'rwkv_k_a': 1.0 + nrm((N_B_LAYERS, 2, D), 0.05),
        'rwkv_r_k': nrm((N_B_LAYERS, 2, D), 0.1),
        'rwkv_gn_g': 1.0 + nrm((N_B_LAYERS, 2, D), 0.02),
        'rwkv_gn_b': nrm((N_B_LAYERS, 2, D), 0.02),
        'rwkv_w_o': nrm((N_B_LAYERS, D, D), DN_BETA * D ** -0.5),
    }


def reference(x_prompt, x_sample, state_hgrn, state_rwkv, c, c_ctx, ada_w, ada_b, ln_g, ln_b,
              ffn_w_up, ffn_w_down, hgrn_w_in, hgrn_lb, hgrn_norm_g, hgrn_w_o,
              rwkv_mu, rwkv_w_rkv, rwkv_w0, rwkv_w_la, rwkv_w_lb, rwkv_a0, rwkv_a_la, rwkv_a_lb,
              rwkv_g_la, rwkv_g_lb, rwkv_k_k, rwkv_k_a, rwkv_r_k, rwkv_gn_g, rwkv_gn_b, rwkv_w_o):
    lb_soft = jax.nn.softmax(hgrn_lb.astype(jnp.float32), axis=0)
    lower_bounds = jnp.cumsum(lb_soft, axis=0) - lb_soft[0]
    x_p = x_prompt
    x_s = x_sample + grid_pos_embed(x_sample.shape[1], x_sample.dtype)
    n_ctx = x_prompt.shape[0]
    new_hgrn = []
    new_rwkv = []
    for l in range(DEPTH):
        j = l // N_MIXERS
        if l % N_MIXERS == 0:
            def mixer(h, s0, j=j):
                return hgrn2_mixer(h, hgrn_w_in[j], lower_bounds[j], hgrn_norm_g[j], hgrn_w_o[j], s0)
            s0_ctx = jnp.zeros((n_ctx, 2, A_HEADS, A_HEAD_DIM, A_HEAD_DIM), x_prompt.dtype)
            s0_lat = state_hgrn[:, j]
        else:
            def mixer(h, s0, j=j):
                return rwkv7_mixer(h, rwkv_mu[j], rwkv_w_rkv[j], rwkv_w0[j], rwkv_w_la[j], rwkv_w_lb[j],
                                   rwkv_a0[j], rwkv_a_la[j], rwkv_a_lb[j], rwkv_g_la[j], rwkv_g_lb[j],
                                   rwkv_k_k[j], rwkv_k_a[j], rwkv_r_k[j], rwkv_gn_g[j], rwkv_gn_b[j],
                                   rwkv_w_o[j], s0)
            s0_ctx = jnp.zeros((n_ctx, 2, B_HEADS, B_HEAD_DIM, B_HEAD_DIM), x_prompt.dtype)
            s0_lat = state_rwkv[:, j]
        mod_p = adaln(c_ctx, ada_w[l], ada_b[l])
        mod_s = adaln(c, ada_w[l], ada_b[l])
        x_p, s_ctx = trunk_layer(x_p, mod_p, mixer, s0_ctx, ln_g[l], ln_b[l], ffn_w_up[l], ffn_w_down[l])
        x_s, _ = trunk_layer(x_s, mod_s, mixer, s0_lat, ln_g[l], ln_b[l], ffn_w_up[l], ffn_w_down[l])
        if l % N_MIXERS == 0:
            new_hgrn.append(s_ctx)
        else:
            new_rwkv.append(s_ctx)
    new_state_hgrn = jnp.stack(new_hgrn, axis=1)
    new_state_rwkv = jnp.stack(new_rwkv, axis=1)
    return (x_p, x_s, new_state_hgrn, new_state_rwkv)
```

```python
import contextlib
import numpy as np
import concourse.bass as bass
import concourse.mybir as mybir
from concourse.bass_utils import run_bass_kernel_spmd

F32 = mybir.dt.float32
BF16 = mybir.dt.bfloat16
ALU = mybir.AluOpType
AF = mybir.ActivationFunctionType
AX = mybir.AxisListType

D = 1024
NT = 16
NSEG = 8
DN_ALPHA = 8 ** 0.25
LN_EPS = 1e-5
RMS_EPS = 1e-6
GN_EPS = 64e-5
C0 = 0.606531
PADL = 2
HTW = 260


class SemGroup:
    def __init__(self, sem):
        self.sem = sem
        self.count = 0


class Buf:
    __slots__ = ("name", "w", "r", "grp", "excl")

    def __init__(self, name, grp=None):
        self.name = name
        self.w = []
        self.r = {}
        self.grp = grp
        self.excl = False


class _Rec:
    def __init__(self):
        self.call = None

    def __getattr__(self, name):
        def f(*a, **k):
            self.call = (name, a, k)
            return self
        return f


class Sched:
    ENG = ("pe", "act", "dve", "pool", "sp")

    def __init__(self, nc, stack):
        self.nc = nc
        self.stack = stack
        self.ops = {e: [] for e in self.ENG}
        self.esem = {e: stack.enter_context(nc.semaphore("es_" + e)) for e in self.ENG}
        self.targets = {e: set() for e in self.ENG}
        self.groups = []
        self.gcache = {}
        self.lastreal = {e: 0 for e in self.ENG}

    def group(self, key=None):
        if key is not None and key in self.gcache:
            return self.gcache[key]
        g = SemGroup(self.stack.enter_context(self.nc.semaphore("dg%d" % len(self.groups))))
        self.groups.append(g)
        if key is not None:
            self.gcache[key] = g
        return g

    def buf(self, name, grp=None):
        if grp == "own":
            grp = self.group(name)
        return Buf(name, grp)

    def _deps(self, eng, reads, writes):
        deps = []
        for b in reads:
            deps.extend(b.w)
        for b in writes:
            deps.extend(b.w)
            for k, v in b.r.items():
                if isinstance(k, str):
                    deps.append(("e", k, v))
                else:
                    deps.append(("d", k[1], v))
        waits = {}
        for d in deps:
            if d[0] == "e" and d[1] == eng and eng in ("pe", "sp"):
                continue
            key = (d[0], d[1])
            waits[key] = max(waits.get(key, 0), d[2])
        for k, v in waits.items():
            if k[0] == "e":
                self.targets[k[1]].add(v)
        return waits

    skip = False

    def op(self, eng, fn, reads=(), writes=()):
        if self.skip:
            return
        ex = [b for b in reads if b.excl]
        if ex:
            writes = list(writes) + ex
        waits = self._deps(eng, reads, writes)
        rec = _Rec()
        fn(rec)
        assert rec.call is not None
        self.ops[eng].append([rec.call, waits, "c", None])
        idx = len(self.ops[eng])
        self.lastreal[eng] = idx
        for b in reads:
            b.r[eng] = idx
        for b in writes:
            b.w = [("e", eng, idx)]
            b.r = {}
        return idx

    def dma(self, eng, out, in_, reads=(), writes=(), grp=None, **kw):
        if self.skip:
            return
        waits = self._deps(eng, reads, writes)
        g = grp
        if g is None:
            for b in writes:
                if b.grp is not None:
                    g = b.grp
        assert g is not None
        g.count += 1
        cnt = g.count

        kw2 = dict(kw)
        kw2["out"] = out
        kw2["in_"] = in_
        self.ops[eng].append([("dma_start", (), kw2), waits, "d", g])
        for b in reads:
            b.r[("d", g)] = cnt
        for b in writes:
            b.w = [("d", g, cnt)]
            b.r = {}

    def barrier(self):
        last = dict(self.lastreal)
        for e in self.ENG:
            waits = {}
            for o in self.ENG:
                if o != e and last[o] > 0:
                    waits[("e", o)] = last[o]
                    self.targets[o].add(last[o])
            for g in self.groups:
                if g.count > 0:
                    waits[("d", g)] = g.count
            self.ops[e].append([None, waits, "w", None])

    def emit(self, block):
        tval = {}
        for e in self.ENG:
            c = 0
            m = {}
            for i in range(1, len(self.ops[e]) + 1):
                if i in self.targets[e]:
                    c += 1
                    m[i] = c
            tval[e] = m
        sched = self

        def run(e, engobj):
            seen = {}
            for i, rec in enumerate(sched.ops[e], start=1):
                fn, waits = rec[0], rec[1]
                for k, v in waits.items():
                    if k[0] == "e":
                        val = tval[k[1]][v]
                        sem = sched.esem[k[1]]
                    else:
                        val = 16 * v
                        sem = k[1].sem
                    if seen.get(id(sem), 0) >= val:
                        continue
                    seen[id(sem)] = val
                    engobj.wait_ge(sem, val)
                if fn is None:
                    assert i not in tval[e]
                    continue
                ins = getattr(engobj, fn[0])(*fn[1], **fn[2])
                if rec[2] == "d":
                    ins.then_inc(rec[3].sem, 16)
                elif i in tval[e]:
                    ins.then_inc(sched.esem[e], 1)

        @block.tensor
        def _(pe):
            run("pe", pe)

        @block.scalar
        def _(act):
            run("act", act)

        @block.vector
        def _(dve):
            run("dve", dve)

        @block.gpsimd
        def _(pool):
            run("pool", pool)

        @block.sync
        def _(sp):
            run("sp", sp)


def _consts():
    c = {}
    idx = np.arange(128)
    c["ident"] = np.eye(128, dtype=np.float32)
    same32 = (idx[:, None] // 32) == (idx[None, :] // 32)
    mh = np.zeros((2, 128, 128), np.float32)
    mh[0] = same32 & (idx[:, None] <= idx[None, :])
    mh[1] = same32 & (idx[:, None] >= idx[None, :])
    c["maskH"] = mh
    cm4 = np.zeros((128, 4, 128), np.float32)
    for cc in range(4):
        cm4[:, cc, cc * 32:(cc + 1) * 32] = 1.0
    c["colmask4"] = cm4
    rm4 = np.zeros((128, 4), np.float32)
    rm4[idx, idx // 32] = 1.0
    c["rowmask4"] = rm4
    r32 = np.ones((128, 128), np.float32)
    r32[:, ::32] = 0.0
    c["rmask32"] = r32
    r64 = np.ones((128, 128), np.float32)
    r64[:, ::64] = 0.0
    c["rmask64"] = r64
    same64 = (idx[:, None] // 64) == (idx[None, :] // 64)
    st = [same64 & (idx[:, None] < idx[None, :]), same64 & (idx[:, None] > idx[None, :])]
    inc = [same64 & (idx[:, None] <= idx[None, :]), same64 & (idx[:, None] >= idx[None, :])]
    mk1 = np.zeros((2, 128, 256), np.float32)
    mk2 = np.zeros((2, 128, 256), np.float32)
    mkn = np.zeros((2, 128, 128), np.float32)
    for d in range(2):
        mk1[d, :, :128] = st[d]
        mk1[d, :, 128:] = inc[d]
        mk2[d, :, :128] = st[d]
        mk2[d, :, 128:] = -inc[d].astype(np.float32)
        mkn[d] = st[d].T
    c["mk1"] = mk1
    c["mk2"] = mk2
    c["mkn"] = mkn
    cm2 = np.zeros((128, 2, 128), np.float32)
    cm2[:, 0, :64] = 1.0
    cm2[:, 1, 64:] = 1.0
    c["colmask2"] = cm2
    hi = np.zeros((128, 2), np.float32)
    hi[idx, idx // 64] = 1.0
    c["headind"] = hi
    c["blockones"] = same64.astype(np.float32)
    sel = np.zeros((128, 2, 64), np.float32)
    for hh in range(2):
        sel[64 * hh + np.arange(64), hh, np.arange(64)] = 1.0
    c["sel64"] = sel
    return c


def _pos_table():
    rows, gw, quarter, half = 2048 // 64, 64, 256, 512
    omega = (1.0 / (10000.0 ** (np.arange(quarter, dtype=np.float32) / np.float32(quarter)))).astype(np.float32)
    r = np.arange(rows, dtype=np.float32)[:, None] * omega
    cc = np.arange(gw, dtype=np.float32)[:, None] * omega
    row_emb = np.concatenate([np.sin(r), np.cos(r)], -1)
    col_emb = np.concatenate([np.sin(cc), np.cos(cc)], -1)
    emb = np.concatenate([np.broadcast_to(row_emb[:, None, :], (rows, gw, half)),
                          np.broadcast_to(col_emb[None, :, :], (rows, gw, half))], -1)
    return np.ascontiguousarray(emb.reshape(rows * gw, D).astype(np.float32))


CONST_SHAPES = {
    "ident": [128, 128], "maskH": [2, 128, 128], "colmask4": [128, 4, 128], "rowmask4": [128, 4],
    "rmask32": [128, 128], "rmask64": [128, 128], "mk1": [2, 128, 256], "mk2": [2, 128, 256],
    "mkn": [2, 128, 128], "colmask2": [128, 2, 128], "headind": [128, 2], "blockones": [128, 128],
    "sel64": [128, 2, 64],
}


def build_program(n_layers=4, mix=True, dbg=False, stop=None):
    nc = bass.Bass("TRN2", target_bir_lowering=False)

    def din(name, shape):
        return nc.dram_tensor(name, list(shape), F32, kind="ExternalInput").ap()

    def dout(name, shape):
        return nc.dram_tensor(name, list(shape), F32, kind="ExternalOutput").ap()

    NLW = n_layers
    NH = max(1, (n_layers + 1) // 2)
    NR = max(1, n_layers // 2)
    x_in = din("x_in", [2048, D])
    pos = din("pos", [2048, D])
    condT = din("condT", [128, 8])
    flags = din("flags", [128, 2])
    ada_w = din("ada_w", [NLW, D, 6 * D])
    ada_bT = din("ada_bT", [NLW, 128, 48])
    ln_g = din("ln_g", [NLW, 2, D])
    ln_b = din("ln_b", [NLW, 2, D])
    w_up = din("ffn_w_up", [NLW, D, 4 * D])
    w_dn = din("ffn_w_down", [NLW, 4 * D, D])
    h_win = din("hgrn_w_in", [NH, D, 7 * D])
    h_lbT = din("hgrn_lbT", [2, 128, 16])
    h_ng = din("hgrn_norm_g", [NH, D])
    h_wo = din("hgrn_w_o", [NH, D, D])
    st_h = din("st_h", [2, 2, 8, 128, 128])
    st_r = din("st_r", [2, 2, 16, 64, 64])
    r_vecT = din("rwkv_vecT", [2, 16, 128, 8])
    r_wrkv = din("rwkv_w_rkv", [NR, 3, D, 2 * D])
    r_wla = din("rwkv_w_la", [NR, D, 2, 64])
    r_wlb = din("rwkv_w_lb", [NR, 2, 64, D])
    r_ala = din("rwkv_a_la", [NR, D, 2, 64])
    r_alb = din("rwkv_a_lb", [NR, 2, 64, D])
    r_gla = din("rwkv_g_la", [NR, D, 160])
    r_glb = din("rwkv_g_lb", [NR, 160, D])
    r_gng = din("rwkv_gn_g", [NR, 2, D])
    r_gnb = din("rwkv_gn_b", [NR, 2, D])
    r_wo = din("rwkv_w_o", [NR, D, D])
    cst = {k: din("c_" + k, v) for k, v in CONST_SHAPES.items()}

    y_out = dout("y_out", [2048, D])
    ns_h = dout("ns_h", [2, NSEG, 2, 8, 128, 128])
    ns_r = dout("ns_r", [2, NSEG, 2, 16, 64, 64])
    oscr = nc.dram_tensor("oscr", [2, 2048, D], F32).ap()
    dbgt = dout("dbg", [16, 128, D]) if dbg else None

    with contextlib.ExitStack() as st:
        S = Sched(nc, st)

        _uid = [0]

        def sb(stack, name, shape, dt=F32):
            _uid[0] += 1
            return stack.enter_context(nc.sbuf_tensor("%s_u%d" % (name, _uid[0]), list(shape), dt))

        X = sb(st, "X", [128, NT, D])
        BX = [S.buf("X%d" % i) for i in range(NT)]
        GX = S.group("GX")
        ident = sb(st, "ident", [128, 128])
        identb = sb(st, "identb", [128, 128], BF16)
        Bc = S.buf("consts", "own")
        flg = sb(st, "flg", [128, 2])
        scond = sb(st, "scond", [128, 8])
        modT = sb(st, "modT", [128, 48])
        sc1p = sb(st, "sc1p", [128, 8])
        sc2p = sb(st, "sc2p", [128, 8])
        Bmod = S.buf("mod")
        PS = [st.enter_context(nc.psum_tensor("ps%d" % i, [128, 512], F32)) for i in range(8)]
        BPS = [S.buf("ps%d" % i) for i in range(8)]
        for b_ in BPS:
            b_.excl = True
        Gout = S.group("Gout")
        By = S.buf("y_out", Gout)
        Gscr = S.group("Gscr")
        Bscr = [S.buf("oscr0", Gscr), S.buf("oscr1", Gscr)]

        block = st.enter_context(nc.Block())
        _dq = [0]
        Bdbg = S.buf("dbg", S.group("dbg"))

        def dump(slot, ap, bufs, p=128, n=None):
            if not dbg:
                return
            n = n if n is not None else ap.shape[-1]
            S.dma("pool", dbgt[slot, 0:p, 0:n], ap, reads=bufs, writes=[Bdbg])

        def hwq():
            _dq[0] += 1
            return "sp" if _dq[0] % 2 == 0 else "act"

        S.dma("sp", ident[:], cst["ident"], writes=[Bc])
        S.dma("pool", identb[:], cst["ident"], writes=[Bc])
        S.dma("sp", flg[:], flags, writes=[Bc])
        S.dma("sp", scond[:], condT, writes=[Bc])
        S.op("act", lambda e: e.activation(out=scond[:], in_=scond[:], func=AF.Silu), reads=[Bc], writes=[Bc])
        for i in range(NT):
            S.dma(hwq(), X[:, i, :], x_in[i * 128:(i + 1) * 128, :], writes=[BX[i]], grp=GX)
        with contextlib.ExitStack() as ph:
            pt = [sb(ph, "pos%d" % i, [128, D]) for i in range(2)]
            Bpt = [S.buf("pos%d" % i, "own") for i in range(2)]
            for i in range(NT):
                S.dma(hwq(), pt[i % 2][:], pos[i * 128:(i + 1) * 128, :], writes=[Bpt[i % 2]])
                eng = "dve"
                S.op(eng, lambda e, i=i: e.scalar_tensor_tensor(out=X[:, i, :], in0=pt[i % 2][:], scalar=flg[:, 1:2], in1=X[:, i, :],
                                                                op0=ALU.mult, op1=ALU.add),
                     reads=[Bpt[i % 2], Bc, BX[i]], writes=[BX[i]])
            dump(1, X[:, 0, :], [BX[0]])
            S.barrier()

        def adaln(l):
            with contextlib.ExitStack() as ph:
                slab = [sb(ph, "adas%d" % i, [128, 8, 256]) for i in range(2)]
                Bsl = [S.buf("adas%d" % i, "own") for i in range(2)]
                abT = sb(ph, "abT", [128, 48])
                Bab = S.buf("abT", "own")
                S.dma("sp", abT[:], ada_bT[l], writes=[Bab])
                wv = ada_w[l].rearrange("(k p) n -> p k n", p=128)
                acc = PS[0]
                for jg in range(24):
                    sl = jg % 2
                    S.dma(hwq(), slab[sl][:], wv[:, :, jg * 256:(jg + 1) * 256], writes=[Bsl[sl]])
                    for jj in range(2):
                        j = jg * 2 + jj
                        for k in range(8):
                            S.op("pe", lambda e, sl=sl, jj=jj, j=j, k=k: e.matmul(
                                out=acc[:, j:j + 1], lhsT=slab[sl][:, k, jj * 128:(jj + 1) * 128], rhs=scond[:, k:k + 1],
                                start=(k == 0), stop=(k == 7)), reads=[Bsl[sl], Bc], writes=[BPS[0]])
                S.op("dve", lambda e: e.tensor_tensor(out=modT[:], in0=acc[:, 0:48], in1=abT[:], op=ALU.add),
                     reads=[BPS[0], Bab], writes=[Bmod])
                S.op("dve", lambda e: e.tensor_scalar_add(out=sc1p[:], in0=modT[:, 8:16], scalar1=1.0), reads=[Bmod], writes=[Bmod])
                S.op("dve", lambda e: e.tensor_scalar_add(out=sc2p[:], in0=modT[:, 32:40], scalar1=1.0), reads=[Bmod], writes=[Bmod])
                if l == 0:
                    dump(0, modT[:], [Bmod])
                S.barrier()

        def make_gbc(ph, which):
            base = (16, 40)[which]
            gb = sb(ph, "gbc", [128, D])
            Bgb = S.buf("gbc%d" % which)
            gtmp = sb(ph, "gtmp", [128, 128])
            Bgt = S.buf("gtmp")
            for k in range(8):
                S.op("dve", lambda e, k=k: e.tensor_scalar(
                    out=gtmp[:], in0=modT[:, base + k:base + k + 1].to_broadcast([128, 128]),
                    scalar1=1.0 / DN_ALPHA, scalar2=None, op0=ALU.mult), reads=[Bmod], writes=[Bgt])
                S.op("pe", lambda e: e.transpose(out=PS[1][:, 0:128], in_=gtmp[:], identity=ident[:]),
                     reads=[Bgt, Bc], writes=[BPS[1]])
                S.op("act", lambda e, k=k: e.activation(out=gb[:, k * 128:(k + 1) * 128], in_=PS[1][:, 0:128], func=AF.Copy),
                     reads=[BPS[1]], writes=[Bgb])
            return gb, Bgb

        def build_hT(dst_fn, Bdst_fn, tiles, scp, shcol):
            n = 0
            for i in tiles:
                for kk in range(2):
                    pb = 2 + (n % 2)
                    n += 1
                    for k4 in range(4):
                        k = kk * 4 + k4
                        S.op("pe", lambda e, i=i, k=k, k4=k4, pb=pb: e.transpose(
                            out=PS[pb][:, k4 * 128:(k4 + 1) * 128], in_=X[:, i, k * 128:(k + 1) * 128], identity=ident[:]),
                            reads=[BX[i], Bc], writes=[BPS[pb]])
                    for k4 in range(4):
                        k = kk * 4 + k4
                        if k % 2 == 0:
                            S.op("act", lambda e, i=i, k=k, k4=k4, pb=pb: e.activation(
                                out=dst_fn(i, k), in_=PS[pb][:, k4 * 128:(k4 + 1) * 128], func=AF.Identity,
                                bias=modT[:, shcol + k:shcol + k + 1], scale=scp[:, k:k + 1]),
                                reads=[BPS[pb], Bmod], writes=[Bdst_fn(i)])
                        else:
                            S.op("dve", lambda e, i=i, k=k, k4=k4, pb=pb: e.tensor_scalar(
                                out=dst_fn(i, k), in0=PS[pb][:, k4 * 128:(k4 + 1) * 128],
                                scalar1=scp[:, k:k + 1], scalar2=modT[:, shcol + k:shcol + k + 1], op0=ALU.mult, op1=ALU.add),
                                reads=[BPS[pb], Bmod], writes=[Bdst_fn(i)])

        def resid_ln(i, pa, pb, Bpa, Bpb, which, lt):
            v, Bv = lt["v"], lt["Bv"]
            stt, Bst = lt["st"], lt["Bst"]
            S.op("dve", lambda e: e.tensor_tensor(out=v[:, 0:512], in0=pa[:, :], in1=lt["gbc"][:, 0:512], op=ALU.mult),
                 reads=[Bpa, lt["Bgbc"]], writes=[Bv])
            S.op("dve", lambda e: e.tensor_tensor(out=v[:, 512:1024], in0=pb[:, :], in1=lt["gbc"][:, 512:1024], op=ALU.mult),
                 reads=[Bpb, lt["Bgbc"]], writes=[Bv])
            S.op("pool", lambda e: e.tensor_tensor(out=v[:], in0=v[:], in1=X[:, i, :], op=ALU.add), reads=[Bv, BX[i]], writes=[Bv])
            if i == 0 and lt.get("dbg"):
                dump(5, v[:], [Bv])
            S.op("dve", lambda e: e.bn_stats(out=stt[:, 0:6], in_=v[:, 0:512]), reads=[Bv], writes=[Bst])
            S.op("dve", lambda e: e.bn_stats(out=stt[:, 6:12], in_=v[:, 512:1024]), reads=[Bv], writes=[Bst])
            S.op("dve", lambda e: e.bn_aggr(out=stt[:, 12:14], in_=stt[:, 0:12]), reads=[Bst], writes=[Bst])
            S.op("dve", lambda e: e.tensor_scalar_add(out=stt[:, 14:15], in0=stt[:, 13:14], scalar1=LN_EPS / (DN_ALPHA ** 2)), reads=[Bst], writes=[Bst])
            S.op("act", lambda e: e.activation(out=stt[:, 14:15], in_=stt[:, 14:15], func=AF.Sqrt), reads=[Bst], writes=[Bst])
            S.op("dve", lambda e: e.reciprocal(out=stt[:, 14:15], in_=stt[:, 14:15]), reads=[Bst], writes=[Bst])
            S.op("dve", lambda e: e.tensor_scalar(out=v[:], in0=v[:], scalar1=stt[:, 12:13], scalar2=stt[:, 14:15],
                                                  op0=ALU.subtract, op1=ALU.mult), reads=[Bv, Bst], writes=[Bv])
            if i == 0 and lt.get("dbg"):
                dump(6, stt[:], [Bst])
                dump(8, v[:], [Bv])
            S.op("pool", lambda e: e.tensor_tensor(out=v[:], in0=v[:], in1=lt["lng"][:], op=ALU.mult),
                 reads=[Bv, lt["Bln"]], writes=[Bv])
            S.op("pool", lambda e: e.tensor_tensor(out=X[:, i, :], in0=v[:], in1=lt["lnb"][:], op=ALU.add),
                 reads=[Bv, lt["Bln"]], writes=[BX[i]])

        def ffn(l):
            with contextlib.ExitStack() as ph:
                wd = sb(ph, "wd", [128, 32, D], BF16)
                Bwd = [S.buf("wd%d" % i, "own") for i in range(4)]
                ups = [sb(ph, "ups%d" % i, [128, 8, 256], BF16) for i in range(2)]
                Bups = [S.buf("ups%d" % i, "own") for i in range(2)]
                aT = sb(ph, "aT", [128, 32, 512], BF16)
                BaT = [S.buf("aT%d" % j) for j in range(32)]
                h2 = sb(ph, "h2", [128, 8, 512], BF16)
                Bh2 = [S.buf("h2_%d" % i) for i in range(4)]
                rl = [sb(ph, "rl%d" % i, [128, 512]) for i in range(2)]
                Brl = [S.buf("rl%d" % i) for i in range(2)]
                lng = sb(ph, "lng", [128, D])
                lnb = sb(ph, "lnb", [128, D])
                Bln = S.buf("lnbc", "own")
                S.dma("sp", lng[:], ln_g[l, 1].partition_broadcast(128), writes=[Bln])
                S.dma("sp", lnb[:], ln_b[l, 1].partition_broadcast(128), writes=[Bln])
                gb_, Bgb_ = make_gbc(ph, 1)
                lt = [dict(v=sb(ph, "lnv%d" % i, [128, D]), Bv=S.buf("lnv%d" % i), st=sb(ph, "lnst%d" % i, [128, 16]), Bst=S.buf("lnst%d" % i),
                           lng=lng, lnb=lnb, Bln=Bln, gbc=gb_, Bgbc=Bgb_) for i in range(1)]
                if l == 0:
                    lt[0]["dbg"] = True
                wdv = w_dn[l].rearrange("(j p) n -> p j n", p=128)
                for q in range(4):
                    S.dma("pool", wd[:, q * 8:(q + 1) * 8, :], wdv[:, q * 8:(q + 1) * 8, :], writes=[Bwd[q]])
                wuv = w_up[l].rearrange("(k p) n -> p k n", p=128)
                nev = 0
                for g in range(4):
                    tiles = list(range(g * 4, g * 4 + 4))
                    build_hT(lambda i, k: h2[:, k, (i % 4) * 128:(i % 4 + 1) * 128], lambda i: Bh2[i % 4], tiles, sc2p, 24)
                    for jg in range(16):
                        sl = jg % 2
                        S.dma("pool", ups[sl][:], wuv[:, :, jg * 256:(jg + 1) * 256], writes=[Bups[sl]])
                        for jj in range(2):
                            j = jg * 2 + jj
                            pb = 4 + (nev % 2)
                            for k in range(8):
                                S.op("pe", lambda e, sl=sl, jj=jj, k=k, pb=pb: e.matmul(
                                    out=PS[pb][:, :], lhsT=ups[sl][:, k, jj * 128:(jj + 1) * 128], rhs=h2[:, k, :],
                                    start=(k == 0), stop=(k == 7)), reads=[Bups[sl]] + Bh2, writes=[BPS[pb]])
                            r = nev % 2
                            nev += 1
                            S.op("act", lambda e, pb=pb, r=r: e.activation(out=rl[r][:], in_=PS[pb][:, :], func=AF.Relu),
                                 reads=[BPS[pb]], writes=[Brl[r]])
                            eng = "dve" if j % 2 == 0 else "pool"
                            S.op(eng, lambda e, j=j, r=r: e.tensor_tensor(out=aT[:, j, :], in0=rl[r][:], in1=rl[r][:], op=ALU.mult),
                                 reads=[Brl[r]], writes=[BaT[j]])
                    if l == 0 and g == 0:
                        dump(3, h2[:, :, 0:128].rearrange("p k t -> p k t"), Bh2, n=None) if False else None
                        for k in range(8):
                            if dbg:
                                S.dma("pool", dbgt[3, :, k * 128:(k + 1) * 128], h2[:, k, 0:128], reads=Bh2, writes=[Bdbg])
                        dump(4, aT[:, 0, :], [BaT[0]])
                    for ii, i in enumerate(tiles):
                        pa, pbk = 6, 7
                        for nh, pbank in ((0, pa), (1, pbk)):
                            for j in range(32):
                                S.op("pe", lambda e, j=j, ii=ii, nh=nh, pbank=pbank: e.matmul(
                                    out=PS[pbank][:, :], lhsT=aT[:, j, ii * 128:(ii + 1) * 128], rhs=wd[:, j, nh * 512:(nh + 1) * 512],
                                    start=(j == 0), stop=(j == 31)), reads=[BaT[j], Bwd[j // 8]], writes=[BPS[pbank]])
                        if l == 0 and i == 0 and dbg:
                            S.dma("pool", dbgt[9, :, 0:512], PS[pa][:, :], reads=[BPS[pa]], writes=[Bdbg]) if False else None
                        resid_ln(i, PS[pa], PS[pbk], BPS[pa], BPS[pbk], 1, lt[0])
                        if l == 0 and i == 0:
                            dump(7, X[:, 0, :], [BX[0]])
                S.barrier()

        def hgrn_layer(l, j, hT, BhT):
            hcols = lambda i, k: hT[:, k, i // 2, PADL + (i % 2) * 128: PADL + (i % 2) * 128 + 128]
            with contextlib.ExitStack() as ph:
                lbv = sb(ph, "lbv", [128, 16])
                oml = sb(ph, "oml", [128, 16])
                Blb = S.buf("lbv", "own")
                if j == 0:
                    S.op("dve", lambda e: e.memset(lbv[:], 0.0), writes=[Blb])
                else:
                    lb0 = sb(ph, "lb0", [128, 16])
                    S.dma("sp", lb0[:], h_lbT[0], writes=[Blb])
                    S.dma("sp", lbv[:], h_lbT[1], writes=[Blb])
                    S.op("dve", lambda e: e.tensor_tensor(out=lbv[:], in0=lbv[:], in1=lb0[:], op=ALU.subtract), reads=[Blb], writes=[Blb])
                    S.op("act", lambda e: e.activation(out=lbv[:], in_=lbv[:], func=AF.Sigmoid), reads=[Blb], writes=[Blb])
                S.op("dve", lambda e: e.tensor_scalar(out=oml[:], in0=lbv[:], scalar1=-1.0, scalar2=1.0, op0=ALU.mult, op1=ALU.add),
                     reads=[Blb], writes=[Blb])
                maskH = sb(ph, "maskH", [128, 128])
                cm4 = sb(ph, "cm4", [128, 4, 128], BF16)
                rm4 = sb(ph, "rm4", [128, 4])
                r32 = sb(ph, "r32", [128, 128])
                Bcm = S.buf("hconst", "own")
                S.dma("pool", cm4[:], cst["colmask4"], writes=[Bcm])
                S.dma("sp", rm4[:], cst["rowmask4"], writes=[Bcm])
                S.dma("sp", r32[:], cst["rmask32"], writes=[Bcm])
                wq = [sb(ph, "hw%d" % i, [128, 8, D], BF16) for i in range(3)]
                Bwq = [S.buf("hw%d" % i, "own") for i in range(3)]
                Tst = sb(ph, "Tst", [128, 8, 128])
                BT = [S.buf("T%d" % h) for h in range(8)]
                GT = S.group("GT")
                vb = sb(ph, "vb", [128, D], BF16)
                Bvb = S.buf("vb")
                oblk = [sb(ph, "oblk%d" % i, [128, D]) for i in range(2)]
                Bob = [S.buf("oblk%d" % i) for i in range(2)]
                NS_ = 2
                tl = []
                for s_ in range(NS_):
                    t = {}
                    for nm in ("q", "nz", "uu", "l1", "l2", "Lp", "Dd", "eD", "emD", "t1", "kf"):
                        t[nm] = sb(ph, "h_%s%d" % (nm, s_), [128, 128])
                    t["wc"] = sb(ph, "h_wc%d" % s_, [128, 4])
                    for nm in ("kout", "qpp", "At"):
                        t[nm] = sb(ph, "h_%s%d" % (nm, s_), [128, 128], BF16)
                    for nm in ("koe", "qe", "Tp"):
                        t[nm] = sb(ph, "h_%s%d" % (nm, s_), [128, 4, 128], BF16)
                    t["B"] = {nm: S.buf("h_%s%d" % (nm, s_)) for nm in
                              ("q", "nz", "uu", "l1", "l2", "Lp", "Dd", "eD", "emD", "t1", "kf", "wc", "kout", "qpp", "At", "koe", "qe", "Tp")}
                    tl.append(t)
                wv = h_win[j].rearrange("(k p) n -> p k n", p=128)
                un = 0
                for d in range(2):
                    S.dma("sp", maskH[:], cst["maskH"][d], reads=[], writes=[Bcm])
                    for w3 in range(3):
                        c0 = d * 3072 + w3 * 1024
                        S.dma("pool", wq[w3][:], wv[:, :, c0:c0 + 1024], writes=[Bwq[w3]])
                    S.dma("sp", Tst[:], st_h[j, d].rearrange("h k v -> k h v"), writes=BT, grp=GT)
                    order = list(range(NT)) if d == 0 else list(range(NT - 1, -1, -1))
                    for bi, i in enumerate(order):
                        seg = i // 2
                        first_of_seg = (i % 2 == 0) if d == 0 else (i % 2 == 1)
                        last_of_seg = not first_of_seg
                        corder = [0, 1, 2, 3] if d == 0 else [3, 2, 1, 0]
                        if first_of_seg and bi > 0:
                            S.op("dve", lambda e: e.tensor_scalar(out=Tst[:], in0=Tst[:], scalar1=flg[:, 0:1], scalar2=None, op0=ALU.mult),
                                 reads=BT + [Bc], writes=BT)
                        for nh in range(2):
                            pb = 6 + nh
                            for k in range(8):
                                S.op("pe", lambda e, i=i, k=k, nh=nh, pb=pb: e.matmul(
                                    out=PS[pb][:, :], lhsT=hcols(i, k), rhs=wq[2][:, k, nh * 512:(nh + 1) * 512],
                                    start=(k == 0), stop=(k == 7)), reads=[BhT[seg], Bwq[2]], writes=[BPS[pb]])
                            S.op("act", lambda e, nh=nh, pb=pb: e.activation(out=vb[:, nh * 512:(nh + 1) * 512], in_=PS[pb][:, :], func=AF.Copy),
                                 reads=[BPS[pb]], writes=[Bvb])
                        ob = oblk[bi % 2]
                        Bo = Bob[bi % 2]
                        for h in range(8):
                            t = tl[un % NS_]
                            B = t["B"]
                            pqz = un % 2
                            pg = 2 + un % 2
                            pu = 4 + un % 2
                            po = 6 + un % 2
                            un += 1
                            col = d * 8 + h
                            S.skip = (_LIM["step"] < 1) or (un > _LIM["heads"])
                            for w3, off in ((0, 0), (1, 128)):
                                for k in range(8):
                                    S.op("pe", lambda e, i=i, k=k, w3=w3, off=off, pqz=pqz, h=h: e.matmul(
                                        out=PS[pqz][:, off:off + 128], lhsT=wq[w3][:, k, h * 128:(h + 1) * 128], rhs=hcols(i, k),
                                        start=(k == 0), stop=(k == 7)), reads=[BhT[seg], Bwq[w3]], writes=[BPS[pqz]])
                            S.skip = (_LIM["step"] < 2) or (un > _LIM["heads"])
                            S.op("act", lambda e, t=t, pqz=pqz: e.activation(out=t["q"][:], in_=PS[pqz][:, 0:128], func=AF.Silu),
                                 reads=[BPS[pqz]], writes=[B["q"]])
                            S.skip = (_LIM["step"] < 2.2) or (un > _LIM["heads"])
                            S.op("dve", lambda e, t=t, pqz=pqz: e.tensor_scalar(out=t["nz"][:], in0=PS[pqz][:, 128:256], scalar1=-1.0, scalar2=80.0,
                                                                                 op0=ALU.mult, op1=ALU.min), reads=[BPS[pqz]], writes=[B["nz"]])
                            S.skip = (_LIM["step"] < 2.3) or (un > _LIM["heads"])
                            S.op("act", lambda e, t=t: e.activation(out=t["uu"][:], in_=t["nz"][:], func=AF.Exp), reads=[B["nz"]], writes=[B["uu"]])
                            S.skip = (_LIM["step"] < 2.4) or (un > _LIM["heads"])
                            S.op("act", lambda e, t=t, col=col: e.activation(out=t["l1"][:], in_=t["uu"][:], func=AF.Ln, bias=1.0,
                                                                              scale=lbv[:, col:col + 1]), reads=[B["uu"], Blb], writes=[B["l1"]])
                            S.skip = (_LIM["step"] < 2.5) or (un > _LIM["heads"])
                            S.op("act", lambda e, t=t: e.activation(out=t["l2"][:], in_=t["uu"][:], func=AF.Ln, bias=1.0, scale=1.0),
                                 reads=[B["uu"]], writes=[B["l2"]])
                            S.skip = (_LIM["step"] < 2.6) or (un > _LIM["heads"])
                            S.op("dve", lambda e, t=t: e.tensor_tensor(out=t["l1"][:], in0=t["l1"][:], in1=t["l2"][:], op=ALU.subtract),
                                 reads=[B["l1"], B["l2"]], writes=[B["l1"]])
                            S.skip = (_LIM["step"] < 2.7) or (un > _LIM["heads"])
                            S.op("dve", lambda e, t=t: e.tensor_tensor_scan(out=t["Lp"][:], data0=r32[:], data1=t["l1"][:], initial=0.0,
                                                                           op0=ALU.mult, op1=ALU.add), reads=[B["l1"], Bcm], writes=[B["Lp"]])
                            S.skip = (_LIM["step"] < 3) or (un > _LIM["heads"])
                            Lv = t["Lp"][:].rearrange("p (c t) -> p c t", t=32)
                            Dv = t["Dd"][:].rearrange("p (c t) -> p c t", t=32)
                            if d == 0:
                                S.op("dve", lambda e, Lv=Lv, Dv=Dv: e.tensor_tensor(out=Dv, in0=Lv[:, :, 31:32].to_broadcast([128, 4, 32]), in1=Lv,
                                                                                    op=ALU.subtract), reads=[B["Lp"]], writes=[B["Dd"]])
                            else:
                                S.op("dve", lambda e, t=t: e.tensor_tensor(out=t["Dd"][:], in0=t["Lp"][:], in1=t["l1"][:], op=ALU.subtract),
                                     reads=[B["Lp"], B["l1"]], writes=[B["Dd"]])
                            S.op("pool", lambda e, t=t: e.tensor_scalar_max(out=t["Dd"][:], in0=t["Dd"][:], scalar1=-80.0), reads=[B["Dd"]], writes=[B["Dd"]])
                            S.op("act", lambda e, t=t: e.activation(out=t["eD"][:], in_=t["Dd"][:], func=AF.Exp), reads=[B["Dd"]], writes=[B["eD"]])
                            S.op("act", lambda e, t=t: e.activation(out=t["emD"][:], in_=t["Dd"][:], func=AF.Exp, scale=-1.0), reads=[B["Dd"]], writes=[B["emD"]])
                            S.op("act", lambda e, t=t, Lv=Lv: e.activation(out=t["wc"][:].unsqueeze(2), in_=Lv[:, :, 31:32], func=AF.Exp),
                                 reads=[B["Lp"]], writes=[B["wc"]])
                            S.skip = (_LIM["step"] < 4) or (un > _LIM["heads"])
                            S.op("dve", lambda e, t=t: e.tensor_scalar_add(out=t["t1"][:], in0=t["uu"][:], scalar1=1.0), reads=[B["uu"]], writes=[B["t1"]])
                            S.op("dve", lambda e, t=t: e.reciprocal(out=t["t1"][:], in_=t["t1"][:]), reads=[B["t1"]], writes=[B["t1"]])
                            S.op("dve", lambda e, t=t, col=col: e.scalar_tensor_tensor(out=t["kf"][:], in0=t["uu"][:], scalar=oml[:, col:col + 1], in1=t["t1"][:],
                                                                                      op0=ALU.mult, op1=ALU.mult), reads=[B["uu"], B["t1"], Blb], writes=[B["kf"]])
                            S.op("pool", lambda e, t=t: e.tensor_tensor(out=t["kout"][:], in0=t["kf"][:], in1=t["eD"][:], op=ALU.mult),
                                 reads=[B["kf"], B["eD"]], writes=[B["kout"]])
                            S.op("pool", lambda e, t=t: e.tensor_tensor(out=t["qpp"][:], in0=t["q"][:], in1=t["emD"][:], op=ALU.mult),
                                 reads=[B["q"], B["emD"]], writes=[B["qpp"]])
                            S.skip = (_LIM["step"] < 5) or (un > _LIM["heads"])
                            pgb = PS[pg][:].bitcast(BF16)
                            S.op("pe", lambda e, t=t, pg=pg: e.matmul(out=PS[pg][:, 0:128], lhsT=t["kout"][:], rhs=t["qpp"][:], start=True, stop=True),
                                 reads=[B["kout"], B["qpp"]], writes=[BPS[pg]])
                            S.op("pe", lambda e, t=t, pgb=pgb: e.transpose(out=pgb[:, 512:640], in_=t["kout"][:], identity=identb[:]),
                                 reads=[B["kout"], Bc], writes=[BPS[pg]])
                            S.op("dve", lambda e, t=t, pg=pg: e.tensor_tensor(out=t["At"][:], in0=PS[pg][:, 0:128], in1=maskH[:], op=ALU.mult),
                                 reads=[BPS[pg], Bcm], writes=[B["At"]])
                            S.op("dve", lambda e, t=t, pgb=pgb: e.tensor_tensor(out=t["koe"][:], in0=pgb[:, 512:640].unsqueeze(1).to_broadcast([128, 4, 128]),
                                                                               in1=rm4[:].unsqueeze(2).to_broadcast([128, 4, 128]), op=ALU.mult),
                                 reads=[BPS[pg], Bcm], writes=[B["koe"]])
                            S.op("pool", lambda e, t=t: e.tensor_tensor(out=t["qe"][:], in0=t["qpp"][:].unsqueeze(1).to_broadcast([128, 4, 128]),
                                                                       in1=cm4[:], op=ALU.mult), reads=[B["qpp"], Bcm], writes=[B["qe"]])
                            S.skip = (_LIM["step"] < 6) or (un > _LIM["heads"])
                            for c in range(4):
                                S.op("pe", lambda e, t=t, c=c, pu=pu, h=h: e.matmul(out=PS[pu][:, c * 128:(c + 1) * 128], lhsT=t["koe"][:, c, :],
                                                                                  rhs=vb[:, h * 128:(h + 1) * 128], start=True, stop=True),
                                     reads=[B["koe"], Bvb], writes=[BPS[pu]])
                            S.skip = (_LIM["step"] < 7) or (un > _LIM["heads"])
                            for c in corder:
                                S.op("dve", lambda e, t=t, c=c, h=h: e.tensor_scalar(out=t["Tp"][:, c, :], in0=Tst[:, h, :], scalar1=t["wc"][:, c:c + 1],
                                                                                    scalar2=None, op0=ALU.mult), reads=[BT[h], B["wc"]], writes=[B["Tp"]])
                                S.op("dve", lambda e, t=t, c=c, h=h, pu=pu: e.scalar_tensor_tensor(out=Tst[:, h, :], in0=Tst[:, h, :], scalar=t["wc"][:, c:c + 1],
                                                                                                  in1=PS[pu][:, c * 128:(c + 1) * 128], op0=ALU.mult, op1=ALU.add),
                                     reads=[BT[h], B["wc"], BPS[pu]], writes=[BT[h]])
                            S.skip = (_LIM["step"] < 8) or (un > _LIM["heads"])
                            S.op("pe", lambda e, t=t, po=po, h=h: e.matmul(out=PS[po][:, 0:128], lhsT=t["At"][:], rhs=vb[:, h * 128:(h + 1) * 128],
                                                                           start=True, stop=False), reads=[B["At"], Bvb], writes=[BPS[po]])
                            for ci, c in enumerate(corder):
                                S.op("pe", lambda e, t=t, po=po, c=c, ci=ci: e.matmul(out=PS[po][:, 0:128], lhsT=t["qe"][:, c, :], rhs=t["Tp"][:, c, :],
                                                                                      start=False, stop=(ci == 3)), reads=[B["qe"], B["Tp"]], writes=[BPS[po]])
                            S.op("act", lambda e, ob=ob, po=po, h=h: e.activation(out=ob[:, h * 128:(h + 1) * 128], in_=PS[po][:, 0:128], func=AF.Copy),
                                 reads=[BPS[po]], writes=[Bo])
                        S.skip = False
                        S.dma("sp", oscr[d, i * 128:(i + 1) * 128, :], ob[:], reads=[Bo], writes=[Bscr[d]])
                        if last_of_seg:
                            S.dma("act", ns_h[j, seg, d].rearrange("h k v -> k h v"), Tst[:], reads=BT, writes=[By])
                S.barrier()

        def post_mixer(l, j, kind, hT, BhT):
            hcols = lambda i, k: hT[:, k, i // 2, PADL + (i % 2) * 128: PADL + (i % 2) * 128 + 128]
            with contextlib.ExitStack() as ph:
                wo = sb(ph, "wo", [128, 8, D], BF16)
                Bwo = S.buf("wo", "own")
                src_wo = h_wo[j] if kind == "h" else r_wo[j]
                S.dma("pool", wo[:], src_wo.rearrange("(k p) n -> p k n", p=128), writes=[Bwo])
                ot = [[sb(ph, "ot%d_%d" % (dd, s_), [128, D]) for s_ in range(2)] for dd in range(2)]
                Bot = [[S.buf("ot%d_%d" % (dd, s_), "own") for s_ in range(2)] for dd in range(2)]
                zb = sb(ph, "zb", [128, D], BF16)
                Bzb = S.buf("zb")
                zT = sb(ph, "zT", [128, 8, 128], BF16)
                BzT = S.buf("zT")
                lng = sb(ph, "lng", [128, D])
                lnb = sb(ph, "lnb", [128, D])
                Bln = S.buf("lnbc", "own")
                S.dma("sp", lng[:], ln_g[l, 0].partition_broadcast(128), writes=[Bln])
                S.dma("sp", lnb[:], ln_b[l, 0].partition_broadcast(128), writes=[Bln])
                gb_, Bgb_ = make_gbc(ph, 0)
                lt = [dict(v=sb(ph, "lnv%d" % i, [128, D]), Bv=S.buf("lnv%d" % i), st=sb(ph, "lnst%d" % i, [128, 16]), Bst=S.buf("lnst%d" % i),
                           lng=lng, lnb=lnb, Bln=Bln, gbc=gb_, Bgbc=Bgb_) for i in range(2)]
                if kind == "h":
                    wg = sb(ph, "wg", [128, 8, D], BF16)
                    Bwg = S.buf("wg", "own")
                    S.dma("pool", wg[:], h_win[j].rearrange("(k p) n -> p k n", p=128)[:, :, 6144:7168], writes=[Bwg])
                    ngbc = sb(ph, "ngbc", [128, D])
                    Bng = S.buf("ngbc", "own")
                    S.dma("sp", ngbc[:], h_ng[j].partition_broadcast(128), writes=[Bng])
                    sq = sb(ph, "sq", [128, D])
                    Bsq = S.buf("sq")
                    sgt = sb(ph, "sgt", [128, D])
                    Bsg = S.buf("sgt")
                    ss = sb(ph, "ss", [128, 8])
                    Bss = S.buf("ss")
                else:
                    gla = sb(ph, "gla", [128, 8, 160], BF16)
                    glb1 = sb(ph, "glb1", [128, D], BF16)
                    glb2 = sb(ph, "glb2", [32, D], BF16)
                    Bgl = S.buf("gl", "own")
                    S.dma("pool", gla[:], r_gla[j].rearrange("(k p) n -> p k n", p=128), writes=[Bgl])
                    S.dma("pool", glb1[:], r_glb[j, 0:128, :], writes=[Bgl])
                    S.dma("pool", glb2[:], r_glb[j, 128:160, :], writes=[Bgl])
                    muT = sb(ph, "muTg", [128, 8])
                    S.dma("sp", muT[:], r_vecT[j, 5], writes=[Bgl])
                    xt = sb(ph, "xg_t", [128, 8, 128])
                    xs = sb(ph, "xg_s", [128, 8, 128], BF16)
                    Bxt = S.buf("xg_t")
                    Bxs = S.buf("xg_s")
                    sg1 = sb(ph, "sg1", [128, 128], BF16)
                    sg2 = sb(ph, "sg2", [32, 128], BF16)
                    Bsgg = S.buf("sgg")
                for i in range(NT):
                    seg = i // 2
                    s_ = i % 2
                    for dd in range(2):
                        S.dma(hwq(), ot[dd][s_][:], oscr[dd, i * 128:(i + 1) * 128, :], reads=[Bscr[dd]], writes=[Bot[dd][s_]])
                    o0, o1 = ot[0][s_], ot[1][s_]
                    S.op("pool", lambda e, o0=o0, o1=o1: e.tensor_tensor(out=o0[:], in0=o0[:], in1=o1[:], op=ALU.add),
                         reads=[Bot[0][s_], Bot[1][s_]], writes=[Bot[0][s_]])
                    if kind == "h":
                        for nh in range(2):
                            for k in range(8):
                                S.op("pe", lambda e, i=i, k=k, nh=nh: e.matmul(out=PS[nh][:, :], lhsT=hcols(i, k), rhs=wg[:, k, nh * 512:(nh + 1) * 512],
                                                                               start=(k == 0), stop=(k == 7)), reads=[BhT[seg], Bwg], writes=[BPS[nh]])
                            S.op("act", lambda e, nh=nh: e.activation(out=sgt[:, nh * 512:(nh + 1) * 512], in_=PS[nh][:, :], func=AF.Silu),
                                 reads=[BPS[nh]], writes=[Bsg])
                        S.op("act", lambda e, o0=o0: e.activation(out=sq[:], in_=o0[:], func=AF.Square), reads=[Bot[0][s_]], writes=[Bsq])
                        S.op("dve", lambda e: e.tensor_reduce(out=ss[:], in_=sq[:].rearrange("p (h k) -> p h k", k=128), axis=AX.X, op=ALU.add),
                             reads=[Bsq], writes=[Bss])
                        S.op("dve", lambda e: e.tensor_scalar(out=ss[:], in0=ss[:], scalar1=1.0 / 128.0, scalar2=RMS_EPS, op0=ALU.mult, op1=ALU.add),
                             reads=[Bss], writes=[Bss])
                        S.op("act", lambda e: e.activation(out=ss[:], in_=ss[:], func=AF.Sqrt), reads=[Bss], writes=[Bss])
                        S.op("dve", lambda e: e.reciprocal(out=ss[:], in_=ss[:]), reads=[Bss], writes=[Bss])
                        S.op("dve", lambda e, o0=o0: e.tensor_tensor(out=o0[:].rearrange("p (h k) -> p h k", k=128), in0=o0[:].rearrange("p (h k) -> p h k", k=128),
                                                                    in1=ss[:].unsqueeze(2).to_broadcast([128, 8, 128]), op=ALU.mult),
                             reads=[Bot[0][s_], Bss], writes=[Bot[0][s_]])
                        S.op("pool", lambda e, o0=o0: e.tensor_tensor(out=o0[:], in0=o0[:], in1=ngbc[:], op=ALU.mult), reads=[Bot[0][s_], Bng], writes=[Bot[0][s_]])
                        S.op("dve", lambda e, o0=o0: e.tensor_tensor(out=zb[:], in0=o0[:], in1=sgt[:], op=ALU.mult), reads=[Bot[0][s_], Bsg], writes=[Bzb])
                    else:
                        b_ = i % 2
                        hL = hT[:, :, seg, PADL - 1 + b_ * 128: PADL - 1 + b_ * 128 + 128]
                        hR = hT[:, :, seg, PADL + 1 + b_ * 128: PADL + 1 + b_ * 128 + 128]
                        hC = hT[:, :, seg, PADL + b_ * 128: PADL + b_ * 128 + 128]
                        S.op("pool", lambda e, hL=hL, hR=hR: e.tensor_tensor(out=xt[:], in0=hL, in1=hR, op=ALU.add), reads=[BhT[seg]], writes=[Bxt])
                        S.op("dve", lambda e, hC=hC: e.scalar_tensor_tensor(out=xt[:], in0=xt[:], scalar=0.5, in1=hC, op0=ALU.mult, op1=ALU.subtract),
                             reads=[Bxt, BhT[seg]], writes=[Bxt])
                        S.op("dve", lambda e: e.tensor_tensor(out=xt[:], in0=xt[:], in1=muT[:].unsqueeze(2).to_broadcast([128, 8, 128]), op=ALU.mult),
                             reads=[Bxt, Bgl], writes=[Bxt])
                        S.op("dve", lambda e, hC=hC: e.tensor_tensor(out=xs[:], in0=xt[:], in1=hC, op=ALU.add), reads=[Bxt, BhT[seg]], writes=[Bxs])
                        for k in range(8):
                            S.op("pe", lambda e, k=k: e.matmul(out=PS[2][:, 0:128], lhsT=gla[:, k, 0:128], rhs=xs[:, k, :], start=(k == 0), stop=(k == 7)),
                                 reads=[Bgl, Bxs], writes=[BPS[2]])
                        for k in range(8):
                            S.op("pe", lambda e, k=k: e.matmul(out=PS[2][0:32, 128:256], lhsT=gla[:, k, 128:160], rhs=xs[:, k, :], start=(k == 0), stop=(k == 7)),
                                 reads=[Bgl, Bxs], writes=[BPS[2]])
                        S.op("act", lambda e: e.activation(out=sg1[:], in_=PS[2][:, 0:128], func=AF.Sigmoid), reads=[BPS[2]], writes=[Bsgg])
                        S.op("act", lambda e: e.activation(out=sg2[:], in_=PS[2][0:32, 128:256], func=AF.Sigmoid), reads=[BPS[2]], writes=[Bsgg])
                        for nh in range(2):
                            S.op("pe", lambda e, nh=nh: e.matmul(out=PS[nh][:, :], lhsT=sg1[:], rhs=glb1[:, nh * 512:(nh + 1) * 512], start=True, stop=False),
                                 reads=[Bsgg, Bgl], writes=[BPS[nh]])
                            S.op("pe", lambda e, nh=nh: e.matmul(out=PS[nh][:, :], lhsT=sg2[:], rhs=glb2[:, nh * 512:(nh + 1) * 512], start=False, stop=True),
                                 reads=[Bsgg, Bgl], writes=[BPS[nh]])
                            S.op("dve", lambda e, nh=nh, o0=o0: e.tensor_tensor(out=zb[:, nh * 512:(nh + 1) * 512], in0=o0[:, nh * 512:(nh + 1) * 512],
                                                                               in1=PS[nh][:, :], op=ALU.mult), reads=[Bot[0][s_], BPS[nh]], writes=[Bzb])
                    pzb = PS[3][:].bitcast(BF16)
                    for k in range(8):
                        S.op("pe", lambda e, k=k, pzb=pzb: e.transpose(out=pzb[:, k * 128:(k + 1) * 128], in_=zb[:, k * 128:(k + 1) * 128], identity=identb[:]),
                             reads=[Bzb, Bc], writes=[BPS[3]])
                    S.op("act", lambda e, pzb=pzb: e.activation(out=zT[:].rearrange("p k t -> p (k t)"), in_=pzb[:, 0:1024], func=AF.Copy),
                         reads=[BPS[3]], writes=[BzT])
                    pa, pbk = 4 + 2 * (i % 2), 5 + 2 * (i % 2)
                    for nh, pbank in ((0, pa), (1, pbk)):
                        for k in range(8):
                            S.op("pe", lambda e, k=k, nh=nh, pbank=pbank: e.matmul(out=PS[pbank][:, :], lhsT=zT[:, k, :], rhs=wo[:, k, nh * 512:(nh + 1) * 512],
                                                                                  start=(k == 0), stop=(k == 7)), reads=[BzT, Bwo], writes=[BPS[pbank]])
                    resid_ln(i, PS[pa], PS[pbk], BPS[pa], BPS[pbk], 0, lt[i % 2])
                S.barrier()

        def rwkv_layer(l, j, hT, BhT):
            with contextlib.ExitStack() as ph:
                vec = sb(ph, "rvec", [128, 16, 8])
                Bvec = S.buf("rvec", "own")
                S.dma("sp", vec[:], r_vecT[j].rearrange("n p k -> p n k"), writes=[Bvec])
                cF = {}
                Bk = S.buf("rconst", "own")
                for nm, shp, dt in (("rmask64", [128, 128], BF16), ("colmask2", [128, 2, 128], BF16), ("headind", [128, 2], F32),
                                    ("blockones", [128, 128], F32), ("sel64", [128, 2, 64], F32)):
                    cF[nm] = sb(ph, "rc_" + nm, shp, dt)
                    S.dma("pool" if dt == BF16 else "sp", cF[nm][:], cst[nm], writes=[Bk])
                mk1 = sb(ph, "mk1", [128, 256], BF16)
                mk2 = sb(ph, "mk2", [128, 256], BF16)
                mkn = sb(ph, "mkn", [128, 128], BF16)
                gng = sb(ph, "gng", [128, D])
                gnb = sb(ph, "gnb", [128, D])
                Bdirc = S.buf("dirc", "own")
                W3 = [sb(ph, "rw%d" % i, [128, 8, D], BF16) for i in range(3)]
                BW3 = [S.buf("rw%d" % i, "own") for i in range(3)]
                wla = sb(ph, "wla", [128, 8, 64], BF16)
                ala = sb(ph, "ala", [128, 8, 64], BF16)
                wlb = sb(ph, "wlb", [64, D], BF16)
                alb = sb(ph, "alb", [64, D], BF16)
                Blo = S.buf("lora", "own")
                Tst = sb(ph, "rT", [64, 16, 64])
                BT = [S.buf("rT%d" % h) for h in range(16)]
                GT = S.group("GT")
                xx = sb(ph, "xx", [128, 8, 128])
                Bxx = S.buf("xx")
                xsn = [sb(ph, "xs%d" % n, [128, 8, 128], BF16) for n in range(5)]
                Bxs = [S.buf("xs%d" % n) for n in range(5)]
                vf = sb(ph, "vf", [128, D])
                vb = vf
                Bvf = S.buf("vf")
                Bvb = Bvf
                tw = sb(ph, "tw", [64, 128], BF16)
                ta = sb(ph, "ta", [64, 128], BF16)
                Btw = S.buf("tw")
                Bta = S.buf("ta")
                yblk = sb(ph, "yblk", [128, D])
                Byb = S.buf("yblk")
                bon = sb(ph, "bon", [128, 16])
                Bbon = S.buf("bon")
                Bg = {nm: S.buf("gn_" + nm) for nm in ("st",)}
                gst = sb(ph, "gn_st", [128, 48])
                ct = {}
                for nm in ("sg", "aa", "kk", "kk2", "t", "Ls", "Dsg", "emD"):
                    ct[nm] = sb(ph, "c_" + nm, [128, 128])
                ct["rs"] = ct["kk2"]
                ct["prod"] = ct["kk2"]
                ct["k2"] = ct["t"]
                ct["Ds2"] = ct["Ls"]
                ct["bb"] = ct["aa"]
                ct["emD2"] = ct["Ls"]
                ct["eD"] = ct["Dsg"]
                ct["wc"] = sb(ph, "c_wc", [128, 2])
                KB = sb(ph, "KB", [128, 2, 128])
                QR = sb(ph, "QR", [128, 2, 128])
                KoT = sb(ph, "KoTm", [128, 2, 128])
                BoTn = sb(ph, "BoTnm", [128, 2, 128])
                Bct = {nm: S.buf("c_" + nm) for nm in list(ct.keys()) + ["KB", "QR", "KoT", "BoTn"]}
                Bct["rs"] = Bct["kk2"]
                Bct["prod"] = Bct["kk2"]
                Bct["k2"] = Bct["t"]
                Bct["Ds2"] = Bct["Ls"]
                Bct["bb"] = Bct["aa"]
                Bct["emD2"] = Bct["Ls"]
                Bct["eD"] = Bct["Dsg"]
                A1 = sb(ph, "A1", [128, 256])
                A2 = sb(ph, "A2", [128, 256])
                Nm = [sb(ph, "Nm0", [128, 128])]
                Zm = [sb(ph, "Zm0", [128, 128])]
                RH = sb(ph, "RH", [128, 128])
                RHb = RH
                RpE = sb(ph, "RpE", [64, 2, 128])
                P2T = sb(ph, "P2T", [64, 2, 64])
                T0p = sb(ph, "T0p", [64, 2, 64])
                wch = sb(ph, "wch", [64, 2])
                Bh = {nm: S.buf("h_" + nm) for nm in ("A1", "A2", "Nm0", "Zm0", "RH", "RpE", "P2T", "T0p", "wch")}
                Bh["RHb"] = Bh["RH"]
                for d in range(2):
                    S.dma("pool", mk1[:], cst["mk1"][d], writes=[Bdirc])
                    S.dma("pool", mk2[:], cst["mk2"][d], writes=[Bdirc])
                    S.dma("pool", mkn[:], cst["mkn"][d], writes=[Bdirc])
                    S.dma("sp", gng[:], r_gng[j, d].partition_broadcast(128), writes=[Bdirc])
                    S.dma("sp", gnb[:], r_gnb[j, d].partition_broadcast(128), writes=[Bdirc])
                    for n3 in range(3):
                        S.dma("pool", W3[n3][:], r_wrkv[j, n3].rearrange("(k p) n -> p k n", p=128)[:, :, d * D:(d + 1) * D], writes=[BW3[n3]])
                    S.dma("pool", wla[:], r_wla[j].rearrange("(k p) z r -> p k z r", p=128)[:, :, d, :], writes=[Blo])
                    S.dma("pool", ala[:], r_ala[j].rearrange("(k p) z r -> p k z r", p=128)[:, :, d, :], writes=[Blo])
                    S.dma("pool", wlb[:], r_wlb[j, d], writes=[Blo])
                    S.dma("pool", alb[:], r_alb[j, d], writes=[Blo])
                    S.dma("sp", Tst[:], st_r[j, d].rearrange("h k v -> k h v"), writes=BT, grp=GT)
                    vcol = lambda n: vec[:, 6 + n * 2 + d, :]
                    order = list(range(NT)) if d == 0 else list(range(NT - 1, -1, -1))
                    for bi, i in enumerate(order):
                        seg = i // 2
                        b_ = i % 2
                        first_of_seg = (b_ == 0) if d == 0 else (b_ == 1)
                        last_of_seg = not first_of_seg
                        corder = [0, 1] if d == 0 else [1, 0]
                        if first_of_seg and bi > 0:
                            S.op("dve", lambda e: e.tensor_scalar(out=Tst[:], in0=Tst[:], scalar1=flg[0:64, 0:1], scalar2=None, op0=ALU.mult),
                                 reads=BT + [Bc], writes=BT)
                        _rt[0] += 1
                        S.skip = (_LIM["rstep"] < 1) or (_rt[0] > _LIM["rtiles"])
                        hL = hT[:, :, seg, PADL - 1 + b_ * 128: PADL - 1 + b_ * 128 + 128]
                        hR = hT[:, :, seg, PADL + 1 + b_ * 128: PADL + 1 + b_ * 128 + 128]
                        hC = hT[:, :, seg, PADL + b_ * 128: PADL + b_ * 128 + 128]
                        S.op("pool", lambda e, hL=hL, hR=hR: e.tensor_tensor(out=xx[:], in0=hL, in1=hR, op=ALU.add), reads=[BhT[seg]], writes=[Bxx])
                        S.op("dve", lambda e, hC=hC: e.scalar_tensor_tensor(out=xx[:], in0=xx[:], scalar=0.5, in1=hC, op0=ALU.mult, op1=ALU.subtract),
                             reads=[Bxx, BhT[seg]], writes=[Bxx])
                        for n in range(5):
                            eng = "dve" if n % 2 == 0 else "pool"
                            S.op(eng, lambda e, n=n: e.tensor_tensor(out=xsn[n][:], in0=xx[:], in1=vec[:, n, :].unsqueeze(2).to_broadcast([128, 8, 128]), op=ALU.mult),
                                 reads=[Bxx, Bvec], writes=[Bxs[n]])
                            S.op(eng, lambda e, n=n, hC=hC: e.tensor_tensor(out=xsn[n][:], in0=xsn[n][:], in1=hC, op=ALU.add), reads=[Bxs[n], BhT[seg]], writes=[Bxs[n]])
                        S.skip = (_LIM["rstep"] < 2) or (_rt[0] > _LIM["rtiles"])
                        for nh in range(2):
                            pb = 6 + nh
                            for k in range(8):
                                S.op("pe", lambda e, k=k, nh=nh, pb=pb: e.matmul(out=PS[pb][:, :], lhsT=xsn[3][:, k, :], rhs=W3[2][:, k, nh * 512:(nh + 1) * 512],
                                                                               start=(k == 0), stop=(k == 7)), reads=[Bxs[3], BW3[2]], writes=[BPS[pb]])
                            S.op("act", lambda e, nh=nh, pb=pb: e.activation(out=vf[:, nh * 512:(nh + 1) * 512], in_=PS[pb][:, :], func=AF.Copy),
                                 reads=[BPS[pb]], writes=[Bvf])
                        S.skip = (_LIM["rstep"] < 3) or (_rt[0] > _LIM["rtiles"])
                        for k in range(8):
                            S.op("pe", lambda e, k=k: e.matmul(out=PS[5][0:64, 0:128], lhsT=wla[:, k, :], rhs=xsn[1][:, k, :], start=(k == 0), stop=(k == 7)),
                                 reads=[Blo, Bxs[1]], writes=[BPS[5]])
                        for k in range(8):
                            S.op("pe", lambda e, k=k: e.matmul(out=PS[5][0:64, 128:256], lhsT=ala[:, k, :], rhs=xsn[4][:, k, :], start=(k == 0), stop=(k == 7)),
                                 reads=[Blo, Bxs[4]], writes=[BPS[5]])
                        S.op("act", lambda e: e.activation(out=tw[:], in_=PS[5][0:64, 0:128], func=AF.Tanh), reads=[BPS[5]], writes=[Btw])
                        S.op("act", lambda e: e.activation(out=ta[:], in_=PS[5][0:64, 128:256], func=AF.Copy), reads=[BPS[5]], writes=[Bta])
                        for c in range(8):
                            cs = slice(c * 128, (c + 1) * 128)
                            S.skip = (_LIM["rstep"] < 4) or (_rt[0] > _LIM["rtiles"])
                            pp = PS[0]
                            for k in range(8):
                                S.op("pe", lambda e, k=k, cs=cs: e.matmul(out=pp[:, 0:128], lhsT=W3[0][:, k, cs], rhs=xsn[0][:, k, :], start=(k == 0), stop=(k == 7)),
                                     reads=[BW3[0], Bxs[0]], writes=[BPS[0]])
                            for k in range(8):
                                S.op("pe", lambda e, k=k, cs=cs: e.matmul(out=pp[:, 128:256], lhsT=W3[1][:, k, cs], rhs=xsn[2][:, k, :], start=(k == 0), stop=(k == 7)),
                                     reads=[BW3[1], Bxs[2]], writes=[BPS[0]])
                            S.op("pe", lambda e, cs=cs: e.matmul(out=pp[:, 256:384], lhsT=wlb[:, cs], rhs=tw[:], start=True, stop=True), reads=[Blo, Btw], writes=[BPS[0]])
                            S.op("pe", lambda e, cs=cs: e.matmul(out=pp[:, 384:512], lhsT=alb[:, cs], rhs=ta[:], start=True, stop=True), reads=[Blo, Bta], writes=[BPS[0]])
                            pr, pk = pp[:, 0:128], pp[:, 128:256]
                            S.skip = (_LIM["rstep"] < 5) or (_rt[0] > _LIM["rtiles"])
                            S.op("act", lambda e, c=c: e.activation(out=ct["sg"][:], in_=pp[:, 256:384], func=AF.Sigmoid, bias=vcol(0)[:, c:c + 1], scale=1.0),
                                 reads=[BPS[0], Bvec], writes=[Bct["sg"]])
                            S.op("act", lambda e, c=c: e.activation(out=ct["aa"][:], in_=pp[:, 384:512], func=AF.Sigmoid, bias=vcol(1)[:, c:c + 1], scale=1.0),
                                 reads=[BPS[0], Bvec], writes=[Bct["aa"]])
                            S.op("dve", lambda e, c=c: e.tensor_scalar(out=ct["kk"][:], in0=pk, scalar1=vcol(2)[:, c:c + 1], scalar2=None, op0=ALU.mult),
                                 reads=[BPS[0], Bvec], writes=[Bct["kk"]])
                            S.op("pool", lambda e: e.tensor_tensor(out=ct["kk2"][:], in0=ct["kk"][:], in1=ct["kk"][:], op=ALU.mult), reads=[Bct["kk"]], writes=[Bct["kk2"]])
                            S.op("pe", lambda e: e.matmul(out=PS[1][:, 0:128], lhsT=cF["blockones"][:], rhs=ct["kk2"][:], start=True, stop=True),
                                 reads=[Bk, Bct["kk2"]], writes=[BPS[1]])
                            S.op("dve", lambda e: e.tensor_scalar_max(out=ct["rs"][:], in0=PS[1][:, 0:128], scalar1=1e-24), reads=[BPS[1]], writes=[Bct["rs"]])
                            S.op("act", lambda e: e.activation(out=ct["rs"][:], in_=ct["rs"][:], func=AF.Sqrt), reads=[Bct["rs"]], writes=[Bct["rs"]])
                            S.op("dve", lambda e: e.reciprocal(out=ct["rs"][:], in_=ct["rs"][:]), reads=[Bct["rs"]], writes=[Bct["rs"]])
                            S.op("dve", lambda e: e.tensor_tensor(out=ct["kk"][:], in0=ct["kk"][:], in1=ct["rs"][:], op=ALU.mult), reads=[Bct["kk"], Bct["rs"]], writes=[Bct["kk"]])
                            S.skip = (_LIM["rstep"] < 6) or (_rt[0] > _LIM["rtiles"])
                            S.op("dve", lambda e, c=c: e.tensor_scalar(out=ct["t"][:], in0=ct["aa"][:], scalar1=1.0, scalar2=vcol(3)[:, c:c + 1], op0=ALU.subtract, op1=ALU.mult),
                                 reads=[Bct["aa"], Bvec], writes=[Bct["t"]])
                            S.op("dve", lambda e: e.scalar_tensor_tensor(out=ct["k2"][:], in0=ct["t"][:], scalar=1.0, in1=pk, op0=ALU.add, op1=ALU.mult),
                                 reads=[Bct["t"], BPS[0]], writes=[Bct["k2"]])
                            S.op("pool", lambda e: e.tensor_tensor(out=ct["bb"][:], in0=ct["kk"][:], in1=ct["aa"][:], op=ALU.mult), reads=[Bct["kk"], Bct["aa"]], writes=[Bct["bb"]])
                            S.op("dve", lambda e, c=c: e.scalar_tensor_tensor(out=ct["prod"][:], in0=pr, scalar=vcol(4)[:, c:c + 1], in1=ct["k2"][:], op0=ALU.mult, op1=ALU.mult),
                                 reads=[BPS[0], Bvec, Bct["k2"]], writes=[Bct["prod"]])
                            S.op("pe", lambda e, c=c: e.matmul(out=PS[4][:, 2 * c:2 * c + 2], lhsT=ct["prod"][:], rhs=cF["headind"][:], start=True, stop=True),
                                 reads=[Bct["prod"], Bk], writes=[BPS[4]])
                            S.skip = (_LIM["rstep"] < 7) or (_rt[0] > _LIM["rtiles"])
                            S.op("dve", lambda e: e.tensor_tensor_scan(out=ct["Ls"][:], data0=cF["rmask64"][:], data1=ct["sg"][:], initial=0.0, op0=ALU.mult, op1=ALU.add),
                                 reads=[Bct["sg"], Bk], writes=[Bct["Ls"]])
                            Lv = ct["Ls"][:].rearrange("p (c t) -> p c t", t=64)
                            Dv = ct["Dsg"][:].rearrange("p (c t) -> p c t", t=64)
                            if d == 0:
                                S.op("dve", lambda e, Lv=Lv, Dv=Dv: e.tensor_tensor(out=Dv, in0=Lv[:, :, 63:64].to_broadcast([128, 2, 64]), in1=Lv, op=ALU.subtract),
                                     reads=[Bct["Ls"]], writes=[Bct["Dsg"]])
                            else:
                                S.op("dve", lambda e: e.tensor_tensor(out=ct["Dsg"][:], in0=ct["Ls"][:], in1=ct["sg"][:], op=ALU.subtract),
                                     reads=[Bct["Ls"], Bct["sg"]], writes=[Bct["Dsg"]])
                            S.op("act", lambda e, Lv=Lv: e.activation(out=ct["wc"][:].unsqueeze(2), in_=Lv[:, :, 63:64], func=AF.Exp, scale=-C0), reads=[Bct["Ls"]], writes=[Bct["wc"]])
                            S.op("pool", lambda e: e.tensor_tensor(out=ct["Ds2"][:], in0=ct["Dsg"][:], in1=ct["sg"][:], op=ALU.add), reads=[Bct["Dsg"], Bct["sg"]], writes=[Bct["Ds2"]])
                            S.op("act", lambda e: e.activation(out=ct["emD"][:], in_=ct["Dsg"][:], func=AF.Exp, scale=C0), reads=[Bct["Dsg"]], writes=[Bct["emD"]])
                            S.op("act", lambda e: e.activation(out=ct["eD"][:], in_=ct["Dsg"][:], func=AF.Exp, scale=-C0), reads=[Bct["Dsg"]], writes=[Bct["eD"]])
                            S.op("act", lambda e: e.activation(out=ct["emD2"][:], in_=ct["Ds2"][:], func=AF.Exp, scale=C0), reads=[Bct["Ds2"]], writes=[Bct["emD2"]])
                            S.skip = (_LIM["rstep"] < 8) or (_rt[0] > _LIM["rtiles"])
                            S.op("pool", lambda e: e.tensor_tensor(out=KB[:, 0, :], in0=ct["k2"][:], in1=ct["eD"][:], op=ALU.mult), reads=[Bct["k2"], Bct["eD"]], writes=[Bct["KB"]])
                            S.op("pool", lambda e: e.tensor_tensor(out=KB[:, 1, :], in0=ct["bb"][:], in1=ct["eD"][:], op=ALU.mult), reads=[Bct["bb"], Bct["eD"]], writes=[Bct["KB"]])
                            S.op("pool", lambda e: e.tensor_tensor(out=QR[:, 0, :], in0=ct["kk"][:], in1=ct["emD2"][:], op=ALU.mult), reads=[Bct["kk"], Bct["emD2"]], writes=[Bct["QR"]])
                            S.op("dve", lambda e: e.tensor_tensor(out=QR[:, 1, :], in0=pr, in1=ct["emD"][:], op=ALU.mult), reads=[BPS[0], Bct["emD"]], writes=[Bct["QR"]])
                            S.skip = (_LIM["rstep"] < 9) or (_rt[0] > _LIM["rtiles"])
                            ptb = PS[1]
                            S.op("pe", lambda e, ptb=ptb: e.transpose(out=ptb[:, 128:256], in_=KB[:, 0, :], identity=ident[:]), reads=[Bct["KB"], Bc], writes=[BPS[1]])
                            S.op("pe", lambda e, ptb=ptb: e.transpose(out=ptb[:, 256:384], in_=KB[:, 1, :], identity=ident[:]), reads=[Bct["KB"], Bc], writes=[BPS[1]])
                            S.op("pe", lambda e, ptb=ptb: e.transpose(out=ptb[:, 384:512], in_=QR[:, 0, :], identity=ident[:]), reads=[Bct["QR"], Bc], writes=[BPS[1]])
                            hib = cF["headind"][:].unsqueeze(2).to_broadcast([128, 2, 128])
                            S.op("dve", lambda e, ptb=ptb, hib=hib: e.tensor_tensor(out=KoT[:], in0=ptb[:, 128:256].unsqueeze(1).to_broadcast([128, 2, 128]), in1=hib, op=ALU.mult),
                                 reads=[BPS[1], Bk], writes=[Bct["KoT"]])
                            S.op("dve", lambda e, ptb=ptb, hib=hib: e.scalar_tensor_tensor(out=BoTn[:], in0=ptb[:, 256:384].unsqueeze(1).to_broadcast([128, 2, 128]), scalar=-1.0, in1=hib,
                                                                                          op0=ALU.mult, op1=ALU.mult), reads=[BPS[1], Bk], writes=[Bct["BoTn"]])
                            for hh in range(2):
                                head = 2 * c + hh
                                prs = slice(64 * hh, 64 * hh + 64)
                                hc = slice(head * 64, head * 64 + 64)
                                hcl = slice(hh * 64, hh * 64 + 64)
                                S.skip = (_LIM["rstep"] < 10) or (_rt[0] > _LIM["rtiles"])
                                S.op("pe", lambda e, prs=prs: e.matmul(out=PS[2][:, 0:256], lhsT=KB[prs, 0, :], rhs=QR[prs, :, :].rearrange("p a t -> p (a t)"), start=True, stop=True),
                                     reads=[Bct["KB"], Bct["QR"]], writes=[BPS[2]])
                                S.op("pe", lambda e, prs=prs: e.matmul(out=PS[2][:, 256:512], lhsT=KB[prs, 1, :], rhs=QR[prs, :, :].rearrange("p a t -> p (a t)"), start=True, stop=True),
                                     reads=[Bct["KB"], Bct["QR"]], writes=[BPS[2]])
                                S.op("pe", lambda e, prs=prs: e.matmul(out=PS[3][:, 0:128], lhsT=QR[prs, 0, :], rhs=KB[prs, 1, :], start=True, stop=True),
                                     reads=[Bct["KB"], Bct["QR"]], writes=[BPS[3]])
                                S.op("dve", lambda e: e.tensor_tensor(out=A1[:], in0=PS[2][:, 0:256], in1=mk1[:], op=ALU.mult), reads=[BPS[2], Bdirc], writes=[Bh["A1"]])
                                S.op("dve", lambda e: e.tensor_tensor(out=A2[:], in0=PS[2][:, 256:512], in1=mk2[:], op=ALU.mult), reads=[BPS[2], Bdirc], writes=[Bh["A2"]])
                                S.op("dve", lambda e: e.tensor_tensor(out=Nm[0][:], in0=PS[3][:, 0:128], in1=mkn[:], op=ALU.mult), reads=[BPS[3], Bdirc], writes=[Bh["Nm0"]])
                                S.skip = (_LIM["rstep"] < 11) or (_rt[0] > _LIM["rtiles"])
                                S.op("pe", lambda e, hc=hc: e.matmul(out=PS[3][:, 128:192], lhsT=A1[:, 0:128], rhs=vb[:, hc], start=True, stop=True),
                                     reads=[Bh["A1"], Bvb], writes=[BPS[3]])
                                S.op("dve", lambda e, ptb=ptb, hh=hh: e.tensor_copy(out=RH[:, 0:64], in_=ptb[:, 384 + 64 * hh:448 + 64 * hh]), reads=[BPS[1]], writes=[Bh["RH"]])
                                S.op("act", lambda e: e.activation(out=RH[:, 64:128], in_=PS[3][:, 128:192], func=AF.Copy), reads=[BPS[3]], writes=[Bh["RH"]])
                                S.skip = (_LIM["rstep"] < 12) or (_rt[0] > _LIM["rtiles"])
                                for lvl in range(6):
                                    if lvl == 0:
                                        Zc, BZc = A2[:, 0:128], Bh["A2"]
                                    else:
                                        Zc, BZc = Zm[0][:], Bh["Zm0"]
                                    Nc, BNc = Nm[0][:], Bh["Nm0"]
                                    S.op("pe", lambda e, Zc=Zc: e.matmul(out=PS[3][:, 256:384], lhsT=Zc, rhs=RHb[:], start=True, stop=True),
                                         reads=[BZc, Bh["RHb"]], writes=[BPS[3]])
                                    S.op("dve", lambda e, lvl=lvl: e.tensor_tensor(out=RH[:], in0=RH[:], in1=PS[3][:, 256:384], op=(ALU.subtract if lvl == 0 else ALU.add)),
                                         reads=[Bh["RH"], BPS[3]], writes=[Bh["RH"]])
                                    if lvl < 5:
                                        S.op("pe", lambda e, Zc=Zc, Nc=Nc: e.matmul(out=PS[5][:, 0:128], lhsT=Nc, rhs=Zc, start=True, stop=True),
                                             reads=[BZc, BNc], writes=[BPS[5]])
                                        if lvl < 4:
                                            S.op("pe", lambda e, Zc=Zc, Nc=Nc: e.matmul(out=PS[5][:, 128:256], lhsT=Zc, rhs=Nc, start=True, stop=True),
                                                 reads=[BZc, BNc], writes=[BPS[5]])
                                        S.op("act", lambda e: e.activation(out=Zm[0][:], in_=PS[5][:, 0:128], func=AF.Copy), reads=[BPS[5]], writes=[Bh["Zm0"]])
                                        if lvl < 4:
                                            S.op("dve", lambda e: e.tensor_copy(out=Nm[0][:], in_=PS[5][:, 128:256]), reads=[BPS[5]], writes=[Bh["Nm0"]])
                                S.skip = (_LIM["rstep"] < 13) or (_rt[0] > _LIM["rtiles"])
                                S.op("pe", lambda e, hh=hh: e.matmul(out=PS[6][0:64, 0:128], lhsT=cF["sel64"][:, hh, :], rhs=QR[:, 1, :], start=True, stop=False),
                                     reads=[Bk, Bct["QR"]], writes=[BPS[6]])
                                S.op("pe", lambda e: e.matmul(out=PS[6][0:64, 0:128], lhsT=RHb[:, 0:64], rhs=A2[:, 128:256], start=False, stop=True),
                                     reads=[Bh["RHb"], Bh["A2"]], writes=[BPS[6]])
                                S.op("dve", lambda e: e.tensor_tensor(out=RpE[:], in0=PS[6][0:64, 0:128].unsqueeze(1).to_broadcast([64, 2, 128]), in1=cF["colmask2"][0:64], op=ALU.mult),
                                     reads=[BPS[6], Bk], writes=[Bh["RpE"]])
                                S.skip = (_LIM["rstep"] < 14) or (_rt[0] > _LIM["rtiles"])
                                for cc in range(2):
                                    ts = slice(64 * cc, 64 * cc + 64)
                                    S.op("pe", lambda e, cc=cc, ts=ts, hcl=hcl: e.matmul(out=PS[6][0:64, 128 + 64 * cc:192 + 64 * cc], lhsT=RHb[:, 0:64], rhs=BoTn[:, cc, hcl], start=True, stop=True),
                                         reads=[Bh["RHb"], Bct["BoTn"]], writes=[BPS[6]])
                                S.skip = (_LIM["rstep"] < 14.2) or (_rt[0] > _LIM["rtiles"])
                                S.op("dve", lambda e: e.tensor_tensor(out=P2T[:], in0=PS[6][0:64, 128:256].rearrange("p (c k) -> p c k", c=2),
                                                                      in1=ident[0:64, 0:64].unsqueeze(1).to_broadcast([64, 2, 64]), op=ALU.add),
                                     reads=[BPS[6], Bc], writes=[Bh["P2T"]])
                                S.skip = (_LIM["rstep"] < 14.3) or (_rt[0] > _LIM["rtiles"])
                                S.op("pe", lambda e, hh=hh: e.matmul(out=PS[6][0:64, 256:258], lhsT=cF["sel64"][:, hh, :], rhs=ct["wc"][:], start=True, stop=True),
                                     reads=[Bk, Bct["wc"]], writes=[BPS[6]])
                                S.skip = (_LIM["rstep"] < 14.4) or (_rt[0] > _LIM["rtiles"])
                                S.op("act", lambda e: e.activation(out=wch[:], in_=PS[6][0:64, 256:258], func=AF.Copy), reads=[BPS[6]], writes=[Bh["wch"]])
                                S.skip = (_LIM["rstep"] < 15) or (_rt[0] > _LIM["rtiles"])
                                for cc in corder:
                                    ts = slice(64 * cc, 64 * cc + 64)
                                    S.op("dve", lambda e, cc=cc, head=head: e.tensor_scalar(out=T0p[:, cc, :], in0=Tst[:, head, :], scalar1=wch[:, cc:cc + 1], scalar2=None, op0=ALU.mult),
                                         reads=[BT[head], Bh["wch"]], writes=[Bh["T0p"]])
                                    S.op("pe", lambda e, cc=cc, hcl=hcl, hc=hc: e.matmul(out=PS[7][0:64, 0:64], lhsT=KoT[:, cc, hcl], rhs=vb[:, hc], start=True, stop=False),
                                         reads=[Bct["KoT"], Bvb], writes=[BPS[7]])
                                    S.op("pe", lambda e, cc=cc, hcl=hcl: e.matmul(out=PS[7][0:64, 0:64], lhsT=BoTn[:, cc, hcl], rhs=RHb[:, 64:128], start=False, stop=False),
                                         reads=[Bct["BoTn"], Bh["RHb"]], writes=[BPS[7]])
                                    S.op("pe", lambda e, cc=cc: e.matmul(out=PS[7][0:64, 0:64], lhsT=P2T[:, cc, :], rhs=T0p[:, cc, :], start=False, stop=True),
                                         reads=[Bh["P2T"], Bh["T0p"]], writes=[BPS[7]])
                                    S.op("dve", lambda e, head=head: e.tensor_copy(out=Tst[:, head, :], in_=PS[7][0:64, 0:64]), reads=[BPS[7]], writes=[BT[head]])
                                S.skip = (_LIM["rstep"] < 16) or (_rt[0] > _LIM["rtiles"])
                                S.op("pe", lambda e, hc=hc: e.matmul(out=PS[7][:, 128:192], lhsT=A1[:, 128:256], rhs=vb[:, hc], start=True, stop=False),
                                     reads=[Bh["A1"], Bvb], writes=[BPS[7]])
                                S.op("pe", lambda e: e.matmul(out=PS[7][:, 128:192], lhsT=A2[:, 128:256], rhs=RHb[:, 64:128], start=False, stop=False),
                                     reads=[Bh["A2"], Bh["RHb"]], writes=[BPS[7]])
                                for cc in range(2):
                                    S.op("pe", lambda e, cc=cc: e.matmul(out=PS[7][:, 128:192], lhsT=RpE[:, cc, :], rhs=T0p[:, cc, :], start=False, stop=(cc == 1)),
                                         reads=[Bh["RpE"], Bh["T0p"]], writes=[BPS[7]])
                                S.op("act", lambda e, hc=hc: e.activation(out=yblk[:, hc], in_=PS[7][:, 128:192], func=AF.Copy), reads=[BPS[7]], writes=[Byb])
                        S.skip = (_LIM["rstep"] < 17) or (_rt[0] > _LIM["rtiles"])
                        yv = yblk[:].rearrange("p (h n) -> p h n", n=64)
                        sqt = xx[:].rearrange("p k t -> p (k t)")
                        sv = sqt.rearrange("p (h n) -> p h n", n=64)
                        S.op("act", lambda e: e.activation(out=bon[:], in_=PS[4][:, 0:16], func=AF.Copy), reads=[BPS[4]], writes=[Bbon])
                        S.op("dve", lambda e: e.tensor_reduce(out=gst[:, 0:16], in_=yv, axis=AX.X, op=ALU.add), reads=[Byb], writes=[Bg["st"]])
                        S.op("dve", lambda e: e.tensor_scalar(out=gst[:, 0:16], in0=gst[:, 0:16], scalar1=1.0 / 64.0, scalar2=None, op0=ALU.mult), reads=[Bg["st"]], writes=[Bg["st"]])
                        S.op("dve", lambda e: e.tensor_tensor(out=yv, in0=yv, in1=gst[:, 0:16].unsqueeze(2).to_broadcast([128, 16, 64]), op=ALU.subtract),
                             reads=[Byb, Bg["st"]], writes=[Byb])
                        S.op("act", lambda e: e.activation(out=sqt, in_=yblk[:], func=AF.Square), reads=[Byb], writes=[Bxx])
                        S.op("dve", lambda e: e.tensor_reduce(out=gst[:, 16:32], in_=sv, axis=AX.X, op=ALU.add), reads=[Bxx], writes=[Bg["st"]])
                        S.op("dve", lambda e: e.tensor_scalar(out=gst[:, 16:32], in0=gst[:, 16:32], scalar1=1.0 / 64.0, scalar2=GN_EPS, op0=ALU.mult, op1=ALU.add),
                             reads=[Bg["st"]], writes=[Bg["st"]])
                        S.op("act", lambda e: e.activation(out=gst[:, 16:32], in_=gst[:, 16:32], func=AF.Sqrt), reads=[Bg["st"]], writes=[Bg["st"]])
                        S.op("dve", lambda e: e.reciprocal(out=gst[:, 16:32], in_=gst[:, 16:32]), reads=[Bg["st"]], writes=[Bg["st"]])
                        S.op("dve", lambda e: e.tensor_tensor(out=yv, in0=yv, in1=gst[:, 16:32].unsqueeze(2).to_broadcast([128, 16, 64]), op=ALU.mult),
                             reads=[Byb, Bg["st"]], writes=[Byb])
                        S.op("pool", lambda e: e.tensor_tensor(out=yblk[:], in0=yblk[:], in1=gng[:], op=ALU.mult), reads=[Byb, Bdirc], writes=[Byb])
                        S.op("pool", lambda e: e.tensor_tensor(out=yblk[:], in0=yblk[:], in1=gnb[:], op=ALU.add), reads=[Byb, Bdirc], writes=[Byb])
                        vfv = vf[:].rearrange("p (h n) -> p h n", n=64)
                        S.op("dve", lambda e: e.tensor_tensor(out=vfv, in0=vfv, in1=bon[:].unsqueeze(2).to_broadcast([128, 16, 64]), op=ALU.mult),
                             reads=[Bvf, Bbon], writes=[Bvf])
                        S.op("pool", lambda e: e.tensor_tensor(out=yblk[:], in0=yblk[:], in1=vf[:], op=ALU.add), reads=[Byb, Bvf], writes=[Byb])
                        S.skip = False
                        S.dma("sp", oscr[d, i * 128:(i + 1) * 128, :], yblk[:], reads=[Byb], writes=[Bscr[d]])
                        if last_of_seg:
                            S.dma("act", ns_r[j, seg, d].rearrange("h k v -> k h v"), Tst[:], reads=BT, writes=[By])
                S.barrier()

        for l in range(n_layers):
            j = l // 2
            adaln(l)
            if mix:
                with contextlib.ExitStack() as lph:
                    hT = sb(lph, "hT", [128, 8, NSEG, HTW], BF16)
                    BhT = [S.buf("hT%d" % s_) for s_ in range(NSEG)]
                    S.op("pool", lambda e: e.memset(hT[:, :, :, PADL - 1:PADL], 0.0), writes=BhT)
                    S.op("pool", lambda e: e.memset(hT[:, :, :, PADL + 256:PADL + 257], 0.0), writes=BhT)
                    build_hT(lambda i, k: hT[:, k, i // 2, PADL + (i % 2) * 128: PADL + (i % 2) * 128 + 128], lambda i: BhT[i // 2], list(range(NT)), sc1p, 0)
                    if l % 2 == 1:
                        S.op("dve", lambda e: e.tensor_scalar(out=hT[:, :, 1:NSEG, PADL - 1], in0=hT[:, :, 0:NSEG - 1, PADL + 255], scalar1=flg[:, 0:1], scalar2=None, op0=ALU.mult),
                             reads=BhT + [Bc], writes=BhT)
                        S.op("dve", lambda e: e.tensor_scalar(out=hT[:, :, 0:NSEG - 1, PADL + 256], in0=hT[:, :, 1:NSEG, PADL], scalar1=flg[:, 0:1], scalar2=None, op0=ALU.mult),
                             reads=BhT + [Bc], writes=BhT)
                    S.barrier()
                    stopped = False
                    if stop == "hT" and l == n_layers - 1:
                        stopped = True
                    elif l % 2 == 0:
                        hgrn_layer(l, j, hT, BhT)
                        if stop == "mix" and l == n_layers - 1:
                            stopped = True
                        else:
                            post_mixer(l, j, "h", hT, BhT)
                    else:
                        rwkv_layer(l, j, hT, BhT)
                        if stop == "mix" and l == n_layers - 1:
                            stopped = True
                        else:
                            post_mixer(l, j, "r", hT, BhT)
                if stopped or (stop == "post" and l == n_layers - 1):
                    break
            ffn(l)

        for i in range(NT):
            S.dma(hwq(), y_out[i * 128:(i + 1) * 128, :], X[:, i, :], reads=[BX[i]], writes=[By])
        S.barrier()
        S.emit(block)
    return nc


_PROMPT_SLOTS = [[(p, p // 6) for p in range(32) if p % 6 == cix] for cix in range(6)]


def _fm(v):
    return np.ascontiguousarray(np.asarray(v, np.float32).reshape(8, 128).T)


def make_in_maps(inp, n_layers=4):
    NLW = n_layers
    NH = max(1, (n_layers + 1) // 2)
    NR = max(1, n_layers // 2)
    f = lambda a: np.ascontiguousarray(np.asarray(a, dtype=np.float32))
    consts = _consts()
    pos = _pos_table()
    shared = {
        "pos": pos,
        "ada_w": f(inp["ada_w"]),
        "ada_bT": np.ascontiguousarray(f(inp["ada_b"]).reshape(4, 48, 128).transpose(0, 2, 1)),
        "ln_g": f(inp["ln_g"]), "ln_b": f(inp["ln_b"]),
        "ffn_w_up": f(inp["ffn_w_up"]), "ffn_w_down": f(inp["ffn_w_down"]),
        "hgrn_w_in": f(inp["hgrn_w_in"]),
        "hgrn_lbT": np.ascontiguousarray(f(inp["hgrn_lb"]).reshape(2, 16, 128).transpose(0, 2, 1)),
        "hgrn_norm_g": f(inp["hgrn_norm_g"]), "hgrn_w_o": f(inp["hgrn_w_o"]),
        "rwkv_w_rkv": f(inp["rwkv_w_rkv"]), "rwkv_w_la": f(inp["rwkv_w_la"]), "rwkv_w_lb": f(inp["rwkv_w_lb"]),
        "rwkv_a_la": f(inp["rwkv_a_la"]), "rwkv_a_lb": f(inp["rwkv_a_lb"]),
        "rwkv_g_la": f(inp["rwkv_g_la"]), "rwkv_g_lb": f(inp["rwkv_g_lb"]),
        "rwkv_gn_g": f(inp["rwkv_gn_g"]), "rwkv_gn_b": f(inp["rwkv_gn_b"]), "rwkv_w_o": f(inp["rwkv_w_o"]),
    }
    vec = np.zeros((2, 16, 128, 8), np.float32)
    for j in range(2):
        for n in range(6):
            vec[j, n] = _fm(inp["rwkv_mu"][j, n])
        for n, nm in enumerate(("rwkv_w0", "rwkv_a0", "rwkv_k_k", "rwkv_k_a", "rwkv_r_k")):
            for d in range(2):
                vec[j, 6 + n * 2 + d] = _fm(inp[nm][j, d])
    shared["rwkv_vecT"] = vec
    for k, v in consts.items():
        shared["c_" + k] = v
    xp = f(inp["x_prompt"])
    xs = f(inp["x_sample"])
    sth = f(inp["state_hgrn"])
    strw = f(inp["state_rwkv"])
    maps = []
    for core in range(8):
        m = dict(shared)
        if core < 2:
            m["x_in"] = np.ascontiguousarray(xs[core])
            m["condT"] = _fm(inp["c"][core])
            fl = np.ones((128, 2), np.float32)
            m["st_h"] = np.ascontiguousarray(sth[core])
            m["st_r"] = np.ascontiguousarray(strw[core].transpose(0, 1, 2, 4, 3))
        else:
            slots = _PROMPT_SLOTS[core - 2]
            xin = np.zeros((NSEG, 256, D), np.float32)
            for s_ in range(NSEG):
                xin[s_] = xp[slots[s_][0]] if s_ < len(slots) else xp[slots[0][0]]
            m["x_in"] = xin.reshape(2048, D)
            m["condT"] = _fm(inp["c_ctx"])
            fl = np.zeros((128, 2), np.float32)
            m["st_h"] = np.zeros((2, 2, 8, 128, 128), np.float32)
            m["st_r"] = np.zeros((2, 2, 16, 64, 64), np.float32)
        m["flags"] = fl
        maps.append(m)
    cut = {"ada_w": NLW, "ada_bT": NLW, "ln_g": NLW, "ln_b": NLW, "ffn_w_up": NLW, "ffn_w_down": NLW,
           "hgrn_w_in": NH, "hgrn_norm_g": NH, "hgrn_w_o": NH, "rwkv_w_rkv": NR, "rwkv_w_la": NR, "rwkv_w_lb": NR,
           "rwkv_a_la": NR, "rwkv_a_lb": NR, "rwkv_g_la": NR, "rwkv_g_lb": NR, "rwkv_gn_g": NR, "rwkv_gn_b": NR, "rwkv_w_o": NR}
    if n_layers < 4:
        for k, n in cut.items():
            sl = np.ascontiguousarray(shared[k][:n])
            for m in maps:
                m[k] = sl
    return maps


def assemble(results):
    y_prompt = np.zeros((32, 256, D), np.float32)
    y_sample = np.zeros((2, 2048, D), np.float32)
    nsh = np.zeros((32, 2, 2, 8, 128, 128), np.float32)
    nsr = np.zeros((32, 2, 2, 16, 64, 64), np.float32)
    for core in range(8):
        r = results[core]
        if core < 2:
            y_sample[core] = r["y_out"]
        else:
            slots = _PROMPT_SLOTS[core - 2]
            yo = r["y_out"].reshape(NSEG, 256, D)
            for s_, (p, _) in enumerate(slots):
                y_prompt[p] = yo[s_]
                nsh[p] = r["ns_h"][:, s_]
                nsr[p] = r["ns_r"][:, s_].transpose(0, 1, 2, 4, 3)
    return y_prompt, y_sample, nsh, nsr


_NC_CACHE = {}
_LIM = {"heads": 10 ** 9, "step": 99, "rstep": 99, "rtiles": 10 ** 9}
_rt = [0]


def kernel(**inputs):
    if "nc" not in _NC_CACHE:
        _NC_CACHE["nc"] = build_program()
    nc = _NC_CACHE["nc"]
    maps = make_in_maps(inputs)
    res = run_bass_kernel_spmd(nc, maps, core_ids=list(range(8)))
    return assemble(res.results)
```

```python
import contextlib
import numpy as np
import concourse.bass as bass
import concourse.mybir as mybir
from concourse.bass_utils import run_bass_kernel_spmd

F32 = mybir.dt.float32
BF16 = mybir.dt.bfloat16
ALU = mybir.AluOpType
AF = mybir.ActivationFunctionType
AX = mybir.AxisListType

D = 1024
NT = 16
NSEG = 8
DN_ALPHA = 8 ** 0.25
LN_EPS = 1e-5
RMS_EPS = 1e-6
GN_EPS = 64e-5
C0 = 0.606531
PADL = 2
HTW = 260


class SemGroup:
    def __init__(self, sem):
        self.sem = sem
        self.count = 0


class Buf:
    __slots__ = ("name", "w", "r", "grp", "excl")

    def __init__(self, name, grp=None):
        self.name = name
        self.w = []
        self.r = {}
        self.grp = grp
        self.excl = False


class _Rec:
    def __init__(self):
        self.call = None

    def __getattr__(self, name):
        def f(*a, **k):
            self.call = (name, a, k)
            return self
        return f


class Sched:
    ENG = ("pe", "act", "dve", "pool", "sp")

    def __init__(self, nc, stack):
        self.nc = nc
        self.stack = stack
        self.ops = {e: [] for e in self.ENG}
        self.esem = {e: stack.enter_context(nc.semaphore("es_" + e)) for e in self.ENG}
        self.targets = {e: set() for e in self.ENG}
        self.groups = []
        self.gcache = {}
        self.lastreal = {e: 0 for e in self.ENG}

    def group(self, key=None):
        if key is not None and key in self.gcache:
            return self.gcache[key]
        g = SemGroup(self.stack.enter_context(self.nc.semaphore("dg%d" % len(self.groups))))
        self.groups.append(g)
        if key is not None:
            self.gcache[key] = g
        return g

    def buf(self, name, grp=None):
        if grp == "own":
            grp = self.group(name)
        return Buf(name, grp)

    def _deps(self, eng, reads, writes):
        deps = []
        for b in reads:
            deps.extend(b.w)
        for b in writes:
            deps.extend(b.w)
            for k, v in b.r.items():
                if isinstance(k, str):
                    deps.append(("e", k, v))
                else:
                    deps.append(("d", k[1], v))
        waits = {}
        for d in deps:
            if d[0] == "e" and d[1] == eng and eng in ("pe", "sp"):
                continue
            key = (d[0], d[1])
            waits[key] = max(waits.get(key, 0), d[2])
        for k, v in waits.items():
            if k[0] == "e":
                self.targets[k[1]].add(v)
        return waits

    skip = False
    phase = ""

    def op(self, eng, fn, reads=(), writes=()):
        if self.skip:
            return
        ex = [b for b in reads if b.excl]
        if ex:
            writes = list(writes) + ex
        waits = self._deps(eng, reads, writes)
        rec = _Rec()
        fn(rec)
        assert rec.call is not None
        self.ops[eng].append([rec.call, waits, "c", None, self.phase])
        idx = len(self.ops[eng])
        self.lastreal[eng] = idx
        for b in reads:
            b.r[eng] = idx
        for b in writes:
            b.w = [("e", eng, idx)]
            b.r = {}
        return idx

    def dma(self, eng, out, in_, reads=(), writes=(), grp=None, **kw):
        if self.skip:
            return
        waits = self._deps(eng, reads, writes)
        g = grp
        if g is None:
            for b in writes:
                if b.grp is not None:
                    g = b.grp
        assert g is not None
        g.count += 1
        cnt = g.count

        kw2 = dict(kw)
        kw2["out"] = out
        kw2["in_"] = in_
        self.ops[eng].append([("dma_start", (), kw2), waits, "d", g])
        for b in reads:
            b.r[("d", g)] = cnt
        for b in writes:
            b.w = [("d", g, cnt)]
            b.r = {}

    def barrier(self):
        last = dict(self.lastreal)
        for e in self.ENG:
            waits = {}
            for o in self.ENG:
                if o != e and last[o] > 0:
                    waits[("e", o)] = last[o]
                    self.targets[o].add(last[o])
            for g in self.groups:
                if g.count > 0:
                    waits[("d", g)] = g.count
            self.ops[e].append([None, waits, "w", None])

    def emit(self, block):
        tval = {}
        for e in self.ENG:
            c = 0
            m = {}
            for i in range(1, len(self.ops[e]) + 1):
                if i in self.targets[e]:
                    c += 1
                    m[i] = c
            tval[e] = m
        sched = self

        def run(e, engobj):
            seen = {}
            for i, rec in enumerate(sched.ops[e], start=1):
                fn, waits = rec[0], rec[1]
                for k, v in waits.items():
                    if k[0] == "e":
                        val = tval[k[1]][v]
                        sem = sched.esem[k[1]]
                    else:
                        val = 16 * v
                        sem = k[1].sem
                    if seen.get(id(sem), 0) >= val:
                        continue
                    seen[id(sem)] = val
                    engobj.wait_ge(sem, val)
                if fn is None:
                    assert i not in tval[e]
                    continue
                ins = getattr(engobj, fn[0])(*fn[1], **fn[2])
                if rec[2] == "d":
                    ins.then_inc(rec[3].sem, 16)
                elif i in tval[e]:
                    ins.then_inc(sched.esem[e], 1)

        @block.tensor
        def _(pe):
            run("pe", pe)

        @block.scalar
        def _(act):
            run("act", act)

        @block.vector
        def _(dve):
            run("dve", dve)

        @block.gpsimd
        def _(pool):
            run("pool", pool)

        @block.sync
        def _(sp):
            run("sp", sp)


def _consts():
    c = {}
    idx = np.arange(128)
    c["ident"] = np.eye(128, dtype=np.float32)
    same32 = (idx[:, None] // 32) == (idx[None, :] // 32)
    mh = np.zeros((2, 128, 128), np.float32)
    mh[0] = same32 & (idx[:, None] <= idx[None, :])
    mh[1] = same32 & (idx[:, None] >= idx[None, :])
    c["maskH"] = mh
    cm4 = np.zeros((128, 4, 128), np.float32)
    for cc in range(4):
        cm4[:, cc, cc * 32:(cc + 1) * 32] = 1.0
    c["colmask4"] = cm4
    rm4 = np.zeros((128, 4), np.float32)
    rm4[idx, idx // 32] = 1.0
    c["rowmask4"] = rm4
    r32 = np.ones((128, 128), np.float32)
    r32[:, ::32] = 0.0
    c["rmask32"] = r32
    r64 = np.ones((128, 128), np.float32)
    r64[:, ::64] = 0.0
    c["rmask64"] = r64
    same64 = (idx[:, None] // 64) == (idx[None, :] // 64)
    st = [same64 & (idx[:, None] < idx[None, :]), same64 & (idx[:, None] > idx[None, :])]
    inc = [same64 & (idx[:, None] <= idx[None, :]), same64 & (idx[:, None] >= idx[None, :])]
    mk1 = np.zeros((2, 128, 256), np.float32)
    mk2 = np.zeros((2, 128, 256), np.float32)
    mkn = np.zeros((2, 128, 128), np.float32)
    for d in range(2):
        mk1[d, :, :128] = st[d]
        mk1[d, :, 128:] = inc[d]
        mk2[d, :, :128] = st[d]
        mk2[d, :, 128:] = -inc[d].astype(np.float32)
        mkn[d] = st[d].T
    c["mk1"] = mk1
    c["mk2"] = mk2
    c["mkn"] = mkn
    cm2 = np.zeros((128, 2, 128), np.float32)
    cm2[:, 0, :64] = 1.0
    cm2[:, 1, 64:] = 1.0
    c["colmask2"] = cm2
    hi = np.zeros((128, 2), np.float32)
    hi[idx, idx // 64] = 1.0
    c["headind"] = hi
    c["blockones"] = same64.astype(np.float32)
    sel = np.zeros((128, 2, 64), np.float32)
    for hh in range(2):
        sel[64 * hh + np.arange(64), hh, np.arange(64)] = 1.0
    c["sel64"] = sel
    return c


def _pos_table():
    rows, gw, quarter, half = 2048 // 64, 64, 256, 512
    omega = (1.0 / (10000.0 ** (np.arange(quarter, dtype=np.float32) / np.float32(quarter)))).astype(np.float32)
    r = np.arange(rows, dtype=np.float32)[:, None] * omega
    cc = np.arange(gw, dtype=np.float32)[:, None] * omega
    row_emb = np.concatenate([np.sin(r), np.cos(r)], -1)
    col_emb = np.concatenate([np.sin(cc), np.cos(cc)], -1)
    emb = np.concatenate([np.broadcast_to(row_emb[:, None, :], (rows, gw, half)),
                          np.broadcast_to(col_emb[None, :, :], (rows, gw, half))], -1)
    return np.ascontiguousarray(emb.reshape(rows * gw, D).astype(np.float32))


CONST_SHAPES = {
    "ident": [128, 128], "maskH": [2, 128, 128], "colmask4": [128, 4, 128], "rowmask4": [128, 4],
    "rmask32": [128, 128], "rmask64": [128, 128], "mk1": [2, 128, 256], "mk2": [2, 128, 256],
    "mkn": [2, 128, 128], "colmask2": [128, 2, 128], "headind": [128, 2], "blockones": [128, 128],
    "sel64": [128, 2, 64],
}


def build_program(n_layers=4, mix=True, dbg=False, stop=None):
    nc = bass.Bass("TRN2", target_bir_lowering=False)

    def din(name, shape):
        return nc.dram_tensor(name, list(shape), F32, kind="ExternalInput").ap()

    def dout(name, shape):
        return nc.dram_tensor(name, list(shape), F32, kind="ExternalOutput").ap()

    NLW = n_layers
    NH = max(1, (n_layers + 1) // 2)
    NR = max(1, n_layers // 2)
    x_in = din("x_in", [2048, D])
    pos = din("pos", [2048, D])
    condT = din("condT", [128, 8])
    flags = din("flags", [128, 2])
    ada_w = din("ada_w", [NLW, D, 6 * D])
    ada_bT = din("ada_bT", [NLW, 128, 48])
    ln_g = din("ln_g", [NLW, 2, D])
    ln_b = din("ln_b", [NLW, 2, D])
    w_up = din("ffn_w_up", [NLW, D, 4 * D])
    w_dn = din("ffn_w_down", [NLW, 4 * D, D])
    h_win = din("hgrn_w_in", [NH, D, 7 * D])
    h_lbT = din("hgrn_lbT", [2, 128, 16])
    h_ng = din("hgrn_norm_g", [NH, D])
    h_wo = din("hgrn_w_o", [NH, D, D])
    st_h = din("st_h", [2, 2, 8, 128, 128])
    st_r = din("st_r", [2, 2, 16, 64, 64])
    r_vecT = din("rwkv_vecT", [2, 16, 128, 8])
    r_wrkv = din("rwkv_w_rkv", [NR, 3, D, 2 * D])
    r_wla = din("rwkv_w_la", [NR, D, 2, 64])
    r_wlb = din("rwkv_w_lb", [NR, 2, 64, D])
    r_ala = din("rwkv_a_la", [NR, D, 2, 64])
    r_alb = din("rwkv_a_lb", [NR, 2, 64, D])
    r_gla = din("rwkv_g_la", [NR, D, 160])
    r_glb = din("rwkv_g_lb", [NR, 160, D])
    r_gng = din("rwkv_gn_g", [NR, 2, D])
    r_gnb = din("rwkv_gn_b", [NR, 2, D])
    r_wo = din("rwkv_w_o", [NR, D, D])
    cst = {k: din("c_" + k, v) for k, v in CONST_SHAPES.items()}

    y_out = dout("y_out", [2048, D])
    ns_h = dout("ns_h", [2, NSEG, 2, 8, 128, 128])
    ns_r = dout("ns_r", [2, NSEG, 2, 16, 64, 64])
    oscr = nc.dram_tensor("oscr", [2, 2048, D], F32).ap()
    dbgt = dout("dbg", [16, 128, D]) if dbg else None

    with contextlib.ExitStack() as st:
        S = Sched(nc, st)

        _uid = [0]

        def sb(stack, name, shape, dt=F32):
            _uid[0] += 1
            return stack.enter_context(nc.sbuf_tensor("%s_u%d" % (name, _uid[0]), list(shape), dt))

        X = sb(st, "X", [128, NT, D])
        BX = [S.buf("X%d" % i) for i in range(NT)]
        GX = S.group("GX")
        ident = sb(st, "ident", [128, 128])
        identb = sb(st, "identb", [128, 128], BF16)
        Bc = S.buf("consts", "own")
        flg = sb(st, "flg", [128, 2])
        scond = sb(st, "scond", [128, 8])
        modT = sb(st, "modT", [128, 48])
        sc1p = sb(st, "sc1p", [128, 8])
        sc2p = sb(st, "sc2p", [128, 8])
        Bmod = S.buf("mod")
        PS = [st.enter_context(nc.psum_tensor("ps%d" % i, [128, 512], F32)) for i in range(8)]
        BPS = [S.buf("ps%d" % i) for i in range(8)]
        for b_ in BPS:
            b_.excl = True
        Gout = S.group("Gout")
        By = S.buf("y_out", Gout)
        Gscr = S.group("Gscr")
        Bscr = [S.buf("oscr0", Gscr), S.buf("oscr1", Gscr)]

        block = st.enter_context(nc.Block())
        _dq = [0]
        Bdbg = S.buf("dbg", S.group("dbg"))

        def dump(slot, ap, bufs, p=128, n=None):
            if not dbg:
                return
            n = n if n is not None else ap.shape[-1]
            S.dma("pool", dbgt[slot, 0:p, 0:n], ap, reads=bufs, writes=[Bdbg])

        def hwq():
            _dq[0] += 1
            return "sp" if _dq[0] % 2 == 0 else "act"

        S.dma("sp", ident[:], cst["ident"], writes=[Bc])
        S.dma("pool", identb[:], cst["ident"], writes=[Bc])
        S.dma("sp", flg[:], flags, writes=[Bc])
        S.dma("sp", scond[:], condT, writes=[Bc])
        S.op("act", lambda e: e.activation(out=scond[:], in_=scond[:], func=AF.Silu), reads=[Bc], writes=[Bc])
        for i in range(NT):
            S.dma(hwq(), X[:, i, :], x_in[i * 128:(i + 1) * 128, :], writes=[BX[i]], grp=GX)
        with contextlib.ExitStack() as ph:
            pt = [sb(ph, "pos%d" % i, [128, D]) for i in range(2)]
            Bpt = [S.buf("pos%d" % i, "own") for i in range(2)]
            for i in range(NT):
                S.dma(hwq(), pt[i % 2][:], pos[i * 128:(i + 1) * 128, :], writes=[Bpt[i % 2]])
                eng = "dve"
                S.op(eng, lambda e, i=i: e.scalar_tensor_tensor(out=X[:, i, :], in0=pt[i % 2][:], scalar=flg[:, 1:2], in1=X[:, i, :],
                                                                op0=ALU.mult, op1=ALU.add),
                     reads=[Bpt[i % 2], Bc, BX[i]], writes=[BX[i]])
            dump(1, X[:, 0, :], [BX[0]])
            S.barrier()

        def run_interleaved(gens, width):
            it = iter(gens)
            active = []
            while True:
                while len(active) < width:
                    g = next(it, None)
                    if g is None:
                        break
                    active.append(g)
                if not active:
                    break
                for g in list(active):
                    try:
                        next(g)
                    except StopIteration:
                        active.remove(g)

        def adaln(l):
            with contextlib.ExitStack() as ph:
                slab = [sb(ph, "adas%d" % i, [128, 8, 256]) for i in range(2)]
                Bsl = [S.buf("adas%d" % i, "own") for i in range(2)]
                abT = sb(ph, "abT", [128, 48])
                Bab = S.buf("abT", "own")
                S.dma("sp", abT[:], ada_bT[l], writes=[Bab])
                wv = ada_w[l].rearrange("(k p) n -> p k n", p=128)
                acc = PS[0]
                for jg in range(24):
                    sl = jg % 2
                    S.dma(hwq(), slab[sl][:], wv[:, :, jg * 256:(jg + 1) * 256], writes=[Bsl[sl]])
                    for jj in range(2):
                        j = jg * 2 + jj
                        for k in range(8):
                            S.op("pe", lambda e, sl=sl, jj=jj, j=j, k=k: e.matmul(
                                out=acc[:, j:j + 1], lhsT=slab[sl][:, k, jj * 128:(jj + 1) * 128], rhs=scond[:, k:k + 1],
                                start=(k == 0), stop=(k == 7)), reads=[Bsl[sl], Bc], writes=[BPS[0]])
                S.op("dve", lambda e: e.tensor_tensor(out=modT[:], in0=acc[:, 0:48], in1=abT[:], op=ALU.add),
                     reads=[BPS[0], Bab], writes=[Bmod])
                S.op("dve", lambda e: e.tensor_scalar_add(out=sc1p[:], in0=modT[:, 8:16], scalar1=1.0), reads=[Bmod], writes=[Bmod])
                S.op("dve", lambda e: e.tensor_scalar_add(out=sc2p[:], in0=modT[:, 32:40], scalar1=1.0), reads=[Bmod], writes=[Bmod])
                if l == 0:
                    dump(0, modT[:], [Bmod])
                S.barrier()

        def make_gbc(ph, which):
            base = (16, 40)[which]
            gb = sb(ph, "gbc", [128, D])
            Bgb = S.buf("gbc%d" % which)
            gtmp = sb(ph, "gtmp", [128, 128])
            Bgt = S.buf("gtmp")
            for k in range(8):
                S.op("dve", lambda e, k=k: e.tensor_scalar(
                    out=gtmp[:], in0=modT[:, base + k:base + k + 1].to_broadcast([128, 128]),
                    scalar1=1.0 / DN_ALPHA, scalar2=None, op0=ALU.mult), reads=[Bmod], writes=[Bgt])
                S.op("pe", lambda e: e.transpose(out=PS[1][:, 0:128], in_=gtmp[:], identity=ident[:]),
                     reads=[Bgt, Bc], writes=[BPS[1]])
                S.op("act", lambda e, k=k: e.activation(out=gb[:, k * 128:(k + 1) * 128], in_=PS[1][:, 0:128], func=AF.Copy),
                     reads=[BPS[1]], writes=[Bgb])
            return gb, Bgb

        def build_hT(dst_fn, Bdst_fn, tiles, scp, shcol):
            n = 0
            for i in tiles:
                for kk in range(2):
                    pb = 2 + (n % 2)
                    n += 1
                    for k4 in range(4):
                        k = kk * 4 + k4
                        S.op("pe", lambda e, i=i, k=k, k4=k4, pb=pb: e.transpose(
                            out=PS[pb][:, k4 * 128:(k4 + 1) * 128], in_=X[:, i, k * 128:(k + 1) * 128], identity=ident[:]),
                            reads=[BX[i], Bc], writes=[BPS[pb]])
                    for k4 in range(4):
                        k = kk * 4 + k4
                        if k % 2 == 0:
                            S.op("act", lambda e, i=i, k=k, k4=k4, pb=pb: e.activation(
                                out=dst_fn(i, k), in_=PS[pb][:, k4 * 128:(k4 + 1) * 128], func=AF.Identity,
                                bias=modT[:, shcol + k:shcol + k + 1], scale=scp[:, k:k + 1]),
                                reads=[BPS[pb], Bmod], writes=[Bdst_fn(i)])
                        else:
                            S.op("dve", lambda e, i=i, k=k, k4=k4, pb=pb: e.tensor_scalar(
                                out=dst_fn(i, k), in0=PS[pb][:, k4 * 128:(k4 + 1) * 128],
                                scalar1=scp[:, k:k + 1], scalar2=modT[:, shcol + k:shcol + k + 1], op0=ALU.mult, op1=ALU.add),
                                reads=[BPS[pb], Bmod], writes=[Bdst_fn(i)])

        def resid_ln(i, pa, pb, Bpa, Bpb, which, lt):
            v, Bv = lt["v"], lt["Bv"]
            stt, Bst = lt["st"], lt["Bst"]
            S.op("dve", lambda e: e.tensor_tensor(out=v[:, 0:512], in0=pa[:, :], in1=lt["gbc"][:, 0:512], op=ALU.mult),
                 reads=[Bpa, lt["Bgbc"]], writes=[Bv])
            S.op("dve", lambda e: e.tensor_tensor(out=v[:, 512:1024], in0=pb[:, :], in1=lt["gbc"][:, 512:1024], op=ALU.mult),
                 reads=[Bpb, lt["Bgbc"]], writes=[Bv])
            S.op("pool", lambda e: e.tensor_tensor(out=v[:], in0=v[:], in1=X[:, i, :], op=ALU.add), reads=[Bv, BX[i]], writes=[Bv])
            if i == 0 and lt.get("dbg"):
                dump(5, v[:], [Bv])
            S.op("dve", lambda e: e.bn_stats(out=stt[:, 0:6], in_=v[:, 0:512]), reads=[Bv], writes=[Bst])
            S.op("dve", lambda e: e.bn_stats(out=stt[:, 6:12], in_=v[:, 512:1024]), reads=[Bv], writes=[Bst])
            S.op("dve", lambda e: e.bn_aggr(out=stt[:, 12:14], in_=stt[:, 0:12]), reads=[Bst], writes=[Bst])
            S.op("dve", lambda e: e.tensor_scalar_add(out=stt[:, 14:15], in0=stt[:, 13:14], scalar1=LN_EPS / (DN_ALPHA ** 2)), reads=[Bst], writes=[Bst])
            S.op("act", lambda e: e.activation(out=stt[:, 14:15], in_=stt[:, 14:15], func=AF.Sqrt), reads=[Bst], writes=[Bst])
            S.op("dve", lambda e: e.reciprocal(out=stt[:, 14:15], in_=stt[:, 14:15]), reads=[Bst], writes=[Bst])
            S.op("dve", lambda e: e.tensor_scalar(out=v[:], in0=v[:], scalar1=stt[:, 12:13], scalar2=stt[:, 14:15],
                                                  op0=ALU.subtract, op1=ALU.mult), reads=[Bv, Bst], writes=[Bv])
            if i == 0 and lt.get("dbg"):
                dump(6, stt[:], [Bst])
                dump(8, v[:], [Bv])
            S.op("pool", lambda e: e.tensor_tensor(out=v[:], in0=v[:], in1=lt["lng"][:], op=ALU.mult),
                 reads=[Bv, lt["Bln"]], writes=[Bv])
            S.op("pool", lambda e: e.tensor_tensor(out=X[:, i, :], in0=v[:], in1=lt["lnb"][:], op=ALU.add),
                 reads=[Bv, lt["Bln"]], writes=[BX[i]])

        def ffn(l):
            with contextlib.ExitStack() as ph:
                wd = sb(ph, "wd", [128, 32, D], BF16)
                Bwd = [S.buf("wd%d" % i, "own") for i in range(4)]
                ups = [sb(ph, "ups%d" % i, [128, 8, 256], BF16) for i in range(2)]
                Bups = [S.buf("ups%d" % i, "own") for i in range(2)]
                aT = sb(ph, "aT", [128, 32, 512], BF16)
                BaT = [S.buf("aT%d" % j) for j in range(32)]
                h2 = sb(ph, "h2", [128, 8, 512], BF16)
                Bh2 = [S.buf("h2_%d" % i) for i in range(4)]
                rl = [sb(ph, "rl%d" % i, [128, 512]) for i in range(2)]
                Brl = [S.buf("rl%d" % i) for i in range(2)]
                lng = sb(ph, "lng", [128, D])
                lnb = sb(ph, "lnb", [128, D])
                Bln = S.buf("lnbc", "own")
                S.dma("sp", lng[:], ln_g[l, 1].partition_broadcast(128), writes=[Bln])
                S.dma("sp", lnb[:], ln_b[l, 1].partition_broadcast(128), writes=[Bln])
                gb_, Bgb_ = make_gbc(ph, 1)
                lt = [dict(v=sb(ph, "lnv%d" % i, [128, D]), Bv=S.buf("lnv%d" % i), st=sb(ph, "lnst%d" % i, [128, 16]), Bst=S.buf("lnst%d" % i),
                           lng=lng, lnb=lnb, Bln=Bln, gbc=gb_, Bgbc=Bgb_) for i in range(1)]
                if l == 0:
                    lt[0]["dbg"] = True
                wdv = w_dn[l].rearrange("(j p) n -> p j n", p=128)
                for q in range(4):
                    S.dma("pool", wd[:, q * 8:(q + 1) * 8, :], wdv[:, q * 8:(q + 1) * 8, :], writes=[Bwd[q]])
                wuv = w_up[l].rearrange("(k p) n -> p k n", p=128)
                nev = 0
                for g in range(4):
                    tiles = list(range(g * 4, g * 4 + 4))
                    build_hT(lambda i, k: h2[:, k, (i % 4) * 128:(i % 4 + 1) * 128], lambda i: Bh2[i % 4], tiles, sc2p, 24)
                    for jg in range(16):
                        sl = jg % 2
                        S.dma("pool", ups[sl][:], wuv[:, :, jg * 256:(jg + 1) * 256], writes=[Bups[sl]])
                        for jj in range(2):
                            j = jg * 2 + jj
                            pb = 4 + (nev % 2)
                            for k in range(8):
                                S.op("pe", lambda e, sl=sl, jj=jj, k=k, pb=pb: e.matmul(
                                    out=PS[pb][:, :], lhsT=ups[sl][:, k, jj * 128:(jj + 1) * 128], rhs=h2[:, k, :],
                                    start=(k == 0), stop=(k == 7)), reads=[Bups[sl]] + Bh2, writes=[BPS[pb]])
                            r = nev % 2
                            nev += 1
                            S.op("act", lambda e, pb=pb, r=r: e.activation(out=rl[r][:], in_=PS[pb][:, :], func=AF.Relu),
                                 reads=[BPS[pb]], writes=[Brl[r]])
                            eng = "dve" if j % 2 == 0 else "pool"
                            S.op(eng, lambda e, j=j, r=r: e.tensor_tensor(out=aT[:, j, :], in0=rl[r][:], in1=rl[r][:], op=ALU.mult),
                                 reads=[Brl[r]], writes=[BaT[j]])
                    if l == 0 and g == 0:
                        dump(3, h2[:, :, 0:128].rearrange("p k t -> p k t"), Bh2, n=None) if False else None
                        for k in range(8):
                            if dbg:
                                S.dma("pool", dbgt[3, :, k * 128:(k + 1) * 128], h2[:, k, 0:128], reads=Bh2, writes=[Bdbg])
                        dump(4, aT[:, 0, :], [BaT[0]])
                    for ii, i in enumerate(tiles):
                        pa, pbk = 6, 7
                        for nh, pbank in ((0, pa), (1, pbk)):
                            for j in range(32):
                                S.op("pe", lambda e, j=j, ii=ii, nh=nh, pbank=pbank: e.matmul(
                                    out=PS[pbank][:, :], lhsT=aT[:, j, ii * 128:(ii + 1) * 128], rhs=wd[:, j, nh * 512:(nh + 1) * 512],
                                    start=(j == 0), stop=(j == 31)), reads=[BaT[j], Bwd[j // 8]], writes=[BPS[pbank]])
                        if l == 0 and i == 0 and dbg:
                            S.dma("pool", dbgt[9, :, 0:512], PS[pa][:, :], reads=[BPS[pa]], writes=[Bdbg]) if False else None
                        resid_ln(i, PS[pa], PS[pbk], BPS[pa], BPS[pbk], 1, lt[0])
                        if l == 0 and i == 0:
                            dump(7, X[:, 0, :], [BX[0]])
                S.barrier()

        def hgrn_layer(l, j, hT, BhT):
            hcols = lambda i, k: hT[:, k, i // 2, PADL + (i % 2) * 128: PADL + (i % 2) * 128 + 128]
            with contextlib.ExitStack() as ph:
                lbv = sb(ph, "lbv", [128, 16])
                oml = sb(ph, "oml", [128, 16])
                Blb = S.buf("lbv", "own")
                if j == 0:
                    S.op("dve", lambda e: e.memset(lbv[:], 0.0), writes=[Blb])
                else:
                    lb0 = sb(ph, "lb0", [128, 16])
                    S.dma("sp", lb0[:], h_lbT[0], writes=[Blb])
                    S.dma("sp", lbv[:], h_lbT[1], writes=[Blb])
                    S.op("dve", lambda e: e.tensor_tensor(out=lbv[:], in0=lbv[:], in1=lb0[:], op=ALU.subtract), reads=[Blb], writes=[Blb])
                    S.op("act", lambda e: e.activation(out=lbv[:], in_=lbv[:], func=AF.Sigmoid), reads=[Blb], writes=[Blb])
                S.op("dve", lambda e: e.tensor_scalar(out=oml[:], in0=lbv[:], scalar1=-1.0, scalar2=1.0, op0=ALU.mult, op1=ALU.add),
                     reads=[Blb], writes=[Blb])
                maskH = sb(ph, "maskH", [128, 128])
                cm4 = sb(ph, "cm4", [128, 4, 128], BF16)
                rm4 = sb(ph, "rm4", [128, 4])
                r32 = sb(ph, "r32", [128, 128])
                Bcm = S.buf("hconst", "own")
                S.dma("pool", cm4[:], cst["colmask4"], writes=[Bcm])
                S.dma("sp", rm4[:], cst["rowmask4"], writes=[Bcm])
                S.dma("sp", r32[:], cst["rmask32"], writes=[Bcm])
                wq = [sb(ph, "hw%d" % i, [128, 8, D], BF16) for i in range(3)]
                Bwq = [S.buf("hw%d" % i, "own") for i in range(3)]
                Tst = sb(ph, "Tst", [128, 8, 128])
                BT = [S.buf("T%d" % h) for h in range(8)]
                GT = S.group("GT")
                vb = sb(ph, "vb", [128, D], BF16)
                Bvb = S.buf("vb")
                oblk = [sb(ph, "oblk%d" % i, [128, D]) for i in range(2)]
                Bob = [S.buf("oblk%d" % i) for i in range(2)]
                NS_ = 2
                tl = []
                for s_ in range(NS_):
                    t = {}
                    for nm in ("q", "nz", "uu", "l1", "l2", "Lp", "Dd", "eD", "emD", "t1", "kf"):
                        t[nm] = sb(ph, "h_%s%d" % (nm, s_), [128, 128])
                    t["wc"] = sb(ph, "h_wc%d" % s_, [128, 4])
                    for nm in ("kout", "qpp", "At"):
                        t[nm] = sb(ph, "h_%s%d" % (nm, s_), [128, 128], BF16)
                    for nm in ("koe", "qe", "Tp"):
                        t[nm] = sb(ph, "h_%s%d" % (nm, s_), [128, 4, 128], BF16)
                    t["B"] = {nm: S.buf("h_%s%d" % (nm, s_)) for nm in
                              ("q", "nz", "uu", "l1", "l2", "Lp", "Dd", "eD", "emD", "t1", "kf", "wc", "kout", "qpp", "At", "koe", "qe", "Tp")}
                    tl.append(t)
                wv = h_win[j].rearrange("(k p) n -> p k n", p=128)
                un = 0
                for d in range(2):
                    S.dma("sp", maskH[:], cst["maskH"][d], reads=[], writes=[Bcm])
                    for w3 in range(3):
                        c0 = d * 3072 + w3 * 1024
                        S.dma("pool", wq[w3][:], wv[:, :, c0:c0 + 1024], writes=[Bwq[w3]])
                    S.dma("sp", Tst[:], st_h[j, d].rearrange("h k v -> k h v"), writes=BT, grp=GT)
                    order = list(range(NT)) if d == 0 else list(range(NT - 1, -1, -1))
                    for bi, i in enumerate(order):
                        seg = i // 2
                        first_of_seg = (i % 2 == 0) if d == 0 else (i % 2 == 1)
                        last_of_seg = not first_of_seg
                        corder = [0, 1, 2, 3] if d == 0 else [3, 2, 1, 0]
                        if first_of_seg and bi > 0:
                            S.op("dve", lambda e: e.tensor_scalar(out=Tst[:], in0=Tst[:], scalar1=flg[:, 0:1], scalar2=None, op0=ALU.mult),
                                 reads=BT + [Bc], writes=BT)
                        for nh in range(2):
                            pb = 6 + nh
                            for k in range(8):
                                S.op("pe", lambda e, i=i, k=k, nh=nh, pb=pb: e.matmul(
                                    out=PS[pb][:, :], lhsT=hcols(i, k), rhs=wq[2][:, k, nh * 512:(nh + 1) * 512],
                                    start=(k == 0), stop=(k == 7)), reads=[BhT[seg], Bwq[2]], writes=[BPS[pb]])
                            S.op("act", lambda e, nh=nh, pb=pb: e.activation(out=vb[:, nh * 512:(nh + 1) * 512], in_=PS[pb][:, :], func=AF.Copy),
                                 reads=[BPS[pb]], writes=[Bvb])
                        ob = oblk[bi % 2]
                        Bo = Bob[bi % 2]
                        for h in range(8):
                            t = tl[un % NS_]
                            B = t["B"]
                            pqz = un % 2
                            pg = 2 + un % 2
                            pu = 4 + un % 2
                            po = 6 + un % 2
                            un += 1
                            col = d * 8 + h
                            S.skip = (_LIM["step"] < 1) or (un > _LIM["heads"])
                            for w3, off in ((0, 0), (1, 128)):
                                for k in range(8):
                                    S.op("pe", lambda e, i=i, k=k, w3=w3, off=off, pqz=pqz, h=h: e.matmul(
                                        out=PS[pqz][:, off:off + 128], lhsT=wq[w3][:, k, h * 128:(h + 1) * 128], rhs=hcols(i, k),
                                        start=(k == 0), stop=(k == 7)), reads=[BhT[seg], Bwq[w3]], writes=[BPS[pqz]])
                            S.skip = (_LIM["step"] < 2) or (un > _LIM["heads"])
                            S.op("act", lambda e, t=t, pqz=pqz: e.activation(out=t["q"][:], in_=PS[pqz][:, 0:128], func=AF.Silu),
                                 reads=[BPS[pqz]], writes=[B["q"]])
                            S.skip = (_LIM["step"] < 2.2) or (un > _LIM["heads"])
                            S.op("dve", lambda e, t=t, pqz=pqz: e.tensor_scalar(out=t["nz"][:], in0=PS[pqz][:, 128:256], scalar1=-1.0, scalar2=80.0,
                                                                                 op0=ALU.mult, op1=ALU.min), reads=[BPS[pqz]], writes=[B["nz"]])
                            S.skip = (_LIM["step"] < 2.3) or (un > _LIM["heads"])
                            S.op("act", lambda e, t=t: e.activation(out=t["uu"][:], in_=t["nz"][:], func=AF.Exp), reads=[B["nz"]], writes=[B["uu"]])
                            S.skip = (_LIM["step"] < 2.4) or (un > _LIM["heads"])
                            S.op("act", lambda e, t=t, col=col: e.activation(out=t["l1"][:], in_=t["uu"][:], func=AF.Ln, bias=1.0,
                                                                              scale=lbv[:, col:col + 1]), reads=[B["uu"], Blb], writes=[B["l1"]])
                            S.skip = (_LIM["step"] < 2.5) or (un > _LIM["heads"])
                            S.op("act", lambda e, t=t: e.activation(out=t["l2"][:], in_=t["uu"][:], func=AF.Ln, bias=1.0, scale=1.0),
                                 reads=[B["uu"]], writes=[B["l2"]])
                            S.skip = (_LIM["step"] < 2.6) or (un > _LIM["heads"])
                            S.op("dve", lambda e, t=t: e.tensor_tensor(out=t["l1"][:], in0=t["l1"][:], in1=t["l2"][:], op=ALU.subtract),
                                 reads=[B["l1"], B["l2"]], writes=[B["l1"]])
                            S.skip = (_LIM["step"] < 2.7) or (un > _LIM["heads"])
                            S.op("dve", lambda e, t=t: e.tensor_tensor_scan(out=t["Lp"][:], data0=r32[:], data1=t["l1"][:], initial=0.0,
                                                                           op0=ALU.mult, op1=ALU.add), reads=[B["l1"], Bcm], writes=[B["Lp"]])
                            S.skip = (_LIM["step"] < 3) or (un > _LIM["heads"])
                            Lv = t["Lp"][:].rearrange("p (c t) -> p c t", t=32)
                            Dv = t["Dd"][:].rearrange("p (c t) -> p c t", t=32)
                            if d == 0:
                                S.op("dve", lambda e, Lv=Lv, Dv=Dv: e.tensor_tensor(out=Dv, in0=Lv[:, :, 31:32].to_broadcast([128, 4, 32]), in1=Lv,
                                                                                    op=ALU.subtract), reads=[B["Lp"]], writes=[B["Dd"]])
                            else:
                                S.op("dve", lambda e, t=t: e.tensor_tensor(out=t["Dd"][:], in0=t["Lp"][:], in1=t["l1"][:], op=ALU.subtract),
                                     reads=[B["Lp"], B["l1"]], writes=[B["Dd"]])
                            S.op("pool", lambda e, t=t: e.tensor_scalar_max(out=t["Dd"][:], in0=t["Dd"][:], scalar1=-80.0), reads=[B["Dd"]], writes=[B["Dd"]])
                            S.op("act", lambda e, t=t: e.activation(out=t["eD"][:], in_=t["Dd"][:], func=AF.Exp), reads=[B["Dd"]], writes=[B["eD"]])
                            S.op("act", lambda e, t=t: e.activation(out=t["emD"][:], in_=t["Dd"][:], func=AF.Exp, scale=-1.0), reads=[B["Dd"]], writes=[B["emD"]])
                            S.op("act", lambda e, t=t, Lv=Lv: e.activation(out=t["wc"][:].unsqueeze(2), in_=Lv[:, :, 31:32], func=AF.Exp),
                                 reads=[B["Lp"]], writes=[B["wc"]])
                            S.skip = (_LIM["step"] < 4) or (un > _LIM["heads"])
                            S.op("dve", lambda e, t=t: e.tensor_scalar_add(out=t["t1"][:], in0=t["uu"][:], scalar1=1.0), reads=[B["uu"]], writes=[B["t1"]])
                            S.op("dve", lambda e, t=t: e.reciprocal(out=t["t1"][:], in_=t["t1"][:]), reads=[B["t1"]], writes=[B["t1"]])
                            S.op("dve", lambda e, t=t, col=col: e.scalar_tensor_tensor(out=t["kf"][:], in0=t["uu"][:], scalar=oml[:, col:col + 1], in1=t["t1"][:],
                                                                                      op0=ALU.mult, op1=ALU.mult), reads=[B["uu"], B["t1"], Blb], writes=[B["kf"]])
                            S.op("pool", lambda e, t=t: e.tensor_tensor(out=t["kout"][:], in0=t["kf"][:], in1=t["eD"][:], op=ALU.mult),
                                 reads=[B["kf"], B["eD"]], writes=[B["kout"]])
                            S.op("pool", lambda e, t=t: e.tensor_tensor(out=t["qpp"][:], in0=t["q"][:], in1=t["emD"][:], op=ALU.mult),
                                 reads=[B["q"], B["emD"]], writes=[B["qpp"]])
                            S.skip = (_LIM["step"] < 5) or (un > _LIM["heads"])
                            pgb = PS[pg][:].bitcast(BF16)
                            S.op("pe", lambda e, t=t, pg=pg: e.matmul(out=PS[pg][:, 0:128], lhsT=t["kout"][:], rhs=t["qpp"][:], start=True, stop=True),
                                 reads=[B["kout"], B["qpp"]], writes=[BPS[pg]])
                            S.op("pe", lambda e, t=t, pgb=pgb: e.transpose(out=pgb[:, 512:640], in_=t["kout"][:], identity=identb[:]),
                                 reads=[B["kout"], Bc], writes=[BPS[pg]])
                            S.op("dve", lambda e, t=t, pg=pg: e.tensor_tensor(out=t["At"][:], in0=PS[pg][:, 0:128], in1=maskH[:], op=ALU.mult),
                                 reads=[BPS[pg], Bcm], writes=[B["At"]])
                            S.op("dve", lambda e, t=t, pgb=pgb: e.tensor_tensor(out=t["koe"][:], in0=pgb[:, 512:640].unsqueeze(1).to_broadcast([128, 4, 128]),
                                                                               in1=rm4[:].unsqueeze(2).to_broadcast([128, 4, 128]), op=ALU.mult),
                                 reads=[BPS[pg], Bcm], writes=[B["koe"]])
                            S.op("pool", lambda e, t=t: e.tensor_tensor(out=t["qe"][:], in0=t["qpp"][:].unsqueeze(1).to_broadcast([128, 4, 128]),
                                                                       in1=cm4[:], op=ALU.mult), reads=[B["qpp"], Bcm], writes=[B["qe"]])
                            S.skip = (_LIM["step"] < 6) or (un > _LIM["heads"])
                            for c in range(4):
                                S.op("pe", lambda e, t=t, c=c, pu=pu, h=h: e.matmul(out=PS[pu][:, c * 128:(c + 1) * 128], lhsT=t["koe"][:, c, :],
                                                                                  rhs=vb[:, h * 128:(h + 1) * 128], start=True, stop=True),
                                     reads=[B["koe"], Bvb], writes=[BPS[pu]])
                            S.skip = (_LIM["step"] < 7) or (un > _LIM["heads"])
                            for c in corder:
                                S.op("dve", lambda e, t=t, c=c, h=h: e.tensor_scalar(out=t["Tp"][:, c, :], in0=Tst[:, h, :], scalar1=t["wc"][:, c:c + 1],
                                                                                    scalar2=None, op0=ALU.mult), reads=[BT[h], B["wc"]], writes=[B["Tp"]])
                                S.op("dve", lambda e, t=t, c=c, h=h, pu=pu: e.scalar_tensor_tensor(out=Tst[:, h, :], in0=Tst[:, h, :], scalar=t["wc"][:, c:c + 1],
                                                                                                  in1=PS[pu][:, c * 128:(c + 1) * 128], op0=ALU.mult, op1=ALU.add),
                                     reads=[BT[h], B["wc"], BPS[pu]], writes=[BT[h]])
                            S.skip = (_LIM["step"] < 8) or (un > _LIM["heads"])
                            S.op("pe", lambda e, t=t, po=po, h=h: e.matmul(out=PS[po][:, 0:128], lhsT=t["At"][:], rhs=vb[:, h * 128:(h + 1) * 128],
                                                                           start=True, stop=False), reads=[B["At"], Bvb], writes=[BPS[po]])
                            for ci, c in enumerate(corder):
                                S.op("pe", lambda e, t=t, po=po, c=c, ci=ci: e.matmul(out=PS[po][:, 0:128], lhsT=t["qe"][:, c, :], rhs=t["Tp"][:, c, :],
                                                                                      start=False, stop=(ci == 3)), reads=[B["qe"], B["Tp"]], writes=[BPS[po]])
                            S.op("act", lambda e, ob=ob, po=po, h=h: e.activation(out=ob[:, h * 128:(h + 1) * 128], in_=PS[po][:, 0:128], func=AF.Copy),
                                 reads=[BPS[po]], writes=[Bo])
                        S.skip = False
                        S.dma("sp", oscr[d, i * 128:(i + 1) * 128, :], ob[:], reads=[Bo], writes=[Bscr[d]])
                        if last_of_seg:
                            S.dma("act", ns_h[j, seg, d].rearrange("h k v -> k h v"), Tst[:], reads=BT, writes=[By])
                S.barrier()

        def post_mixer(l, j, kind, hT, BhT):
            hcols = lambda i, k: hT[:, k, i // 2, PADL + (i % 2) * 128: PADL + (i % 2) * 128 + 128]
            with contextlib.ExitStack() as ph:
                wo = sb(ph, "wo", [128, 8, D], BF16)
                Bwo = S.buf("wo", "own")
                src_wo = h_wo[j] if kind == "h" else r_wo[j]
                S.dma("pool", wo[:], src_wo.rearrange("(k p) n -> p k n", p=128), writes=[Bwo])
                ot = [[sb(ph, "ot%d_%d" % (dd, s_), [128, D]) for s_ in range(2)] for dd in range(2)]
                Bot = [[S.buf("ot%d_%d" % (dd, s_), "own") for s_ in range(2)] for dd in range(2)]
                zb = sb(ph, "zb", [128, D], BF16)
                Bzb = S.buf("zb")
                zT = sb(ph, "zT", [128, 8, 128], BF16)
                BzT = S.buf("zT")
                lng = sb(ph, "lng", [128, D])
                lnb = sb(ph, "lnb", [128, D])
                Bln = S.buf("lnbc", "own")
                S.dma("sp", lng[:], ln_g[l, 0].partition_broadcast(128), writes=[Bln])
                S.dma("sp", lnb[:], ln_b[l, 0].partition_broadcast(128), writes=[Bln])
                gb_, Bgb_ = make_gbc(ph, 0)
                lt = [dict(v=sb(ph, "lnv%d" % i, [128, D]), Bv=S.buf("lnv%d" % i), st=sb(ph, "lnst%d" % i, [128, 16]), Bst=S.buf("lnst%d" % i),
                           lng=lng, lnb=lnb, Bln=Bln, gbc=gb_, Bgbc=Bgb_) for i in range(2)]
                if kind == "h":
                    wg = sb(ph, "wg", [128, 8, D], BF16)
                    Bwg = S.buf("wg", "own")
                    S.dma("pool", wg[:], h_win[j].rearrange("(k p) n -> p k n", p=128)[:, :, 6144:7168], writes=[Bwg])
                    ngbc = sb(ph, "ngbc", [128, D])
                    Bng = S.buf("ngbc", "own")
                    S.dma("sp", ngbc[:], h_ng[j].partition_broadcast(128), writes=[Bng])
                    sq = sb(ph, "sq", [128, D])
                    Bsq = S.buf("sq")
                    sgt = sb(ph, "sgt", [128, D])
                    Bsg = S.buf("sgt")
                    ss = sb(ph, "ss", [128, 8])
                    Bss = S.buf("ss")
                else:
                    gla = sb(ph, "gla", [128, 8, 160], BF16)
                    glb1 = sb(ph, "glb1", [128, D], BF16)
                    glb2 = sb(ph, "glb2", [32, D], BF16)
                    Bgl = S.buf("gl", "own")
                    S.dma("pool", gla[:], r_gla[j].rearrange("(k p) n -> p k n", p=128), writes=[Bgl])
                    S.dma("pool", glb1[:], r_glb[j, 0:128, :], writes=[Bgl])
                    S.dma("pool", glb2[:], r_glb[j, 128:160, :], writes=[Bgl])
                    muT = sb(ph, "muTg", [128, 8])
                    S.dma("sp", muT[:], r_vecT[j, 5], writes=[Bgl])
                    xt = sb(ph, "xg_t", [128, 8, 128])
                    xs = sb(ph, "xg_s", [128, 8, 128], BF16)
                    Bxt = S.buf("xg_t")
                    Bxs = S.buf("xg_s")
                    sg1 = sb(ph, "sg1", [128, 128], BF16)
                    sg2 = sb(ph, "sg2", [32, 128], BF16)
                    Bsgg = S.buf("sgg")
                for i in range(NT):
                    seg = i // 2
                    s_ = i % 2
                    for dd in range(2):
                        S.dma(hwq(), ot[dd][s_][:], oscr[dd, i * 128:(i + 1) * 128, :], reads=[Bscr[dd]], writes=[Bot[dd][s_]])
                    o0, o1 = ot[0][s_], ot[1][s_]
                    S.op("pool", lambda e, o0=o0, o1=o1: e.tensor_tensor(out=o0[:], in0=o0[:], in1=o1[:], op=ALU.add),
                         reads=[Bot[0][s_], Bot[1][s_]], writes=[Bot[0][s_]])
                    if kind == "h":
                        for nh in range(2):
                            for k in range(8):
                                S.op("pe", lambda e, i=i, k=k, nh=nh: e.matmul(out=PS[nh][:, :], lhsT=hcols(i, k), rhs=wg[:, k, nh * 512:(nh + 1) * 512],
                                                                               start=(k == 0), stop=(k == 7)), reads=[BhT[seg], Bwg], writes=[BPS[nh]])
                            S.op("act", lambda e, nh=nh: e.activation(out=sgt[:, nh * 512:(nh + 1) * 512], in_=PS[nh][:, :], func=AF.Silu),
                                 reads=[BPS[nh]], writes=[Bsg])
                        S.op("act", lambda e, o0=o0: e.activation(out=sq[:], in_=o0[:], func=AF.Square), reads=[Bot[0][s_]], writes=[Bsq])
                        S.op("dve", lambda e: e.tensor_reduce(out=ss[:], in_=sq[:].rearrange("p (h k) -> p h k", k=128), axis=AX.X, op=ALU.add),
                             reads=[Bsq], writes=[Bss])
                        S.op("dve", lambda e: e.tensor_scalar(out=ss[:], in0=ss[:], scalar1=1.0 / 128.0, scalar2=RMS_EPS, op0=ALU.mult, op1=ALU.add),
                             reads=[Bss], writes=[Bss])
                        S.op("act", lambda e: e.activation(out=ss[:], in_=ss[:], func=AF.Sqrt), reads=[Bss], writes=[Bss])
                        S.op("dve", lambda e: e.reciprocal(out=ss[:], in_=ss[:]), reads=[Bss], writes=[Bss])
                        S.op("dve", lambda e, o0=o0: e.tensor_tensor(out=o0[:].rearrange("p (h k) -> p h k", k=128), in0=o0[:].rearrange("p (h k) -> p h k", k=128),
                                                                    in1=ss[:].unsqueeze(2).to_broadcast([128, 8, 128]), op=ALU.mult),
                             reads=[Bot[0][s_], Bss], writes=[Bot[0][s_]])
                        S.op("pool", lambda e, o0=o0: e.tensor_tensor(out=o0[:], in0=o0[:], in1=ngbc[:], op=ALU.mult), reads=[Bot[0][s_], Bng], writes=[Bot[0][s_]])
                        S.op("dve", lambda e, o0=o0: e.tensor_tensor(out=zb[:], in0=o0[:], in1=sgt[:], op=ALU.mult), reads=[Bot[0][s_], Bsg], writes=[Bzb])
                    else:
                        b_ = i % 2
                        hL = hT[:, :, seg, PADL - 1 + b_ * 128: PADL - 1 + b_ * 128 + 128]
                        hR = hT[:, :, seg, PADL + 1 + b_ * 128: PADL + 1 + b_ * 128 + 128]
                        hC = hT[:, :, seg, PADL + b_ * 128: PADL + b_ * 128 + 128]
                        S.op("pool", lambda e, hL=hL, hR=hR: e.tensor_tensor(out=xt[:], in0=hL, in1=hR, op=ALU.add), reads=[BhT[seg]], writes=[Bxt])
                        S.op("dve", lambda e, hC=hC: e.scalar_tensor_tensor(out=xt[:], in0=xt[:], scalar=0.5, in1=hC, op0=ALU.mult, op1=ALU.subtract),
                             reads=[Bxt, BhT[seg]], writes=[Bxt])
                        S.op("dve", lambda e: e.tensor_tensor(out=xt[:], in0=xt[:], in1=muT[:].unsqueeze(2).to_broadcast([128, 8, 128]), op=ALU.mult),
                             reads=[Bxt, Bgl], writes=[Bxt])
                        S.op("dve", lambda e, hC=hC: e.tensor_tensor(out=xs[:], in0=xt[:], in1=hC, op=ALU.add), reads=[Bxt, BhT[seg]], writes=[Bxs])
                        for k in range(8):
                            S.op("pe", lambda e, k=k: e.matmul(out=PS[2][:, 0:128], lhsT=gla[:, k, 0:128], rhs=xs[:, k, :], start=(k == 0), stop=(k == 7)),
                                 reads=[Bgl, Bxs], writes=[BPS[2]])
                        for k in range(8):
                            S.op("pe", lambda e, k=k: e.matmul(out=PS[2][0:32, 128:256], lhsT=gla[:, k, 128:160], rhs=xs[:, k, :], start=(k == 0), stop=(k == 7)),
                                 reads=[Bgl, Bxs], writes=[BPS[2]])
                        S.op("act", lambda e: e.activation(out=sg1[:], in_=PS[2][:, 0:128], func=AF.Sigmoid), reads=[BPS[2]], writes=[Bsgg])
                        S.op("act", lambda e: e.activation(out=sg2[:], in_=PS[2][0:32, 128:256], func=AF.Sigmoid), reads=[BPS[2]], writes=[Bsgg])
                        for nh in range(2):
                            S.op("pe", lambda e, nh=nh: e.matmul(out=PS[nh][:, :], lhsT=sg1[:], rhs=glb1[:, nh * 512:(nh + 1) * 512], start=True, stop=False),
                                 reads=[Bsgg, Bgl], writes=[BPS[nh]])
                            S.op("pe", lambda e, nh=nh: e.matmul(out=PS[nh][:, :], lhsT=sg2[:], rhs=glb2[:, nh * 512:(nh + 1) * 512], start=False, stop=True),
                                 reads=[Bsgg, Bgl], writes=[BPS[nh]])
                            S.op("dve", lambda e, nh=nh, o0=o0: e.tensor_tensor(out=zb[:, nh * 512:(nh + 1) * 512], in0=o0[:, nh * 512:(nh + 1) * 512],
                                                                               in1=PS[nh][:, :], op=ALU.mult), reads=[Bot[0][s_], BPS[nh]], writes=[Bzb])
                    pzb = PS[3][:].bitcast(BF16)
                    for k in range(8):
                        S.op("pe", lambda e, k=k, pzb=pzb: e.transpose(out=pzb[:, k * 128:(k + 1) * 128], in_=zb[:, k * 128:(k + 1) * 128], identity=identb[:]),
                             reads=[Bzb, Bc], writes=[BPS[3]])
                    S.op("act", lambda e, pzb=pzb: e.activation(out=zT[:].rearrange("p k t -> p (k t)"), in_=pzb[:, 0:1024], func=AF.Copy),
                         reads=[BPS[3]], writes=[BzT])
                    pa, pbk = 4 + 2 * (i % 2), 5 + 2 * (i % 2)
                    for nh, pbank in ((0, pa), (1, pbk)):
                        for k in range(8):
                            S.op("pe", lambda e, k=k, nh=nh, pbank=pbank: e.matmul(out=PS[pbank][:, :], lhsT=zT[:, k, :], rhs=wo[:, k, nh * 512:(nh + 1) * 512],
                                                                                  start=(k == 0), stop=(k == 7)), reads=[BzT, Bwo], writes=[BPS[pbank]])
                    resid_ln(i, PS[pa], PS[pbk], BPS[pa], BPS[pbk], 0, lt[i % 2])
                S.barrier()

        def rwkv_layer(l, j, hT, BhT):
            with contextlib.ExitStack() as ph:
                vec = sb(ph, "rvec", [128, 16, 8])
                Bvec = S.buf("rvec", "own")
                S.dma("sp", vec[:], r_vecT[j].rearrange("n p k -> p n k"), writes=[Bvec])
                cF = {}
                Bk = S.buf("rconst", "own")
                for nm, shp, dt in (("rmask64", [128, 128], BF16), ("colmask2", [128, 2, 128], BF16), ("headind", [128, 2], F32),
                                    ("blockones", [128, 128], F32), ("sel64", [128, 2, 64], F32)):
                    cF[nm] = sb(ph, "rc_" + nm, shp, dt)
                    S.dma("pool" if dt == BF16 else "sp", cF[nm][:], cst[nm], writes=[Bk])
                mk1 = sb(ph, "mk1", [128, 256], BF16)
                mk2 = sb(ph, "mk2", [128, 256], BF16)
                mkn = sb(ph, "mkn", [128, 128], BF16)
                gng = sb(ph, "gng", [128, D])
                gnb = sb(ph, "gnb", [128, D])
                Bdirc = S.buf("dirc", "own")
                W3 = [sb(ph, "rw%d" % i, [128, 8, D], BF16) for i in range(3)]
                BW3 = [S.buf("rw%d" % i, "own") for i in range(3)]
                wla = sb(ph, "wla", [128, 8, 64], BF16)
                ala = sb(ph, "ala", [128, 8, 64], BF16)
                wlb = sb(ph, "wlb", [64, D], BF16)
                alb = sb(ph, "alb", [64, D], BF16)
                Blo = S.buf("lora", "own")
                Tst = sb(ph, "rT", [64, 16, 64])
                BT = [S.buf("rT%d" % h) for h in range(16)]
                GT = S.group("GT")
                xx = sb(ph, "xx", [128, 8, 128])
                Bxx = S.buf("xx")
                xs_r = sb(ph, "xs_r", [128, 8, 128], BF16)
                xs_k = sb(ph, "xs_k", [128, 8, 128], BF16)
                xs_s = sb(ph, "xs_s", [128, 8, 128], BF16)
                xsn = [xs_r, xs_s, xs_k, xs_s, xs_s]
                Bxs_r, Bxs_k, Bxs_s = S.buf("xs_r"), S.buf("xs_k"), S.buf("xs_s")
                Bxs = [Bxs_r, Bxs_s, Bxs_k, Bxs_s, Bxs_s]
                vf = sb(ph, "vf", [128, D])
                vb = vf
                Bvf = S.buf("vf")
                Bvb = Bvf
                tw = sb(ph, "tw", [64, 128], BF16)
                ta = sb(ph, "ta", [64, 128], BF16)
                Btw = S.buf("tw")
                Bta = S.buf("ta")
                yblk = sb(ph, "yblk", [128, D])
                Byb = S.buf("yblk")
                bon = sb(ph, "bon", [128, 16])
                Bbon = S.buf("bon")
                Bg = {nm: S.buf("gn_" + nm) for nm in ("st",)}
                gst = sb(ph, "gn_st", [128, 48])
                ct = {}
                for nm in ("sg", "aa", "kk", "kk2", "t", "Ls", "Dsg", "emD"):
                    ct[nm] = sb(ph, "c_" + nm, [128, 128])
                ct["rs"] = ct["kk2"]
                ct["prod"] = ct["kk2"]
                ct["k2"] = ct["t"]
                ct["Ds2"] = ct["Ls"]
                ct["bb"] = ct["aa"]
                ct["emD2"] = ct["Ls"]
                ct["eD"] = ct["Dsg"]
                ct["wc"] = sb(ph, "c_wc", [128, 2])
                KB = sb(ph, "KB", [128, 2, 128])
                QR = sb(ph, "QR", [128, 2, 128])
                KoT = sb(ph, "KoTm", [128, 2, 128])
                BoTn = sb(ph, "BoTnm", [128, 2, 128])
                Bct = {nm: S.buf("c_" + nm) for nm in list(ct.keys()) + ["KB", "QR", "KoT", "BoTn"]}
                Bct["rs"] = Bct["kk2"]
                Bct["prod"] = Bct["kk2"]
                Bct["k2"] = Bct["t"]
                Bct["Ds2"] = Bct["Ls"]
                Bct["bb"] = Bct["aa"]
                Bct["emD2"] = Bct["Ls"]
                Bct["eD"] = Bct["Dsg"]
                HX = []
                for s_ in range(2):
                    hx = dict(A1=sb(ph, "A1", [128, 256]), A2=sb(ph, "A2", [128, 256]), Nm=sb(ph, "Nm", [128, 128]), Zm=sb(ph, "Zm", [128, 128]),
                              RH=sb(ph, "RH", [128, 128]), RpE=sb(ph, "RpE", [64, 2, 128]), P2T=sb(ph, "P2T", [64, 2, 64]), T0p=sb(ph, "T0p", [64, 2, 64]),
                              wch=sb(ph, "wch", [64, 2]))
                    hx["B"] = {nm: S.buf("h%d_%s" % (s_, nm)) for nm in ("A1", "A2", "Nm", "Zm", "RH", "RpE", "P2T", "T0p", "wch")}
                    HX.append(hx)
                for d in range(2):
                    S.dma("pool", mk1[:], cst["mk1"][d], writes=[Bdirc])
                    S.dma("pool", mk2[:], cst["mk2"][d], writes=[Bdirc])
                    S.dma("pool", mkn[:], cst["mkn"][d], writes=[Bdirc])
                    S.dma("sp", gng[:], r_gng[j, d].partition_broadcast(128), writes=[Bdirc])
                    S.dma("sp", gnb[:], r_gnb[j, d].partition_broadcast(128), writes=[Bdirc])
                    for n3 in range(3):
                        S.dma("pool", W3[n3][:], r_wrkv[j, n3].rearrange("(k p) n -> p k n", p=128)[:, :, d * D:(d + 1) * D], writes=[BW3[n3]])
                    S.dma("pool", wla[:], r_wla[j].rearrange("(k p) z r -> p k z r", p=128)[:, :, d, :], writes=[Blo])
                    S.dma("pool", ala[:], r_ala[j].rearrange("(k p) z r -> p k z r", p=128)[:, :, d, :], writes=[Blo])
                    S.dma("pool", wlb[:], r_wlb[j, d], writes=[Blo])
                    S.dma("pool", alb[:], r_alb[j, d], writes=[Blo])
                    S.dma("sp", Tst[:], st_r[j, d].rearrange("h k v -> k h v"), writes=BT, grp=GT)
                    vcol = lambda n: vec[:, 6 + n * 2 + d, :]
                    order = list(range(NT)) if d == 0 else list(range(NT - 1, -1, -1))
                    for bi, i in enumerate(order):
                        seg = i // 2
                        b_ = i % 2
                        first_of_seg = (b_ == 0) if d == 0 else (b_ == 1)
                        last_of_seg = not first_of_seg
                        corder = [0, 1] if d == 0 else [1, 0]
                        if first_of_seg and bi > 0:
                            S.op("dve", lambda e: e.tensor_scalar(out=Tst[:], in0=Tst[:], scalar1=flg[0:64, 0:1], scalar2=None, op0=ALU.mult),
                                 reads=BT + [Bc], writes=BT)
                        _rt[0] += 1
                        S.skip = (_LIM["rstep"] < 1) or (_rt[0] > _LIM["rtiles"])
                        hL = hT[:, :, seg, PADL - 1 + b_ * 128: PADL - 1 + b_ * 128 + 128]
                        hR = hT[:, :, seg, PADL + 1 + b_ * 128: PADL + 1 + b_ * 128 + 128]
                        hC = hT[:, :, seg, PADL + b_ * 128: PADL + b_ * 128 + 128]
                        S.op("pool", lambda e, hL=hL, hR=hR: e.tensor_tensor(out=xx[:], in0=hL, in1=hR, op=ALU.add), reads=[BhT[seg]], writes=[Bxx])
                        S.op("dve", lambda e, hC=hC: e.scalar_tensor_tensor(out=xx[:], in0=xx[:], scalar=0.5, in1=hC, op0=ALU.mult, op1=ALU.subtract),
                             reads=[Bxx, BhT[seg]], writes=[Bxx])
                        def mk_xs(n, eng):
                            S.op(eng, lambda e: e.tensor_tensor(out=xsn[n][:], in0=xx[:], in1=vec[:, n, :].unsqueeze(2).to_broadcast([128, 8, 128]), op=ALU.mult),
                                 reads=[Bxx, Bvec], writes=[Bxs[n]])
                            S.op(eng, lambda e: e.tensor_tensor(out=xsn[n][:], in0=xsn[n][:], in1=hC, op=ALU.add), reads=[Bxs[n], BhT[seg]], writes=[Bxs[n]])
                        mk_xs(3, "dve")
                        mk_xs(0, "pool")
                        for nh in range(2):
                            pb = 6 + nh
                            for k in range(8):
                                S.op("pe", lambda e, k=k, nh=nh, pb=pb: e.matmul(out=PS[pb][:, :], lhsT=xsn[3][:, k, :], rhs=W3[2][:, k, nh * 512:(nh + 1) * 512],
                                                                               start=(k == 0), stop=(k == 7)), reads=[Bxs[3], BW3[2]], writes=[BPS[pb]])
                            S.op("act", lambda e, nh=nh, pb=pb: e.activation(out=vf[:, nh * 512:(nh + 1) * 512], in_=PS[pb][:, :], func=AF.Copy),
                                 reads=[BPS[pb]], writes=[Bvf])
                        mk_xs(1, "dve")
                        mk_xs(2, "pool")
                        for k in range(8):
                            S.op("pe", lambda e, k=k: e.matmul(out=PS[6][0:64, 0:128], lhsT=wla[:, k, :], rhs=xsn[1][:, k, :], start=(k == 0), stop=(k == 7)),
                                 reads=[Blo, Bxs[1]], writes=[BPS[6]])
                        mk_xs(4, "dve")
                        for k in range(8):
                            S.op("pe", lambda e, k=k: e.matmul(out=PS[6][0:64, 128:256], lhsT=ala[:, k, :], rhs=xsn[4][:, k, :], start=(k == 0), stop=(k == 7)),
                                 reads=[Blo, Bxs[4]], writes=[BPS[6]])
                        S.op("act", lambda e: e.activation(out=tw[:], in_=PS[6][0:64, 0:128], func=AF.Tanh), reads=[BPS[6]], writes=[Btw])
                        S.op("act", lambda e: e.activation(out=ta[:], in_=PS[6][0:64, 128:256], func=AF.Copy), reads=[BPS[6]], writes=[Bta])
                        for c in range(8):
                            cs = slice(c * 128, (c + 1) * 128)
                            S.skip = (_LIM["rstep"] < 4) or (_rt[0] > _LIM["rtiles"])
                            pp = PS[0]
                            for k in range(8):
                                S.op("pe", lambda e, k=k, cs=cs: e.matmul(out=pp[:, 0:128], lhsT=W3[0][:, k, cs], rhs=xsn[0][:, k, :], start=(k == 0), stop=(k == 7)),
                                     reads=[BW3[0], Bxs[0]], writes=[BPS[0]])
                            for k in range(8):
                                S.op("pe", lambda e, k=k, cs=cs: e.matmul(out=pp[:, 128:256], lhsT=W3[1][:, k, cs], rhs=xsn[2][:, k, :], start=(k == 0), stop=(k == 7)),
                                     reads=[BW3[1], Bxs[2]], writes=[BPS[0]])
                            S.op("pe", lambda e, cs=cs: e.matmul(out=pp[:, 256:384], lhsT=wlb[:, cs], rhs=tw[:], start=True, stop=True), reads=[Blo, Btw], writes=[BPS[0]])
                            S.op("pe", lambda e, cs=cs: e.matmul(out=pp[:, 384:512], lhsT=alb[:, cs], rhs=ta[:], start=True, stop=True), reads=[Blo, Bta], writes=[BPS[0]])
                            pr, pk = pp[:, 0:128], pp[:, 128:256]
                            S.skip = (_LIM["rstep"] < 5) or (_rt[0] > _LIM["rtiles"])
                            S.op("act", lambda e, c=c: e.activation(out=ct["sg"][:], in_=pp[:, 256:384], func=AF.Sigmoid, bias=vcol(0)[:, c:c + 1], scale=1.0),
                                 reads=[BPS[0], Bvec], writes=[Bct["sg"]])
                            S.op("act", lambda e, c=c: e.activation(out=ct["aa"][:], in_=pp[:, 384:512], func=AF.Sigmoid, bias=vcol(1)[:, c:c + 1], scale=1.0),
                                 reads=[BPS[0], Bvec], writes=[Bct["aa"]])
                            S.op("dve", lambda e, c=c: e.tensor_scalar(out=ct["kk"][:], in0=pk, scalar1=vcol(2)[:, c:c + 1], scalar2=None, op0=ALU.mult),
                                 reads=[BPS[0], Bvec], writes=[Bct["kk"]])
                            S.op("pool", lambda e: e.tensor_tensor(out=ct["kk2"][:], in0=ct["kk"][:], in1=ct["kk"][:], op=ALU.mult), reads=[Bct["kk"]], writes=[Bct["kk2"]])
                            S.op("pe", lambda e: e.matmul(out=PS[1][:, 0:128], lhsT=cF["blockones"][:], rhs=ct["kk2"][:], start=True, stop=True),
                                 reads=[Bk, Bct["kk2"]], writes=[BPS[1]])
                            S.op("dve", lambda e: e.tensor_scalar_max(out=ct["rs"][:], in0=PS[1][:, 0:128], scalar1=1e-24), reads=[BPS[1]], writes=[Bct["rs"]])
                            S.op("act", lambda e: e.activation(out=ct["rs"][:], in_=ct["rs"][:], func=AF.Sqrt), reads=[Bct["rs"]], writes=[Bct["rs"]])
                            S.op("dve", lambda e: e.reciprocal(out=ct["rs"][:], in_=ct["rs"][:]), reads=[Bct["rs"]], writes=[Bct["rs"]])
                            S.op("dve", lambda e: e.tensor_tensor(out=ct["kk"][:], in0=ct["kk"][:], in1=ct["rs"][:], op=ALU.mult), reads=[Bct["kk"], Bct["rs"]], writes=[Bct["kk"]])
                            S.skip = (_LIM["rstep"] < 6) or (_rt[0] > _LIM["rtiles"])
                            S.op("dve", lambda e, c=c: e.tensor_scalar(out=ct["t"][:], in0=ct["aa"][:], scalar1=1.0, scalar2=vcol(3)[:, c:c + 1], op0=ALU.subtract, op1=ALU.mult),
                                 reads=[Bct["aa"], Bvec], writes=[Bct["t"]])
                            S.op("dve", lambda e: e.scalar_tensor_tensor(out=ct["k2"][:], in0=ct["t"][:], scalar=1.0, in1=pk, op0=ALU.add, op1=ALU.mult),
                                 reads=[Bct["t"], BPS[0]], writes=[Bct["k2"]])
                            S.op("pool", lambda e: e.tensor_tensor(out=ct["bb"][:], in0=ct["kk"][:], in1=ct["aa"][:], op=ALU.mult), reads=[Bct["kk"], Bct["aa"]], writes=[Bct["bb"]])
                            S.op("dve", lambda e, c=c: e.scalar_tensor_tensor(out=ct["prod"][:], in0=pr, scalar=vcol(4)[:, c:c + 1], in1=ct["k2"][:], op0=ALU.mult, op1=ALU.mult),
                                 reads=[BPS[0], Bvec, Bct["k2"]], writes=[Bct["prod"]])
                            S.op("pe", lambda e, c=c: e.matmul(out=PS[7][:, 2 * c:2 * c + 2], lhsT=ct["prod"][:], rhs=cF["headind"][:], start=True, stop=True),
                                 reads=[Bct["prod"], Bk], writes=[BPS[7]])
                            S.skip = (_LIM["rstep"] < 7) or (_rt[0] > _LIM["rtiles"])
                            S.op("dve", lambda e: e.tensor_tensor_scan(out=ct["Ls"][:], data0=cF["rmask64"][:], data1=ct["sg"][:], initial=0.0, op0=ALU.mult, op1=ALU.add),
                                 reads=[Bct["sg"], Bk], writes=[Bct["Ls"]])
                            Lv = ct["Ls"][:].rearrange("p (c t) -> p c t", t=64)
                            Dv = ct["Dsg"][:].rearrange("p (c t) -> p c t", t=64)
                            if d == 0:
                                S.op("dve", lambda e, Lv=Lv, Dv=Dv: e.tensor_tensor(out=Dv, in0=Lv[:, :, 63:64].to_broadcast([128, 2, 64]), in1=Lv, op=ALU.subtract),
                                     reads=[Bct["Ls"]], writes=[Bct["Dsg"]])
                            else:
                                S.op("dve", lambda e: e.tensor_tensor(out=ct["Dsg"][:], in0=ct["Ls"][:], in1=ct["sg"][:], op=ALU.subtract),
                                     reads=[Bct["Ls"], Bct["sg"]], writes=[Bct["Dsg"]])
                            S.op("act", lambda e, Lv=Lv: e.activation(out=ct["wc"][:].unsqueeze(2), in_=Lv[:, :, 63:64], func=AF.Exp, scale=-C0), reads=[Bct["Ls"]], writes=[Bct["wc"]])
                            S.op("pool", lambda e: e.tensor_tensor(out=ct["Ds2"][:], in0=ct["Dsg"][:], in1=ct["sg"][:], op=ALU.add), reads=[Bct["Dsg"], Bct["sg"]], writes=[Bct["Ds2"]])
                            S.op("act", lambda e: e.activation(out=ct["emD"][:], in_=ct["Dsg"][:], func=AF.Exp, scale=C0), reads=[Bct["Dsg"]], writes=[Bct["emD"]])
                            S.op("act", lambda e: e.activation(out=ct["eD"][:], in_=ct["Dsg"][:], func=AF.Exp, scale=-C0), reads=[Bct["Dsg"]], writes=[Bct["eD"]])
                            S.op("act", lambda e: e.activation(out=ct["emD2"][:], in_=ct["Ds2"][:], func=AF.Exp, scale=C0), reads=[Bct["Ds2"]], writes=[Bct["emD2"]])
                            S.skip = (_LIM["rstep"] < 8) or (_rt[0] > _LIM["rtiles"])
                            S.op("pool", lambda e: e.tensor_tensor(out=KB[:, 0, :], in0=ct["k2"][:], in1=ct["eD"][:], op=ALU.mult), reads=[Bct["k2"], Bct["eD"]], writes=[Bct["KB"]])
                            S.op("pool", lambda e: e.tensor_tensor(out=KB[:, 1, :], in0=ct["bb"][:], in1=ct["eD"][:], op=ALU.mult), reads=[Bct["bb"], Bct["eD"]], writes=[Bct["KB"]])
                            S.op("pool", lambda e: e.tensor_tensor(out=QR[:, 0, :], in0=ct["kk"][:], in1=ct["emD2"][:], op=ALU.mult), reads=[Bct["kk"], Bct["emD2"]], writes=[Bct["QR"]])
                            S.op("dve", lambda e: e.tensor_tensor(out=QR[:, 1, :], in0=pr, in1=ct["emD"][:], op=ALU.mult), reads=[BPS[0], Bct["emD"]], writes=[Bct["QR"]])
                            S.skip = (_LIM["rstep"] < 9) or (_rt[0] > _LIM["rtiles"])
                            ptb = PS[1]
                            S.op("pe", lambda e, ptb=ptb: e.transpose(out=ptb[:, 128:256], in_=KB[:, 0, :], identity=ident[:]), reads=[Bct["KB"], Bc], writes=[BPS[1]])
                            S.op("pe", lambda e, ptb=ptb: e.transpose(out=ptb[:, 256:384], in_=KB[:, 1, :], identity=ident[:]), reads=[Bct["KB"], Bc], writes=[BPS[1]])
                            S.op("pe", lambda e, ptb=ptb: e.transpose(out=ptb[:, 384:512], in_=QR[:, 0, :], identity=ident[:]), reads=[Bct["QR"], Bc], writes=[BPS[1]])
                            hib = cF["headind"][:].unsqueeze(2).to_broadcast([128, 2, 128])
                            S.op("dve", lambda e, ptb=ptb, hib=hib: e.tensor_tensor(out=KoT[:], in0=ptb[:, 128:256].unsqueeze(1).to_broadcast([128, 2, 128]), in1=hib, op=ALU.mult),
                                 reads=[BPS[1], Bk], writes=[Bct["KoT"]])
                            S.op("dve", lambda e, ptb=ptb, hib=hib: e.scalar_tensor_tensor(out=BoTn[:], in0=ptb[:, 256:384].unsqueeze(1).to_broadcast([128, 2, 128]), scalar=-1.0, in1=hib,
                                                                                          op0=ALU.mult, op1=ALU.mult), reads=[BPS[1], Bk], writes=[Bct["BoTn"]])
                            def head_gen(c=c, hh=None, ptb=ptb):
                                head = 2 * c + hh
                                prs = slice(64 * hh, 64 * hh + 64)
                                hc = slice(head * 64, head * 64 + 64)
                                hcl = slice(hh * 64, hh * 64 + 64)
                                hx = HX[hh]
                                A1, A2, Nmt, Zmt, RH, RpE, P2T, T0p, wch = hx["A1"], hx["A2"], hx["Nm"], hx["Zm"], hx["RH"], hx["RpE"], hx["P2T"], hx["T0p"], hx["wch"]
                                Bh = hx["B"]
                                PA, PB, BA, BB = PS[2 + 2 * hh], PS[3 + 2 * hh], BPS[2 + 2 * hh], BPS[3 + 2 * hh]
                                qr2 = QR[prs, :, :].rearrange("p a t -> p (a t)")
                                S.op("pe", lambda e: e.matmul(out=PA[:, 0:256], lhsT=KB[prs, 0, :], rhs=qr2, start=True, stop=True), reads=[Bct["KB"], Bct["QR"]], writes=[BA])
                                S.op("pe", lambda e: e.matmul(out=PA[:, 256:512], lhsT=KB[prs, 1, :], rhs=qr2, start=True, stop=True), reads=[Bct["KB"], Bct["QR"]], writes=[BA])
                                S.op("pe", lambda e: e.matmul(out=PB[:, 0:128], lhsT=QR[prs, 0, :], rhs=KB[prs, 1, :], start=True, stop=True), reads=[Bct["KB"], Bct["QR"]], writes=[BB])
                                yield
                                S.op("dve", lambda e: e.tensor_tensor(out=A1[:], in0=PA[:, 0:256], in1=mk1[:], op=ALU.mult), reads=[BA, Bdirc], writes=[Bh["A1"]])
                                S.op("dve", lambda e: e.tensor_tensor(out=A2[:], in0=PA[:, 256:512], in1=mk2[:], op=ALU.mult), reads=[BA, Bdirc], writes=[Bh["A2"]])
                                S.op("dve", lambda e: e.tensor_tensor(out=Nmt[:], in0=PB[:, 0:128], in1=mkn[:], op=ALU.mult), reads=[BB, Bdirc], writes=[Bh["Nm"]])
                                yield
                                S.op("pe", lambda e: e.matmul(out=PB[:, 128:192], lhsT=A1[:, 0:128], rhs=vb[:, hc], start=True, stop=True), reads=[Bh["A1"], Bvb], writes=[BB])
                                S.op("dve", lambda e: e.tensor_copy(out=RH[:, 0:64], in_=ptb[:, 384 + 64 * hh:448 + 64 * hh]), reads=[BPS[1]], writes=[Bh["RH"]])
                                yield
                                S.op("act", lambda e: e.activation(out=RH[:, 64:128], in_=PB[:, 128:192], func=AF.Copy), reads=[BB], writes=[Bh["RH"]])
                                yield
                                for lvl in range(6):
                                    if lvl == 0:
                                        Zc, BZc = A2[:, 0:128], Bh["A2"]
                                    else:
                                        Zc, BZc = Zmt[:], Bh["Zm"]
                                    Nc, BNc = Nmt[:], Bh["Nm"]
                                    S.op("pe", lambda e, Zc=Zc: e.matmul(out=PB[:, 256:384], lhsT=Zc, rhs=RH[:], start=True, stop=True), reads=[BZc, Bh["RH"]], writes=[BB])
                                    if lvl < 5:
                                        S.op("pe", lambda e, Zc=Zc, Nc=Nc: e.matmul(out=PA[:, 0:128], lhsT=Nc, rhs=Zc, start=True, stop=True), reads=[BZc, BNc], writes=[BA])
                                        if lvl < 4:
                                            S.op("pe", lambda e, Zc=Zc, Nc=Nc: e.matmul(out=PA[:, 128:256], lhsT=Zc, rhs=Nc, start=True, stop=True), reads=[BZc, BNc], writes=[BA])
                                    yield
                                    S.op("dve", lambda e, lvl=lvl: e.tensor_tensor(out=RH[:], in0=RH[:], in1=PB[:, 256:384], op=(ALU.subtract if lvl == 0 else ALU.add)),
                                         reads=[Bh["RH"], BB], writes=[Bh["RH"]])
                                    if lvl < 5:
                                        S.op("act", lambda e: e.activation(out=Zmt[:], in_=PA[:, 0:128], func=AF.Copy), reads=[BA], writes=[Bh["Zm"]])
                                        if lvl < 4:
                                            S.op("act", lambda e: e.activation(out=Nmt[:], in_=PA[:, 128:256], func=AF.Copy), reads=[BA], writes=[Bh["Nm"]])
                                    yield
                                S.op("pe", lambda e: e.matmul(out=PA[0:64, 256:384], lhsT=cF["sel64"][:, hh, :], rhs=QR[:, 1, :], start=True, stop=False), reads=[Bk, Bct["QR"]], writes=[BA])
                                S.op("pe", lambda e: e.matmul(out=PA[0:64, 256:384], lhsT=RH[:, 0:64], rhs=A2[:, 128:256], start=False, stop=True), reads=[Bh["RH"], Bh["A2"]], writes=[BA])
                                for cc in range(2):
                                    S.op("pe", lambda e, cc=cc: e.matmul(out=PA[0:64, 384 + 64 * cc:448 + 64 * cc], lhsT=RH[:, 0:64], rhs=BoTn[:, cc, hcl], start=True, stop=True),
                                         reads=[Bh["RH"], Bct["BoTn"]], writes=[BA])
                                S.op("pe", lambda e: e.matmul(out=PB[0:64, 192:194], lhsT=cF["sel64"][:, hh, :], rhs=ct["wc"][:], start=True, stop=True), reads=[Bk, Bct["wc"]], writes=[BB])
                                yield
                                S.op("dve", lambda e: e.tensor_tensor(out=RpE[:], in0=PA[0:64, 256:384].unsqueeze(1).to_broadcast([64, 2, 128]), in1=cF["colmask2"][0:64], op=ALU.mult),
                                     reads=[BA, Bk], writes=[Bh["RpE"]])
                                S.op("dve", lambda e: e.tensor_tensor(out=P2T[:], in0=PA[0:64, 384:512].rearrange("p (c k) -> p c k", c=2),
                                                                      in1=ident[0:64, 0:64].unsqueeze(1).to_broadcast([64, 2, 64]), op=ALU.add), reads=[BA, Bc], writes=[Bh["P2T"]])
                                S.op("act", lambda e: e.activation(out=wch[:], in_=PB[0:64, 192:194], func=AF.Copy), reads=[BB], writes=[Bh["wch"]])
                                yield
                                for cc in corder:
                                    S.op("dve", lambda e, cc=cc: e.tensor_scalar(out=T0p[:, cc, :], in0=Tst[:, head, :], scalar1=wch[:, cc:cc + 1], scalar2=None, op0=ALU.mult),
                                         reads=[BT[head], Bh["wch"]], writes=[Bh["T0p"]])
                                    S.op("pe", lambda e, cc=cc: e.matmul(out=PB[0:64, 384:448], lhsT=KoT[:, cc, hcl], rhs=vb[:, hc], start=True, stop=False), reads=[Bct["KoT"], Bvb], writes=[BB])
                                    S.op("pe", lambda e, cc=cc: e.matmul(out=PB[0:64, 384:448], lhsT=BoTn[:, cc, hcl], rhs=RH[:, 64:128], start=False, stop=False), reads=[Bct["BoTn"], Bh["RH"]], writes=[BB])
                                    S.op("pe", lambda e, cc=cc: e.matmul(out=PB[0:64, 384:448], lhsT=P2T[:, cc, :], rhs=T0p[:, cc, :], start=False, stop=True), reads=[Bh["P2T"], Bh["T0p"]], writes=[BB])
                                    yield
                                    S.op("dve", lambda e: e.tensor_copy(out=Tst[:, head, :], in_=PB[0:64, 384:448]), reads=[BB], writes=[BT[head]])
                                    yield
                                S.op("pe", lambda e: e.matmul(out=PB[:, 448:512], lhsT=A1[:, 128:256], rhs=vb[:, hc], start=True, stop=False), reads=[Bh["A1"], Bvb], writes=[BB])
                                S.op("pe", lambda e: e.matmul(out=PB[:, 448:512], lhsT=A2[:, 128:256], rhs=RH[:, 64:128], start=False, stop=False), reads=[Bh["A2"], Bh["RH"]], writes=[BB])
                                for cc in range(2):
                                    S.op("pe", lambda e, cc=cc: e.matmul(out=PB[:, 448:512], lhsT=RpE[:, cc, :], rhs=T0p[:, cc, :], start=False, stop=(cc == 1)), reads=[Bh["RpE"], Bh["T0p"]], writes=[BB])
                                yield
                                S.op("act", lambda e: e.activation(out=yblk[:, hc], in_=PB[:, 448:512], func=AF.Copy), reads=[BB], writes=[Byb])
                            run_interleaved([head_gen(hh=0), head_gen(hh=1)], 2)
                        S.skip = (_LIM["rstep"] < 17) or (_rt[0] > _LIM["rtiles"])
                        yv = yblk[:].rearrange("p (h n) -> p h n", n=64)
                        sqt = xx[:].rearrange("p k t -> p (k t)")
                        sv = sqt.rearrange("p (h n) -> p h n", n=64)
                        S.op("act", lambda e: e.activation(out=bon[:], in_=PS[7][:, 0:16], func=AF.Copy), reads=[BPS[7]], writes=[Bbon])
                        S.op("dve", lambda e: e.tensor_reduce(out=gst[:, 0:16], in_=yv, axis=AX.X, op=ALU.add), reads=[Byb], writes=[Bg["st"]])
                        S.op("dve", lambda e: e.tensor_scalar(out=gst[:, 0:16], in0=gst[:, 0:16], scalar1=1.0 / 64.0, scalar2=None, op0=ALU.mult), reads=[Bg["st"]], writes=[Bg["st"]])
                        S.op("dve", lambda e: e.tensor_tensor(out=yv, in0=yv, in1=gst[:, 0:16].unsqueeze(2).to_broadcast([128, 16, 64]), op=ALU.subtract),
                             reads=[Byb, Bg["st"]], writes=[Byb])
                        S.op("act", lambda e: e.activation(out=sqt, in_=yblk[:], func=AF.Square), reads=[Byb], writes=[Bxx])
                        S.op("dve", lambda e: e.tensor_reduce(out=gst[:, 16:32], in_=sv, axis=AX.X, op=ALU.add), reads=[Bxx], writes=[Bg["st"]])
                        S.op("dve", lambda e: e.tensor_scalar(out=gst[:, 16:32], in0=gst[:, 16:32], scalar1=1.0 / 64.0, scalar2=GN_EPS, op0=ALU.mult, op1=ALU.add),
                             reads=[Bg["st"]], writes=[Bg["st"]])
                        S.op("act", lambda e: e.activation(out=gst[:, 16:32], in_=gst[:, 16:32], func=AF.Sqrt), reads=[Bg["st"]], writes=[Bg["st"]])
                        S.op("dve", lambda e: e.reciprocal(out=gst[:, 16:32], in_=gst[:, 16:32]), reads=[Bg["st"]], writes=[Bg["st"]])
                        S.op("dve", lambda e: e.tensor_tensor(out=yv, in0=yv, in1=gst[:, 16:32].unsqueeze(2).to_broadcast([128, 16, 64]), op=ALU.mult),
                             reads=[Byb, Bg["st"]], writes=[Byb])
                        S.op("pool", lambda e: e.tensor_tensor(out=yblk[:], in0=yblk[:], in1=gng[:], op=ALU.mult), reads=[Byb, Bdirc], writes=[Byb])
                        S.op("pool", lambda e: e.tensor_tensor(out=yblk[:], in0=yblk[:], in1=gnb[:], op=ALU.add), reads=[Byb, Bdirc], writes=[Byb])
                        vfv = vf[:].rearrange("p (h n) -> p h n", n=64)
                        S.op("dve", lambda e: e.tensor_tensor(out=vfv, in0=vfv, in1=bon[:].unsqueeze(2).to_broadcast([128, 16, 64]), op=ALU.mult),
                             reads=[Bvf, Bbon], writes=[Bvf])
                        S.op("pool", lambda e: e.tensor_tensor(out=yblk[:], in0=yblk[:], in1=vf[:], op=ALU.add), reads=[Byb, Bvf], writes=[Byb])
                        S.skip = False
                        S.dma("sp", oscr[d, i * 128:(i + 1) * 128, :], yblk[:], reads=[Byb], writes=[Bscr[d]])
                        if last_of_seg:
                            S.dma("act", ns_r[j, seg, d].rearrange("h k v -> k h v"), Tst[:], reads=BT, writes=[By])
                S.barrier()

        for l in range(n_layers):
            j = l // 2
            S.phase = "L%d.ada" % l
            adaln(l)
            if mix:
                with contextlib.ExitStack() as lph:
                    hT = sb(lph, "hT", [128, 8, NSEG, HTW], BF16)
                    BhT = [S.buf("hT%d" % s_) for s_ in range(NSEG)]
                    S.op("pool", lambda e: e.memset(hT[:, :, :, PADL - 1:PADL], 0.0), writes=BhT)
                    S.op("pool", lambda e: e.memset(hT[:, :, :, PADL + 256:PADL + 257], 0.0), writes=BhT)
                    S.phase = "L%d.hT" % l
                    build_hT(lambda i, k: hT[:, k, i // 2, PADL + (i % 2) * 128: PADL + (i % 2) * 128 + 128], lambda i: BhT[i // 2], list(range(NT)), sc1p, 0)
                    if l % 2 == 1:
                        S.op("dve", lambda e: e.tensor_scalar(out=hT[:, :, 1:NSEG, PADL - 1], in0=hT[:, :, 0:NSEG - 1, PADL + 255], scalar1=flg[:, 0:1], scalar2=None, op0=ALU.mult),
                             reads=BhT + [Bc], writes=BhT)
                        S.op("dve", lambda e: e.tensor_scalar(out=hT[:, :, 0:NSEG - 1, PADL + 256], in0=hT[:, :, 1:NSEG, PADL], scalar1=flg[:, 0:1], scalar2=None, op0=ALU.mult),
                             reads=BhT + [Bc], writes=BhT)
                    S.barrier()
                    stopped = False
                    if stop == "hT" and l == n_layers - 1:
                        stopped = True
                    elif l % 2 == 0:
                        S.phase = "L%d.mix" % l
                        hgrn_layer(l, j, hT, BhT)
                        if stop == "mix" and l == n_layers - 1:
                            stopped = True
                        else:
                            S.phase = "L%d.post" % l
                            post_mixer(l, j, "h", hT, BhT)
                    else:
                        S.phase = "L%d.mix" % l
                        rwkv_layer(l, j, hT, BhT)
                        if stop == "mix" and l == n_layers - 1:
                            stopped = True
                        else:
                            S.phase = "L%d.post" % l
                            post_mixer(l, j, "r", hT, BhT)
                if stopped or (stop == "post" and l == n_layers - 1):
                    break
            S.phase = "L%d.ffn" % l
            ffn(l)

        for i in range(NT):
            S.dma(hwq(), y_out[i * 128:(i + 1) * 128, :], X[:, i, :], reads=[BX[i]], writes=[By])
        S.barrier()
        S.emit(block)
        globals()["_LAST_SCHED"] = S
    return nc


_PROMPT_SLOTS = [[(p, p // 6) for p in range(32) if p % 6 == cix] for cix in range(6)]


def _fm(v):
    return np.ascontiguousarray(np.asarray(v, np.float32).reshape(8, 128).T)


def make_in_maps(inp, n_layers=4):
    NLW = n_layers
    NH = max(1, (n_layers + 1) // 2)
    NR = max(1, n_layers // 2)
    f = lambda a: np.ascontiguousarray(np.asarray(a, dtype=np.float32))
    consts = _consts()
    pos = _pos_table()
    shared = {
        "pos": pos,
        "ada_w": f(inp["ada_w"]),
        "ada_bT": np.ascontiguousarray(f(inp["ada_b"]).reshape(4, 48, 128).transpose(0, 2, 1)),
        "ln_g": f(inp["ln_g"]), "ln_b": f(inp["ln_b"]),
        "ffn_w_up": f(inp["ffn_w_up"]), "ffn_w_down": f(inp["ffn_w_down"]),
        "hgrn_w_in": f(inp["hgrn_w_in"]),
        "hgrn_lbT": np.ascontiguousarray(f(inp["hgrn_lb"]).reshape(2, 16, 128).transpose(0, 2, 1)),
        "hgrn_norm_g": f(inp["hgrn_norm_g"]), "hgrn_w_o": f(inp["hgrn_w_o"]),
        "rwkv_w_rkv": f(inp["rwkv_w_rkv"]), "rwkv_w_la": f(inp["rwkv_w_la"]), "rwkv_w_lb": f(inp["rwkv_w_lb"]),
        "rwkv_a_la": f(inp["rwkv_a_la"]), "rwkv_a_lb": f(inp["rwkv_a_lb"]),
        "rwkv_g_la": f(inp["rwkv_g_la"]), "rwkv_g_lb": f(inp["rwkv_g_lb"]),
        "rwkv_gn_g": f(inp["rwkv_gn_g"]), "rwkv_gn_b": f(inp["rwkv_gn_b"]), "rwkv_w_o": f(inp["rwkv_w_o"]),
    }
    vec = np.zeros((2, 16, 128, 8), np.float32)
    for j in range(2):
        for n in range(6):
            vec[j, n] = _fm(inp["rwkv_mu"][j, n])
        for n, nm in enumerate(("rwkv_w0", "rwkv_a0", "rwkv_k_k", "rwkv_k_a", "rwkv_r_k")):
            for d in range(2):
                vec[j, 6 + n * 2 + d] = _fm(inp[nm][j, d])
    shared["rwkv_vecT"] = vec
    for k, v in consts.items():
        shared["c_" + k] = v
    xp = f(inp["x_prompt"])
    xs = f(inp["x_sample"])
    sth = f(inp["state_hgrn"])
    strw = f(inp["state_rwkv"])
    maps = []
    for core in range(8):
        m = dict(shared)
        if core < 2:
            m["x_in"] = np.ascontiguousarray(xs[core])
            m["condT"] = _fm(inp["c"][core])
            fl = np.ones((128, 2), np.float32)
            m["st_h"] = np.ascontiguousarray(sth[core])
            m["st_r"] = np.ascontiguousarray(strw[core].transpose(0, 1, 2, 4, 3))
        else:
            slots = _PROMPT_SLOTS[core - 2]
            xin = np.zeros((NSEG, 256, D), np.float32)
            for s_ in range(NSEG):
                xin[s_] = xp[slots[s_][0]] if s_ < len(slots) else xp[slots[0][0]]
            m["x_in"] = xin.reshape(2048, D)
            m["condT"] = _fm(inp["c_ctx"])
            fl = np.zeros((128, 2), np.float32)
            m["st_h"] = np.zeros((2, 2, 8, 128, 128), np.float32)
            m["st_r"] = np.zeros((2, 2, 16, 64, 64), np.float32)
        m["flags"] = fl
        maps.append(m)
    cut = {"ada_w": NLW, "ada_bT": NLW, "ln_g": NLW, "ln_b": NLW, "ffn_w_up": NLW, "ffn_w_down": NLW,
           "hgrn_w_in": NH, "hgrn_norm_g": NH, "hgrn_w_o": NH, "rwkv_w_rkv": NR, "rwkv_w_la": NR, "rwkv_w_lb": NR,
           "rwkv_a_la": NR, "rwkv_a_lb": NR, "rwkv_g_la": NR, "rwkv_g_lb": NR, "rwkv_gn_g": NR, "rwkv_gn_b": NR, "rwkv_w_o": NR}
    if n_layers < 4:
        for k, n in cut.items():
            sl = np.ascontiguousarray(shared[k][:n])
            for m in maps:
                m[k] = sl
    return maps


def assemble(results):
    y_prompt = np.zeros((32, 256, D), np.float32)
    y_sample = np.zeros((2, 2048, D), np.float32)
    nsh = np.zeros((32, 2, 2, 8, 128, 128), np.float32)
    nsr = np.zeros((32, 2, 2, 16, 64, 64), np.float32)
    for core in range(8):
        r = results[core]
        if core < 2:
            y_sample[core] = r["y_out"]
        else:
            slots = _PROMPT_SLOTS[core - 2]
            yo = r["y_out"].reshape(NSEG, 256, D)
            for s_, (p, _) in enumerate(slots):
                y_prompt[p] = yo[s_]
                nsh[p] = r["ns_h"][:, s_]
                nsr[p] = r["ns_r"][:, s_].transpose(0, 1, 2, 4, 3)
    return y_prompt, y_sample, nsh, nsr


_NC_CACHE = {}
_LIM = {"heads": 10 ** 9, "step": 99, "rstep": 99, "rtiles": 10 ** 9}
_rt = [0]


def kernel(**inputs):
    if "nc" not in _NC_CACHE:
        _NC_CACHE["nc"] = build_program()
    nc = _NC_CACHE["nc"]
    maps = make_in_maps(inputs)
    res = run_bass_kernel_spmd(nc, maps, core_ids=list(range(8)))
    return assemble(res.results)
```

```python
import contextlib
import numpy as np
import concourse.bass as bass
import concourse.mybir as mybir
from concourse.bass_utils import run_bass_kernel_spmd

F32 = mybir.dt.float32
BF16 = mybir.dt.bfloat16
ALU = mybir.AluOpType
AF = mybir.ActivationFunctionType
AX = mybir.AxisListType

D = 1024
NT = 16
NSEG = 8
DN_ALPHA = 8 ** 0.25
LN_EPS = 1e-5
RMS_EPS = 1e-6
GN_EPS = 64e-5
C0 = 0.606531
PADL = 2
HTW = 260


class SemGroup:
    def __init__(self, sem):
        self.sem = sem
        self.count = 0


class Buf:
    __slots__ = ("name", "w", "r", "grp", "excl")

    def __init__(self, name, grp=None):
        self.name = name
        self.w = []
        self.r = {}
        self.grp = grp
        self.excl = False


class _Rec:
    def __init__(self):
        self.call = None

    def __getattr__(self, name):
        def f(*a, **k):
            self.call = (name, a, k)
            return self
        return f


class Sched:
    ENG = ("pe", "act", "dve", "pool", "sp")

    def __init__(self, nc, stack):
        self.nc = nc
        self.stack = stack
        self.ops = {e: [] for e in self.ENG}
        self.esem = {e: stack.enter_context(nc.semaphore("es_" + e)) for e in self.ENG}
        self.targets = {e: set() for e in self.ENG}
        self.groups = []
        self.gcache = {}
        self.lastreal = {e: 0 for e in self.ENG}

    def group(self, key=None):
        if key is not None and key in self.gcache:
            return self.gcache[key]
        g = SemGroup(self.stack.enter_context(self.nc.semaphore("dg%d" % len(self.groups))))
        self.groups.append(g)
        if key is not None:
            self.gcache[key] = g
        return g

    def buf(self, name, grp=None):
        if grp == "own":
            grp = self.group(name)
        return Buf(name, grp)

    def _deps(self, eng, reads, writes):
        deps = []
        for b in reads:
            deps.extend(b.w)
        for b in writes:
            deps.extend(b.w)
            for k, v in b.r.items():
                if isinstance(k, str):
                    deps.append(("e", k, v))
                else:
                    deps.append(("d", k[1], v))
        waits = {}
        for d in deps:
            if d[0] == "e" and d[1] == eng and eng in ("pe", "sp"):
                continue
            key = (d[0], d[1])
            waits[key] = max(waits.get(key, 0), d[2])
        for k, v in waits.items():
            if k[0] == "e":
                self.targets[k[1]].add(v)
        return waits

    skip = False
    phase = ""

    def op(self, eng, fn, reads=(), writes=()):
        if self.skip:
            return
        ex = [b for b in reads if b.excl]
        if ex:
            writes = list(writes) + ex
        waits = self._deps(eng, reads, writes)
        rec = _Rec()
        fn(rec)
        assert rec.call is not None
        self.ops[eng].append([rec.call, waits, "c", None, self.phase])
        idx = len(self.ops[eng])
        self.lastreal[eng] = idx
        for b in reads:
            b.r[eng] = idx
        for b in writes:
            b.w = [("e", eng, idx)]
            b.r = {}
        return idx

    def dma(self, eng, out, in_, reads=(), writes=(), grp=None, **kw):
        if self.skip:
            return
        waits = self._deps(eng, reads, writes)
        g = grp
        if g is None:
            for b in writes:
                if b.grp is not None:
                    g = b.grp
        assert g is not None
        g.count += 1
        cnt = g.count

        kw2 = dict(kw)
        kw2["out"] = out
        kw2["in_"] = in_
        self.ops[eng].append([("dma_start", (), kw2), waits, "d", g])
        for b in reads:
            b.r[("d", g)] = cnt
        for b in writes:
            b.w = [("d", g, cnt)]
            b.r = {}

    def barrier(self):
        last = dict(self.lastreal)
        for e in self.ENG:
            waits = {}
            for o in self.ENG:
                if o != e and last[o] > 0:
                    waits[("e", o)] = last[o]
                    self.targets[o].add(last[o])
            for g in self.groups:
                if g.count > 0:
                    waits[("d", g)] = g.count
            self.ops[e].append([None, waits, "w", None])

    def emit(self, block):
        tval = {}
        for e in self.ENG:
            c = 0
            m = {}
            for i in range(1, len(self.ops[e]) + 1):
                if i in self.targets[e]:
                    c += 1
                    m[i] = c
            tval[e] = m
        sched = self

        def run(e, engobj):
            seen = {}
            for i, rec in enumerate(sched.ops[e], start=1):
                fn, waits = rec[0], rec[1]
                for k, v in waits.items():
                    if k[0] == "e":
                        val = tval[k[1]][v]
                        sem = sched.esem[k[1]]
                    else:
                        val = 16 * v
                        sem = k[1].sem
                    if seen.get(id(sem), 0) >= val:
                        continue
                    seen[id(sem)] = val
                    engobj.wait_ge(sem, val)
                if fn is None:
                    assert i not in tval[e]
                    continue
                ins = getattr(engobj, fn[0])(*fn[1], **fn[2])
                if rec[2] == "d":
                    ins.then_inc(rec[3].sem, 16)
                elif i in tval[e]:
                    ins.then_inc(sched.esem[e], 1)

        @block.tensor
        def _(pe):
            run("pe", pe)

        @block.scalar
        def _(act):
            run("act", act)

        @block.vector
        def _(dve):
            run("dve", dve)

        @block.gpsimd
        def _(pool):
            run("pool", pool)

        @block.sync
        def _(sp):
            run("sp", sp)


def _consts():
    c = {}
    idx = np.arange(128)
    c["ident"] = np.eye(128, dtype=np.float32)
    same32 = (idx[:, None] // 32) == (idx[None, :] // 32)
    mh = np.zeros((2, 128, 128), np.float32)
    mh[0] = same32 & (idx[:, None] <= idx[None, :])
    mh[1] = same32 & (idx[:, None] >= idx[None, :])
    c["maskH"] = mh
    cm4 = np.zeros((128, 4, 128), np.float32)
    for cc in range(4):
        cm4[:, cc, cc * 32:(cc + 1) * 32] = 1.0
    c["colmask4"] = cm4
    rm4 = np.zeros((128, 4), np.float32)
    rm4[idx, idx // 32] = 1.0
    c["rowmask4"] = rm4
    r32 = np.ones((128, 128), np.float32)
    r32[:, ::32] = 0.0
    c["rmask32"] = r32
    r64 = np.ones((128, 128), np.float32)
    r64[:, ::64] = 0.0
    c["rmask64"] = r64
    same64 = (idx[:, None] // 64) == (idx[None, :] // 64)
    st = [same64 & (idx[:, None] < idx[None, :]), same64 & (idx[:, None] > idx[None, :])]
    inc = [same64 & (idx[:, None] <= idx[None, :]), same64 & (idx[:, None] >= idx[None, :])]
    mk1 = np.zeros((2, 128, 256), np.float32)
    mk2 = np.zeros((2, 128, 256), np.float32)
    mkn = np.zeros((2, 128, 128), np.float32)
    for d in range(2):
        mk1[d, :, :128] = st[d]
        mk1[d, :, 128:] = inc[d]
        mk2[d, :, :128] = st[d]
        mk2[d, :, 128:] = -inc[d].astype(np.float32)
        mkn[d] = st[d].T
    c["mk1"] = mk1
    c["mk2"] = mk2
    c["mkn"] = mkn
    cm2 = np.zeros((128, 2, 128), np.float32)
    cm2[:, 0, :64] = 1.0
    cm2[:, 1, 64:] = 1.0
    c["colmask2"] = cm2
    hi = np.zeros((128, 2), np.float32)
    hi[idx, idx // 64] = 1.0
    c["headind"] = hi
    c["blockones"] = same64.astype(np.float32)
    sel = np.zeros((128, 2, 64), np.float32)
    for hh in range(2):
        sel[64 * hh + np.arange(64), hh, np.arange(64)] = 1.0
    c["sel64"] = sel
    return c


def _pos_table():
    rows, gw, quarter, half = 2048 // 64, 64, 256, 512
    omega = (1.0 / (10000.0 ** (np.arange(quarter, dtype=np.float32) / np.float32(quarter)))).astype(np.float32)
    r = np.arange(rows, dtype=np.float32)[:, None] * omega
    cc = np.arange(gw, dtype=np.float32)[:, None] * omega
    row_emb = np.concatenate([np.sin(r), np.cos(r)], -1)
    col_emb = np.concatenate([np.sin(cc), np.cos(cc)], -1)
    emb = np.concatenate([np.broadcast_to(row_emb[:, None, :], (rows, gw, half)),
                          np.broadcast_to(col_emb[None, :, :], (rows, gw, half))], -1)
    return np.ascontiguousarray(emb.reshape(rows * gw, D).astype(np.float32))


CONST_SHAPES = {
    "ident": [128, 128], "maskH": [2, 128, 128], "colmask4": [128, 4, 128], "rowmask4": [128, 4],
    "rmask32": [128, 128], "rmask64": [128, 128], "mk1": [2, 128, 256], "mk2": [2, 128, 256],
    "mkn": [2, 128, 128], "colmask2": [128, 2, 128], "headind": [128, 2], "blockones": [128, 128],
    "sel64": [128, 2, 64],
}


def build_program(n_layers=4, mix=True, dbg=False, stop=None):
    nc = bass.Bass("TRN2", target_bir_lowering=False)

    def din(name, shape):
        return nc.dram_tensor(name, list(shape), F32, kind="ExternalInput").ap()

    def dout(name, shape):
        return nc.dram_tensor(name, list(shape), F32, kind="ExternalOutput").ap()

    NLW = n_layers
    NH = max(1, (n_layers + 1) // 2)
    NR = max(1, n_layers // 2)
    x_in = din("x_in", [2048, D])
    pos = din("pos", [2048, D])
    condT = din("condT", [128, 8])
    flags = din("flags", [128, 2])
    ada_w = din("ada_w", [NLW, D, 6 * D])
    ada_bT = din("ada_bT", [NLW, 128, 48])
    ln_g = din("ln_g", [NLW, 2, D])
    ln_b = din("ln_b", [NLW, 2, D])
    w_up = din("ffn_w_up", [NLW, D, 4 * D])
    w_dn = din("ffn_w_down", [NLW, 4 * D, D])
    h_win = din("hgrn_w_in", [NH, D, 7 * D])
    h_lbT = din("hgrn_lbT", [2, 128, 16])
    h_ng = din("hgrn_norm_g", [NH, D])
    h_wo = din("hgrn_w_o", [NH, D, D])
    st_h = din("st_h", [2, 2, 8, 128, 128])
    st_r = din("st_r", [2, 2, 16, 64, 64])
    r_vecT = din("rwkv_vecT", [2, 16, 128, 8])
    r_wrkv = din("rwkv_w_rkv", [NR, 3, D, 2 * D])
    r_wla = din("rwkv_w_la", [NR, D, 2, 64])
    r_wlb = din("rwkv_w_lb", [NR, 2, 64, D])
    r_ala = din("rwkv_a_la", [NR, D, 2, 64])
    r_alb = din("rwkv_a_lb", [NR, 2, 64, D])
    r_gla = din("rwkv_g_la", [NR, D, 160])
    r_glb = din("rwkv_g_lb", [NR, 160, D])
    r_gng = din("rwkv_gn_g", [NR, 2, D])
    r_gnb = din("rwkv_gn_b", [NR, 2, D])
    r_wo = din("rwkv_w_o", [NR, D, D])
    cst = {k: din("c_" + k, v) for k, v in CONST_SHAPES.items()}

    y_out = dout("y_out", [2048, D])
    ns_h = dout("ns_h", [2, NSEG, 2, 8, 128, 128])
    ns_r = dout("ns_r", [2, NSEG, 2, 16, 64, 64])
    oscr = nc.dram_tensor("oscr", [2, 2048, D], F32).ap()
    dbgt = dout("dbg", [16, 128, D]) if dbg else None

    with contextlib.ExitStack() as st:
        S = Sched(nc, st)

        _uid = [0]

        def sb(stack, name, shape, dt=F32):
            _uid[0] += 1
            return stack.enter_context(nc.sbuf_tensor("%s_u%d" % (name, _uid[0]), list(shape), dt))

        X = sb(st, "X", [128, NT, D])
        BX = [S.buf("X%d" % i) for i in range(NT)]
        GX = S.group("GX")
        ident = sb(st, "ident", [128, 128])
        identb = sb(st, "identb", [128, 128], BF16)
        Bc = S.buf("consts", "own")
        flg = sb(st, "flg", [128, 2])
        scond = sb(st, "scond", [128, 8])
        modT = sb(st, "modT", [128, 48])
        sc1p = sb(st, "sc1p", [128, 8])
        sc2p = sb(st, "sc2p", [128, 8])
        Bmod = S.buf("mod")
        PS = [st.enter_context(nc.psum_tensor("ps%d" % i, [128, 512], F32)) for i in range(8)]
        BPS = [S.buf("ps%d" % i) for i in range(8)]
        for b_ in BPS:
            b_.excl = True
        Gout = S.group("Gout")
        By = S.buf("y_out", Gout)
        Gscr = S.group("Gscr")
        Bscr = [S.buf("oscr0", Gscr), S.buf("oscr1", Gscr)]

        block = st.enter_context(nc.Block())
        _dq = [0]
        Bdbg = S.buf("dbg", S.group("dbg"))

        def dump(slot, ap, bufs, p=128, n=None):
            if not dbg:
                return
            n = n if n is not None else ap.shape[-1]
            S.dma("pool", dbgt[slot, 0:p, 0:n], ap, reads=bufs, writes=[Bdbg])

        def hwq():
            _dq[0] += 1
            return "sp" if _dq[0] % 2 == 0 else "act"

        S.dma("sp", ident[:], cst["ident"], writes=[Bc])
        S.dma("pool", identb[:], cst["ident"], writes=[Bc])
        S.dma("sp", flg[:], flags, writes=[Bc])
        S.dma("sp", scond[:], condT, writes=[Bc])
        S.op("act", lambda e: e.activation(out=scond[:], in_=scond[:], func=AF.Silu), reads=[Bc], writes=[Bc])
        for i in range(NT):
            S.dma(hwq(), X[:, i, :], x_in[i * 128:(i + 1) * 128, :], writes=[BX[i]], grp=GX)
        with contextlib.ExitStack() as ph:
            pt = [sb(ph, "pos%d" % i, [128, D]) for i in range(2)]
            Bpt = [S.buf("pos%d" % i, "own") for i in range(2)]
            for i in range(NT):
                S.dma(hwq(), pt[i % 2][:], pos[i * 128:(i + 1) * 128, :], writes=[Bpt[i % 2]])
                eng = "dve"
                S.op(eng, lambda e, i=i: e.scalar_tensor_tensor(out=X[:, i, :], in0=pt[i % 2][:], scalar=flg[:, 1:2], in1=X[:, i, :],
                                                                op0=ALU.mult, op1=ALU.add),
                     reads=[Bpt[i % 2], Bc, BX[i]], writes=[BX[i]])
            dump(1, X[:, 0, :], [BX[0]])
            S.barrier()

        def run_interleaved(gens, width):
            it = iter(gens)
            active = []
            while True:
                while len(active) < width:
                    g = next(it, None)
                    if g is None:
                        break
                    active.append(g)
                if not active:
                    break
                for g in list(active):
                    try:
                        next(g)
                    except StopIteration:
                        active.remove(g)

        def adaln(l):
            with contextlib.ExitStack() as ph:
                slab = [sb(ph, "adas%d" % i, [128, 8, 256]) for i in range(2)]
                Bsl = [S.buf("adas%d" % i, "own") for i in range(2)]
                abT = sb(ph, "abT", [128, 48])
                Bab = S.buf("abT", "own")
                S.dma("sp", abT[:], ada_bT[l], writes=[Bab])
                wv = ada_w[l].rearrange("(k p) n -> p k n", p=128)
                acc = PS[0]
                for jg in range(24):
                    sl = jg % 2
                    S.dma(hwq(), slab[sl][:], wv[:, :, jg * 256:(jg + 1) * 256], writes=[Bsl[sl]])
                    for jj in range(2):
                        j = jg * 2 + jj
                        for k in range(8):
                            S.op("pe", lambda e, sl=sl, jj=jj, j=j, k=k: e.matmul(
                                out=acc[:, j:j + 1], lhsT=slab[sl][:, k, jj * 128:(jj + 1) * 128], rhs=scond[:, k:k + 1],
                                start=(k == 0), stop=(k == 7)), reads=[Bsl[sl], Bc], writes=[BPS[0]])
                S.op("dve", lambda e: e.tensor_tensor(out=modT[:], in0=acc[:, 0:48], in1=abT[:], op=ALU.add),
                     reads=[BPS[0], Bab], writes=[Bmod])
                S.op("dve", lambda e: e.tensor_scalar_add(out=sc1p[:], in0=modT[:, 8:16], scalar1=1.0), reads=[Bmod], writes=[Bmod])
                S.op("dve", lambda e: e.tensor_scalar_add(out=sc2p[:], in0=modT[:, 32:40], scalar1=1.0), reads=[Bmod], writes=[Bmod])
                if l == 0:
                    dump(0, modT[:], [Bmod])
                S.barrier()

        def make_gbc(ph, which):
            base = (16, 40)[which]
            gb = sb(ph, "gbc", [128, D])
            Bgb = S.buf("gbc%d" % which)
            gtmp = sb(ph, "gtmp", [128, 128])
            Bgt = S.buf("gtmp")
            for k in range(8):
                S.op("dve", lambda e, k=k: e.tensor_scalar(
                    out=gtmp[:], in0=modT[:, base + k:base + k + 1].to_broadcast([128, 128]),
                    scalar1=1.0 / DN_ALPHA, scalar2=None, op0=ALU.mult), reads=[Bmod], writes=[Bgt])
                S.op("pe", lambda e: e.transpose(out=PS[1][:, 0:128], in_=gtmp[:], identity=ident[:]),
                     reads=[Bgt, Bc], writes=[BPS[1]])
                S.op("act", lambda e, k=k: e.activation(out=gb[:, k * 128:(k + 1) * 128], in_=PS[1][:, 0:128], func=AF.Copy),
                     reads=[BPS[1]], writes=[Bgb])
            return gb, Bgb

        def build_hT(dst_fn, Bdst_fn, tiles, scp, shcol):
            n = 0
            for i in tiles:
                for kk in range(2):
                    pb = 2 + (n % 2)
                    n += 1
                    for k4 in range(4):
                        k = kk * 4 + k4
                        S.op("pe", lambda e, i=i, k=k, k4=k4, pb=pb: e.transpose(
                            out=PS[pb][:, k4 * 128:(k4 + 1) * 128], in_=X[:, i, k * 128:(k + 1) * 128], identity=ident[:]),
                            reads=[BX[i], Bc], writes=[BPS[pb]])
                    for k4 in range(4):
                        k = kk * 4 + k4
                        if k % 2 == 0:
                            S.op("act", lambda e, i=i, k=k, k4=k4, pb=pb: e.activation(
                                out=dst_fn(i, k), in_=PS[pb][:, k4 * 128:(k4 + 1) * 128], func=AF.Identity,
                                bias=modT[:, shcol + k:shcol + k + 1], scale=scp[:, k:k + 1]),
                                reads=[BPS[pb], Bmod], writes=[Bdst_fn(i)])
                        else:
                            S.op("dve", lambda e, i=i, k=k, k4=k4, pb=pb: e.tensor_scalar(
                                out=dst_fn(i, k), in0=PS[pb][:, k4 * 128:(k4 + 1) * 128],
                                scalar1=scp[:, k:k + 1], scalar2=modT[:, shcol + k:shcol + k + 1], op0=ALU.mult, op1=ALU.add),
                                reads=[BPS[pb], Bmod], writes=[Bdst_fn(i)])

        def resid_ln(i, pa, pb, Bpa, Bpb, which, lt):
            v, Bv = lt["v"], lt["Bv"]
            stt, Bst = lt["st"], lt["Bst"]
            S.op("dve", lambda e: e.tensor_tensor(out=v[:, 0:512], in0=pa[:, :], in1=lt["gbc"][:, 0:512], op=ALU.mult),
                 reads=[Bpa, lt["Bgbc"]], writes=[Bv])
            S.op("dve", lambda e: e.tensor_tensor(out=v[:, 512:1024], in0=pb[:, :], in1=lt["gbc"][:, 512:1024], op=ALU.mult),
                 reads=[Bpb, lt["Bgbc"]], writes=[Bv])
            S.op("pool", lambda e: e.tensor_tensor(out=v[:], in0=v[:], in1=X[:, i, :], op=ALU.add), reads=[Bv, BX[i]], writes=[Bv])
            if i == 0 and lt.get("dbg"):
                dump(5, v[:], [Bv])
            S.op("dve", lambda e: e.bn_stats(out=stt[:, 0:6], in_=v[:, 0:512]), reads=[Bv], writes=[Bst])
            S.op("dve", lambda e: e.bn_stats(out=stt[:, 6:12], in_=v[:, 512:1024]), reads=[Bv], writes=[Bst])
            S.op("dve", lambda e: e.bn_aggr(out=stt[:, 12:14], in_=stt[:, 0:12]), reads=[Bst], writes=[Bst])
            S.op("dve", lambda e: e.tensor_scalar_add(out=stt[:, 14:15], in0=stt[:, 13:14], scalar1=LN_EPS / (DN_ALPHA ** 2)), reads=[Bst], writes=[Bst])
            S.op("act", lambda e: e.activation(out=stt[:, 14:15], in_=stt[:, 14:15], func=AF.Sqrt), reads=[Bst], writes=[Bst])
            S.op("dve", lambda e: e.reciprocal(out=stt[:, 14:15], in_=stt[:, 14:15]), reads=[Bst], writes=[Bst])
            S.op("dve", lambda e: e.tensor_scalar(out=v[:], in0=v[:], scalar1=stt[:, 12:13], scalar2=stt[:, 14:15],
                                                  op0=ALU.subtract, op1=ALU.mult), reads=[Bv, Bst], writes=[Bv])
            if i == 0 and lt.get("dbg"):
                dump(6, stt[:], [Bst])
                dump(8, v[:], [Bv])
            S.op("pool", lambda e: e.tensor_tensor(out=v[:], in0=v[:], in1=lt["lng"][:], op=ALU.mult),
                 reads=[Bv, lt["Bln"]], writes=[Bv])
            S.op("pool", lambda e: e.tensor_tensor(out=X[:, i, :], in0=v[:], in1=lt["lnb"][:], op=ALU.add),
                 reads=[Bv, lt["Bln"]], writes=[BX[i]])

        def ffn(l):
            with contextlib.ExitStack() as ph:
                wd = sb(ph, "wd", [128, 32, D], BF16)
                Bwd = [S.buf("wd%d" % i, "own") for i in range(4)]
                ups = [sb(ph, "ups%d" % i, [128, 8, 256], BF16) for i in range(2)]
                Bups = [S.buf("ups%d" % i, "own") for i in range(2)]
                aT = sb(ph, "aT", [128, 32, 512], BF16)
                BaT = [S.buf("aT%d" % j) for j in range(32)]
                h2 = sb(ph, "h2", [128, 8, 512], BF16)
                Bh2 = [S.buf("h2_%d" % i) for i in range(4)]
                rl = [sb(ph, "rl%d" % i, [128, 512]) for i in range(2)]
                Brl = [S.buf("rl%d" % i) for i in range(2)]
                lng = sb(ph, "lng", [128, D])
                lnb = sb(ph, "lnb", [128, D])
                Bln = S.buf("lnbc", "own")
                S.dma("sp", lng[:], ln_g[l, 1].partition_broadcast(128), writes=[Bln])
                S.dma("sp", lnb[:], ln_b[l, 1].partition_broadcast(128), writes=[Bln])
                gb_, Bgb_ = make_gbc(ph, 1)
                lt = [dict(v=sb(ph, "lnv%d" % i, [128, D]), Bv=S.buf("lnv%d" % i), st=sb(ph, "lnst%d" % i, [128, 16]), Bst=S.buf("lnst%d" % i),
                           lng=lng, lnb=lnb, Bln=Bln, gbc=gb_, Bgbc=Bgb_) for i in range(1)]
                if l == 0:
                    lt[0]["dbg"] = True
                wdv = w_dn[l].rearrange("(j p) n -> p j n", p=128)
                for q in range(4):
                    S.dma("pool", wd[:, q * 8:(q + 1) * 8, :], wdv[:, q * 8:(q + 1) * 8, :], writes=[Bwd[q]])
                wuv = w_up[l].rearrange("(k p) n -> p k n", p=128)
                nev = 0
                for g in range(4):
                    tiles = list(range(g * 4, g * 4 + 4))
                    build_hT(lambda i, k: h2[:, k, (i % 4) * 128:(i % 4 + 1) * 128], lambda i: Bh2[i % 4], tiles, sc2p, 24)
                    for jg in range(16):
                        sl = jg % 2
                        S.dma("pool", ups[sl][:], wuv[:, :, jg * 256:(jg + 1) * 256], writes=[Bups[sl]])
                        for jj in range(2):
                            j = jg * 2 + jj
                            pb = 4 + (nev % 2)
                            for k in range(8):
                                S.op("pe", lambda e, sl=sl, jj=jj, k=k, pb=pb: e.matmul(
                                    out=PS[pb][:, :], lhsT=ups[sl][:, k, jj * 128:(jj + 1) * 128], rhs=h2[:, k, :],
                                    start=(k == 0), stop=(k == 7)), reads=[Bups[sl]] + Bh2, writes=[BPS[pb]])
                            r = nev % 2
                            nev += 1
                            S.op("act", lambda e, pb=pb, r=r: e.activation(out=rl[r][:], in_=PS[pb][:, :], func=AF.Relu),
                                 reads=[BPS[pb]], writes=[Brl[r]])
                            eng = "dve" if j % 2 == 0 else "pool"
                            S.op(eng, lambda e, j=j, r=r: e.tensor_tensor(out=aT[:, j, :], in0=rl[r][:], in1=rl[r][:], op=ALU.mult),
                                 reads=[Brl[r]], writes=[BaT[j]])
                    if l == 0 and g == 0:
                        dump(3, h2[:, :, 0:128].rearrange("p k t -> p k t"), Bh2, n=None) if False else None
                        for k in range(8):
                            if dbg:
                                S.dma("pool", dbgt[3, :, k * 128:(k + 1) * 128], h2[:, k, 0:128], reads=Bh2, writes=[Bdbg])
                        dump(4, aT[:, 0, :], [BaT[0]])
                    for ii, i in enumerate(tiles):
                        pa, pbk = 6, 7
                        for nh, pbank in ((0, pa), (1, pbk)):
                            for j in range(32):
                                S.op("pe", lambda e, j=j, ii=ii, nh=nh, pbank=pbank: e.matmul(
                                    out=PS[pbank][:, :], lhsT=aT[:, j, ii * 128:(ii + 1) * 128], rhs=wd[:, j, nh * 512:(nh + 1) * 512],
                                    start=(j == 0), stop=(j == 31)), reads=[BaT[j], Bwd[j // 8]], writes=[BPS[pbank]])
                        if l == 0 and i == 0 and dbg:
                            S.dma("pool", dbgt[9, :, 0:512], PS[pa][:, :], reads=[BPS[pa]], writes=[Bdbg]) if False else None
                        resid_ln(i, PS[pa], PS[pbk], BPS[pa], BPS[pbk], 1, lt[0])
                        if l == 0 and i == 0:
                            dump(7, X[:, 0, :], [BX[0]])
                S.barrier()

        def hgrn_layer(l, j, hT, BhT):
            hcols = lambda i, k: hT[:, k, i // 2, PADL + (i % 2) * 128: PADL + (i % 2) * 128 + 128]
            with contextlib.ExitStack() as ph:
                lbv = sb(ph, "lbv", [128, 16])
                oml = sb(ph, "oml", [128, 16])
                Blb = S.buf("lbv", "own")
                if j == 0:
                    S.op("dve", lambda e: e.memset(lbv[:], 0.0), writes=[Blb])
                else:
                    lb0 = sb(ph, "lb0", [128, 16])
                    S.dma("sp", lb0[:], h_lbT[0], writes=[Blb])
                    S.dma("sp", lbv[:], h_lbT[1], writes=[Blb])
                    S.op("dve", lambda e: e.tensor_tensor(out=lbv[:], in0=lbv[:], in1=lb0[:], op=ALU.subtract), reads=[Blb], writes=[Blb])
                    S.op("act", lambda e: e.activation(out=lbv[:], in_=lbv[:], func=AF.Sigmoid), reads=[Blb], writes=[Blb])
                S.op("dve", lambda e: e.tensor_scalar(out=oml[:], in0=lbv[:], scalar1=-1.0, scalar2=1.0, op0=ALU.mult, op1=ALU.add),
                     reads=[Blb], writes=[Blb])
                maskH = sb(ph, "maskH", [128, 128])
                cm4 = sb(ph, "cm4", [128, 4, 128], BF16)
                rm4 = sb(ph, "rm4", [128, 4])
                r32 = sb(ph, "r32", [128, 128])
                Bcm = S.buf("hconst", "own")
                S.dma("pool", cm4[:], cst["colmask4"], writes=[Bcm])
                S.dma("sp", rm4[:], cst["rowmask4"], writes=[Bcm])
                S.dma("sp", r32[:], cst["rmask32"], writes=[Bcm])
                wq = [sb(ph, "hw%d" % i, [128, 8, D], BF16) for i in range(3)]
                Bwq = [S.buf("hw%d" % i, "own") for i in range(3)]
                Tst = sb(ph, "Tst", [128, 8, 128])
                BT = [S.buf("T%d" % h) for h in range(8)]
                GT = S.group("GT")
                vbs = [sb(ph, "vb%d" % i, [128, D], BF16) for i in range(2)]
                Bvbs = [S.buf("vb%d" % i) for i in range(2)]
                oblk = [sb(ph, "oblk%d" % i, [128, D]) for i in range(2)]
                Bob = [S.buf("oblk%d" % i) for i in range(2)]
                NS_ = 2
                tl = []
                for s_ in range(NS_):
                    t = {}
                    for nm in ("q", "nz", "uu", "l1", "l2", "Lp", "Dd", "eD", "emD", "t1", "kf"):
                        t[nm] = sb(ph, "h_%s%d" % (nm, s_), [128, 128])
                    t["wc"] = sb(ph, "h_wc%d" % s_, [128, 4])
                    for nm in ("kout", "qpp", "At"):
                        t[nm] = sb(ph, "h_%s%d" % (nm, s_), [128, 128], BF16)
                    for nm in ("koe", "qe", "Tp"):
                        t[nm] = sb(ph, "h_%s%d" % (nm, s_), [128, 4, 128], BF16)
                    t["B"] = {nm: S.buf("h_%s%d" % (nm, s_)) for nm in
                              ("q", "nz", "uu", "l1", "l2", "Lp", "Dd", "eD", "emD", "t1", "kf", "wc", "kout", "qpp", "At", "koe", "qe", "Tp")}
                    tl.append(t)
                wv = h_win[j].rearrange("(k p) n -> p k n", p=128)
                un = 0
                for d in range(2):
                    S.dma("sp", maskH[:], cst["maskH"][d], reads=[], writes=[Bcm])
                    for w3 in range(3):
                        c0 = d * 3072 + w3 * 1024
                        S.dma("pool", wq[w3][:], wv[:, :, c0:c0 + 1024], writes=[Bwq[w3]])
                    S.dma("sp", Tst[:], st_h[j, d].rearrange("h k v -> k h v"), writes=BT, grp=GT)
                    order = list(range(NT)) if d == 0 else list(range(NT - 1, -1, -1))
                    def hgen(bi, i, h, d=d, order=order):
                        seg = i // 2
                        first_of_seg = (i % 2 == 0) if d == 0 else (i % 2 == 1)
                        last_of_seg = not first_of_seg
                        corder = [0, 1, 2, 3] if d == 0 else [3, 2, 1, 0]
                        vbt, Bvbt = vbs[bi % 2], Bvbs[bi % 2]
                        ob, Bo = oblk[bi % 2], Bob[bi % 2]
                        idx = bi * 8 + h
                        t = tl[idx % NS_]
                        B = t["B"]
                        pqz, pg, pu, po = idx % 2, 2 + idx % 2, 4 + idx % 2, 6 + idx % 2
                        col = d * 8 + h
                        if h == 0:
                            for nh in range(2):
                                pb = 6 + nh
                                for k in range(8):
                                    S.op("pe", lambda e, k=k, nh=nh, pb=pb: e.matmul(
                                        out=PS[pb][:, :], lhsT=hcols(i, k), rhs=wq[2][:, k, nh * 512:(nh + 1) * 512],
                                        start=(k == 0), stop=(k == 7)), reads=[BhT[seg], Bwq[2]], writes=[BPS[pb]])
                                S.op("act", lambda e, nh=nh, pb=pb: e.activation(out=vbt[:, nh * 512:(nh + 1) * 512], in_=PS[pb][:, :], func=AF.Copy),
                                     reads=[BPS[pb]], writes=[Bvbt])
                            yield
                        if first_of_seg and bi > 0:
                            S.op("dve", lambda e: e.tensor_scalar(out=Tst[:, h, :], in0=Tst[:, h, :], scalar1=flg[:, 0:1], scalar2=None, op0=ALU.mult),
                                 reads=[BT[h], Bc], writes=[BT[h]])
                        for w3, off in ((0, 0), (1, 128)):
                            for k in range(8):
                                S.op("pe", lambda e, i=i, k=k, w3=w3, off=off, pqz=pqz, h=h: e.matmul(
                                    out=PS[pqz][:, off:off + 128], lhsT=wq[w3][:, k, h * 128:(h + 1) * 128], rhs=hcols(i, k),
                                    start=(k == 0), stop=(k == 7)), reads=[BhT[seg], Bwq[w3]], writes=[BPS[pqz]])
                        yield
                        S.op("act", lambda e, t=t, pqz=pqz: e.activation(out=t["q"][:], in_=PS[pqz][:, 0:128], func=AF.Silu),
                             reads=[BPS[pqz]], writes=[B["q"]])
                        yield
                        S.op("dve", lambda e, t=t, pqz=pqz: e.tensor_scalar(out=t["nz"][:], in0=PS[pqz][:, 128:256], scalar1=-1.0, scalar2=80.0,
                                                                             op0=ALU.mult, op1=ALU.min), reads=[BPS[pqz]], writes=[B["nz"]])
                        yield
                        S.op("act", lambda e, t=t: e.activation(out=t["uu"][:], in_=t["nz"][:], func=AF.Exp), reads=[B["nz"]], writes=[B["uu"]])
                        yield
                        S.op("act", lambda e, t=t, col=col: e.activation(out=t["l1"][:], in_=t["uu"][:], func=AF.Ln, bias=1.0,
                                                                          scale=lbv[:, col:col + 1]), reads=[B["uu"], Blb], writes=[B["l1"]])
                        yield
                        S.op("act", lambda e, t=t: e.activation(out=t["l2"][:], in_=t["uu"][:], func=AF.Ln, bias=1.0, scale=1.0),
                             reads=[B["uu"]], writes=[B["l2"]])
                        yield
                        S.op("dve", lambda e, t=t: e.tensor_tensor(out=t["l1"][:], in0=t["l1"][:], in1=t["l2"][:], op=ALU.subtract),
                             reads=[B["l1"], B["l2"]], writes=[B["l1"]])
                        yield
                        S.op("dve", lambda e, t=t: e.tensor_tensor_scan(out=t["Lp"][:], data0=r32[:], data1=t["l1"][:], initial=0.0,
                                                                       op0=ALU.mult, op1=ALU.add), reads=[B["l1"], Bcm], writes=[B["Lp"]])
                        yield
                        Lv = t["Lp"][:].rearrange("p (c t) -> p c t", t=32)
                        yield
                        Dv = t["Dd"][:].rearrange("p (c t) -> p c t", t=32)
                        yield
                        if d == 0:
                            S.op("dve", lambda e, Lv=Lv, Dv=Dv: e.tensor_tensor(out=Dv, in0=Lv[:, :, 31:32].to_broadcast([128, 4, 32]), in1=Lv,
                                                                                op=ALU.subtract), reads=[B["Lp"]], writes=[B["Dd"]])
                        else:
                            S.op("dve", lambda e, t=t: e.tensor_tensor(out=t["Dd"][:], in0=t["Lp"][:], in1=t["l1"][:], op=ALU.subtract),
                                 reads=[B["Lp"], B["l1"]], writes=[B["Dd"]])
                        yield
                        S.op("pool", lambda e, t=t: e.tensor_scalar_max(out=t["Dd"][:], in0=t["Dd"][:], scalar1=-80.0), reads=[B["Dd"]], writes=[B["Dd"]])
                        yield
                        S.op("act", lambda e, t=t: e.activation(out=t["eD"][:], in_=t["Dd"][:], func=AF.Exp), reads=[B["Dd"]], writes=[B["eD"]])
                        yield
                        S.op("act", lambda e, t=t: e.activation(out=t["emD"][:], in_=t["Dd"][:], func=AF.Exp, scale=-1.0), reads=[B["Dd"]], writes=[B["emD"]])
                        yield
                        S.op("act", lambda e, t=t, Lv=Lv: e.activation(out=t["wc"][:].unsqueeze(2), in_=Lv[:, :, 31:32], func=AF.Exp),
                             reads=[B["Lp"]], writes=[B["wc"]])
                        yield
                        S.op("dve", lambda e, t=t: e.tensor_scalar_add(out=t["t1"][:], in0=t["uu"][:], scalar1=1.0), reads=[B["uu"]], writes=[B["t1"]])
                        yield
                        S.op("dve", lambda e, t=t: e.reciprocal(out=t["t1"][:], in_=t["t1"][:]), reads=[B["t1"]], writes=[B["t1"]])
                        yield
                        S.op("dve", lambda e, t=t, col=col: e.scalar_tensor_tensor(out=t["kf"][:], in0=t["uu"][:], scalar=oml[:, col:col + 1], in1=t["t1"][:],
                                                                                  op0=ALU.mult, op1=ALU.mult), reads=[B["uu"], B["t1"], Blb], writes=[B["kf"]])
                        yield
                        S.op("pool", lambda e, t=t: e.tensor_tensor(out=t["kout"][:], in0=t["kf"][:], in1=t["eD"][:], op=ALU.mult),
                             reads=[B["kf"], B["eD"]], writes=[B["kout"]])
                        yield
                        S.op("pool", lambda e, t=t: e.tensor_tensor(out=t["qpp"][:], in0=t["q"][:], in1=t["emD"][:], op=ALU.mult),
                             reads=[B["q"], B["emD"]], writes=[B["qpp"]])
                        yield
                        pgb = PS[pg][:].bitcast(BF16)
                        yield
                        S.op("pe", lambda e, t=t, pg=pg: e.matmul(out=PS[pg][:, 0:128], lhsT=t["kout"][:], rhs=t["qpp"][:], start=True, stop=True),
                             reads=[B["kout"], B["qpp"]], writes=[BPS[pg]])
                        yield
                        S.op("pe", lambda e, t=t, pgb=pgb: e.transpose(out=pgb[:, 512:640], in_=t["kout"][:], identity=identb[:]),
                             reads=[B["kout"], Bc], writes=[BPS[pg]])
                        yield
                        S.op("dve", lambda e, t=t, pg=pg: e.tensor_tensor(out=t["At"][:], in0=PS[pg][:, 0:128], in1=maskH[:], op=ALU.mult),
                             reads=[BPS[pg], Bcm], writes=[B["At"]])
                        yield
                        S.op("dve", lambda e, t=t, pgb=pgb: e.tensor_tensor(out=t["koe"][:], in0=pgb[:, 512:640].unsqueeze(1).to_broadcast([128, 4, 128]),
                                                                           in1=rm4[:].unsqueeze(2).to_broadcast([128, 4, 128]), op=ALU.mult),
                             reads=[BPS[pg], Bcm], writes=[B["koe"]])
                        yield
                        S.op("pool", lambda e, t=t: e.tensor_tensor(out=t["qe"][:], in0=t["qpp"][:].unsqueeze(1).to_broadcast([128, 4, 128]),
                                                                   in1=cm4[:], op=ALU.mult), reads=[B["qpp"], Bcm], writes=[B["qe"]])
                        yield
                        for c in range(4):
                            S.op("pe", lambda e, t=t, c=c, pu=pu, h=h: e.matmul(out=PS[pu][:, c * 128:(c + 1) * 128], lhsT=t["koe"][:, c, :],
                                                                              rhs=vbt[:, h * 128:(h + 1) * 128], start=True, stop=True),
                                 reads=[B["koe"], Bvbt], writes=[BPS[pu]])
                        yield
                        for c in corder:
                            S.op("dve", lambda e, t=t, c=c, h=h: e.tensor_scalar(out=t["Tp"][:, c, :], in0=Tst[:, h, :], scalar1=t["wc"][:, c:c + 1],
                                                                                scalar2=None, op0=ALU.mult), reads=[BT[h], B["wc"]], writes=[B["Tp"]])
                            S.op("dve", lambda e, t=t, c=c, h=h, pu=pu: e.scalar_tensor_tensor(out=Tst[:, h, :], in0=Tst[:, h, :], scalar=t["wc"][:, c:c + 1],
                                                                                              in1=PS[pu][:, c * 128:(c + 1) * 128], op0=ALU.mult, op1=ALU.add),
                                 reads=[BT[h], B["wc"], BPS[pu]], writes=[BT[h]])
                        yield
                        S.op("pe", lambda e, t=t, po=po, h=h: e.matmul(out=PS[po][:, 0:128], lhsT=t["At"][:], rhs=vbt[:, h * 128:(h + 1) * 128],
                                                                       start=True, stop=False), reads=[B["At"], Bvbt], writes=[BPS[po]])
                        yield
                        for ci, c in enumerate(corder):
                            S.op("pe", lambda e, t=t, po=po, c=c, ci=ci: e.matmul(out=PS[po][:, 0:128], lhsT=t["qe"][:, c, :], rhs=t["Tp"][:, c, :],
                                                                                  start=False, stop=(ci == 3)), reads=[B["qe"], B["Tp"]], writes=[BPS[po]])
                        yield
                        S.op("act", lambda e, ob=ob, po=po, h=h: e.activation(out=ob[:, h * 128:(h + 1) * 128], in_=PS[po][:, 0:128], func=AF.Copy),
                             reads=[BPS[po]], writes=[Bo])
                        yield
                        if last_of_seg:
                            S.dma("act", ns_h[j, seg, d, h], Tst[:, h, :], reads=[BT[h]], writes=[By])
                        if h == 7:
                            S.dma("sp", oscr[d, i * 128:(i + 1) * 128, :], ob[:], reads=[Bo], writes=[Bscr[d]])
                    run_interleaved((hgen(bi, i, h) for bi, i in enumerate(order) for h in range(8)), 2)
                S.barrier()

        def post_mixer(l, j, kind, hT, BhT):
            hcols = lambda i, k: hT[:, k, i // 2, PADL + (i % 2) * 128: PADL + (i % 2) * 128 + 128]
            with contextlib.ExitStack() as ph:
                wo = sb(ph, "wo", [128, 8, D], BF16)
                Bwo = S.buf("wo", "own")
                src_wo = h_wo[j] if kind == "h" else r_wo[j]
                S.dma("pool", wo[:], src_wo.rearrange("(k p) n -> p k n", p=128), writes=[Bwo])
                ot = [[sb(ph, "ot%d_%d" % (dd, s_), [128, D]) for s_ in range(2)] for dd in range(2)]
                Bot = [[S.buf("ot%d_%d" % (dd, s_), "own") for s_ in range(2)] for dd in range(2)]
                zb = sb(ph, "zb", [128, D], BF16)
                Bzb = S.buf("zb")
                zT = sb(ph, "zT", [128, 8, 128], BF16)
                BzT = S.buf("zT")
                lng = sb(ph, "lng", [128, D])
                lnb = sb(ph, "lnb", [128, D])
                Bln = S.buf("lnbc", "own")
                S.dma("sp", lng[:], ln_g[l, 0].partition_broadcast(128), writes=[Bln])
                S.dma("sp", lnb[:], ln_b[l, 0].partition_broadcast(128), writes=[Bln])
                gb_, Bgb_ = make_gbc(ph, 0)
                lt = [dict(v=sb(ph, "lnv%d" % i, [128, D]), Bv=S.buf("lnv%d" % i), st=sb(ph, "lnst%d" % i, [128, 16]), Bst=S.buf("lnst%d" % i),
                           lng=lng, lnb=lnb, Bln=Bln, gbc=gb_, Bgbc=Bgb_) for i in range(2)]
                if kind == "h":
                    wg = sb(ph, "wg", [128, 8, D], BF16)
                    Bwg = S.buf("wg", "own")
                    S.dma("pool", wg[:], h_win[j].rearrange("(k p) n -> p k n", p=128)[:, :, 6144:7168], writes=[Bwg])
                    ngbc = sb(ph, "ngbc", [128, D])
                    Bng = S.buf("ngbc", "own")
                    S.dma("sp", ngbc[:], h_ng[j].partition_broadcast(128), writes=[Bng])
                    sq = sb(ph, "sq", [128, D])
                    Bsq = S.buf("sq")
                    sgt = sb(ph, "sgt", [128, D])
                    Bsg = S.buf("sgt")
                    ss = sb(ph, "ss", [128, 8])
                    Bss = S.buf("ss")
                else:
                    gla = sb(ph, "gla", [128, 8, 160], BF16)
                    glb1 = sb(ph, "glb1", [128, D], BF16)
                    glb2 = sb(ph, "glb2", [32, D], BF16)
                    Bgl = S.buf("gl", "own")
                    S.dma("pool", gla[:], r_gla[j].rearrange("(k p) n -> p k n", p=128), writes=[Bgl])
                    S.dma("pool", glb1[:], r_glb[j, 0:128, :], writes=[Bgl])
                    S.dma("pool", glb2[:], r_glb[j, 128:160, :], writes=[Bgl])
                    muT = sb(ph, "muTg", [128, 8])
                    S.dma("sp", muT[:], r_vecT[j, 5], writes=[Bgl])
                    xt = sb(ph, "xg_t", [128, 8, 128])
                    xs = sb(ph, "xg_s", [128, 8, 128], BF16)
                    Bxt = S.buf("xg_t")
                    Bxs = S.buf("xg_s")
                    sg1 = sb(ph, "sg1", [128, 128], BF16)
                    sg2 = sb(ph, "sg2", [32, 128], BF16)
                    Bsgg = S.buf("sgg")
                for i in range(NT):
                    seg = i // 2
                    s_ = i % 2
                    for dd in range(2):
                        S.dma(hwq(), ot[dd][s_][:], oscr[dd, i * 128:(i + 1) * 128, :], reads=[Bscr[dd]], writes=[Bot[dd][s_]])
                    o0, o1 = ot[0][s_], ot[1][s_]
                    S.op("pool", lambda e, o0=o0, o1=o1: e.tensor_tensor(out=o0[:], in0=o0[:], in1=o1[:], op=ALU.add),
                         reads=[Bot[0][s_], Bot[1][s_]], writes=[Bot[0][s_]])
                    if kind == "h":
                        for nh in range(2):
                            for k in range(8):
                                S.op("pe", lambda e, i=i, k=k, nh=nh: e.matmul(out=PS[nh][:, :], lhsT=hcols(i, k), rhs=wg[:, k, nh * 512:(nh + 1) * 512],
                                                                               start=(k == 0), stop=(k == 7)), reads=[BhT[seg], Bwg], writes=[BPS[nh]])
                            S.op("act", lambda e, nh=nh: e.activation(out=sgt[:, nh * 512:(nh + 1) * 512], in_=PS[nh][:, :], func=AF.Silu),
                                 reads=[BPS[nh]], writes=[Bsg])
                        S.op("act", lambda e, o0=o0: e.activation(out=sq[:], in_=o0[:], func=AF.Square), reads=[Bot[0][s_]], writes=[Bsq])
                        S.op("dve", lambda e: e.tensor_reduce(out=ss[:], in_=sq[:].rearrange("p (h k) -> p h k", k=128), axis=AX.X, op=ALU.add),
                             reads=[Bsq], writes=[Bss])
                        S.op("dve", lambda e: e.tensor_scalar(out=ss[:], in0=ss[:], scalar1=1.0 / 128.0, scalar2=RMS_EPS, op0=ALU.mult, op1=ALU.add),
                             reads=[Bss], writes=[Bss])
                        S.op("act", lambda e: e.activation(out=ss[:], in_=ss[:], func=AF.Sqrt), reads=[Bss], writes=[Bss])
                        S.op("dve", lambda e: e.reciprocal(out=ss[:], in_=ss[:]), reads=[Bss], writes=[Bss])
                        S.op("dve", lambda e, o0=o0: e.tensor_tensor(out=o0[:].rearrange("p (h k) -> p h k", k=128), in0=o0[:].rearrange("p (h k) -> p h k", k=128),
                                                                    in1=ss[:].unsqueeze(2).to_broadcast([128, 8, 128]), op=ALU.mult),
                             reads=[Bot[0][s_], Bss], writes=[Bot[0][s_]])
                        S.op("pool", lambda e, o0=o0: e.tensor_tensor(out=o0[:], in0=o0[:], in1=ngbc[:], op=ALU.mult), reads=[Bot[0][s_], Bng], writes=[Bot[0][s_]])
                        S.op("dve", lambda e, o0=o0: e.tensor_tensor(out=zb[:], in0=o0[:], in1=sgt[:], op=ALU.mult), reads=[Bot[0][s_], Bsg], writes=[Bzb])
                    else:
                        b_ = i % 2
                        hL = hT[:, :, seg, PADL - 1 + b_ * 128: PADL - 1 + b_ * 128 + 128]
                        hR = hT[:, :, seg, PADL + 1 + b_ * 128: PADL + 1 + b_ * 128 + 128]
                        hC = hT[:, :, seg, PADL + b_ * 128: PADL + b_ * 128 + 128]
                        S.op("pool", lambda e, hL=hL, hR=hR: e.tensor_tensor(out=xt[:], in0=hL, in1=hR, op=ALU.add), reads=[BhT[seg]], writes=[Bxt])
                        S.op("dve", lambda e, hC=hC: e.scalar_tensor_tensor(out=xt[:], in0=xt[:], scalar=0.5, in1=hC, op0=ALU.mult, op1=ALU.subtract),
                             reads=[Bxt, BhT[seg]], writes=[Bxt])
                        S.op("dve", lambda e: e.tensor_tensor(out=xt[:], in0=xt[:], in1=muT[:].unsqueeze(2).to_broadcast([128, 8, 128]), op=ALU.mult),
                             reads=[Bxt, Bgl], writes=[Bxt])
                        S.op("dve", lambda e, hC=hC: e.tensor_tensor(out=xs[:], in0=xt[:], in1=hC, op=ALU.add), reads=[Bxt, BhT[seg]], writes=[Bxs])
                        for k in range(8):
                            S.op("pe", lambda e, k=k: e.matmul(out=PS[2][:, 0:128], lhsT=gla[:, k, 0:128], rhs=xs[:, k, :], start=(k == 0), stop=(k == 7)),
                                 reads=[Bgl, Bxs], writes=[BPS[2]])
                        for k in range(8):
                            S.op("pe", lambda e, k=k: e.matmul(out=PS[2][0:32, 128:256], lhsT=gla[:, k, 128:160], rhs=xs[:, k, :], start=(k == 0), stop=(k == 7)),
                                 reads=[Bgl, Bxs], writes=[BPS[2]])
                        S.op("act", lambda e: e.activation(out=sg1[:], in_=PS[2][:, 0:128], func=AF.Sigmoid), reads=[BPS[2]], writes=[Bsgg])
                        S.op("act", lambda e: e.activation(out=sg2[:], in_=PS[2][0:32, 128:256], func=AF.Sigmoid), reads=[BPS[2]], writes=[Bsgg])
                        for nh in range(2):
                            S.op("pe", lambda e, nh=nh: e.matmul(out=PS[nh][:, :], lhsT=sg1[:], rhs=glb1[:, nh * 512:(nh + 1) * 512], start=True, stop=False),
                                 reads=[Bsgg, Bgl], writes=[BPS[nh]])
                            S.op("pe", lambda e, nh=nh: e.matmul(out=PS[nh][:, :], lhsT=sg2[:], rhs=glb2[:, nh * 512:(nh + 1) * 512], start=False, stop=True),
                                 reads=[Bsgg, Bgl], writes=[BPS[nh]])
                            S.op("dve", lambda e, nh=nh, o0=o0: e.tensor_tensor(out=zb[:, nh * 512:(nh + 1) * 512], in0=o0[:, nh * 512:(nh + 1) * 512],
                                                                               in1=PS[nh][:, :], op=ALU.mult), reads=[Bot[0][s_], BPS[nh]], writes=[Bzb])
                    pzb = PS[3][:].bitcast(BF16)
                    for k in range(8):
                        S.op("pe", lambda e, k=k, pzb=pzb: e.transpose(out=pzb[:, k * 128:(k + 1) * 128], in_=zb[:, k * 128:(k + 1) * 128], identity=identb[:]),
                             reads=[Bzb, Bc], writes=[BPS[3]])
                    S.op("act", lambda e, pzb=pzb: e.activation(out=zT[:].rearrange("p k t -> p (k t)"), in_=pzb[:, 0:1024], func=AF.Copy),
                         reads=[BPS[3]], writes=[BzT])
                    pa, pbk = 4 + 2 * (i % 2), 5 + 2 * (i % 2)
                    for nh, pbank in ((0, pa), (1, pbk)):
                        for k in range(8):
                            S.op("pe", lambda e, k=k, nh=nh, pbank=pbank: e.matmul(out=PS[pbank][:, :], lhsT=zT[:, k, :], rhs=wo[:, k, nh * 512:(nh + 1) * 512],
                                                                                  start=(k == 0), stop=(k == 7)), reads=[BzT, Bwo], writes=[BPS[pbank]])
                    resid_ln(i, PS[pa], PS[pbk], BPS[pa], BPS[pbk], 0, lt[i % 2])
                S.barrier()

        def rwkv_layer(l, j, hT, BhT):
            with contextlib.ExitStack() as ph:
                vec = sb(ph, "rvec", [128, 16, 8])
                Bvec = S.buf("rvec", "own")
                S.dma("sp", vec[:], r_vecT[j].rearrange("n p k -> p n k"), writes=[Bvec])
                cF = {}
                Bk = S.buf("rconst", "own")
                for nm, shp, dt in (("rmask64", [128, 128], BF16), ("colmask2", [128, 2, 128], BF16), ("headind", [128, 2], F32),
                                    ("blockones", [128, 128], F32), ("sel64", [128, 2, 64], F32)):
                    cF[nm] = sb(ph, "rc_" + nm, shp, dt)
                    S.dma("pool" if dt == BF16 else "sp", cF[nm][:], cst[nm], writes=[Bk])
                mk1 = sb(ph, "mk1", [128, 256], BF16)
                mk2 = sb(ph, "mk2", [128, 256], BF16)
                mkn = sb(ph, "mkn", [128, 128], BF16)
                gng = sb(ph, "gng", [128, D])
                gnb = sb(ph, "gnb", [128, D])
                Bdirc = S.buf("dirc", "own")
                W3 = [sb(ph, "rw%d" % i, [128, 8, D], BF16) for i in range(3)]
                BW3 = [S.buf("rw%d" % i, "own") for i in range(3)]
                wla = sb(ph, "wla", [128, 8, 64], BF16)
                ala = sb(ph, "ala", [128, 8, 64], BF16)
                wlb = sb(ph, "wlb", [64, D], BF16)
                alb = sb(ph, "alb", [64, D], BF16)
                Blo = S.buf("lora", "own")
                Tst = sb(ph, "rT", [64, 16, 64])
                BT = [S.buf("rT%d" % h) for h in range(16)]
                GT = S.group("GT")
                xx = sb(ph, "xx", [128, 8, 128])
                Bxx = S.buf("xx")
                xs_r = sb(ph, "xs_r", [128, 8, 128], BF16)
                xs_k = sb(ph, "xs_k", [128, 8, 128], BF16)
                xs_s = sb(ph, "xs_s", [128, 8, 128], BF16)
                xsn = [xs_r, xs_s, xs_k, xs_s, xs_s]
                Bxs_r, Bxs_k, Bxs_s = S.buf("xs_r"), S.buf("xs_k"), S.buf("xs_s")
                Bxs = [Bxs_r, Bxs_s, Bxs_k, Bxs_s, Bxs_s]
                vf = sb(ph, "vf", [128, D])
                vb = vf
                Bvf = S.buf("vf")
                Bvb = Bvf
                tw = sb(ph, "tw", [64, 128], BF16)
                ta = sb(ph, "ta", [64, 128], BF16)
                Btw = S.buf("tw")
                Bta = S.buf("ta")
                yblk = sb(ph, "yblk", [128, D])
                Byb = S.buf("yblk")
                bon = sb(ph, "bon", [128, 16])
                Bbon = S.buf("bon")
                Bg = {nm: S.buf("gn_" + nm) for nm in ("st",)}
                gst = sb(ph, "gn_st", [128, 48])
                ct = {}
                for nm in ("sg", "aa", "kk", "kk2", "t", "Ls", "Dsg", "emD"):
                    ct[nm] = sb(ph, "c_" + nm, [128, 128])
                ct["rs"] = ct["kk2"]
                ct["prod"] = ct["kk2"]
                ct["k2"] = ct["t"]
                ct["Ds2"] = ct["Ls"]
                ct["bb"] = ct["aa"]
                ct["emD2"] = ct["Ls"]
                ct["eD"] = ct["Dsg"]
                ct["wc"] = sb(ph, "c_wc", [128, 2])
                KB = sb(ph, "KB", [128, 2, 128])
                QR = sb(ph, "QR", [128, 2, 128])
                KoT = sb(ph, "KoTm", [128, 2, 128])
                BoTn = sb(ph, "BoTnm", [128, 2, 128])
                Bct = {nm: S.buf("c_" + nm) for nm in list(ct.keys()) + ["KB", "QR", "KoT", "BoTn"]}
                Bct["rs"] = Bct["kk2"]
                Bct["prod"] = Bct["kk2"]
                Bct["k2"] = Bct["t"]
                Bct["Ds2"] = Bct["Ls"]
                Bct["bb"] = Bct["aa"]
                Bct["emD2"] = Bct["Ls"]
                Bct["eD"] = Bct["Dsg"]
                HX = []
                for s_ in range(2):
                    hx = dict(A1=sb(ph, "A1", [128, 256]), A2=sb(ph, "A2", [128, 256]), Nm=sb(ph, "Nm", [128, 128]), Zm=sb(ph, "Zm", [128, 128]),
                              RH=sb(ph, "RH", [128, 128]), RpE=sb(ph, "RpE", [64, 2, 128]), P2T=sb(ph, "P2T", [64, 2, 64]), T0p=sb(ph, "T0p", [64, 2, 64]),
                              wch=sb(ph, "wch", [64, 2]))
                    hx["B"] = {nm: S.buf("h%d_%s" % (s_, nm)) for nm in ("A1", "A2", "Nm", "Zm", "RH", "RpE", "P2T", "T0p", "wch")}
                    HX.append(hx)
                for d in range(2):
                    S.dma("pool", mk1[:], cst["mk1"][d], writes=[Bdirc])
                    S.dma("pool", mk2[:], cst["mk2"][d], writes=[Bdirc])
                    S.dma("pool", mkn[:], cst["mkn"][d], writes=[Bdirc])
                    S.dma("sp", gng[:], r_gng[j, d].partition_broadcast(128), writes=[Bdirc])
                    S.dma("sp", gnb[:], r_gnb[j, d].partition_broadcast(128), writes=[Bdirc])
                    for n3 in range(3):
                        S.dma("pool", W3[n3][:], r_wrkv[j, n3].rearrange("(k p) n -> p k n", p=128)[:, :, d * D:(d + 1) * D], writes=[BW3[n3]])
                    S.dma("pool", wla[:], r_wla[j].rearrange("(k p) z r -> p k z r", p=128)[:, :, d, :], writes=[Blo])
                    S.dma("pool", ala[:], r_ala[j].rearrange("(k p) z r -> p k z r", p=128)[:, :, d, :], writes=[Blo])
                    S.dma("pool", wlb[:], r_wlb[j, d], writes=[Blo])
                    S.dma("pool", alb[:], r_alb[j, d], writes=[Blo])
                    S.dma("sp", Tst[:], st_r[j, d].rearrange("h k v -> k h v"), writes=BT, grp=GT)
                    vcol = lambda n: vec[:, 6 + n * 2 + d, :]
                    order = list(range(NT)) if d == 0 else list(range(NT - 1, -1, -1))
                    for bi, i in enumerate(order):
                        seg = i // 2
                        b_ = i % 2
                        first_of_seg = (b_ == 0) if d == 0 else (b_ == 1)
                        last_of_seg = not first_of_seg
                        corder = [0, 1] if d == 0 else [1, 0]
                        if first_of_seg and bi > 0:
                            S.op("dve", lambda e: e.tensor_scalar(out=Tst[:], in0=Tst[:], scalar1=flg[0:64, 0:1], scalar2=None, op0=ALU.mult),
                                 reads=BT + [Bc], writes=BT)
                        _rt[0] += 1
                        S.skip = (_LIM["rstep"] < 1) or (_rt[0] > _LIM["rtiles"])
                        hL = hT[:, :, seg, PADL - 1 + b_ * 128: PADL - 1 + b_ * 128 + 128]
                        hR = hT[:, :, seg, PADL + 1 + b_ * 128: PADL + 1 + b_ * 128 + 128]
                        hC = hT[:, :, seg, PADL + b_ * 128: PADL + b_ * 128 + 128]
                        S.op("pool", lambda e, hL=hL, hR=hR: e.tensor_tensor(out=xx[:], in0=hL, in1=hR, op=ALU.add), reads=[BhT[seg]], writes=[Bxx])
                        S.op("dve", lambda e, hC=hC: e.scalar_tensor_tensor(out=xx[:], in0=xx[:], scalar=0.5, in1=hC, op0=ALU.mult, op1=ALU.subtract),
                             reads=[Bxx, BhT[seg]], writes=[Bxx])
                        def mk_xs(n, eng):
                            S.op(eng, lambda e: e.tensor_tensor(out=xsn[n][:], in0=xx[:], in1=vec[:, n, :].unsqueeze(2).to_broadcast([128, 8, 128]), op=ALU.mult),
                                 reads=[Bxx, Bvec], writes=[Bxs[n]])
                            S.op(eng, lambda e: e.tensor_tensor(out=xsn[n][:], in0=xsn[n][:], in1=hC, op=ALU.add), reads=[Bxs[n], BhT[seg]], writes=[Bxs[n]])
                        mk_xs(3, "dve")
                        mk_xs(0, "pool")
                        for nh in range(2):
                            pb = 6 + nh
                            for k in range(8):
                                S.op("pe", lambda e, k=k, nh=nh, pb=pb: e.matmul(out=PS[pb][:, :], lhsT=xsn[3][:, k, :], rhs=W3[2][:, k, nh * 512:(nh + 1) * 512],
                                                                               start=(k == 0), stop=(k == 7)), reads=[Bxs[3], BW3[2]], writes=[BPS[pb]])
                            S.op("act", lambda e, nh=nh, pb=pb: e.activation(out=vf[:, nh * 512:(nh + 1) * 512], in_=PS[pb][:, :], func=AF.Copy),
                                 reads=[BPS[pb]], writes=[Bvf])
                        mk_xs(1, "dve")
                        mk_xs(2, "pool")
                        for k in range(8):
                            S.op("pe", lambda e, k=k: e.matmul(out=PS[6][0:64, 0:128], lhsT=wla[:, k, :], rhs=xsn[1][:, k, :], start=(k == 0), stop=(k == 7)),
                                 reads=[Blo, Bxs[1]], writes=[BPS[6]])
                        mk_xs(4, "dve")
                        for k in range(8):
                            S.op("pe", lambda e, k=k: e.matmul(out=PS[6][0:64, 128:256], lhsT=ala[:, k, :], rhs=xsn[4][:, k, :], start=(k == 0), stop=(k == 7)),
                                 reads=[Blo, Bxs[4]], writes=[BPS[6]])
                        S.op("act", lambda e: e.activation(out=tw[:], in_=PS[6][0:64, 0:128], func=AF.Tanh), reads=[BPS[6]], writes=[Btw])
                        S.op("act", lambda e: e.activation(out=ta[:], in_=PS[6][0:64, 128:256], func=AF.Copy), reads=[BPS[6]], writes=[Bta])
                        for c in range(8):
                            cs = slice(c * 128, (c + 1) * 128)
                            S.skip = (_LIM["rstep"] < 4) or (_rt[0] > _LIM["rtiles"])
                            pp = PS[0]
                            for k in range(8):
                                S.op("pe", lambda e, k=k, cs=cs: e.matmul(out=pp[:, 0:128], lhsT=W3[0][:, k, cs], rhs=xsn[0][:, k, :], start=(k == 0), stop=(k == 7)),
                                     reads=[BW3[0], Bxs[0]], writes=[BPS[0]])
                            for k in range(8):
                                S.op("pe", lambda e, k=k, cs=cs: e.matmul(out=pp[:, 128:256], lhsT=W3[1][:, k, cs], rhs=xsn[2][:, k, :], start=(k == 0), stop=(k == 7)),
                                     reads=[BW3[1], Bxs[2]], writes=[BPS[0]])
                            S.op("pe", lambda e, cs=cs: e.matmul(out=pp[:, 256:384], lhsT=wlb[:, cs], rhs=tw[:], start=True, stop=True), reads=[Blo, Btw], writes=[BPS[0]])
                            S.op("pe", lambda e, cs=cs: e.matmul(out=pp[:, 384:512], lhsT=alb[:, cs], rhs=ta[:], start=True, stop=True), reads=[Blo, Bta], writes=[BPS[0]])
                            pr, pk = pp[:, 0:128], pp[:, 128:256]
                            S.skip = (_LIM["rstep"] < 5) or (_rt[0] > _LIM["rtiles"])
                            S.op("act", lambda e, c=c: e.activation(out=ct["sg"][:], in_=pp[:, 256:384], func=AF.Sigmoid, bias=vcol(0)[:, c:c + 1], scale=1.0),
                                 reads=[BPS[0], Bvec], writes=[Bct["sg"]])
                            S.op("act", lambda e, c=c: e.activation(out=ct["aa"][:], in_=pp[:, 384:512], func=AF.Sigmoid, bias=vcol(1)[:, c:c + 1], scale=1.0),
                                 reads=[BPS[0], Bvec], writes=[Bct["aa"]])
                            S.op("dve", lambda e, c=c: e.tensor_scalar(out=ct["kk"][:], in0=pk, scalar1=vcol(2)[:, c:c + 1], scalar2=None, op0=ALU.mult),
                                 reads=[BPS[0], Bvec], writes=[Bct["kk"]])
                            S.op("pool", lambda e: e.tensor_tensor(out=ct["kk2"][:], in0=ct["kk"][:], in1=ct["kk"][:], op=ALU.mult), reads=[Bct["kk"]], writes=[Bct["kk2"]])
                            S.op("pe", lambda e: e.matmul(out=PS[1][:, 0:128], lhsT=cF["blockones"][:], rhs=ct["kk2"][:], start=True, stop=True),
                                 reads=[Bk, Bct["kk2"]], writes=[BPS[1]])
                            S.op("dve", lambda e: e.tensor_scalar_max(out=ct["rs"][:], in0=PS[1][:, 0:128], scalar1=1e-24), reads=[BPS[1]], writes=[Bct["rs"]])
                            S.op("act", lambda e: e.activation(out=ct["rs"][:], in_=ct["rs"][:], func=AF.Sqrt), reads=[Bct["rs"]], writes=[Bct["rs"]])
                            S.op("dve", lambda e: e.reciprocal(out=ct["rs"][:], in_=ct["rs"][:]), reads=[Bct["rs"]], writes=[Bct["rs"]])
                            S.op("dve", lambda e: e.tensor_tensor(out=ct["kk"][:], in0=ct["kk"][:], in1=ct["rs"][:], op=ALU.mult), reads=[Bct["kk"], Bct["rs"]], writes=[Bct["kk"]])
                            S.skip = (_LIM["rstep"] < 6) or (_rt[0] > _LIM["rtiles"])
                            S.op("dve", lambda e, c=c: e.tensor_scalar(out=ct["t"][:], in0=ct["aa"][:], scalar1=1.0, scalar2=vcol(3)[:, c:c + 1], op0=ALU.subtract, op1=ALU.mult),
                                 reads=[Bct["aa"], Bvec], writes=[Bct["t"]])
                            S.op("dve", lambda e: e.scalar_tensor_tensor(out=ct["k2"][:], in0=ct["t"][:], scalar=1.0, in1=pk, op0=ALU.add, op1=ALU.mult),
                                 reads=[Bct["t"], BPS[0]], writes=[Bct["k2"]])
                            S.op("pool", lambda e: e.tensor_tensor(out=ct["bb"][:], in0=ct["kk"][:], in1=ct["aa"][:], op=ALU.mult), reads=[Bct["kk"], Bct["aa"]], writes=[Bct["bb"]])
                            S.op("dve", lambda e, c=c: e.scalar_tensor_tensor(out=ct["prod"][:], in0=pr, scalar=vcol(4)[:, c:c + 1], in1=ct["k2"][:], op0=ALU.mult, op1=ALU.mult),
                                 reads=[BPS[0], Bvec, Bct["k2"]], writes=[Bct["prod"]])
                            S.op("pe", lambda e, c=c: e.matmul(out=PS[7][:, 2 * c:2 * c + 2], lhsT=ct["prod"][:], rhs=cF["headind"][:], start=True, stop=True),
                                 reads=[Bct["prod"], Bk], writes=[BPS[7]])
                            S.skip = (_LIM["rstep"] < 7) or (_rt[0] > _LIM["rtiles"])
                            S.op("dve", lambda e: e.tensor_tensor_scan(out=ct["Ls"][:], data0=cF["rmask64"][:], data1=ct["sg"][:], initial=0.0, op0=ALU.mult, op1=ALU.add),
                                 reads=[Bct["sg"], Bk], writes=[Bct["Ls"]])
                            Lv = ct["Ls"][:].rearrange("p (c t) -> p c t", t=64)
                            Dv = ct["Dsg"][:].rearrange("p (c t) -> p c t", t=64)
                            if d == 0:
                                S.op("dve", lambda e, Lv=Lv, Dv=Dv: e.tensor_tensor(out=Dv, in0=Lv[:, :, 63:64].to_broadcast([128, 2, 64]), in1=Lv, op=ALU.subtract),
                                     reads=[Bct["Ls"]], writes=[Bct["Dsg"]])
                            else:
                                S.op("dve", lambda e: e.tensor_tensor(out=ct["Dsg"][:], in0=ct["Ls"][:], in1=ct["sg"][:], op=ALU.subtract),
                                     reads=[Bct["Ls"], Bct["sg"]], writes=[Bct["Dsg"]])
                            S.op("act", lambda e, Lv=Lv: e.activation(out=ct["wc"][:].unsqueeze(2), in_=Lv[:, :, 63:64], func=AF.Exp, scale=-C0), reads=[Bct["Ls"]], writes=[Bct["wc"]])
                            S.op("pool", lambda e: e.tensor_tensor(out=ct["Ds2"][:], in0=ct["Dsg"][:], in1=ct["sg"][:], op=ALU.add), reads=[Bct["Dsg"], Bct["sg"]], writes=[Bct["Ds2"]])
                            S.op("act", lambda e: e.activation(out=ct["emD"][:], in_=ct["Dsg"][:], func=AF.Exp, scale=C0), reads=[Bct["Dsg"]], writes=[Bct["emD"]])
                            S.op("act", lambda e: e.activation(out=ct["eD"][:], in_=ct["Dsg"][:], func=AF.Exp, scale=-C0), reads=[Bct["Dsg"]], writes=[Bct["eD"]])
                            S.op("act", lambda e: e.activation(out=ct["emD2"][:], in_=ct["Ds2"][:], func=AF.Exp, scale=C0), reads=[Bct["Ds2"]], writes=[Bct["emD2"]])
                            S.skip = (_LIM["rstep"] < 8) or (_rt[0] > _LIM["rtiles"])
                            S.op("pool", lambda e: e.tensor_tensor(out=KB[:, 0, :], in0=ct["k2"][:], in1=ct["eD"][:], op=ALU.mult), reads=[Bct["k2"], Bct["eD"]], writes=[Bct["KB"]])
                            S.op("pool", lambda e: e.tensor_tensor(out=KB[:, 1, :], in0=ct["bb"][:], in1=ct["eD"][:], op=ALU.mult), reads=[Bct["bb"], Bct["eD"]], writes=[Bct["KB"]])
                            S.op("pool", lambda e: e.tensor_tensor(out=QR[:, 0, :], in0=ct["kk"][:], in1=ct["emD2"][:], op=ALU.mult), reads=[Bct["kk"], Bct["emD2"]], writes=[Bct["QR"]])
                            S.op("dve", lambda e: e.tensor_tensor(out=QR[:, 1, :], in0=pr, in1=ct["emD"][:], op=ALU.mult), reads=[BPS[0], Bct["emD"]], writes=[Bct["QR"]])
                            S.skip = (_LIM["rstep"] < 9) or (_rt[0] > _LIM["rtiles"])
                            ptb = PS[1]
                            S.op("pe", lambda e, ptb=ptb: e.transpose(out=ptb[:, 128:256], in_=KB[:, 0, :], identity=ident[:]), reads=[Bct["KB"], Bc], writes=[BPS[1]])
                            S.op("pe", lambda e, ptb=ptb: e.transpose(out=ptb[:, 256:384], in_=KB[:, 1, :], identity=ident[:]), reads=[Bct["KB"], Bc], writes=[BPS[1]])
                            S.op("pe", lambda e, ptb=ptb: e.transpose(out=ptb[:, 384:512], in_=QR[:, 0, :], identity=ident[:]), reads=[Bct["QR"], Bc], writes=[BPS[1]])
                            hib = cF["headind"][:].unsqueeze(2).to_broadcast([128, 2, 128])
                            S.op("dve", lambda e, ptb=ptb, hib=hib: e.tensor_tensor(out=KoT[:], in0=ptb[:, 128:256].unsqueeze(1).to_broadcast([128, 2, 128]), in1=hib, op=ALU.mult),
                                 reads=[BPS[1], Bk], writes=[Bct["KoT"]])
                            S.op("dve", lambda e, ptb=ptb, hib=hib: e.scalar_tensor_tensor(out=BoTn[:], in0=ptb[:, 256:384].unsqueeze(1).to_broadcast([128, 2, 128]), scalar=-1.0, in1=hib,
                                                                                          op0=ALU.mult, op1=ALU.mult), reads=[BPS[1], Bk], writes=[Bct["BoTn"]])
                            def head_gen(c=c, hh=None, ptb=ptb):
                                head = 2 * c + hh
                                prs = slice(64 * hh, 64 * hh + 64)
                                hc = slice(head * 64, head * 64 + 64)
                                hcl = slice(hh * 64, hh * 64 + 64)
                                hx = HX[hh]
                                A1, A2, Nmt, Zmt, RH, RpE, P2T, T0p, wch = hx["A1"], hx["A2"], hx["Nm"], hx["Zm"], hx["RH"], hx["RpE"], hx["P2T"], hx["T0p"], hx["wch"]
                                Bh = hx["B"]
                                PA, PB, BA, BB = PS[2 + 2 * hh], PS[3 + 2 * hh], BPS[2 + 2 * hh], BPS[3 + 2 * hh]
                                qr2 = QR[prs, :, :].rearrange("p a t -> p (a t)")
                                S.op("pe", lambda e: e.matmul(out=PA[:, 0:256], lhsT=KB[prs, 0, :], rhs=qr2, start=True, stop=True), reads=[Bct["KB"], Bct["QR"]], writes=[BA])
                                S.op("pe", lambda e: e.matmul(out=PA[:, 256:512], lhsT=KB[prs, 1, :], rhs=qr2, start=True, stop=True), reads=[Bct["KB"], Bct["QR"]], writes=[BA])
                                S.op("pe", lambda e: e.matmul(out=PB[:, 0:128], lhsT=QR[prs, 0, :], rhs=KB[prs, 1, :], start=True, stop=True), reads=[Bct["KB"], Bct["QR"]], writes=[BB])
                                yield
                                S.op("dve", lambda e: e.tensor_tensor(out=A1[:], in0=PA[:, 0:256], in1=mk1[:], op=ALU.mult), reads=[BA, Bdirc], writes=[Bh["A1"]])
                                S.op("dve", lambda e: e.tensor_tensor(out=A2[:], in0=PA[:, 256:512], in1=mk2[:], op=ALU.mult), reads=[BA, Bdirc], writes=[Bh["A2"]])
                                S.op("dve", lambda e: e.tensor_tensor(out=Nmt[:], in0=PB[:, 0:128], in1=mkn[:], op=ALU.mult), reads=[BB, Bdirc], writes=[Bh["Nm"]])
                                yield
                                S.op("pe", lambda e: e.matmul(out=PB[:, 128:192], lhsT=A1[:, 0:128], rhs=vb[:, hc], start=True, stop=True), reads=[Bh["A1"], Bvb], writes=[BB])
                                S.op("dve", lambda e: e.tensor_copy(out=RH[:, 0:64], in_=ptb[:, 384 + 64 * hh:448 + 64 * hh]), reads=[BPS[1]], writes=[Bh["RH"]])
                                yield
                                S.op("act", lambda e: e.activation(out=RH[:, 64:128], in_=PB[:, 128:192], func=AF.Copy), reads=[BB], writes=[Bh["RH"]])
                                yield
                                for lvl in range(6):
                                    if lvl == 0:
                                        Zc, BZc = A2[:, 0:128], Bh["A2"]
                                    else:
                                        Zc, BZc = Zmt[:], Bh["Zm"]
                                    Nc, BNc = Nmt[:], Bh["Nm"]
                                    S.op("pe", lambda e, Zc=Zc: e.matmul(out=PB[:, 256:384], lhsT=Zc, rhs=RH[:], start=True, stop=True), reads=[BZc, Bh["RH"]], writes=[BB])
                                    if lvl < 5:
                                        S.op("pe", lambda e, Zc=Zc, Nc=Nc: e.matmul(out=PA[:, 0:128], lhsT=Nc, rhs=Zc, start=True, stop=True), reads=[BZc, BNc], writes=[BA])
                                        if lvl < 4:
                                            S.op("pe", lambda e, Zc=Zc, Nc=Nc: e.matmul(out=PA[:, 128:256], lhsT=Zc, rhs=Nc, start=True, stop=True), reads=[BZc, BNc], writes=[BA])
                                    yield
                                    S.op("dve", lambda e, lvl=lvl: e.tensor_tensor(out=RH[:], in0=RH[:], in1=PB[:, 256:384], op=(ALU.subtract if lvl == 0 else ALU.add)),
                                         reads=[Bh["RH"], BB], writes=[Bh["RH"]])
                                    if lvl < 5:
                                        S.op("act", lambda e: e.activation(out=Zmt[:], in_=PA[:, 0:128], func=AF.Copy), reads=[BA], writes=[Bh["Zm"]])
                                        if lvl < 4:
                                            S.op("act", lambda e: e.activation(out=Nmt[:], in_=PA[:, 128:256], func=AF.Copy), reads=[BA], writes=[Bh["Nm"]])
                                    yield
                                S.op("pe", lambda e: e.matmul(out=PA[0:64, 256:384], lhsT=cF["sel64"][:, hh, :], rhs=QR[:, 1, :], start=True, stop=False), reads=[Bk, Bct["QR"]], writes=[BA])
                                S.op("pe", lambda e: e.matmul(out=PA[0:64, 256:384], lhsT=RH[:, 0:64], rhs=A2[:, 128:256], start=False, stop=True), reads=[Bh["RH"], Bh["A2"]], writes=[BA])
                                for cc in range(2):
                                    S.op("pe", lambda e, cc=cc: e.matmul(out=PA[0:64, 384 + 64 * cc:448 + 64 * cc], lhsT=RH[:, 0:64], rhs=BoTn[:, cc, hcl], start=True, stop=True),
                                         reads=[Bh["RH"], Bct["BoTn"]], writes=[BA])
                                S.op("pe", lambda e: e.matmul(out=PB[0:64, 192:194], lhsT=cF["sel64"][:, hh, :], rhs=ct["wc"][:], start=True, stop=True), reads=[Bk, Bct["wc"]], writes=[BB])
                                yield
                                S.op("dve", lambda e: e.tensor_tensor(out=RpE[:], in0=PA[0:64, 256:384].unsqueeze(1).to_broadcast([64, 2, 128]), in1=cF["colmask2"][0:64], op=ALU.mult),
                                     reads=[BA, Bk], writes=[Bh["RpE"]])
                                S.op("dve", lambda e: e.tensor_tensor(out=P2T[:], in0=PA[0:64, 384:512].rearrange("p (c k) -> p c k", c=2),
                                                                      in1=ident[0:64, 0:64].unsqueeze(1).to_broadcast([64, 2, 64]), op=ALU.add), reads=[BA, Bc], writes=[Bh["P2T"]])
                                S.op("act", lambda e: e.activation(out=wch[:], in_=PB[0:64, 192:194], func=AF.Copy), reads=[BB], writes=[Bh["wch"]])
                                yield
                                for cc in corder:
                                    S.op("dve", lambda e, cc=cc: e.tensor_scalar(out=T0p[:, cc, :], in0=Tst[:, head, :], scalar1=wch[:, cc:cc + 1], scalar2=None, op0=ALU.mult),
                                         reads=[BT[head], Bh["wch"]], writes=[Bh["T0p"]])
                                    S.op("pe", lambda e, cc=cc: e.matmul(out=PB[0:64, 384:448], lhsT=KoT[:, cc, hcl], rhs=vb[:, hc], start=True, stop=False), reads=[Bct["KoT"], Bvb], writes=[BB])
                                    S.op("pe", lambda e, cc=cc: e.matmul(out=PB[0:64, 384:448], lhsT=BoTn[:, cc, hcl], rhs=RH[:, 64:128], start=False, stop=False), reads=[Bct["BoTn"], Bh["RH"]], writes=[BB])
                                    S.op("pe", lambda e, cc=cc: e.matmul(out=PB[0:64, 384:448], lhsT=P2T[:, cc, :], rhs=T0p[:, cc, :], start=False, stop=True), reads=[Bh["P2T"], Bh["T0p"]], writes=[BB])
                                    yield
                                    S.op("dve", lambda e: e.tensor_copy(out=Tst[:, head, :], in_=PB[0:64, 384:448]), reads=[BB], writes=[BT[head]])
                                    yield
                                S.op("pe", lambda e: e.matmul(out=PB[:, 448:512], lhsT=A1[:, 128:256], rhs=vb[:, hc], start=True, stop=False), reads=[Bh["A1"], Bvb], writes=[BB])
                                S.op("pe", lambda e: e.matmul(out=PB[:, 448:512], lhsT=A2[:, 128:256], rhs=RH[:, 64:128], start=False, stop=False), reads=[Bh["A2"], Bh["RH"]], writes=[BB])
                                for cc in range(2):
                                    S.op("pe", lambda e, cc=cc: e.matmul(out=PB[:, 448:512], lhsT=RpE[:, cc, :], rhs=T0p[:, cc, :], start=False, stop=(cc == 1)), reads=[Bh["RpE"], Bh["T0p"]], writes=[BB])
                                yield
                                S.op("act", lambda e: e.activation(out=yblk[:, hc], in_=PB[:, 448:512], func=AF.Copy), reads=[BB], writes=[Byb])
                            run_interleaved([head_gen(hh=0), head_gen(hh=1)], 2)
                        S.skip = (_LIM["rstep"] < 17) or (_rt[0] > _LIM["rtiles"])
                        yv = yblk[:].rearrange("p (h n) -> p h n", n=64)
                        sqt = xx[:].rearrange("p k t -> p (k t)")
                        sv = sqt.rearrange("p (h n) -> p h n", n=64)
                        S.op("act", lambda e: e.activation(out=bon[:], in_=PS[7][:, 0:16], func=AF.Copy), reads=[BPS[7]], writes=[Bbon])
                        S.op("dve", lambda e: e.tensor_reduce(out=gst[:, 0:16], in_=yv, axis=AX.X, op=ALU.add), reads=[Byb], writes=[Bg["st"]])
                        S.op("dve", lambda e: e.tensor_scalar(out=gst[:, 0:16], in0=gst[:, 0:16], scalar1=1.0 / 64.0, scalar2=None, op0=ALU.mult), reads=[Bg["st"]], writes=[Bg["st"]])
                        S.op("dve", lambda e: e.tensor_tensor(out=yv, in0=yv, in1=gst[:, 0:16].unsqueeze(2).to_broadcast([128, 16, 64]), op=ALU.subtract),
                             reads=[Byb, Bg["st"]], writes=[Byb])
                        S.op("act", lambda e: e.activation(out=sqt, in_=yblk[:], func=AF.Square), reads=[Byb], writes=[Bxx])
                        S.op("dve", lambda e: e.tensor_reduce(out=gst[:, 16:32], in_=sv, axis=AX.X, op=ALU.add), reads=[Bxx], writes=[Bg["st"]])
                        S.op("dve", lambda e: e.tensor_scalar(out=gst[:, 16:32], in0=gst[:, 16:32], scalar1=1.0 / 64.0, scalar2=GN_EPS, op0=ALU.mult, op1=ALU.add),
                             reads=[Bg["st"]], writes=[Bg["st"]])
                        S.op("act", lambda e: e.activation(out=gst[:, 16:32], in_=gst[:, 16:32], func=AF.Sqrt), reads=[Bg["st"]], writes=[Bg["st"]])
                        S.op("dve", lambda e: e.reciprocal(out=gst[:, 16:32], in_=gst[:, 16:32]), reads=[Bg["st"]], writes=[Bg["st"]])
                        S.op("dve", lambda e: e.tensor_tensor(out=yv, in0=yv, in1=gst[:, 16:32].unsqueeze(2).to_broadcast([128, 16, 64]), op=ALU.mult),
                             reads=[Byb, Bg["st"]], writes=[Byb])
                        S.op("pool", lambda e: e.tensor_tensor(out=yblk[:], in0=yblk[:], in1=gng[:], op=ALU.mult), reads=[Byb, Bdirc], writes=[Byb])
                        S.op("pool", lambda e: e.tensor_tensor(out=yblk[:], in0=yblk[:], in1=gnb[:], op=ALU.add), reads=[Byb, Bdirc], writes=[Byb])
                        vfv = vf[:].rearrange("p (h n) -> p h n", n=64)
                        S.op("dve", lambda e: e.tensor_tensor(out=vfv, in0=vfv, in1=bon[:].unsqueeze(2).to_broadcast([128, 16, 64]), op=ALU.mult),
                             reads=[Bvf, Bbon], writes=[Bvf])
                        S.op("pool", lambda e: e.tensor_tensor(out=yblk[:], in0=yblk[:], in1=vf[:], op=ALU.add), reads=[Byb, Bvf], writes=[Byb])
                        S.skip = False
                        S.dma("sp", oscr[d, i * 128:(i + 1) * 128, :], yblk[:], reads=[Byb], writes=[Bscr[d]])
                        if last_of_seg:
                            S.dma("act", ns_r[j, seg, d].rearrange("h k v -> k h v"), Tst[:], reads=BT, writes=[By])
                S.barrier()

        for l in range(n_layers):
            j = l // 2
            S.phase = "L%d.ada" % l
            adaln(l)
            if mix:
                with contextlib.ExitStack() as lph:
                    hT = sb(lph, "hT", [128, 8, NSEG, HTW], BF16)
                    BhT = [S.buf("hT%d" % s_) for s_ in range(NSEG)]
                    S.op("pool", lambda e: e.memset(hT[:, :, :, PADL - 1:PADL], 0.0), writes=BhT)
                    S.op("pool", lambda e: e.memset(hT[:, :, :, PADL + 256:PADL + 257], 0.0), writes=BhT)
                    S.phase = "L%d.hT" % l
                    build_hT(lambda i, k: hT[:, k, i // 2, PADL + (i % 2) * 128: PADL + (i % 2) * 128 + 128], lambda i: BhT[i // 2], list(range(NT)), sc1p, 0)
                    if l % 2 == 1:
                        S.op("dve", lambda e: e.tensor_scalar(out=hT[:, :, 1:NSEG, PADL - 1], in0=hT[:, :, 0:NSEG - 1, PADL + 255], scalar1=flg[:, 0:1], scalar2=None, op0=ALU.mult),
                             reads=BhT + [Bc], writes=BhT)
                        S.op("dve", lambda e: e.tensor_scalar(out=hT[:, :, 0:NSEG - 1, PADL + 256], in0=hT[:, :, 1:NSEG, PADL], scalar1=flg[:, 0:1], scalar2=None, op0=ALU.mult),
                             reads=BhT + [Bc], writes=BhT)
                    S.barrier()
                    stopped = False
                    if stop == "hT" and l == n_layers - 1:
                        stopped = True
                    elif l % 2 == 0:
                        S.phase = "L%d.mix" % l
                        hgrn_layer(l, j, hT, BhT)
                        if stop == "mix" and l == n_layers - 1:
                            stopped = True
                        else:
                            S.phase = "L%d.post" % l
                            post_mixer(l, j, "h", hT, BhT)
                    else:
                        S.phase = "L%d.mix" % l
                        rwkv_layer(l, j, hT, BhT)
                        if stop == "mix" and l == n_layers - 1:
                            stopped = True
                        else:
                            S.phase = "L%d.post" % l
                            post_mixer(l, j, "r", hT, BhT)
                if stopped or (stop == "post" and l == n_layers - 1):
                    break
            S.phase = "L%d.ffn" % l
            ffn(l)

        for i in range(NT):
            S.dma(hwq(), y_out[i * 128:(i + 1) * 128, :], X[:, i, :], reads=[BX[i]], writes=[By])
        S.barrier()
        S.emit(block)
        globals()["_LAST_SCHED"] = S
    return nc


_PROMPT_SLOTS = [[(p, p // 6) for p in range(32) if p % 6 == cix] for cix in range(6)]


def _fm(v):
    return np.ascontiguousarray(np.asarray(v, np.float32).reshape(8, 128).T)


def make_in_maps(inp, n_layers=4):
    NLW = n_layers
    NH = max(1, (n_layers + 1) // 2)
    NR = max(1, n_layers // 2)
    f = lambda a: np.ascontiguousarray(np.asarray(a, dtype=np.float32))
    consts = _consts()
    pos = _pos_table()
    shared = {
        "pos": pos,
        "ada_w": f(inp["ada_w"]),
        "ada_bT": np.ascontiguousarray(f(inp["ada_b"]).reshape(4, 48, 128).transpose(0, 2, 1)),
        "ln_g": f(inp["ln_g"]), "ln_b": f(inp["ln_b"]),
        "ffn_w_up": f(inp["ffn_w_up"]), "ffn_w_down": f(inp["ffn_w_down"]),
        "hgrn_w_in": f(inp["hgrn_w_in"]),
        "hgrn_lbT": np.ascontiguousarray(f(inp["hgrn_lb"]).reshape(2, 16, 128).transpose(0, 2, 1)),
        "hgrn_norm_g": f(inp["hgrn_norm_g"]), "hgrn_w_o": f(inp["hgrn_w_o"]),
        "rwkv_w_rkv": f(inp["rwkv_w_rkv"]), "rwkv_w_la": f(inp["rwkv_w_la"]), "rwkv_w_lb": f(inp["rwkv_w_lb"]),
        "rwkv_a_la": f(inp["rwkv_a_la"]), "rwkv_a_lb": f(inp["rwkv_a_lb"]),
        "rwkv_g_la": f(inp["rwkv_g_la"]), "rwkv_g_lb": f(inp["rwkv_g_lb"]),
        "rwkv_gn_g": f(inp["rwkv_gn_g"]), "rwkv_gn_b": f(inp["rwkv_gn_b"]), "rwkv_w_o": f(inp["rwkv_w_o"]),
    }
    vec = np.zeros((2, 16, 128, 8), np.float32)
    for j in range(2):
        for n in range(6):
            vec[j, n] = _fm(inp["rwkv_mu"][j, n])
        for n, nm in enumerate(("rwkv_w0", "rwkv_a0", "rwkv_k_k", "rwkv_k_a", "rwkv_r_k")):
            for d in range(2):
                vec[j, 6 + n * 2 + d] = _fm(inp[nm][j, d])
    shared["rwkv_vecT"] = vec
    for k, v in consts.items():
        shared["c_" + k] = v
    xp = f(inp["x_prompt"])
    xs = f(inp["x_sample"])
    sth = f(inp["state_hgrn"])
    strw = f(inp["state_rwkv"])
    maps = []
    for core in range(8):
        m = dict(shared)
        if core < 2:
            m["x_in"] = np.ascontiguousarray(xs[core])
            m["condT"] = _fm(inp["c"][core])
            fl = np.ones((128, 2), np.float32)
            m["st_h"] = np.ascontiguousarray(sth[core])
            m["st_r"] = np.ascontiguousarray(strw[core].transpose(0, 1, 2, 4, 3))
        else:
            slots = _PROMPT_SLOTS[core - 2]
            xin = np.zeros((NSEG, 256, D), np.float32)
            for s_ in range(NSEG):
                xin[s_] = xp[slots[s_][0]] if s_ < len(slots) else xp[slots[0][0]]
            m["x_in"] = xin.reshape(2048, D)
            m["condT"] = _fm(inp["c_ctx"])
            fl = np.zeros((128, 2), np.float32)
            m["st_h"] = np.zeros((2, 2, 8, 128, 128), np.float32)
            m["st_r"] = np.zeros((2, 2, 16, 64, 64), np.float32)
        m["flags"] = fl
        maps.append(m)
    cut = {"ada_w": NLW, "ada_bT": NLW, "ln_g": NLW, "ln_b": NLW, "ffn_w_up": NLW, "ffn_w_down": NLW,
           "hgrn_w_in": NH, "hgrn_norm_g": NH, "hgrn_w_o": NH, "rwkv_w_rkv": NR, "rwkv_w_la": NR, "rwkv_w_lb": NR,
           "rwkv_a_la": NR, "rwkv_a_lb": NR, "rwkv_g_la": NR, "rwkv_g_lb": NR, "rwkv_gn_g": NR, "rwkv_gn_b": NR, "rwkv_w_o": NR}
    if n_layers < 4:
        for k, n in cut.items():
            sl = np.ascontiguousarray(shared[k][:n])
            for m in maps:
                m[k] = sl
    return maps


def assemble(results):
    y_prompt = np.zeros((32, 256, D), np.float32)
    y_sample = np.zeros((2, 2048, D), np.float32)
    nsh = np.zeros((32, 2, 2, 8, 128, 128), np.float32)
    nsr = np.zeros((32, 2, 2, 16, 64, 64), np.float32)
    for core in range(8):
        r = results[core]
        if core < 2:
            y_sample[core] = r["y_out"]
        else:
            slots = _PROMPT_SLOTS[core - 2]
            yo = r["y_out"].reshape(NSEG, 256, D)
            for s_, (p, _) in enumerate(slots):
                y_prompt[p] = yo[s_]
                nsh[p] = r["ns_h"][:, s_]
                nsr[p] = r["ns_r"][:, s_].transpose(0, 1, 2, 4, 3)
    return y_prompt, y_sample, nsh, nsr


_NC_CACHE = {}
_LIM = {"heads": 10 ** 9, "step": 99, "rstep": 99, "rtiles": 10 ** 9}
_rt = [0]


def kernel(**inputs):
    if "nc" not in _NC_CACHE:
        _NC_CACHE["nc"] = build_program()
    nc = _NC_CACHE["nc"]
    maps = make_in_maps(inputs)
    res = run_bass_kernel_spmd(nc, maps, core_ids=list(range(8)))
    return assemble(res.results)
```

```python
import contextlib
import numpy as np
import concourse.bass as bass
import concourse.mybir as mybir
from concourse.bass_utils import run_bass_kernel_spmd

F32 = mybir.dt.float32
BF16 = mybir.dt.bfloat16
ALU = mybir.AluOpType
AF = mybir.ActivationFunctionType
AX = mybir.AxisListType

D = 1024
NT = 16
NSEG = 8
DN_ALPHA = 8 ** 0.25
LN_EPS = 1e-5
RMS_EPS = 1e-6
GN_EPS = 64e-5
C0 = 0.606531
PADL = 2
HTW = 260


class SemGroup:
    def __init__(self, sem):
        self.sem = sem
        self.count = 0


class Buf:
    __slots__ = ("name", "w", "r", "grp", "excl")

    def __init__(self, name, grp=None):
        self.name = name
        self.w = []
        self.r = {}
        self.grp = grp
        self.excl = False


class _Rec:
    def __init__(self):
        self.call = None

    def __getattr__(self, name):
        def f(*a, **k):
            self.call = (name, a, k)
            return self
        return f


class Sched:
    ENG = ("pe", "act", "dve", "pool", "sp")

    def __init__(self, nc, stack):
        self.nc = nc
        self.stack = stack
        self.ops = {e: [] for e in self.ENG}
        self.esem = {e: stack.enter_context(nc.semaphore("es_" + e)) for e in self.ENG}
        self.targets = {e: set() for e in self.ENG}
        self.groups = []
        self.gcache = {}
        self.lastreal = {e: 0 for e in self.ENG}

    def group(self, key=None):
        if key is not None and key in self.gcache:
            return self.gcache[key]
        g = SemGroup(self.stack.enter_context(self.nc.semaphore("dg%d" % len(self.groups))))
        self.groups.append(g)
        if key is not None:
            self.gcache[key] = g
        return g

    def buf(self, name, grp=None):
        if grp == "own":
            grp = self.group(name)
        return Buf(name, grp)

    def _deps(self, eng, reads, writes):
        deps = []
        for b in reads:
            deps.extend(b.w)
        for b in writes:
            deps.extend(b.w)
            for k, v in b.r.items():
                if isinstance(k, str):
                    deps.append(("e", k, v))
                else:
                    deps.append(("d", k[1], v))
        waits = {}
        for d in deps:
            if d[0] == "e" and d[1] == eng and eng in ("pe", "sp"):
                continue
            key = (d[0], d[1])
            waits[key] = max(waits.get(key, 0), d[2])
        for k, v in waits.items():
            if k[0] == "e":
                self.targets[k[1]].add(v)
        return waits

    skip = False
    phase = ""

    def op(self, eng, fn, reads=(), writes=()):
        if self.skip:
            return
        ex = [b for b in reads if b.excl]
        if ex:
            writes = list(writes) + ex
        waits = self._deps(eng, reads, writes)
        rec = _Rec()
        fn(rec)
        assert rec.call is not None
        self.ops[eng].append([rec.call, waits, "c", None, self.phase])
        idx = len(self.ops[eng])
        self.lastreal[eng] = idx
        for b in reads:
            b.r[eng] = idx
        for b in writes:
            b.w = [("e", eng, idx)]
            b.r = {}
        return idx

    def dma(self, eng, out, in_, reads=(), writes=(), grp=None, **kw):
        if self.skip:
            return
        waits = self._deps(eng, reads, writes)
        g = grp
        if g is None:
            for b in writes:
                if b.grp is not None:
                    g = b.grp
        assert g is not None
        g.count += 1
        cnt = g.count

        kw2 = dict(kw)
        kw2["out"] = out
        kw2["in_"] = in_
        self.ops[eng].append([("dma_start", (), kw2), waits, "d", g])
        for b in reads:
            b.r[("d", g)] = cnt
        for b in writes:
            b.w = [("d", g, cnt)]
            b.r = {}

    def barrier(self):
        last = dict(self.lastreal)
        for e in self.ENG:
            waits = {}
            for o in self.ENG:
                if o != e and last[o] > 0:
                    waits[("e", o)] = last[o]
                    self.targets[o].add(last[o])
            for g in self.groups:
                if g.count > 0:
                    waits[("d", g)] = g.count
            self.ops[e].append([None, waits, "w", None])

    def emit(self, block):
        tval = {}
        for e in self.ENG:
            c = 0
            m = {}
            for i in range(1, len(self.ops[e]) + 1):
                if i in self.targets[e]:
                    c += 1
                    m[i] = c
            tval[e] = m
        sched = self

        def run(e, engobj):
            seen = {}
            for i, rec in enumerate(sched.ops[e], start=1):
                fn, waits = rec[0], rec[1]
                for k, v in waits.items():
                    if k[0] == "e":
                        val = tval[k[1]][v]
                        sem = sched.esem[k[1]]
                    else:
                        val = 16 * v
                        sem = k[1].sem
                    if seen.get(id(sem), 0) >= val:
                        continue
                    seen[id(sem)] = val
                    engobj.wait_ge(sem, val)
                if fn is None:
                    assert i not in tval[e]
                    continue
                ins = getattr(engobj, fn[0])(*fn[1], **fn[2])
                if rec[2] == "d":
                    ins.then_inc(rec[3].sem, 16)
                elif i in tval[e]:
                    ins.then_inc(sched.esem[e], 1)

        @block.tensor
        def _(pe):
            run("pe", pe)

        @block.scalar
        def _(act):
            run("act", act)

        @block.vector
        def _(dve):
            run("dve", dve)

        @block.gpsimd
        def _(pool):
            run("pool", pool)

        @block.sync
        def _(sp):
            run("sp", sp)


def _consts():
    c = {}
    idx = np.arange(128)
    c["ident"] = np.eye(128, dtype=np.float32)
    same32 = (idx[:, None] // 32) == (idx[None, :] // 32)
    mh = np.zeros((2, 128, 128), np.float32)
    mh[0] = same32 & (idx[:, None] <= idx[None, :])
    mh[1] = same32 & (idx[:, None] >= idx[None, :])
    c["maskH"] = mh
    cm4 = np.zeros((128, 4, 128), np.float32)
    for cc in range(4):
        cm4[:, cc, cc * 32:(cc + 1) * 32] = 1.0
    c["colmask4"] = cm4
    rm4 = np.zeros((128, 4), np.float32)
    rm4[idx, idx // 32] = 1.0
    c["rowmask4"] = rm4
    r32 = np.ones((128, 128), np.float32)
    r32[:, ::32] = 0.0
    c["rmask32"] = r32
    r64 = np.ones((128, 128), np.float32)
    r64[:, ::64] = 0.0
    c["rmask64"] = r64
    same64 = (idx[:, None] // 64) == (idx[None, :] // 64)
    st = [same64 & (idx[:, None] < idx[None, :]), same64 & (idx[:, None] > idx[None, :])]
    inc = [same64 & (idx[:, None] <= idx[None, :]), same64 & (idx[:, None] >= idx[None, :])]
    mk1 = np.zeros((2, 128, 256), np.float32)
    mk2 = np.zeros((2, 128, 256), np.float32)
    mkn = np.zeros((2, 128, 128), np.float32)
    for d in range(2):
        mk1[d, :, :128] = st[d]
        mk1[d, :, 128:] = inc[d]
        mk2[d, :, :128] = st[d]
        mk2[d, :, 128:] = -inc[d].astype(np.float32)
        mkn[d] = st[d].T
    c["mk1"] = mk1
    c["mk2"] = mk2
    c["mkn"] = mkn
    cm2 = np.zeros((128, 2, 128), np.float32)
    cm2[:, 0, :64] = 1.0
    cm2[:, 1, 64:] = 1.0
    c["colmask2"] = cm2
    hi = np.zeros((128, 2), np.float32)
    hi[idx, idx // 64] = 1.0
    c["headind"] = hi
    c["blockones"] = same64.astype(np.float32)
    sel = np.zeros((128, 2, 64), np.float32)
    for hh in range(2):
        sel[64 * hh + np.arange(64), hh, np.arange(64)] = 1.0
    c["sel64"] = sel
    return c


def _pos_table():
    rows, gw, quarter, half = 2048 // 64, 64, 256, 512
    omega = (1.0 / (10000.0 ** (np.arange(quarter, dtype=np.float32) / np.float32(quarter)))).astype(np.float32)
    r = np.arange(rows, dtype=np.float32)[:, None] * omega
    cc = np.arange(gw, dtype=np.float32)[:, None] * omega
    row_emb = np.concatenate([np.sin(r), np.cos(r)], -1)
    col_emb = np.concatenate([np.sin(cc), np.cos(cc)], -1)
    emb = np.concatenate([np.broadcast_to(row_emb[:, None, :], (rows, gw, half)),
                          np.broadcast_to(col_emb[None, :, :], (rows, gw, half))], -1)
    return np.ascontiguousarray(emb.reshape(rows * gw, D).astype(np.float32))


CONST_SHAPES = {
    "ident": [128, 128], "maskH": [2, 128, 128], "colmask4": [128, 4, 128], "rowmask4": [128, 4],
    "rmask32": [128, 128], "rmask64": [128, 128], "mk1": [2, 128, 256], "mk2": [2, 128, 256],
    "mkn": [2, 128, 128], "colmask2": [128, 2, 128], "headind": [128, 2], "blockones": [128, 128],
    "sel64": [128, 2, 64],
}


def build_program(n_layers=4, mix=True, dbg=False, stop=None):
    nc = bass.Bass("TRN2", target_bir_lowering=False)

    def din(name, shape):
        return nc.dram_tensor(name, list(shape), F32, kind="ExternalInput").ap()

    def dout(name, shape):
        return nc.dram_tensor(name, list(shape), F32, kind="ExternalOutput").ap()

    NLW = n_layers
    NH = max(1, (n_layers + 1) // 2)
    NR = max(1, n_layers // 2)
    x_in = din("x_in", [2048, D])
    pos = din("pos", [2048, D])
    condT = din("condT", [128, 8])
    flags = din("flags", [128, 2])
    ada_w = din("ada_w", [NLW, D, 6 * D])
    ada_bT = din("ada_bT", [NLW, 128, 48])
    ln_g = din("ln_g", [NLW, 2, D])
    ln_b = din("ln_b", [NLW, 2, D])
    w_up = din("ffn_w_up", [NLW, D, 4 * D])
    w_dn = din("ffn_w_down", [NLW, 4 * D, D])
    h_win = din("hgrn_w_in", [NH, D, 7 * D])
    h_lbT = din("hgrn_lbT", [2, 128, 16])
    h_ng = din("hgrn_norm_g", [NH, D])
    h_wo = din("hgrn_w_o", [NH, D, D])
    st_h = din("st_h", [2, 2, 8, 128, 128])
    st_r = din("st_r", [2, 2, 16, 64, 64])
    r_vecT = din("rwkv_vecT", [2, 16, 128, 8])
    r_wrkv = din("rwkv_w_rkv", [NR, 3, D, 2 * D])
    r_wla = din("rwkv_w_la", [NR, D, 2, 64])
    r_wlb = din("rwkv_w_lb", [NR, 2, 64, D])
    r_ala = din("rwkv_a_la", [NR, D, 2, 64])
    r_alb = din("rwkv_a_lb", [NR, 2, 64, D])
    r_gla = din("rwkv_g_la", [NR, D, 160])
    r_glb = din("rwkv_g_lb", [NR, 160, D])
    r_gng = din("rwkv_gn_g", [NR, 2, D])
    r_gnb = din("rwkv_gn_b", [NR, 2, D])
    r_wo = din("rwkv_w_o", [NR, D, D])
    cst = {k: din("c_" + k, v) for k, v in CONST_SHAPES.items()}

    y_out = dout("y_out", [2048, D])
    ns_h = dout("ns_h", [2, NSEG, 2, 8, 128, 128])
    ns_r = dout("ns_r", [2, NSEG, 2, 16, 64, 64])
    oscr = nc.dram_tensor("oscr", [2, 2048, D], F32).ap()
    dbgt = dout("dbg", [16, 128, D]) if dbg else None

    with contextlib.ExitStack() as st:
        S = Sched(nc, st)

        _uid = [0]

        def sb(stack, name, shape, dt=F32):
            _uid[0] += 1
            return stack.enter_context(nc.sbuf_tensor("%s_u%d" % (name, _uid[0]), list(shape), dt))

        X = sb(st, "X", [128, NT, D])
        BX = [S.buf("X%d" % i) for i in range(NT)]
        GX = S.group("GX")
        ident = sb(st, "ident", [128, 128])
        identb = sb(st, "identb", [128, 128], BF16)
        Bc = S.buf("consts", "own")
        flg = sb(st, "flg", [128, 2])
        scond = sb(st, "scond", [128, 8])
        modT = sb(st, "modT", [128, 48])
        sc1p = sb(st, "sc1p", [128, 8])
        sc2p = sb(st, "sc2p", [128, 8])
        Bmod = S.buf("mod")
        PS = [st.enter_context(nc.psum_tensor("ps%d" % i, [128, 512], F32)) for i in range(8)]
        BPS = [S.buf("ps%d" % i) for i in range(8)]
        for b_ in BPS:
            b_.excl = True
        Gout = S.group("Gout")
        By = S.buf("y_out", Gout)
        Gscr = S.group("Gscr")
        Bscr = [S.buf("oscr0", Gscr), S.buf("oscr1", Gscr)]

        block = st.enter_context(nc.Block())
        _dq = [0]
        Bdbg = S.buf("dbg", S.group("dbg"))

        def dump(slot, ap, bufs, p=128, n=None):
            if not dbg:
                return
            n = n if n is not None else ap.shape[-1]
            S.dma("pool", dbgt[slot, 0:p, 0:n], ap, reads=bufs, writes=[Bdbg])

        def hwq():
            _dq[0] += 1
            return "sp" if _dq[0] % 2 == 0 else "act"

        S.dma("sp", ident[:], cst["ident"], writes=[Bc])
        S.dma("pool", identb[:], cst["ident"], writes=[Bc])
        S.dma("sp", flg[:], flags, writes=[Bc])
        S.dma("sp", scond[:], condT, writes=[Bc])
        S.op("act", lambda e: e.activation(out=scond[:], in_=scond[:], func=AF.Silu), reads=[Bc], writes=[Bc])
        for i in range(NT):
            S.dma(hwq(), X[:, i, :], x_in[i * 128:(i + 1) * 128, :], writes=[BX[i]], grp=GX)
        with contextlib.ExitStack() as ph:
            pt = [sb(ph, "pos%d" % i, [128, D]) for i in range(2)]
            Bpt = [S.buf("pos%d" % i, "own") for i in range(2)]
            for i in range(NT):
                S.dma(hwq(), pt[i % 2][:], pos[i * 128:(i + 1) * 128, :], writes=[Bpt[i % 2]])
                eng = "dve"
                S.op(eng, lambda e, i=i: e.scalar_tensor_tensor(out=X[:, i, :], in0=pt[i % 2][:], scalar=flg[:, 1:2], in1=X[:, i, :],
                                                                op0=ALU.mult, op1=ALU.add),
                     reads=[Bpt[i % 2], Bc, BX[i]], writes=[BX[i]])
            dump(1, X[:, 0, :], [BX[0]])
            S.barrier()

        def run_interleaved(gens, width):
            it = iter(gens)
            active = []
            while True:
                while len(active) < width:
                    g = next(it, None)
                    if g is None:
                        break
                    active.append(g)
                if not active:
                    break
                for g in list(active):
                    try:
                        next(g)
                    except StopIteration:
                        active.remove(g)

        def adaln(l):
            with contextlib.ExitStack() as ph:
                slab = [sb(ph, "adas%d" % i, [128, 8, 256]) for i in range(2)]
                Bsl = [S.buf("adas%d" % i, "own") for i in range(2)]
                abT = sb(ph, "abT", [128, 48])
                Bab = S.buf("abT", "own")
                S.dma("sp", abT[:], ada_bT[l], writes=[Bab])
                wv = ada_w[l].rearrange("(k p) n -> p k n", p=128)
                acc = PS[0]
                for jg in range(24):
                    sl = jg % 2
                    S.dma(hwq(), slab[sl][:], wv[:, :, jg * 256:(jg + 1) * 256], writes=[Bsl[sl]])
                    for jj in range(2):
                        j = jg * 2 + jj
                        for k in range(8):
                            S.op("pe", lambda e, sl=sl, jj=jj, j=j, k=k: e.matmul(
                                out=acc[:, j:j + 1], lhsT=slab[sl][:, k, jj * 128:(jj + 1) * 128], rhs=scond[:, k:k + 1],
                                start=(k == 0), stop=(k == 7)), reads=[Bsl[sl], Bc], writes=[BPS[0]])
                S.op("dve", lambda e: e.tensor_tensor(out=modT[:], in0=acc[:, 0:48], in1=abT[:], op=ALU.add),
                     reads=[BPS[0], Bab], writes=[Bmod])
                S.op("dve", lambda e: e.tensor_scalar_add(out=sc1p[:], in0=modT[:, 8:16], scalar1=1.0), reads=[Bmod], writes=[Bmod])
                S.op("dve", lambda e: e.tensor_scalar_add(out=sc2p[:], in0=modT[:, 32:40], scalar1=1.0), reads=[Bmod], writes=[Bmod])
                if l == 0:
                    dump(0, modT[:], [Bmod])
                S.barrier()

        def make_gbc(ph, which):
            base = (16, 40)[which]
            gb = sb(ph, "gbc", [128, D])
            Bgb = S.buf("gbc%d" % which)
            gtmp = sb(ph, "gtmp", [128, 128])
            Bgt = S.buf("gtmp")
            for k in range(8):
                S.op("dve", lambda e, k=k: e.tensor_scalar(
                    out=gtmp[:], in0=modT[:, base + k:base + k + 1].to_broadcast([128, 128]),
                    scalar1=1.0 / DN_ALPHA, scalar2=None, op0=ALU.mult), reads=[Bmod], writes=[Bgt])
                S.op("pe", lambda e: e.transpose(out=PS[1][:, 0:128], in_=gtmp[:], identity=ident[:]),
                     reads=[Bgt, Bc], writes=[BPS[1]])
                S.op("act", lambda e, k=k: e.activation(out=gb[:, k * 128:(k + 1) * 128], in_=PS[1][:, 0:128], func=AF.Copy),
                     reads=[BPS[1]], writes=[Bgb])
            return gb, Bgb

        def build_hT(dst_fn, Bdst_fn, tiles, scp, shcol):
            n = 0
            for i in tiles:
                for kk in range(2):
                    pb = 2 + (n % 2)
                    n += 1
                    for k4 in range(4):
                        k = kk * 4 + k4
                        S.op("pe", lambda e, i=i, k=k, k4=k4, pb=pb: e.transpose(
                            out=PS[pb][:, k4 * 128:(k4 + 1) * 128], in_=X[:, i, k * 128:(k + 1) * 128], identity=ident[:]),
                            reads=[BX[i], Bc], writes=[BPS[pb]])
                    for k4 in range(4):
                        k = kk * 4 + k4
                        if k % 2 == 0:
                            S.op("act", lambda e, i=i, k=k, k4=k4, pb=pb: e.activation(
                                out=dst_fn(i, k), in_=PS[pb][:, k4 * 128:(k4 + 1) * 128], func=AF.Identity,
                                bias=modT[:, shcol + k:shcol + k + 1], scale=scp[:, k:k + 1]),
                                reads=[BPS[pb], Bmod], writes=[Bdst_fn(i)])
                        else:
                            S.op("dve", lambda e, i=i, k=k, k4=k4, pb=pb: e.tensor_scalar(
                                out=dst_fn(i, k), in0=PS[pb][:, k4 * 128:(k4 + 1) * 128],
                                scalar1=scp[:, k:k + 1], scalar2=modT[:, shcol + k:shcol + k + 1], op0=ALU.mult, op1=ALU.add),
                                reads=[BPS[pb], Bmod], writes=[Bdst_fn(i)])

        def resid_ln(i, pa, pb, Bpa, Bpb, which, lt):
            v, Bv = lt["v"], lt["Bv"]
            stt, Bst = lt["st"], lt["Bst"]
            S.op("dve", lambda e: e.tensor_tensor(out=v[:, 0:512], in0=pa[:, :], in1=lt["gbc"][:, 0:512], op=ALU.mult),
                 reads=[Bpa, lt["Bgbc"]], writes=[Bv])
            S.op("dve", lambda e: e.tensor_tensor(out=v[:, 512:1024], in0=pb[:, :], in1=lt["gbc"][:, 512:1024], op=ALU.mult),
                 reads=[Bpb, lt["Bgbc"]], writes=[Bv])
            peng = lt.get("peng", "pool")
            S.op(peng, lambda e: e.tensor_tensor(out=v[:], in0=v[:], in1=X[:, i, :], op=ALU.add), reads=[Bv, BX[i]], writes=[Bv])
            if i == 0 and lt.get("dbg"):
                dump(5, v[:], [Bv])
            S.op("dve", lambda e: e.bn_stats(out=stt[:, 0:6], in_=v[:, 0:512]), reads=[Bv], writes=[Bst])
            S.op("dve", lambda e: e.bn_stats(out=stt[:, 6:12], in_=v[:, 512:1024]), reads=[Bv], writes=[Bst])
            S.op("dve", lambda e: e.bn_aggr(out=stt[:, 12:14], in_=stt[:, 0:12]), reads=[Bst], writes=[Bst])
            S.op("dve", lambda e: e.tensor_scalar_add(out=stt[:, 14:15], in0=stt[:, 13:14], scalar1=LN_EPS / (DN_ALPHA ** 2)), reads=[Bst], writes=[Bst])
            S.op("act", lambda e: e.activation(out=stt[:, 14:15], in_=stt[:, 14:15], func=AF.Sqrt), reads=[Bst], writes=[Bst])
            S.op("dve", lambda e: e.reciprocal(out=stt[:, 14:15], in_=stt[:, 14:15]), reads=[Bst], writes=[Bst])
            S.op("dve", lambda e: e.tensor_scalar(out=v[:], in0=v[:], scalar1=stt[:, 12:13], scalar2=stt[:, 14:15],
                                                  op0=ALU.subtract, op1=ALU.mult), reads=[Bv, Bst], writes=[Bv])
            if i == 0 and lt.get("dbg"):
                dump(6, stt[:], [Bst])
                dump(8, v[:], [Bv])
            S.op(peng, lambda e: e.tensor_tensor(out=v[:], in0=v[:], in1=lt["lng"][:], op=ALU.mult),
                 reads=[Bv, lt["Bln"]], writes=[Bv])
            S.op(peng, lambda e: e.tensor_tensor(out=X[:, i, :], in0=v[:], in1=lt["lnb"][:], op=ALU.add),
                 reads=[Bv, lt["Bln"]], writes=[BX[i]])

        def ffn(l):
            with contextlib.ExitStack() as ph:
                wd = sb(ph, "wd", [128, 32, D], BF16)
                Bwd = [S.buf("wd%d" % i, "own") for i in range(4)]
                ups = [sb(ph, "ups%d" % i, [128, 8, 256], BF16) for i in range(2)]
                Bups = [S.buf("ups%d" % i, "own") for i in range(2)]
                aT = sb(ph, "aT", [128, 32, 512], BF16)
                BaT = [S.buf("aT%d" % j) for j in range(32)]
                h2 = sb(ph, "h2", [128, 8, 512], BF16)
                Bh2 = [S.buf("h2_%d" % i) for i in range(4)]
                rl = [sb(ph, "rl%d" % i, [128, 512]) for i in range(2)]
                Brl = [S.buf("rl%d" % i) for i in range(2)]
                lng = sb(ph, "lng", [128, D])
                lnb = sb(ph, "lnb", [128, D])
                Bln = S.buf("lnbc", "own")
                S.dma("sp", lng[:], ln_g[l, 1].partition_broadcast(128), writes=[Bln])
                S.dma("sp", lnb[:], ln_b[l, 1].partition_broadcast(128), writes=[Bln])
                gb_, Bgb_ = make_gbc(ph, 1)
                lt = [dict(v=sb(ph, "lnv%d" % i, [128, D]), Bv=S.buf("lnv%d" % i), st=sb(ph, "lnst%d" % i, [128, 16]), Bst=S.buf("lnst%d" % i),
                           lng=lng, lnb=lnb, Bln=Bln, gbc=gb_, Bgbc=Bgb_, peng="dve") for i in range(1)]
                if l == 0:
                    lt[0]["dbg"] = True
                wdv = w_dn[l].rearrange("(j p) n -> p j n", p=128)
                for q in range(4):
                    S.dma("pool", wd[:, q * 8:(q + 1) * 8, :], wdv[:, q * 8:(q + 1) * 8, :], writes=[Bwd[q]])
                wuv = w_up[l].rearrange("(k p) n -> p k n", p=128)
                nev = 0
                for g in range(4):
                    tiles = list(range(g * 4, g * 4 + 4))
                    build_hT(lambda i, k: h2[:, k, (i % 4) * 128:(i % 4 + 1) * 128], lambda i: Bh2[i % 4], tiles, sc2p, 24)
                    for jg in range(16):
                        sl = jg % 2
                        S.dma("pool", ups[sl][:], wuv[:, :, jg * 256:(jg + 1) * 256], writes=[Bups[sl]])
                        for jj in range(2):
                            j = jg * 2 + jj
                            pb = 4 + (nev % 2)
                            for k in range(8):
                                S.op("pe", lambda e, sl=sl, jj=jj, k=k, pb=pb: e.matmul(
                                    out=PS[pb][:, :], lhsT=ups[sl][:, k, jj * 128:(jj + 1) * 128], rhs=h2[:, k, :],
                                    start=(k == 0), stop=(k == 7)), reads=[Bups[sl]] + Bh2, writes=[BPS[pb]])
                            r = nev % 2
                            nev += 1
                            S.op("act", lambda e, pb=pb, r=r: e.activation(out=rl[r][:], in_=PS[pb][:, :], func=AF.Relu),
                                 reads=[BPS[pb]], writes=[Brl[r]])
                            if j % 2 == 0:
                                S.op("dve", lambda e, j=j, r=r: e.tensor_tensor(out=aT[:, j, :], in0=rl[r][:], in1=rl[r][:], op=ALU.mult),
                                     reads=[Brl[r]], writes=[BaT[j]])
                            else:
                                S.op("act", lambda e, j=j, r=r: e.activation(out=aT[:, j, :], in_=rl[r][:], func=AF.Square),
                                     reads=[Brl[r]], writes=[BaT[j]])
                    if l == 0 and g == 0:
                        dump(3, h2[:, :, 0:128].rearrange("p k t -> p k t"), Bh2, n=None) if False else None
                        for k in range(8):
                            if dbg:
                                S.dma("pool", dbgt[3, :, k * 128:(k + 1) * 128], h2[:, k, 0:128], reads=Bh2, writes=[Bdbg])
                        dump(4, aT[:, 0, :], [BaT[0]])
                    for ii, i in enumerate(tiles):
                        pa, pbk = 6, 7
                        for nh, pbank in ((0, pa), (1, pbk)):
                            for j in range(32):
                                S.op("pe", lambda e, j=j, ii=ii, nh=nh, pbank=pbank: e.matmul(
                                    out=PS[pbank][:, :], lhsT=aT[:, j, ii * 128:(ii + 1) * 128], rhs=wd[:, j, nh * 512:(nh + 1) * 512],
                                    start=(j == 0), stop=(j == 31)), reads=[BaT[j], Bwd[j // 8]], writes=[BPS[pbank]])
                        if l == 0 and i == 0 and dbg:
                            S.dma("pool", dbgt[9, :, 0:512], PS[pa][:, :], reads=[BPS[pa]], writes=[Bdbg]) if False else None
                        resid_ln(i, PS[pa], PS[pbk], BPS[pa], BPS[pbk], 1, lt[0])
                        if l == 0 and i == 0:
                            dump(7, X[:, 0, :], [BX[0]])
                S.barrier()

        def hgrn_layer(l, j, hT, BhT):
            hcols = lambda i, k: hT[:, k, i // 2, PADL + (i % 2) * 128: PADL + (i % 2) * 128 + 128]
            with contextlib.ExitStack() as ph:
                lbv = sb(ph, "lbv", [128, 16])
                oml = sb(ph, "oml", [128, 16])
                Blb = S.buf("lbv", "own")
                if j == 0:
                    S.op("dve", lambda e: e.memset(lbv[:], 0.0), writes=[Blb])
                else:
                    lb0 = sb(ph, "lb0", [128, 16])
                    S.dma("sp", lb0[:], h_lbT[0], writes=[Blb])
                    S.dma("sp", lbv[:], h_lbT[1], writes=[Blb])
                    S.op("dve", lambda e: e.tensor_tensor(out=lbv[:], in0=lbv[:], in1=lb0[:], op=ALU.subtract), reads=[Blb], writes=[Blb])
                    S.op("act", lambda e: e.activation(out=lbv[:], in_=lbv[:], func=AF.Sigmoid), reads=[Blb], writes=[Blb])
                S.op("dve", lambda e: e.tensor_scalar(out=oml[:], in0=lbv[:], scalar1=-1.0, scalar2=1.0, op0=ALU.mult, op1=ALU.add),
                     reads=[Blb], writes=[Blb])
                maskH = sb(ph, "maskH", [128, 128])
                cm4 = sb(ph, "cm4", [128, 4, 128], BF16)
                rm4 = sb(ph, "rm4", [128, 4])
                r32 = sb(ph, "r32", [128, 128])
                Bcm = S.buf("hconst", "own")
                S.dma("pool", cm4[:], cst["colmask4"], writes=[Bcm])
                S.dma("sp", rm4[:], cst["rowmask4"], writes=[Bcm])
                S.dma("sp", r32[:], cst["rmask32"], writes=[Bcm])
                wq = [sb(ph, "hw%d" % i, [128, 8, D], BF16) for i in range(3)]
                Bwq = [S.buf("hw%d" % i, "own") for i in range(3)]
                Tst = sb(ph, "Tst", [128, 8, 128])
                BT = [S.buf("T%d" % h) for h in range(8)]
                GT = S.group("GT")
                vbs = [sb(ph, "vb%d" % i, [128, D], BF16) for i in range(2)]
                Bvbs = [S.buf("vb%d" % i) for i in range(2)]
                oblk = [sb(ph, "oblk%d" % i, [128, D]) for i in range(2)]
                Bob = [S.buf("oblk%d" % i) for i in range(2)]
                NS_ = 4
                tl = []
                for s_ in range(NS_):
                    t = {}
                    for nm in ("q", "nz", "uu", "l1", "l2", "Lp", "Dd", "eD", "emD", "t1", "kf"):
                        t[nm] = sb(ph, "h_%s%d" % (nm, s_), [128, 128])
                    t["wc"] = sb(ph, "h_wc%d" % s_, [128, 4])
                    for nm in ("kout", "qpp", "At"):
                        t[nm] = sb(ph, "h_%s%d" % (nm, s_), [128, 128], BF16)
                    for nm in ("koe", "qe", "Tp"):
                        t[nm] = sb(ph, "h_%s%d" % (nm, s_), [128, 4, 128], BF16)
                    t["B"] = {nm: S.buf("h_%s%d" % (nm, s_)) for nm in
                              ("q", "nz", "uu", "l1", "l2", "Lp", "Dd", "eD", "emD", "t1", "kf", "wc", "kout", "qpp", "At", "koe", "qe", "Tp")}
                    tl.append(t)
                wv = h_win[j].rearrange("(k p) n -> p k n", p=128)
                un = 0
                for d in range(2):
                    S.dma("sp", maskH[:], cst["maskH"][d], reads=[], writes=[Bcm])
                    for w3 in range(3):
                        c0 = d * 3072 + w3 * 1024
                        S.dma("pool", wq[w3][:], wv[:, :, c0:c0 + 1024], writes=[Bwq[w3]])
                    S.dma("sp", Tst[:], st_h[j, d].rearrange("h k v -> k h v"), writes=BT, grp=GT)
                    order = list(range(NT)) if d == 0 else list(range(NT - 1, -1, -1))
                    def hgen(bi, i, h, d=d, order=order):
                        seg = i // 2
                        first_of_seg = (i % 2 == 0) if d == 0 else (i % 2 == 1)
                        last_of_seg = not first_of_seg
                        corder = [0, 1, 2, 3] if d == 0 else [3, 2, 1, 0]
                        vbt, Bvbt = vbs[bi % 2], Bvbs[bi % 2]
                        ob, Bo = oblk[bi % 2], Bob[bi % 2]
                        idx = bi * 8 + h
                        t = tl[idx % NS_]
                        B = t["B"]
                        pqz = pg = po = 2 * (idx % NS_)
                        pu = 2 * (idx % NS_) + 1
                        col = d * 8 + h
                        if h == 0:
                            for nh in range(2):
                                pb = 6 + nh
                                for k in range(8):
                                    S.op("pe", lambda e, k=k, nh=nh, pb=pb: e.matmul(
                                        out=PS[pb][:, :], lhsT=hcols(i, k), rhs=wq[2][:, k, nh * 512:(nh + 1) * 512],
                                        start=(k == 0), stop=(k == 7)), reads=[BhT[seg], Bwq[2]], writes=[BPS[pb]])
                                S.op("act", lambda e, nh=nh, pb=pb: e.activation(out=vbt[:, nh * 512:(nh + 1) * 512], in_=PS[pb][:, :], func=AF.Copy),
                                     reads=[BPS[pb]], writes=[Bvbt])
                            yield
                        if first_of_seg and bi > 0:
                            S.op("dve", lambda e: e.tensor_scalar(out=Tst[:, h, :], in0=Tst[:, h, :], scalar1=flg[:, 0:1], scalar2=None, op0=ALU.mult),
                                 reads=[BT[h], Bc], writes=[BT[h]])
                        for w3, off in ((0, 0), (1, 128)):
                            for k in range(8):
                                S.op("pe", lambda e, i=i, k=k, w3=w3, off=off, pqz=pqz, h=h: e.matmul(
                                    out=PS[pqz][:, off:off + 128], lhsT=wq[w3][:, k, h * 128:(h + 1) * 128], rhs=hcols(i, k),
                                    start=(k == 0), stop=(k == 7)), reads=[BhT[seg], Bwq[w3]], writes=[BPS[pqz]])
                        yield
                        S.op("act", lambda e, t=t, pqz=pqz: e.activation(out=t["q"][:], in_=PS[pqz][:, 0:128], func=AF.Silu),
                             reads=[BPS[pqz]], writes=[B["q"]])
                        yield
                        S.op("dve", lambda e, t=t, pqz=pqz: e.tensor_scalar(out=t["nz"][:], in0=PS[pqz][:, 128:256], scalar1=-1.0, scalar2=80.0,
                                                                             op0=ALU.mult, op1=ALU.min), reads=[BPS[pqz]], writes=[B["nz"]])
                        yield
                        S.op("act", lambda e, t=t: e.activation(out=t["uu"][:], in_=t["nz"][:], func=AF.Exp), reads=[B["nz"]], writes=[B["uu"]])
                        yield
                        S.op("act", lambda e, t=t, col=col: e.activation(out=t["l1"][:], in_=t["uu"][:], func=AF.Ln, bias=1.0,
                                                                          scale=lbv[:, col:col + 1]), reads=[B["uu"], Blb], writes=[B["l1"]])
                        yield
                        S.op("act", lambda e, t=t: e.activation(out=t["l2"][:], in_=t["uu"][:], func=AF.Ln, bias=1.0, scale=1.0),
                             reads=[B["uu"]], writes=[B["l2"]])
                        yield
                        S.op("dve", lambda e, t=t: e.tensor_tensor(out=t["l1"][:], in0=t["l1"][:], in1=t["l2"][:], op=ALU.subtract),
                             reads=[B["l1"], B["l2"]], writes=[B["l1"]])
                        yield
                        S.op("dve", lambda e, t=t: e.tensor_tensor_scan(out=t["Lp"][:], data0=r32[:], data1=t["l1"][:], initial=0.0,
                                                                       op0=ALU.mult, op1=ALU.add), reads=[B["l1"], Bcm], writes=[B["Lp"]])
                        yield
                        Lv = t["Lp"][:].rearrange("p (c t) -> p c t", t=32)
                        yield
                        Dv = t["Dd"][:].rearrange("p (c t) -> p c t", t=32)
                        yield
                        if d == 0:
                            S.op("dve", lambda e, Lv=Lv, Dv=Dv: e.tensor_tensor(out=Dv, in0=Lv[:, :, 31:32].to_broadcast([128, 4, 32]), in1=Lv,
                                                                                op=ALU.subtract), reads=[B["Lp"]], writes=[B["Dd"]])
                        else:
                            S.op("dve", lambda e, t=t: e.tensor_tensor(out=t["Dd"][:], in0=t["Lp"][:], in1=t["l1"][:], op=ALU.subtract),
                                 reads=[B["Lp"], B["l1"]], writes=[B["Dd"]])
                        yield
                        S.op("pool", lambda e, t=t: e.tensor_scalar_max(out=t["Dd"][:], in0=t["Dd"][:], scalar1=-80.0), reads=[B["Dd"]], writes=[B["Dd"]])
                        yield
                        S.op("act", lambda e, t=t: e.activation(out=t["eD"][:], in_=t["Dd"][:], func=AF.Exp), reads=[B["Dd"]], writes=[B["eD"]])
                        yield
                        S.op("act", lambda e, t=t: e.activation(out=t["emD"][:], in_=t["Dd"][:], func=AF.Exp, scale=-1.0), reads=[B["Dd"]], writes=[B["emD"]])
                        yield
                        S.op("act", lambda e, t=t, Lv=Lv: e.activation(out=t["wc"][:].unsqueeze(2), in_=Lv[:, :, 31:32], func=AF.Exp),
                             reads=[B["Lp"]], writes=[B["wc"]])
                        yield
                        S.op("dve", lambda e, t=t: e.tensor_scalar_add(out=t["t1"][:], in0=t["uu"][:], scalar1=1.0), reads=[B["uu"]], writes=[B["t1"]])
                        yield
                        S.op("dve", lambda e, t=t: e.reciprocal(out=t["t1"][:], in_=t["t1"][:]), reads=[B["t1"]], writes=[B["t1"]])
                        yield
                        S.op("dve", lambda e, t=t, col=col: e.scalar_tensor_tensor(out=t["kf"][:], in0=t["uu"][:], scalar=oml[:, col:col + 1], in1=t["t1"][:],
                                                                                  op0=ALU.mult, op1=ALU.mult), reads=[B["uu"], B["t1"], Blb], writes=[B["kf"]])
                        yield
                        S.op("pool", lambda e, t=t: e.tensor_tensor(out=t["kout"][:], in0=t["kf"][:], in1=t["eD"][:], op=ALU.mult),
                             reads=[B["kf"], B["eD"]], writes=[B["kout"]])
                        yield
                        S.op("pool", lambda e, t=t: e.tensor_tensor(out=t["qpp"][:], in0=t["q"][:], in1=t["emD"][:], op=ALU.mult),
                             reads=[B["q"], B["emD"]], writes=[B["qpp"]])
                        yield
                        pgb = PS[pg][:].bitcast(BF16)
                        yield
                        S.op("pe", lambda e, t=t, pg=pg: e.matmul(out=PS[pg][:, 256:384], lhsT=t["kout"][:], rhs=t["qpp"][:], start=True, stop=True),
                             reads=[B["kout"], B["qpp"]], writes=[BPS[pg]])
                        yield
                        S.op("pe", lambda e, t=t, pgb=pgb: e.transpose(out=pgb[:, 768:896], in_=t["kout"][:], identity=identb[:]),
                             reads=[B["kout"], Bc], writes=[BPS[pg]])
                        yield
                        S.op("dve", lambda e, t=t, pg=pg: e.tensor_tensor(out=t["At"][:], in0=PS[pg][:, 256:384], in1=maskH[:], op=ALU.mult),
                             reads=[BPS[pg], Bcm], writes=[B["At"]])
                        yield
                        S.op("dve", lambda e, t=t, pgb=pgb: e.tensor_tensor(out=t["koe"][:], in0=pgb[:, 768:896].unsqueeze(1).to_broadcast([128, 4, 128]),
                                                                           in1=rm4[:].unsqueeze(2).to_broadcast([128, 4, 128]), op=ALU.mult),
                             reads=[BPS[pg], Bcm], writes=[B["koe"]])
                        yield
                        S.op("pool", lambda e, t=t: e.tensor_tensor(out=t["qe"][:], in0=t["qpp"][:].unsqueeze(1).to_broadcast([128, 4, 128]),
                                                                   in1=cm4[:], op=ALU.mult), reads=[B["qpp"], Bcm], writes=[B["qe"]])
                        yield
                        for c in range(4):
                            S.op("pe", lambda e, t=t, c=c, pu=pu, h=h: e.matmul(out=PS[pu][:, c * 128:(c + 1) * 128], lhsT=t["koe"][:, c, :],
                                                                              rhs=vbt[:, h * 128:(h + 1) * 128], start=True, stop=True),
                                 reads=[B["koe"], Bvbt], writes=[BPS[pu]])
                        yield
                        for c in corder:
                            S.op("dve", lambda e, t=t, c=c, h=h: e.tensor_scalar(out=t["Tp"][:, c, :], in0=Tst[:, h, :], scalar1=t["wc"][:, c:c + 1],
                                                                                scalar2=None, op0=ALU.mult), reads=[BT[h], B["wc"]], writes=[B["Tp"]])
                            S.op("dve", lambda e, t=t, c=c, h=h, pu=pu: e.scalar_tensor_tensor(out=Tst[:, h, :], in0=Tst[:, h, :], scalar=t["wc"][:, c:c + 1],
                                                                                              in1=PS[pu][:, c * 128:(c + 1) * 128], op0=ALU.mult, op1=ALU.add),
                                 reads=[BT[h], B["wc"], BPS[pu]], writes=[BT[h]])
                        yield
                        S.op("pe", lambda e, t=t, po=po, h=h: e.matmul(out=PS[po][:, 0:128], lhsT=t["At"][:], rhs=vbt[:, h * 128:(h + 1) * 128],
                                                                       start=True, stop=False), reads=[B["At"], Bvbt], writes=[BPS[po]])
                        yield
                        for ci, c in enumerate(corder):
                            S.op("pe", lambda e, t=t, po=po, c=c, ci=ci: e.matmul(out=PS[po][:, 0:128], lhsT=t["qe"][:, c, :], rhs=t["Tp"][:, c, :],
                                                                                  start=False, stop=(ci == 3)), reads=[B["qe"], B["Tp"]], writes=[BPS[po]])
                        yield
                        S.op("act", lambda e, ob=ob, po=po, h=h: e.activation(out=ob[:, h * 128:(h + 1) * 128], in_=PS[po][:, 0:128], func=AF.Copy),
                             reads=[BPS[po]], writes=[Bo])
                        yield
                        if last_of_seg:
                            S.dma("act", ns_h[j, seg, d, h], Tst[:, h, :], reads=[BT[h]], writes=[By])
                        if h == 7:
                            S.dma("sp", oscr[d, i * 128:(i + 1) * 128, :], ob[:], reads=[Bo], writes=[Bscr[d]])
                    run_interleaved((hgen(bi, i, h) for bi, i in enumerate(order) for h in range(8)), NS_)
                S.barrier()

        def post_mixer(l, j, kind, hT, BhT):
            hcols = lambda i, k: hT[:, k, i // 2, PADL + (i % 2) * 128: PADL + (i % 2) * 128 + 128]
            with contextlib.ExitStack() as ph:
                wo = sb(ph, "wo", [128, 8, D], BF16)
                Bwo = S.buf("wo", "own")
                src_wo = h_wo[j] if kind == "h" else r_wo[j]
                S.dma("pool", wo[:], src_wo.rearrange("(k p) n -> p k n", p=128), writes=[Bwo])
                ot = [[sb(ph, "ot%d_%d" % (dd, s_), [128, D]) for s_ in range(2)] for dd in range(2)]
                Bot = [[S.buf("ot%d_%d" % (dd, s_), "own") for s_ in range(2)] for dd in range(2)]
                zb = sb(ph, "zb", [128, D], BF16)
                Bzb = S.buf("zb")
                zT = sb(ph, "zT", [128, 8, 128], BF16)
                BzT = S.buf("zT")
                lng = sb(ph, "lng", [128, D])
                lnb = sb(ph, "lnb", [128, D])
                Bln = S.buf("lnbc", "own")
                S.dma("sp", lng[:], ln_g[l, 0].partition_broadcast(128), writes=[Bln])
                S.dma("sp", lnb[:], ln_b[l, 0].partition_broadcast(128), writes=[Bln])
                gb_, Bgb_ = make_gbc(ph, 0)
                lt = [dict(v=sb(ph, "lnv%d" % i, [128, D]), Bv=S.buf("lnv%d" % i), st=sb(ph, "lnst%d" % i, [128, 16]), Bst=S.buf("lnst%d" % i),
                           lng=lng, lnb=lnb, Bln=Bln, gbc=gb_, Bgbc=Bgb_) for i in range(2)]
                if kind == "h":
                    wg = sb(ph, "wg", [128, 8, D], BF16)
                    Bwg = S.buf("wg", "own")
                    S.dma("pool", wg[:], h_win[j].rearrange("(k p) n -> p k n", p=128)[:, :, 6144:7168], writes=[Bwg])
                    ngbc = sb(ph, "ngbc", [128, D])
                    Bng = S.buf("ngbc", "own")
                    S.dma("sp", ngbc[:], h_ng[j].partition_broadcast(128), writes=[Bng])
                    sq = sb(ph, "sq", [128, D])
                    Bsq = S.buf("sq")
                    sgt = sb(ph, "sgt", [128, D])
                    Bsg = S.buf("sgt")
                    ss = sb(ph, "ss", [128, 8])
                    Bss = S.buf("ss")
                else:
                    gla = sb(ph, "gla", [128, 8, 160], BF16)
                    glb1 = sb(ph, "glb1", [128, D], BF16)
                    glb2 = sb(ph, "glb2", [32, D], BF16)
                    Bgl = S.buf("gl", "own")
                    S.dma("pool", gla[:], r_gla[j].rearrange("(k p) n -> p k n", p=128), writes=[Bgl])
                    S.dma("pool", glb1[:], r_glb[j, 0:128, :], writes=[Bgl])
                    S.dma("pool", glb2[:], r_glb[j, 128:160, :], writes=[Bgl])
                    muT = sb(ph, "muTg", [128, 8])
                    S.dma("sp", muT[:], r_vecT[j, 5], writes=[Bgl])
                    xt = sb(ph, "xg_t", [128, 8, 128])
                    xs = sb(ph, "xg_s", [128, 8, 128], BF16)
                    Bxt = S.buf("xg_t")
                    Bxs = S.buf("xg_s")
                    sg1 = sb(ph, "sg1", [128, 128], BF16)
                    sg2 = sb(ph, "sg2", [32, 128], BF16)
                    Bsgg = S.buf("sgg")
                for i in range(NT):
                    seg = i // 2
                    s_ = i % 2
                    for dd in range(2):
                        S.dma(hwq(), ot[dd][s_][:], oscr[dd, i * 128:(i + 1) * 128, :], reads=[Bscr[dd]], writes=[Bot[dd][s_]])
                    o0, o1 = ot[0][s_], ot[1][s_]
                    S.op("pool", lambda e, o0=o0, o1=o1: e.tensor_tensor(out=o0[:], in0=o0[:], in1=o1[:], op=ALU.add),
                         reads=[Bot[0][s_], Bot[1][s_]], writes=[Bot[0][s_]])
                    if kind == "h":
                        for nh in range(2):
                            for k in range(8):
                                S.op("pe", lambda e, i=i, k=k, nh=nh: e.matmul(out=PS[nh][:, :], lhsT=hcols(i, k), rhs=wg[:, k, nh * 512:(nh + 1) * 512],
                                                                               start=(k == 0), stop=(k == 7)), reads=[BhT[seg], Bwg], writes=[BPS[nh]])
                            S.op("act", lambda e, nh=nh: e.activation(out=sgt[:, nh * 512:(nh + 1) * 512], in_=PS[nh][:, :], func=AF.Silu),
                                 reads=[BPS[nh]], writes=[Bsg])
                        S.op("act", lambda e, o0=o0: e.activation(out=sq[:], in_=o0[:], func=AF.Square), reads=[Bot[0][s_]], writes=[Bsq])
                        S.op("dve", lambda e: e.tensor_reduce(out=ss[:], in_=sq[:].rearrange("p (h k) -> p h k", k=128), axis=AX.X, op=ALU.add),
                             reads=[Bsq], writes=[Bss])
                        S.op("dve", lambda e: e.tensor_scalar(out=ss[:], in0=ss[:], scalar1=1.0 / 128.0, scalar2=RMS_EPS, op0=ALU.mult, op1=ALU.add),
                             reads=[Bss], writes=[Bss])
                        S.op("act", lambda e: e.activation(out=ss[:], in_=ss[:], func=AF.Sqrt), reads=[Bss], writes=[Bss])
                        S.op("dve", lambda e: e.reciprocal(out=ss[:], in_=ss[:]), reads=[Bss], writes=[Bss])
                        S.op("dve", lambda e, o0=o0: e.tensor_tensor(out=o0[:].rearrange("p (h k) -> p h k", k=128), in0=o0[:].rearrange("p (h k) -> p h k", k=128),
                                                                    in1=ss[:].unsqueeze(2).to_broadcast([128, 8, 128]), op=ALU.mult),
                             reads=[Bot[0][s_], Bss], writes=[Bot[0][s_]])
                        S.op("pool", lambda e, o0=o0: e.tensor_tensor(out=o0[:], in0=o0[:], in1=ngbc[:], op=ALU.mult), reads=[Bot[0][s_], Bng], writes=[Bot[0][s_]])
                        S.op("dve", lambda e, o0=o0: e.tensor_tensor(out=zb[:], in0=o0[:], in1=sgt[:], op=ALU.mult), reads=[Bot[0][s_], Bsg], writes=[Bzb])
                    else:
                        b_ = i % 2
                        hL = hT[:, :, seg, PADL - 1 + b_ * 128: PADL - 1 + b_ * 128 + 128]
                        hR = hT[:, :, seg, PADL + 1 + b_ * 128: PADL + 1 + b_ * 128 + 128]
                        hC = hT[:, :, seg, PADL + b_ * 128: PADL + b_ * 128 + 128]
                        S.op("pool", lambda e, hL=hL, hR=hR: e.tensor_tensor(out=xt[:], in0=hL, in1=hR, op=ALU.add), reads=[BhT[seg]], writes=[Bxt])
                        S.op("dve", lambda e, hC=hC: e.scalar_tensor_tensor(out=xt[:], in0=xt[:], scalar=0.5, in1=hC, op0=ALU.mult, op1=ALU.subtract),
                             reads=[Bxt, BhT[seg]], writes=[Bxt])
                        S.op("dve", lambda e: e.tensor_tensor(out=xt[:], in0=xt[:], in1=muT[:].unsqueeze(2).to_broadcast([128, 8, 128]), op=ALU.mult),
                             reads=[Bxt, Bgl], writes=[Bxt])
                        S.op("dve", lambda e, hC=hC: e.tensor_tensor(out=xs[:], in0=xt[:], in1=hC, op=ALU.add), reads=[Bxt, BhT[seg]], writes=[Bxs])
                        for k in range(8):
                            S.op("pe", lambda e, k=k: e.matmul(out=PS[2][:, 0:128], lhsT=gla[:, k, 0:128], rhs=xs[:, k, :], start=(k == 0), stop=(k == 7)),
                                 reads=[Bgl, Bxs], writes=[BPS[2]])
                        for k in range(8):
                            S.op("pe", lambda e, k=k: e.matmul(out=PS[2][0:32, 128:256], lhsT=gla[:, k, 128:160], rhs=xs[:, k, :], start=(k == 0), stop=(k == 7)),
                                 reads=[Bgl, Bxs], writes=[BPS[2]])
                        S.op("act", lambda e: e.activation(out=sg1[:], in_=PS[2][:, 0:128], func=AF.Sigmoid), reads=[BPS[2]], writes=[Bsgg])
                        S.op("act", lambda e: e.activation(out=sg2[:], in_=PS[2][0:32, 128:256], func=AF.Sigmoid), reads=[BPS[2]], writes=[Bsgg])
                        for nh in range(2):
                            S.op("pe", lambda e, nh=nh: e.matmul(out=PS[nh][:, :], lhsT=sg1[:], rhs=glb1[:, nh * 512:(nh + 1) * 512], start=True, stop=False),
                                 reads=[Bsgg, Bgl], writes=[BPS[nh]])
                            S.op("pe", lambda e, nh=nh: e.matmul(out=PS[nh][:, :], lhsT=sg2[:], rhs=glb2[:, nh * 512:(nh + 1) * 512], start=False, stop=True),
                                 reads=[Bsgg, Bgl], writes=[BPS[nh]])
                            S.op("dve", lambda e, nh=nh, o0=o0: e.tensor_tensor(out=zb[:, nh * 512:(nh + 1) * 512], in0=o0[:, nh * 512:(nh + 1) * 512],
                                                                               in1=PS[nh][:, :], op=ALU.mult), reads=[Bot[0][s_], BPS[nh]], writes=[Bzb])
                    pzb = PS[3][:].bitcast(BF16)
                    for k in range(8):
                        S.op("pe", lambda e, k=k, pzb=pzb: e.transpose(out=pzb[:, k * 128:(k + 1) * 128], in_=zb[:, k * 128:(k + 1) * 128], identity=identb[:]),
                             reads=[Bzb, Bc], writes=[BPS[3]])
                    S.op("act", lambda e, pzb=pzb: e.activation(out=zT[:].rearrange("p k t -> p (k t)"), in_=pzb[:, 0:1024], func=AF.Copy),
                         reads=[BPS[3]], writes=[BzT])
                    pa, pbk = 4 + 2 * (i % 2), 5 + 2 * (i % 2)
                    for nh, pbank in ((0, pa), (1, pbk)):
                        for k in range(8):
                            S.op("pe", lambda e, k=k, nh=nh, pbank=pbank: e.matmul(out=PS[pbank][:, :], lhsT=zT[:, k, :], rhs=wo[:, k, nh * 512:(nh + 1) * 512],
                                                                                  start=(k == 0), stop=(k == 7)), reads=[BzT, Bwo], writes=[BPS[pbank]])
                    resid_ln(i, PS[pa], PS[pbk], BPS[pa], BPS[pbk], 0, lt[i % 2])
                S.barrier()

        def rwkv_layer(l, j, hT, BhT):
            with contextlib.ExitStack() as ph:
                vec = sb(ph, "rvec", [128, 16, 8])
                Bvec = S.buf("rvec", "own")
                S.dma("sp", vec[:], r_vecT[j].rearrange("n p k -> p n k"), writes=[Bvec])
                cF = {}
                Bk = S.buf("rconst", "own")
                for nm, shp, dt in (("rmask64", [128, 128], BF16), ("colmask2", [128, 2, 128], BF16), ("headind", [128, 2], F32),
                                    ("blockones", [128, 128], F32), ("sel64", [128, 2, 64], F32)):
                    cF[nm] = sb(ph, "rc_" + nm, shp, dt)
                    S.dma("pool" if dt == BF16 else "sp", cF[nm][:], cst[nm], writes=[Bk])
                mk1 = sb(ph, "mk1", [128, 256], BF16)
                mk2 = sb(ph, "mk2", [128, 256], BF16)
                mkn = sb(ph, "mkn", [128, 128], BF16)
                gng = sb(ph, "gng", [128, D])
                gnb = sb(ph, "gnb", [128, D])
                Bdirc = S.buf("dirc", "own")
                W3 = [sb(ph, "rw%d" % i, [128, 8, D], BF16) for i in range(3)]
                BW3 = [S.buf("rw%d" % i, "own") for i in range(3)]
                wla = sb(ph, "wla", [128, 8, 64], BF16)
                ala = sb(ph, "ala", [128, 8, 64], BF16)
                wlb = sb(ph, "wlb", [64, D], BF16)
                alb = sb(ph, "alb", [64, D], BF16)
                Blo = S.buf("lora", "own")
                Tst = sb(ph, "rT", [64, 16, 64])
                BT = [S.buf("rT%d" % h) for h in range(16)]
                GT = S.group("GT")
                xx = sb(ph, "xx", [128, 8, 128])
                Bxx = S.buf("xx")
                xs_r = sb(ph, "xs_r", [128, 8, 128], BF16)
                xs_k = sb(ph, "xs_k", [128, 8, 128], BF16)
                xs_s = sb(ph, "xs_s", [128, 8, 128], BF16)
                xsn = [xs_r, xs_s, xs_k, xs_s, xs_s]
                Bxs_r, Bxs_k, Bxs_s = S.buf("xs_r"), S.buf("xs_k"), S.buf("xs_s")
                Bxs = [Bxs_r, Bxs_s, Bxs_k, Bxs_s, Bxs_s]
                vf = sb(ph, "vf", [128, D])
                vb = vf
                Bvf = S.buf("vf")
                Bvb = Bvf
                tw = sb(ph, "tw", [64, 128], BF16)
                ta = sb(ph, "ta", [64, 128], BF16)
                Btw = S.buf("tw")
                Bta = S.buf("ta")
                yblk = sb(ph, "yblk", [128, D])
                Byb = S.buf("yblk")
                bon = sb(ph, "bon", [128, 16])
                Bbon = S.buf("bon")
                Bg = {nm: S.buf("gn_" + nm) for nm in ("st",)}
                gst = sb(ph, "gn_st", [128, 48])
                ct = {}
                for nm in ("sg", "aa", "kk", "kk2", "t", "Ls", "Dsg", "emD"):
                    ct[nm] = sb(ph, "c_" + nm, [128, 128])
                ct["rs"] = ct["kk2"]
                ct["prod"] = ct["kk2"]
                ct["k2"] = ct["t"]
                ct["Ds2"] = ct["Ls"]
                ct["bb"] = ct["aa"]
                ct["emD2"] = ct["Ls"]
                ct["eD"] = ct["Dsg"]
                ct["wc"] = sb(ph, "c_wc", [128, 2])
                KB = sb(ph, "KB", [128, 2, 128])
                QR = sb(ph, "QR", [128, 2, 128])
                KoT = sb(ph, "KoTm", [128, 2, 128])
                BoTn = sb(ph, "BoTnm", [128, 2, 128])
                Bct = {nm: S.buf("c_" + nm) for nm in list(ct.keys()) + ["KB", "QR", "KoT", "BoTn"]}
                Bct["rs"] = Bct["kk2"]
                Bct["prod"] = Bct["kk2"]
                Bct["k2"] = Bct["t"]
                Bct["Ds2"] = Bct["Ls"]
                Bct["bb"] = Bct["aa"]
                Bct["emD2"] = Bct["Ls"]
                Bct["eD"] = Bct["Dsg"]
                HX = []
                for s_ in range(2):
                    hx = dict(A1=sb(ph, "A1", [128, 256]), A2=sb(ph, "A2", [128, 256]), Nm=sb(ph, "Nm", [128, 128]), Zm=sb(ph, "Zm", [128, 128]),
                              RH=sb(ph, "RH", [128, 128]), RpE=sb(ph, "RpE", [64, 2, 128]), P2T=sb(ph, "P2T", [64, 2, 64]), T0p=sb(ph, "T0p", [64, 2, 64]),
                              wch=sb(ph, "wch", [64, 2]))
                    hx["B"] = {nm: S.buf("h%d_%s" % (s_, nm)) for nm in ("A1", "A2", "Nm", "Zm", "RH", "RpE", "P2T", "T0p", "wch")}
                    HX.append(hx)
                for d in range(2):
                    S.dma("pool", mk1[:], cst["mk1"][d], writes=[Bdirc])
                    S.dma("pool", mk2[:], cst["mk2"][d], writes=[Bdirc])
                    S.dma("pool", mkn[:], cst["mkn"][d], writes=[Bdirc])
                    S.dma("sp", gng[:], r_gng[j, d].partition_broadcast(128), writes=[Bdirc])
                    S.dma("sp", gnb[:], r_gnb[j, d].partition_broadcast(128), writes=[Bdirc])
                    for n3 in range(3):
                        S.dma("pool", W3[n3][:], r_wrkv[j, n3].rearrange("(k p) n -> p k n", p=128)[:, :, d * D:(d + 1) * D], writes=[BW3[n3]])
                    S.dma("pool", wla[:], r_wla[j].rearrange("(k p) z r -> p k z r", p=128)[:, :, d, :], writes=[Blo])
                    S.dma("pool", ala[:], r_ala[j].rearrange("(k p) z r -> p k z r", p=128)[:, :, d, :], writes=[Blo])
                    S.dma("pool", wlb[:], r_wlb[j, d], writes=[Blo])
                    S.dma("pool", alb[:], r_alb[j, d], writes=[Blo])
                    S.dma("sp", Tst[:], st_r[j, d].rearrange("h k v -> k h v"), writes=BT, grp=GT)
                    vcol = lambda n: vec[:, 6 + n * 2 + d, :]
                    order = list(range(NT)) if d == 0 else list(range(NT - 1, -1, -1))
                    for bi, i in enumerate(order):
                        seg = i // 2
                        b_ = i % 2
                        first_of_seg = (b_ == 0) if d == 0 else (b_ == 1)
                        last_of_seg = not first_of_seg
                        corder = [0, 1] if d == 0 else [1, 0]
                        if first_of_seg and bi > 0:
                            S.op("dve", lambda e: e.tensor_scalar(out=Tst[:], in0=Tst[:], scalar1=flg[0:64, 0:1], scalar2=None, op0=ALU.mult),
                                 reads=BT + [Bc], writes=BT)
                        _rt[0] += 1
                        S.skip = (_LIM["rstep"] < 1) or (_rt[0] > _LIM["rtiles"])
                        hL = hT[:, :, seg, PADL - 1 + b_ * 128: PADL - 1 + b_ * 128 + 128]
                        hR = hT[:, :, seg, PADL + 1 + b_ * 128: PADL + 1 + b_ * 128 + 128]
                        hC = hT[:, :, seg, PADL + b_ * 128: PADL + b_ * 128 + 128]
                        S.op("pool", lambda e, hL=hL, hR=hR: e.tensor_tensor(out=xx[:], in0=hL, in1=hR, op=ALU.add), reads=[BhT[seg]], writes=[Bxx])
                        S.op("dve", lambda e, hC=hC: e.scalar_tensor_tensor(out=xx[:], in0=xx[:], scalar=0.5, in1=hC, op0=ALU.mult, op1=ALU.subtract),
                             reads=[Bxx, BhT[seg]], writes=[Bxx])
                        def mk_xs(n, eng):
                            S.op(eng, lambda e: e.tensor_tensor(out=xsn[n][:], in0=xx[:], in1=vec[:, n, :].unsqueeze(2).to_broadcast([128, 8, 128]), op=ALU.mult),
                                 reads=[Bxx, Bvec], writes=[Bxs[n]])
                            S.op(eng, lambda e: e.tensor_tensor(out=xsn[n][:], in0=xsn[n][:], in1=hC, op=ALU.add), reads=[Bxs[n], BhT[seg]], writes=[Bxs[n]])
                        mk_xs(3, "dve")
                        mk_xs(0, "pool")
                        for nh in range(2):
                            pb = 6 + nh
                            for k in range(8):
                                S.op("pe", lambda e, k=k, nh=nh, pb=pb: e.matmul(out=PS[pb][:, :], lhsT=xsn[3][:, k, :], rhs=W3[2][:, k, nh * 512:(nh + 1) * 512],
                                                                               start=(k == 0), stop=(k == 7)), reads=[Bxs[3], BW3[2]], writes=[BPS[pb]])
                            S.op("act", lambda e, nh=nh, pb=pb: e.activation(out=vf[:, nh * 512:(nh + 1) * 512], in_=PS[pb][:, :], func=AF.Copy),
                                 reads=[BPS[pb]], writes=[Bvf])
                        mk_xs(1, "dve")
                        mk_xs(2, "pool")
                        for k in range(8):
                            S.op("pe", lambda e, k=k: e.matmul(out=PS[6][0:64, 0:128], lhsT=wla[:, k, :], rhs=xsn[1][:, k, :], start=(k == 0), stop=(k == 7)),
                                 reads=[Blo, Bxs[1]], writes=[BPS[6]])
                        mk_xs(4, "dve")
                        for k in range(8):
                            S.op("pe", lambda e, k=k: e.matmul(out=PS[6][0:64, 128:256], lhsT=ala[:, k, :], rhs=xsn[4][:, k, :], start=(k == 0), stop=(k == 7)),
                                 reads=[Blo, Bxs[4]], writes=[BPS[6]])
                        S.op("act", lambda e: e.activation(out=tw[:], in_=PS[6][0:64, 0:128], func=AF.Tanh), reads=[BPS[6]], writes=[Btw])
                        S.op("act", lambda e: e.activation(out=ta[:], in_=PS[6][0:64, 128:256], func=AF.Copy), reads=[BPS[6]], writes=[Bta])
                        for c in range(8):
                            cs = slice(c * 128, (c + 1) * 128)
                            S.skip = (_LIM["rstep"] < 4) or (_rt[0] > _LIM["rtiles"])
                            pp = PS[0]
                            for k in range(8):
                                S.op("pe", lambda e, k=k, cs=cs: e.matmul(out=pp[:, 0:128], lhsT=W3[0][:, k, cs], rhs=xsn[0][:, k, :], start=(k == 0), stop=(k == 7)),
                                     reads=[BW3[0], Bxs[0]], writes=[BPS[0]])
                            for k in range(8):
                                S.op("pe", lambda e, k=k, cs=cs: e.matmul(out=pp[:, 128:256], lhsT=W3[1][:, k, cs], rhs=xsn[2][:, k, :], start=(k == 0), stop=(k == 7)),
                                     reads=[BW3[1], Bxs[2]], writes=[BPS[0]])
                            S.op("pe", lambda e, cs=cs: e.matmul(out=pp[:, 256:384], lhsT=wlb[:, cs], rhs=tw[:], start=True, stop=True), reads=[Blo, Btw], writes=[BPS[0]])
                            S.op("pe", lambda e, cs=cs: e.matmul(out=pp[:, 384:512], lhsT=alb[:, cs], rhs=ta[:], start=True, stop=True), reads=[Blo, Bta], writes=[BPS[0]])
                            pr, pk = pp[:, 0:128], pp[:, 128:256]
                            S.skip = (_LIM["rstep"] < 5) or (_rt[0] > _LIM["rtiles"])
                            S.op("act", lambda e, c=c: e.activation(out=ct["sg"][:], in_=pp[:, 256:384], func=AF.Sigmoid, bias=vcol(0)[:, c:c + 1], scale=1.0),
                                 reads=[BPS[0], Bvec], writes=[Bct["sg"]])
                            S.op("act", lambda e, c=c: e.activation(out=ct["aa"][:], in_=pp[:, 384:512], func=AF.Sigmoid, bias=vcol(1)[:, c:c + 1], scale=1.0),
                                 reads=[BPS[0], Bvec], writes=[Bct["aa"]])
                            S.op("dve", lambda e, c=c: e.tensor_scalar(out=ct["kk"][:], in0=pk, scalar1=vcol(2)[:, c:c + 1], scalar2=None, op0=ALU.mult),
                                 reads=[BPS[0], Bvec], writes=[Bct["kk"]])
                            S.op("pool", lambda e: e.tensor_tensor(out=ct["kk2"][:], in0=ct["kk"][:], in1=ct["kk"][:], op=ALU.mult), reads=[Bct["kk"]], writes=[Bct["kk2"]])
                            S.op("pe", lambda e: e.matmul(out=PS[1][:, 0:128], lhsT=cF["blockones"][:], rhs=ct["kk2"][:], start=True, stop=True),
                                 reads=[Bk, Bct["kk2"]], writes=[BPS[1]])
                            S.op("dve", lambda e: e.tensor_scalar_max(out=ct["rs"][:], in0=PS[1][:, 0:128], scalar1=1e-24), reads=[BPS[1]], writes=[Bct["rs"]])
                            S.op("act", lambda e: e.activation(out=ct["rs"][:], in_=ct["rs"][:], func=AF.Sqrt), reads=[Bct["rs"]], writes=[Bct["rs"]])
                            S.op("dve", lambda e: e.reciprocal(out=ct["rs"][:], in_=ct["rs"][:]), reads=[Bct["rs"]], writes=[Bct["rs"]])
                            S.op("dve", lambda e: e.tensor_tensor(out=ct["kk"][:], in0=ct["kk"][:], in1=ct["rs"][:], op=ALU.mult), reads=[Bct["kk"], Bct["rs"]], writes=[Bct["kk"]])
                            S.skip = (_LIM["rstep"] < 6) or (_rt[0] > _LIM["rtiles"])
                            S.op("dve", lambda e, c=c: e.tensor_scalar(out=ct["t"][:], in0=ct["aa"][:], scalar1=1.0, scalar2=vcol(3)[:, c:c + 1], op0=ALU.subtract, op1=ALU.mult),
                                 reads=[Bct["aa"], Bvec], writes=[Bct["t"]])
                            S.op("dve", lambda e: e.scalar_tensor_tensor(out=ct["k2"][:], in0=ct["t"][:], scalar=1.0, in1=pk, op0=ALU.add, op1=ALU.mult),
                                 reads=[Bct["t"], BPS[0]], writes=[Bct["k2"]])
                            S.op("pool", lambda e: e.tensor_tensor(out=ct["bb"][:], in0=ct["kk"][:], in1=ct["aa"][:], op=ALU.mult), reads=[Bct["kk"], Bct["aa"]], writes=[Bct["bb"]])
                            S.op("dve", lambda e, c=c: e.scalar_tensor_tensor(out=ct["prod"][:], in0=pr, scalar=vcol(4)[:, c:c + 1], in1=ct["k2"][:], op0=ALU.mult, op1=ALU.mult),
                                 reads=[BPS[0], Bvec, Bct["k2"]], writes=[Bct["prod"]])
                            S.op("pe", lambda e, c=c: e.matmul(out=PS[7][:, 2 * c:2 * c + 2], lhsT=ct["prod"][:], rhs=cF["headind"][:], start=True, stop=True),
                                 reads=[Bct["prod"], Bk], writes=[BPS[7]])
                            S.skip = (_LIM["rstep"] < 7) or (_rt[0] > _LIM["rtiles"])
                            S.op("dve", lambda e: e.tensor_tensor_scan(out=ct["Ls"][:], data0=cF["rmask64"][:], data1=ct["sg"][:], initial=0.0, op0=ALU.mult, op1=ALU.add),
                                 reads=[Bct["sg"], Bk], writes=[Bct["Ls"]])
                            Lv = ct["Ls"][:].rearrange("p (c t) -> p c t", t=64)
                            Dv = ct["Dsg"][:].rearrange("p (c t) -> p c t", t=64)
                            if d == 0:
                                S.op("dve", lambda e, Lv=Lv, Dv=Dv: e.tensor_tensor(out=Dv, in0=Lv[:, :, 63:64].to_broadcast([128, 2, 64]), in1=Lv, op=ALU.subtract),
                                     reads=[Bct["Ls"]], writes=[Bct["Dsg"]])
                            else:
                                S.op("dve", lambda e: e.tensor_tensor(out=ct["Dsg"][:], in0=ct["Ls"][:], in1=ct["sg"][:], op=ALU.subtract),
                                     reads=[Bct["Ls"], Bct["sg"]], writes=[Bct["Dsg"]])
                            S.op("act", lambda e, Lv=Lv: e.activation(out=ct["wc"][:].unsqueeze(2), in_=Lv[:, :, 63:64], func=AF.Exp, scale=-C0), reads=[Bct["Ls"]], writes=[Bct["wc"]])
                            S.op("pool", lambda e: e.tensor_tensor(out=ct["Ds2"][:], in0=ct["Dsg"][:], in1=ct["sg"][:], op=ALU.add), reads=[Bct["Dsg"], Bct["sg"]], writes=[Bct["Ds2"]])
                            S.op("act", lambda e: e.activation(out=ct["emD"][:], in_=ct["Dsg"][:], func=AF.Exp, scale=C0), reads=[Bct["Dsg"]], writes=[Bct["emD"]])
                            S.op("act", lambda e: e.activation(out=ct["eD"][:], in_=ct["Dsg"][:], func=AF.Exp, scale=-C0), reads=[Bct["Dsg"]], writes=[Bct["eD"]])
                            S.op("act", lambda e: e.activation(out=ct["emD2"][:], in_=ct["Ds2"][:], func=AF.Exp, scale=C0), reads=[Bct["Ds2"]], writes=[Bct["emD2"]])
                            S.skip = (_LIM["rstep"] < 8) or (_rt[0] > _LIM["rtiles"])
                            S.op("pool", lambda e: e.tensor_tensor(out=KB[:, 0, :], in0=ct["k2"][:], in1=ct["eD"][:], op=ALU.mult), reads=[Bct["k2"], Bct["eD"]], writes=[Bct["KB"]])
                            S.op("pool", lambda e: e.tensor_tensor(out=KB[:, 1, :], in0=ct["bb"][:], in1=ct["eD"][:], op=ALU.mult), reads=[Bct["bb"], Bct["eD"]], writes=[Bct["KB"]])
                            S.op("pool", lambda e: e.tensor_tensor(out=QR[:, 0, :], in0=ct["kk"][:], in1=ct["emD2"][:], op=ALU.mult), reads=[Bct["kk"], Bct["emD2"]], writes=[Bct["QR"]])
                            S.op("dve", lambda e: e.tensor_tensor(out=QR[:, 1, :], in0=pr, in1=ct["emD"][:], op=ALU.mult), reads=[BPS[0], Bct["emD"]], writes=[Bct["QR"]])
                            S.skip = (_LIM["rstep"] < 9) or (_rt[0] > _LIM["rtiles"])
                            ptb = PS[1]
                            S.op("pe", lambda e, ptb=ptb: e.transpose(out=ptb[:, 128:256], in_=KB[:, 0, :], identity=ident[:]), reads=[Bct["KB"], Bc], writes=[BPS[1]])
                            S.op("pe", lambda e, ptb=ptb: e.transpose(out=ptb[:, 256:384], in_=KB[:, 1, :], identity=ident[:]), reads=[Bct["KB"], Bc], writes=[BPS[1]])
                            S.op("pe", lambda e, ptb=ptb: e.transpose(out=ptb[:, 384:512], in_=QR[:, 0, :], identity=ident[:]), reads=[Bct["QR"], Bc], writes=[BPS[1]])
                            hib = cF["headind"][:].unsqueeze(2).to_broadcast([128, 2, 128])
                            S.op("dve", lambda e, ptb=ptb, hib=hib: e.tensor_tensor(out=KoT[:], in0=ptb[:, 128:256].unsqueeze(1).to_broadcast([128, 2, 128]), in1=hib, op=ALU.mult),
                                 reads=[BPS[1], Bk], writes=[Bct["KoT"]])
                            S.op("dve", lambda e, ptb=ptb, hib=hib: e.scalar_tensor_tensor(out=BoTn[:], in0=ptb[:, 256:384].unsqueeze(1).to_broadcast([128, 2, 128]), scalar=-1.0, in1=hib,
                                                                                          op0=ALU.mult, op1=ALU.mult), reads=[BPS[1], Bk], writes=[Bct["BoTn"]])
                            def head_gen(c=c, hh=None, ptb=ptb):
                                head = 2 * c + hh
                                prs = slice(64 * hh, 64 * hh + 64)
                                hc = slice(head * 64, head * 64 + 64)
                                hcl = slice(hh * 64, hh * 64 + 64)
                                hx = HX[hh]
                                A1, A2, Nmt, Zmt, RH, RpE, P2T, T0p, wch = hx["A1"], hx["A2"], hx["Nm"], hx["Zm"], hx["RH"], hx["RpE"], hx["P2T"], hx["T0p"], hx["wch"]
                                Bh = hx["B"]
                                PA, PB, BA, BB = PS[2 + 2 * hh], PS[3 + 2 * hh], BPS[2 + 2 * hh], BPS[3 + 2 * hh]
                                qr2 = QR[prs, :, :].rearrange("p a t -> p (a t)")
                                S.op("pe", lambda e: e.matmul(out=PA[:, 0:256], lhsT=KB[prs, 0, :], rhs=qr2, start=True, stop=True), reads=[Bct["KB"], Bct["QR"]], writes=[BA])
                                S.op("pe", lambda e: e.matmul(out=PA[:, 256:512], lhsT=KB[prs, 1, :], rhs=qr2, start=True, stop=True), reads=[Bct["KB"], Bct["QR"]], writes=[BA])
                                S.op("pe", lambda e: e.matmul(out=PB[:, 0:128], lhsT=QR[prs, 0, :], rhs=KB[prs, 1, :], start=True, stop=True), reads=[Bct["KB"], Bct["QR"]], writes=[BB])
                                yield
                                S.op("dve", lambda e: e.tensor_tensor(out=A1[:], in0=PA[:, 0:256], in1=mk1[:], op=ALU.mult), reads=[BA, Bdirc], writes=[Bh["A1"]])
                                S.op("dve", lambda e: e.tensor_tensor(out=A2[:], in0=PA[:, 256:512], in1=mk2[:], op=ALU.mult), reads=[BA, Bdirc], writes=[Bh["A2"]])
                                S.op("dve", lambda e: e.tensor_tensor(out=Nmt[:], in0=PB[:, 0:128], in1=mkn[:], op=ALU.mult), reads=[BB, Bdirc], writes=[Bh["Nm"]])
                                yield
                                S.op("pe", lambda e: e.matmul(out=PB[:, 128:192], lhsT=A1[:, 0:128], rhs=vb[:, hc], start=True, stop=True), reads=[Bh["A1"], Bvb], writes=[BB])
                                S.op("dve", lambda e: e.tensor_copy(out=RH[:, 0:64], in_=ptb[:, 384 + 64 * hh:448 + 64 * hh]), reads=[BPS[1]], writes=[Bh["RH"]])
                                yield
                                S.op("act", lambda e: e.activation(out=RH[:, 64:128], in_=PB[:, 128:192], func=AF.Copy), reads=[BB], writes=[Bh["RH"]])
                                yield
                                for lvl in range(6):
                                    if lvl == 0:
                                        Zc, BZc = A2[:, 0:128], Bh["A2"]
                                    else:
                                        Zc, BZc = Zmt[:], Bh["Zm"]
                                    Nc, BNc = Nmt[:], Bh["Nm"]
                                    S.op("pe", lambda e, Zc=Zc: e.matmul(out=PB[:, 256:384], lhsT=Zc, rhs=RH[:], start=True, stop=True), reads=[BZc, Bh["RH"]], writes=[BB])
                                    if lvl < 5:
                                        S.op("pe", lambda e, Zc=Zc, Nc=Nc: e.matmul(out=PA[:, 0:128], lhsT=Nc, rhs=Zc, start=True, stop=True), reads=[BZc, BNc], writes=[BA])
                                        if lvl < 4:
                                            S.op("pe", lambda e, Zc=Zc, Nc=Nc: e.matmul(out=PA[:, 128:256], lhsT=Zc, rhs=Nc, start=True, stop=True), reads=[BZc, BNc], writes=[BA])
                                    yield
                                    S.op("dve", lambda e, lvl=lvl: e.tensor_tensor(out=RH[:], in0=RH[:], in1=PB[:, 256:384], op=(ALU.subtract if lvl == 0 else ALU.add)),
                                         reads=[Bh["RH"], BB], writes=[Bh["RH"]])
                                    if lvl < 5:
                                        S.op("act", lambda e: e.activation(out=Zmt[:], in_=PA[:, 0:128], func=AF.Copy), reads=[BA], writes=[Bh["Zm"]])
                                        if lvl < 4:
                                            S.op("act", lambda e: e.activation(out=Nmt[:], in_=PA[:, 128:256], func=AF.Copy), reads=[BA], writes=[Bh["Nm"]])
                                    yield
                                S.op("pe", lambda e: e.matmul(out=PA[0:64, 256:384], lhsT=cF["sel64"][:, hh, :], rhs=QR[:, 1, :], start=True, stop=False), reads=[Bk, Bct["QR"]], writes=[BA])
                                S.op("pe", lambda e: e.matmul(out=PA[0:64, 256:384], lhsT=RH[:, 0:64], rhs=A2[:, 128:256], start=False, stop=True), reads=[Bh["RH"], Bh["A2"]], writes=[BA])
                                for cc in range(2):
                                    S.op("pe", lambda e, cc=cc: e.matmul(out=PA[0:64, 384 + 64 * cc:448 + 64 * cc], lhsT=RH[:, 0:64], rhs=BoTn[:, cc, hcl], start=True, stop=True),
                                         reads=[Bh["RH"], Bct["BoTn"]], writes=[BA])
                                S.op("pe", lambda e: e.matmul(out=PB[0:64, 192:194], lhsT=cF["sel64"][:, hh, :], rhs=ct["wc"][:], start=True, stop=True), reads=[Bk, Bct["wc"]], writes=[BB])
                                yield
                                S.op("dve", lambda e: e.tensor_tensor(out=RpE[:], in0=PA[0:64, 256:384].unsqueeze(1).to_broadcast([64, 2, 128]), in1=cF["colmask2"][0:64], op=ALU.mult),
                                     reads=[BA, Bk], writes=[Bh["RpE"]])
                                S.op("dve", lambda e: e.tensor_tensor(out=P2T[:], in0=PA[0:64, 384:512].rearrange("p (c k) -> p c k", c=2),
                                                                      in1=ident[0:64, 0:64].unsqueeze(1).to_broadcast([64, 2, 64]), op=ALU.add), reads=[BA, Bc], writes=[Bh["P2T"]])
                                S.op("act", lambda e: e.activation(out=wch[:], in_=PB[0:64, 192:194], func=AF.Copy), reads=[BB], writes=[Bh["wch"]])
                                yield
                                for cc in corder:
                                    S.op("dve", lambda e, cc=cc: e.tensor_scalar(out=T0p[:, cc, :], in0=Tst[:, head, :], scalar1=wch[:, cc:cc + 1], scalar2=None, op0=ALU.mult),
                                         reads=[BT[head], Bh["wch"]], writes=[Bh["T0p"]])
                                    S.op("pe", lambda e, cc=cc: e.matmul(out=PB[0:64, 384:448], lhsT=KoT[:, cc, hcl], rhs=vb[:, hc], start=True, stop=False), reads=[Bct["KoT"], Bvb], writes=[BB])
                                    S.op("pe", lambda e, cc=cc: e.matmul(out=PB[0:64, 384:448], lhsT=BoTn[:, cc, hcl], rhs=RH[:, 64:128], start=False, stop=False), reads=[Bct["BoTn"], Bh["RH"]], writes=[BB])
                                    S.op("pe", lambda e, cc=cc: e.matmul(out=PB[0:64, 384:448], lhsT=P2T[:, cc, :], rhs=T0p[:, cc, :], start=False, stop=True), reads=[Bh["P2T"], Bh["T0p"]], writes=[BB])
                                    yield
                                    S.op("dve", lambda e: e.tensor_copy(out=Tst[:, head, :], in_=PB[0:64, 384:448]), reads=[BB], writes=[BT[head]])
                                    yield
                                S.op("pe", lambda e: e.matmul(out=PB[:, 448:512], lhsT=A1[:, 128:256], rhs=vb[:, hc], start=True, stop=False), reads=[Bh["A1"], Bvb], writes=[BB])
                                S.op("pe", lambda e: e.matmul(out=PB[:, 448:512], lhsT=A2[:, 128:256], rhs=RH[:, 64:128], start=False, stop=False), reads=[Bh["A2"], Bh["RH"]], writes=[BB])
                                for cc in range(2):
                                    S.op("pe", lambda e, cc=cc: e.matmul(out=PB[:, 448:512], lhsT=RpE[:, cc, :], rhs=T0p[:, cc, :], start=False, stop=(cc == 1)), reads=[Bh["RpE"], Bh["T0p"]], writes=[BB])
                                yield
                                S.op("act", lambda e: e.activation(out=yblk[:, hc], in_=PB[:, 448:512], func=AF.Copy), reads=[BB], writes=[Byb])
                            run_interleaved([head_gen(hh=0), head_gen(hh=1)], 2)
                        S.skip = (_LIM["rstep"] < 17) or (_rt[0] > _LIM["rtiles"])
                        yv = yblk[:].rearrange("p (h n) -> p h n", n=64)
                        sqt = xx[:].rearrange("p k t -> p (k t)")
                        sv = sqt.rearrange("p (h n) -> p h n", n=64)
                        S.op("act", lambda e: e.activation(out=bon[:], in_=PS[7][:, 0:16], func=AF.Copy), reads=[BPS[7]], writes=[Bbon])
                        S.op("dve", lambda e: e.tensor_reduce(out=gst[:, 0:16], in_=yv, axis=AX.X, op=ALU.add), reads=[Byb], writes=[Bg["st"]])
                        S.op("dve", lambda e: e.tensor_scalar(out=gst[:, 0:16], in0=gst[:, 0:16], scalar1=1.0 / 64.0, scalar2=None, op0=ALU.mult), reads=[Bg["st"]], writes=[Bg["st"]])
                        S.op("dve", lambda e: e.tensor_tensor(out=yv, in0=yv, in1=gst[:, 0:16].unsqueeze(2).to_broadcast([128, 16, 64]), op=ALU.subtract),
                             reads=[Byb, Bg["st"]], writes=[Byb])
                        S.op("act", lambda e: e.activation(out=sqt, in_=yblk[:], func=AF.Square), reads=[Byb], writes=[Bxx])
                        S.op("dve", lambda e: e.tensor_reduce(out=gst[:, 16:32], in_=sv, axis=AX.X, op=ALU.add), reads=[Bxx], writes=[Bg["st"]])
                        S.op("dve", lambda e: e.tensor_scalar(out=gst[:, 16:32], in0=gst[:, 16:32], scalar1=1.0 / 64.0, scalar2=GN_EPS, op0=ALU.mult, op1=ALU.add),
                             reads=[Bg["st"]], writes=[Bg["st"]])
                        S.op("act", lambda e: e.activation(out=gst[:, 16:32], in_=gst[:, 16:32], func=AF.Sqrt), reads=[Bg["st"]], writes=[Bg["st"]])
                        S.op("dve", lambda e: e.reciprocal(out=gst[:, 16:32], in_=gst[:, 16:32]), reads=[Bg["st"]], writes=[Bg["st"]])
                        S.op("dve", lambda e: e.tensor_tensor(out=yv, in0=yv, in1=gst[:, 16:32].unsqueeze(2).to_broadcast([128, 16, 64]), op=ALU.mult),
                             reads=[Byb, Bg["st"]], writes=[Byb])
                        S.op("pool", lambda e: e.tensor_tensor(out=yblk[:], in0=yblk[:], in1=gng[:], op=ALU.mult), reads=[Byb, Bdirc], writes=[Byb])
                        S.op("pool", lambda e: e.tensor_tensor(out=yblk[:], in0=yblk[:], in1=gnb[:], op=ALU.add), reads=[Byb, Bdirc], writes=[Byb])
                        vfv = vf[:].rearrange("p (h n) -> p h n", n=64)
                        S.op("dve", lambda e: e.tensor_tensor(out=vfv, in0=vfv, in1=bon[:].unsqueeze(2).to_broadcast([128, 16, 64]), op=ALU.mult),
                             reads=[Bvf, Bbon], writes=[Bvf])
                        S.op("pool", lambda e: e.tensor_tensor(out=yblk[:], in0=yblk[:], in1=vf[:], op=ALU.add), reads=[Byb, Bvf], writes=[Byb])
                        S.skip = False
                        S.dma("sp", oscr[d, i * 128:(i + 1) * 128, :], yblk[:], reads=[Byb], writes=[Bscr[d]])
                        if last_of_seg:
                            S.dma("act", ns_r[j, seg, d].rearrange("h k v -> k h v"), Tst[:], reads=BT, writes=[By])
                S.barrier()

        for l in range(n_layers):
            j = l // 2
            S.phase = "L%d.ada" % l
            adaln(l)
            if mix:
                with contextlib.ExitStack() as lph:
                    hT = sb(lph, "hT", [128, 8, NSEG, HTW], BF16)
                    BhT = [S.buf("hT%d" % s_) for s_ in range(NSEG)]
                    S.op("pool", lambda e: e.memset(hT[:, :, :, PADL - 1:PADL], 0.0), writes=BhT)
                    S.op("pool", lambda e: e.memset(hT[:, :, :, PADL + 256:PADL + 257], 0.0), writes=BhT)
                    S.phase = "L%d.hT" % l
                    build_hT(lambda i, k: hT[:, k, i // 2, PADL + (i % 2) * 128: PADL + (i % 2) * 128 + 128], lambda i: BhT[i // 2], list(range(NT)), sc1p, 0)
                    if l % 2 == 1:
                        S.op("dve", lambda e: e.tensor_scalar(out=hT[:, :, 1:NSEG, PADL - 1], in0=hT[:, :, 0:NSEG - 1, PADL + 255], scalar1=flg[:, 0:1], scalar2=None, op0=ALU.mult),
                             reads=BhT + [Bc], writes=BhT)
                        S.op("dve", lambda e: e.tensor_scalar(out=hT[:, :, 0:NSEG - 1, PADL + 256], in0=hT[:, :, 1:NSEG, PADL], scalar1=flg[:, 0:1], scalar2=None, op0=ALU.mult),
                             reads=BhT + [Bc], writes=BhT)
                    S.barrier()
                    stopped = False
                    if stop == "hT" and l == n_layers - 1:
                        stopped = True
                    elif l % 2 == 0:
                        S.phase = "L%d.mix" % l
                        hgrn_layer(l, j, hT, BhT)
                        if stop == "mix" and l == n_layers - 1:
                            stopped = True
                        else:
                            S.phase = "L%d.post" % l
                            post_mixer(l, j, "h", hT, BhT)
                    else:
                        S.phase = "L%d.mix" % l
                        rwkv_layer(l, j, hT, BhT)
                        if stop == "mix" and l == n_layers - 1:
                            stopped = True
                        else:
                            S.phase = "L%d.post" % l
                            post_mixer(l, j, "r", hT, BhT)
                if stopped or (stop == "post" and l == n_layers - 1):
                    break
            S.phase = "L%d.ffn" % l
            ffn(l)

        for i in range(NT):
            S.dma(hwq(), y_out[i * 128:(i + 1) * 128, :], X[:, i, :], reads=[BX[i]], writes=[By])
        S.barrier()
        S.emit(block)
        globals()["_LAST_SCHED"] = S
    return nc


_PROMPT_SLOTS = [[(p, p // 6) for p in range(32) if p % 6 == cix] for cix in range(6)]


def _fm(v):
    return np.ascontiguousarray(np.asarray(v, np.float32).reshape(8, 128).T)


def make_in_maps(inp, n_layers=4):
    NLW = n_layers
    NH = max(1, (n_layers + 1) // 2)
    NR = max(1, n_layers // 2)
    f = lambda a: np.ascontiguousarray(np.asarray(a, dtype=np.float32))
    consts = _consts()
    pos = _pos_table()
    shared = {
        "pos": pos,
        "ada_w": f(inp["ada_w"]),
        "ada_bT": np.ascontiguousarray(f(inp["ada_b"]).reshape(4, 48, 128).transpose(0, 2, 1)),
        "ln_g": f(inp["ln_g"]), "ln_b": f(inp["ln_b"]),
        "ffn_w_up": f(inp["ffn_w_up"]), "ffn_w_down": f(inp["ffn_w_down"]),
        "hgrn_w_in": f(inp["hgrn_w_in"]),
        "hgrn_lbT": np.ascontiguousarray(f(inp["hgrn_lb"]).reshape(2, 16, 128).transpose(0, 2, 1)),
        "hgrn_norm_g": f(inp["hgrn_norm_g"]), "hgrn_w_o": f(inp["hgrn_w_o"]),
        "rwkv_w_rkv": f(inp["rwkv_w_rkv"]), "rwkv_w_la": f(inp["rwkv_w_la"]), "rwkv_w_lb": f(inp["rwkv_w_lb"]),
        "rwkv_a_la": f(inp["rwkv_a_la"]), "rwkv_a_lb": f(inp["rwkv_a_lb"]),
        "rwkv_g_la": f(inp["rwkv_g_la"]), "rwkv_g_lb": f(inp["rwkv_g_lb"]),
        "rwkv_gn_g": f(inp["rwkv_gn_g"]), "rwkv_gn_b": f(inp["rwkv_gn_b"]), "rwkv_w_o": f(inp["rwkv_w_o"]),
    }
    vec = np.zeros((2, 16, 128, 8), np.float32)
    for j in range(2):
        for n in range(6):
            vec[j, n] = _fm(inp["rwkv_mu"][j, n])
        for n, nm in enumerate(("rwkv_w0", "rwkv_a0", "rwkv_k_k", "rwkv_k_a", "rwkv_r_k")):
            for d in range(2):
                vec[j, 6 + n * 2 + d] = _fm(inp[nm][j, d])
    shared["rwkv_vecT"] = vec
    for k, v in consts.items():
        shared["c_" + k] = v
    xp = f(inp["x_prompt"])
    xs = f(inp["x_sample"])
    sth = f(inp["state_hgrn"])
    strw = f(inp["state_rwkv"])
    maps = []
    for core in range(8):
        m = dict(shared)
        if core < 2:
            m["x_in"] = np.ascontiguousarray(xs[core])
            m["condT"] = _fm(inp["c"][core])
            fl = np.ones((128, 2), np.float32)
            m["st_h"] = np.ascontiguousarray(sth[core])
            m["st_r"] = np.ascontiguousarray(strw[core].transpose(0, 1, 2, 4, 3))
        else:
            slots = _PROMPT_SLOTS[core - 2]
            xin = np.zeros((NSEG, 256, D), np.float32)
            for s_ in range(NSEG):
                xin[s_] = xp[slots[s_][0]] if s_ < len(slots) else xp[slots[0][0]]
            m["x_in"] = xin.reshape(2048, D)
            m["condT"] = _fm(inp["c_ctx"])
            fl = np.zeros((128, 2), np.float32)
            m["st_h"] = np.zeros((2, 2, 8, 128, 128), np.float32)
            m["st_r"] = np.zeros((2, 2, 16, 64, 64), np.float32)
        m["flags"] = fl
        maps.append(m)
    cut = {"ada_w": NLW, "ada_bT": NLW, "ln_g": NLW, "ln_b": NLW, "ffn_w_up": NLW, "ffn_w_down": NLW,
           "hgrn_w_in": NH, "hgrn_norm_g": NH, "hgrn_w_o": NH, "rwkv_w_rkv": NR, "rwkv_w_la": NR, "rwkv_w_lb": NR,
           "rwkv_a_la": NR, "rwkv_a_lb": NR, "rwkv_g_la": NR, "rwkv_g_lb": NR, "rwkv_gn_g": NR, "rwkv_gn_b": NR, "rwkv_w_o": NR}
    if n_layers < 4:
        for k, n in cut.items():
            sl = np.ascontiguousarray(shared[k][:n])
            for m in maps:
                m[k] = sl
    return maps


def assemble(results):
    y_prompt = np.zeros((32, 256, D), np.float32)
    y_sample = np.zeros((2, 2048, D), np.float32)
    nsh = np.zeros((32, 2, 2, 8, 128, 128), np.float32)
    nsr = np.zeros((32, 2, 2, 16, 64, 64), np.float32)
    for core in range(8):
        r = results[core]
        if core < 2:
            y_sample[core] = r["y_out"]
        else:
            slots = _PROMPT_SLOTS[core - 2]
            yo = r["y_out"].reshape(NSEG, 256, D)
            for s_, (p, _) in enumerate(slots):
                y_prompt[p] = yo[s_]
                nsh[p] = r["ns_h"][:, s_]
                nsr[p] = r["ns_r"][:, s_].transpose(0, 1, 2, 4, 3)
    return y_prompt, y_sample, nsh, nsr


_NC_CACHE = {}
_LIM = {"heads": 10 ** 9, "step": 99, "rstep": 99, "rtiles": 10 ** 9}
_rt = [0]


def kernel(**inputs):
    if "nc" not in _NC_CACHE:
        _NC_CACHE["nc"] = build_program()
    nc = _NC_CACHE["nc"]
    maps = make_in_maps(inputs)
    res = run_bass_kernel_spmd(nc, maps, core_ids=list(range(8)))
    return assemble(res.results)
```

```python
import contextlib
import numpy as np
import concourse.bass as bass
import concourse.mybir as mybir
from concourse.bass_utils import run_bass_kernel_spmd

F32 = mybir.dt.float32
BF16 = mybir.dt.bfloat16
ALU = mybir.AluOpType
AF = mybir.ActivationFunctionType
AX = mybir.AxisListType

D = 1024
NT = 16
NSEG = 8
DN_ALPHA = 8 ** 0.25
LN_EPS = 1e-5
RMS_EPS = 1e-6
GN_EPS = 64e-5
C0 = 0.606531
PADL = 2
HTW = 260


class SemGroup:
    def __init__(self, sem):
        self.sem = sem
        self.count = 0
        self.sw = None


class Buf:
    __slots__ = ("name", "w", "r", "grp", "excl")

    def __init__(self, name, grp=None):
        self.name = name
        self.w = []
        self.r = {}
        self.grp = grp
        self.excl = False


class _Rec:
    def __init__(self):
        self.call = None

    def __getattr__(self, name):
        def f(*a, **k):
            self.call = (name, a, k)
            return self
        return f


class Sched:
    ENG = ("pe", "act", "dve", "pool", "sp")

    def __init__(self, nc, stack):
        self.nc = nc
        self.stack = stack
        self.ops = {e: [] for e in self.ENG}
        self.esem = {e: stack.enter_context(nc.semaphore("es_" + e)) for e in self.ENG}
        self.targets = {e: set() for e in self.ENG}
        self.groups = []
        self.gcache = {}
        self.lastreal = {e: 0 for e in self.ENG}

    def group(self, key=None):
        if key is not None and key in self.gcache:
            return self.gcache[key]
        g = SemGroup(self.stack.enter_context(self.nc.semaphore("dg%d" % len(self.groups))))
        self.groups.append(g)
        if key is not None:
            self.gcache[key] = g
        return g

    def buf(self, name, grp=None):
        if grp == "own":
            grp = self.group(name)
        return Buf(name, grp)

    def _deps(self, eng, reads, writes):
        deps = []
        for b in reads:
            deps.extend(b.w)
        for b in writes:
            deps.extend(b.w)
            for k, v in b.r.items():
                if isinstance(k, str):
                    deps.append(("e", k, v))
                else:
                    deps.append(("d", k[1], v))
        waits = {}
        for d in deps:
            if d[0] == "e" and d[1] == eng and eng in ("pe", "sp"):
                continue
            key = (d[0], d[1])
            val = d[1].count if d[0] == "d" else d[2]
            waits[key] = max(waits.get(key, 0), val)
        for k, v in waits.items():
            if k[0] == "e":
                self.targets[k[1]].add(v)
        return waits

    skip = False
    phase = ""

    def op(self, eng, fn, reads=(), writes=()):
        if self.skip:
            return
        ex = [b for b in reads if b.excl]
        if ex:
            writes = list(writes) + ex
        waits = self._deps(eng, reads, writes)
        rec = _Rec()
        fn(rec)
        assert rec.call is not None
        self.ops[eng].append([rec.call, waits, "c", None, self.phase])
        idx = len(self.ops[eng])
        self.lastreal[eng] = idx
        for b in reads:
            b.r[eng] = idx
        for b in writes:
            b.w = [("e", eng, idx)]
            b.r = {}
        return idx

    def dma(self, eng, out, in_, reads=(), writes=(), grp=None, **kw):
        if self.skip:
            return
        waits = self._deps(eng, reads, writes)
        g = grp
        if g is None:
            for b in writes:
                if b.grp is not None:
                    g = b.grp
        assert g is not None
        if eng == "pool":
            if g.sw is None:
                g.sw = self.group()
            g = g.sw
        g.count += 1
        cnt = g.count

        kw2 = dict(kw)
        kw2["out"] = out
        kw2["in_"] = in_
        self.ops[eng].append([("dma_start", (), kw2), waits, "d", g])
        for b in reads:
            b.r[("d", g)] = cnt
        for b in writes:
            if b.w and all(x[0] == "d" for x in b.w):
                b.w = [x for x in b.w if x[1] is not g] + [("d", g, cnt)]
            else:
                b.w = [("d", g, cnt)]
            b.r = {}

    def barrier(self):
        last = dict(self.lastreal)
        for e in self.ENG:
            waits = {}
            for o in self.ENG:
                if o != e and last[o] > 0:
                    waits[("e", o)] = last[o]
                    self.targets[o].add(last[o])
            for g in self.groups:
                if g.count > 0:
                    waits[("d", g)] = g.count
            self.ops[e].append([None, waits, "w", None])

    def emit(self, block):
        tval = {}
        for e in self.ENG:
            c = 0
            m = {}
            for i in range(1, len(self.ops[e]) + 1):
                if i in self.targets[e]:
                    c += 1
                    m[i] = c
            tval[e] = m
        sched = self

        def run(e, engobj):
            seen = {}
            for i, rec in enumerate(sched.ops[e], start=1):
                fn, waits = rec[0], rec[1]
                for k, v in waits.items():
                    if k[0] == "e":
                        val = tval[k[1]][v]
                        sem = sched.esem[k[1]]
                    else:
                        val = 16 * v
                        sem = k[1].sem
                    if seen.get(id(sem), 0) >= val:
                        continue
                    seen[id(sem)] = val
                    engobj.wait_ge(sem, val)
                if fn is None:
                    assert i not in tval[e]
                    continue
                ins = getattr(engobj, fn[0])(*fn[1], **fn[2])
                if rec[2] == "d":
                    ins.then_inc(rec[3].sem, 16)
                elif i in tval[e]:
                    ins.then_inc(sched.esem[e], 1)

        @block.tensor
        def _(pe):
            run("pe", pe)

        @block.scalar
        def _(act):
            run("act", act)

        @block.vector
        def _(dve):
            run("dve", dve)

        @block.gpsimd
        def _(pool):
            run("pool", pool)

        @block.sync
        def _(sp):
            run("sp", sp)


def _consts():
    c = {}
    idx = np.arange(128)
    c["ident"] = np.eye(128, dtype=np.float32)
    same32 = (idx[:, None] // 32) == (idx[None, :] // 32)
    mh = np.zeros((2, 128, 128), np.float32)
    mh[0] = same32 & (idx[:, None] <= idx[None, :])
    mh[1] = same32 & (idx[:, None] >= idx[None, :])
    c["maskH"] = mh
    cm4 = np.zeros((128, 4, 128), np.float32)
    for cc in range(4):
        cm4[:, cc, cc * 32:(cc + 1) * 32] = 1.0
    c["colmask4"] = cm4
    rm4 = np.zeros((128, 4), np.float32)
    rm4[idx, idx // 32] = 1.0
    c["rowmask4"] = rm4
    r32 = np.ones((128, 128), np.float32)
    r32[:, ::32] = 0.0
    c["rmask32"] = r32
    r64 = np.ones((128, 128), np.float32)
    r64[:, ::64] = 0.0
    c["rmask64"] = r64
    same64 = (idx[:, None] // 64) == (idx[None, :] // 64)
    st = [same64 & (idx[:, None] < idx[None, :]), same64 & (idx[:, None] > idx[None, :])]
    inc = [same64 & (idx[:, None] <= idx[None, :]), same64 & (idx[:, None] >= idx[None, :])]
    mk1 = np.zeros((2, 128, 256), np.float32)
    mk2 = np.zeros((2, 128, 256), np.float32)
    mkn = np.zeros((2, 128, 128), np.float32)
    for d in range(2):
        mk1[d, :, :128] = st[d]
        mk1[d, :, 128:] = inc[d]
        mk2[d, :, :128] = st[d]
        mk2[d, :, 128:] = -inc[d].astype(np.float32)
        mkn[d] = st[d].T
    c["mk1"] = mk1
    c["mk2"] = mk2
    c["mkn"] = mkn
    cm2 = np.zeros((128, 2, 128), np.float32)
    cm2[:, 0, :64] = 1.0
    cm2[:, 1, 64:] = 1.0
    c["colmask2"] = cm2
    hi = np.zeros((128, 2), np.float32)
    hi[idx, idx // 64] = 1.0
    c["headind"] = hi
    c["blockones"] = same64.astype(np.float32)
    sel = np.zeros((128, 2, 64), np.float32)
    for hh in range(2):
        sel[64 * hh + np.arange(64), hh, np.arange(64)] = 1.0
    c["sel64"] = sel
    return c


def _pos_table():
    rows, gw, quarter, half = 2048 // 64, 64, 256, 512
    omega = (1.0 / (10000.0 ** (np.arange(quarter, dtype=np.float32) / np.float32(quarter)))).astype(np.float32)
    r = np.arange(rows, dtype=np.float32)[:, None] * omega
    cc = np.arange(gw, dtype=np.float32)[:, None] * omega
    row_emb = np.concatenate([np.sin(r), np.cos(r)], -1)
    col_emb = np.concatenate([np.sin(cc), np.cos(cc)], -1)
    emb = np.concatenate([np.broadcast_to(row_emb[:, None, :], (rows, gw, half)),
                          np.broadcast_to(col_emb[None, :, :], (rows, gw, half))], -1)
    return np.ascontiguousarray(emb.reshape(rows * gw, D).astype(np.float32))


CONST_SHAPES = {
    "ident": [128, 128], "maskH": [2, 128, 128], "colmask4": [128, 4, 128], "rowmask4": [128, 4],
    "rmask32": [128, 128], "rmask64": [128, 128], "mk1": [2, 128, 256], "mk2": [2, 128, 256],
    "mkn": [2, 128, 128], "colmask2": [128, 2, 128], "headind": [128, 2], "blockones": [128, 128],
    "sel64": [128, 2, 64],
}


def build_program(n_layers=4, mix=True, dbg=False, stop=None):
    nc = bass.Bass("TRN2", target_bir_lowering=False)

    def din(name, shape):
        return nc.dram_tensor(name, list(shape), F32, kind="ExternalInput").ap()

    def dout(name, shape):
        return nc.dram_tensor(name, list(shape), F32, kind="ExternalOutput").ap()

    NLW = n_layers
    NH = max(1, (n_layers + 1) // 2)
    NR = max(1, n_layers // 2)
    x_in = din("x_in", [2048, D])
    pos = din("pos", [2048, D])
    condT = din("condT", [128, 8])
    flags = din("flags", [128, 2])
    ada_w = din("ada_w", [NLW, D, 6 * D])
    ada_bT = din("ada_bT", [NLW, 128, 48])
    ln_g = din("ln_g", [NLW, 2, D])
    ln_b = din("ln_b", [NLW, 2, D])
    w_up = din("ffn_w_up", [NLW, D, 4 * D])
    w_dn = din("ffn_w_down", [NLW, 4 * D, D])
    h_win = din("hgrn_w_in", [NH, D, 7 * D])
    h_lbT = din("hgrn_lbT", [2, 128, 16])
    h_ng = din("hgrn_norm_g", [NH, D])
    h_wo = din("hgrn_w_o", [NH, D, D])
    st_h = din("st_h", [2, 2, 8, 128, 128])
    st_r = din("st_r", [2, 2, 16, 64, 64])
    r_vecT = din("rwkv_vecT", [2, 16, 128, 8])
    r_wrkv = din("rwkv_w_rkv", [NR, 3, D, 2 * D])
    r_wla = din("rwkv_w_la", [NR, D, 2, 64])
    r_wlb = din("rwkv_w_lb", [NR, 2, 64, D])
    r_ala = din("rwkv_a_la", [NR, D, 2, 64])
    r_alb = din("rwkv_a_lb", [NR, 2, 64, D])
    r_gla = din("rwkv_g_la", [NR, D, 160])
    r_glb = din("rwkv_g_lb", [NR, 160, D])
    r_gng = din("rwkv_gn_g", [NR, 2, D])
    r_gnb = din("rwkv_gn_b", [NR, 2, D])
    r_wo = din("rwkv_w_o", [NR, D, D])
    cst = {k: din("c_" + k, v) for k, v in CONST_SHAPES.items()}

    y_out = dout("y_out", [2048, D])
    ns_h = dout("ns_h", [2, NSEG, 2, 8, 128, 128])
    ns_r = dout("ns_r", [2, NSEG, 2, 16, 64, 64])
    oscr = nc.dram_tensor("oscr", [2, 2048, D], F32).ap()
    dbgt = dout("dbg", [16, 128, D]) if dbg else None

    with contextlib.ExitStack() as st:
        S = Sched(nc, st)

        _uid = [0]

        def sb(stack, name, shape, dt=F32):
            _uid[0] += 1
            return stack.enter_context(nc.sbuf_tensor("%s_u%d" % (name, _uid[0]), list(shape), dt))

        X = sb(st, "X", [128, NT, D])
        BX = [S.buf("X%d" % i) for i in range(NT)]
        GX = S.group("GX")
        ident = sb(st, "ident", [128, 128])
        identb = sb(st, "identb", [128, 128], BF16)
        Bc = S.buf("consts", "own")
        flg = sb(st, "flg", [128, 2])
        scond = sb(st, "scond", [128, 8])
        modT = sb(st, "modT", [128, 48])
        sc1p = sb(st, "sc1p", [128, 8])
        sc2p = sb(st, "sc2p", [128, 8])
        Bmod = S.buf("mod")
        PS = [st.enter_context(nc.psum_tensor("ps%d" % i, [128, 512], F32)) for i in range(8)]
        BPS = [S.buf("ps%d" % i) for i in range(8)]
        for b_ in BPS:
            b_.excl = True
        Gout = S.group("Gout")
        By = S.buf("y_out", Gout)
        Gscr = S.group("Gscr")
        Bscr = [S.buf("oscr0", Gscr), S.buf("oscr1", Gscr)]

        block = st.enter_context(nc.Block())
        _dq = [0]
        Bdbg = S.buf("dbg", S.group("dbg"))

        def dump(slot, ap, bufs, p=128, n=None):
            if not dbg:
                return
            n = n if n is not None else ap.shape[-1]
            S.dma("pool", dbgt[slot, 0:p, 0:n], ap, reads=bufs, writes=[Bdbg])

        def hwq():
            _dq[0] += 1
            return "sp" if _dq[0] % 2 == 0 else "act"

        S.dma("sp", ident[:], cst["ident"], writes=[Bc])
        S.dma("pool", identb[:], cst["ident"], writes=[Bc])
        S.dma("sp", flg[:], flags, writes=[Bc])
        S.dma("sp", scond[:], condT, writes=[Bc])
        S.op("act", lambda e: e.activation(out=scond[:], in_=scond[:], func=AF.Silu), reads=[Bc], writes=[Bc])
        for i in range(NT):
            S.dma(hwq(), X[:, i, :], x_in[i * 128:(i + 1) * 128, :], writes=[BX[i]], grp=GX)
        with contextlib.ExitStack() as ph:
            pt = [sb(ph, "pos%d" % i, [128, D]) for i in range(2)]
            Bpt = [S.buf("pos%d" % i, "own") for i in range(2)]
            for i in range(NT):
                S.dma(hwq(), pt[i % 2][:], pos[i * 128:(i + 1) * 128, :], writes=[Bpt[i % 2]])
                eng = "dve"
                S.op(eng, lambda e, i=i: e.scalar_tensor_tensor(out=X[:, i, :], in0=pt[i % 2][:], scalar=flg[:, 1:2], in1=X[:, i, :],
                                                                op0=ALU.mult, op1=ALU.add),
                     reads=[Bpt[i % 2], Bc, BX[i]], writes=[BX[i]])
            dump(1, X[:, 0, :], [BX[0]])
            S.barrier()

        def run_interleaved(gens, width):
            it = iter(gens)
            active = []
            while True:
                while len(active) < width:
                    g = next(it, None)
                    if g is None:
                        break
                    active.append(g)
                if not active:
                    break
                for g in list(active):
                    try:
                        next(g)
                    except StopIteration:
                        active.remove(g)

        def adaln(l):
            with contextlib.ExitStack() as ph:
                slab = [sb(ph, "adas%d" % i, [128, 8, 256]) for i in range(2)]
                Bsl = [S.buf("adas%d" % i, "own") for i in range(2)]
                abT = sb(ph, "abT", [128, 48])
                Bab = S.buf("abT", "own")
                S.dma("sp", abT[:], ada_bT[l], writes=[Bab])
                wv = ada_w[l].rearrange("(k p) n -> p k n", p=128)
                acc = PS[0]
                for jg in range(24):
                    sl = jg % 2
                    S.dma(hwq(), slab[sl][:], wv[:, :, jg * 256:(jg + 1) * 256], writes=[Bsl[sl]])
                    for jj in range(2):
                        j = jg * 2 + jj
                        for k in range(8):
                            S.op("pe", lambda e, sl=sl, jj=jj, j=j, k=k: e.matmul(
                                out=acc[:, j:j + 1], lhsT=slab[sl][:, k, jj * 128:(jj + 1) * 128], rhs=scond[:, k:k + 1],
                                start=(k == 0), stop=(k == 7)), reads=[Bsl[sl], Bc], writes=[BPS[0]])
                S.op("dve", lambda e: e.tensor_tensor(out=modT[:], in0=acc[:, 0:48], in1=abT[:], op=ALU.add),
                     reads=[BPS[0], Bab], writes=[Bmod])
                S.op("dve", lambda e: e.tensor_scalar_add(out=sc1p[:], in0=modT[:, 8:16], scalar1=1.0), reads=[Bmod], writes=[Bmod])
                S.op("dve", lambda e: e.tensor_scalar_add(out=sc2p[:], in0=modT[:, 32:40], scalar1=1.0), reads=[Bmod], writes=[Bmod])
                if l == 0:
                    dump(0, modT[:], [Bmod])
                S.barrier()

        def make_gbc(ph, which):
            base = (16, 40)[which]
            gb = sb(ph, "gbc", [128, D])
            Bgb = S.buf("gbc%d" % which)
            gtmp = sb(ph, "gtmp", [128, 128])
            Bgt = S.buf("gtmp")
            for k in range(8):
                S.op("dve", lambda e, k=k: e.tensor_scalar(
                    out=gtmp[:], in0=modT[:, base + k:base + k + 1].to_broadcast([128, 128]),
                    scalar1=1.0 / DN_ALPHA, scalar2=None, op0=ALU.mult), reads=[Bmod], writes=[Bgt])
                S.op("pe", lambda e: e.transpose(out=PS[1][:, 0:128], in_=gtmp[:], identity=ident[:]),
                     reads=[Bgt, Bc], writes=[BPS[1]])
                S.op("act", lambda e, k=k: e.activation(out=gb[:, k * 128:(k + 1) * 128], in_=PS[1][:, 0:128], func=AF.Copy),
                     reads=[BPS[1]], writes=[Bgb])
            return gb, Bgb

        def build_hT(dst_fn, Bdst_fn, tiles, scp, shcol):
            n = 0
            for i in tiles:
                for kk in range(2):
                    pb = 2 + (n % 2)
                    n += 1
                    for k4 in range(4):
                        k = kk * 4 + k4
                        S.op("pe", lambda e, i=i, k=k, k4=k4, pb=pb: e.transpose(
                            out=PS[pb][:, k4 * 128:(k4 + 1) * 128], in_=X[:, i, k * 128:(k + 1) * 128], identity=ident[:]),
                            reads=[BX[i], Bc], writes=[BPS[pb]])
                    for k4 in range(4):
                        k = kk * 4 + k4
                        if k % 2 == 0:
                            S.op("act", lambda e, i=i, k=k, k4=k4, pb=pb: e.activation(
                                out=dst_fn(i, k), in_=PS[pb][:, k4 * 128:(k4 + 1) * 128], func=AF.Identity,
                                bias=modT[:, shcol + k:shcol + k + 1], scale=scp[:, k:k + 1]),
                                reads=[BPS[pb], Bmod], writes=[Bdst_fn(i)])
                        else:
                            S.op("dve", lambda e, i=i, k=k, k4=k4, pb=pb: e.tensor_scalar(
                                out=dst_fn(i, k), in0=PS[pb][:, k4 * 128:(k4 + 1) * 128],
                                scalar1=scp[:, k:k + 1], scalar2=modT[:, shcol + k:shcol + k + 1], op0=ALU.mult, op1=ALU.add),
                                reads=[BPS[pb], Bmod], writes=[Bdst_fn(i)])

        def resid_ln(i, pa, pb, Bpa, Bpb, which, lt):
            v, Bv = lt["v"], lt["Bv"]
            stt, Bst = lt["st"], lt["Bst"]
            S.op("dve", lambda e: e.tensor_tensor(out=v[:, 0:512], in0=pa[:, :], in1=lt["gbc"][:, 0:512], op=ALU.mult),
                 reads=[Bpa, lt["Bgbc"]], writes=[Bv])
            S.op("dve", lambda e: e.tensor_tensor(out=v[:, 512:1024], in0=pb[:, :], in1=lt["gbc"][:, 512:1024], op=ALU.mult),
                 reads=[Bpb, lt["Bgbc"]], writes=[Bv])
            peng = lt.get("peng", "pool")
            S.op(peng, lambda e: e.tensor_tensor(out=v[:], in0=v[:], in1=X[:, i, :], op=ALU.add), reads=[Bv, BX[i]], writes=[Bv])
            if i == 0 and lt.get("dbg"):
                dump(5, v[:], [Bv])
            S.op("dve", lambda e: e.bn_stats(out=stt[:, 0:6], in_=v[:, 0:512]), reads=[Bv], writes=[Bst])
            S.op("dve", lambda e: e.bn_stats(out=stt[:, 6:12], in_=v[:, 512:1024]), reads=[Bv], writes=[Bst])
            S.op("dve", lambda e: e.bn_aggr(out=stt[:, 12:14], in_=stt[:, 0:12]), reads=[Bst], writes=[Bst])
            S.op("dve", lambda e: e.tensor_scalar_add(out=stt[:, 14:15], in0=stt[:, 13:14], scalar1=LN_EPS / (DN_ALPHA ** 2)), reads=[Bst], writes=[Bst])
            S.op("act", lambda e: e.activation(out=stt[:, 14:15], in_=stt[:, 14:15], func=AF.Sqrt), reads=[Bst], writes=[Bst])
            S.op("dve", lambda e: e.reciprocal(out=stt[:, 14:15], in_=stt[:, 14:15]), reads=[Bst], writes=[Bst])
            S.op("dve", lambda e: e.tensor_scalar(out=v[:], in0=v[:], scalar1=stt[:, 12:13], scalar2=stt[:, 14:15],
                                                  op0=ALU.subtract, op1=ALU.mult), reads=[Bv, Bst], writes=[Bv])
            if i == 0 and lt.get("dbg"):
                dump(6, stt[:], [Bst])
                dump(8, v[:], [Bv])
            S.op(peng, lambda e: e.tensor_tensor(out=v[:], in0=v[:], in1=lt["lng"][:], op=ALU.mult),
                 reads=[Bv, lt["Bln"]], writes=[Bv])
            S.op(peng, lambda e: e.tensor_tensor(out=X[:, i, :], in0=v[:], in1=lt["lnb"][:], op=ALU.add),
                 reads=[Bv, lt["Bln"]], writes=[BX[i]])

        def ffn(l):
            with contextlib.ExitStack() as ph:
                wd = sb(ph, "wd", [128, 32, D], BF16)
                Bwd = [S.buf("wd%d" % i, "own") for i in range(4)]
                ups = [sb(ph, "ups%d" % i, [128, 8, 256], BF16) for i in range(2)]
                Bups = [S.buf("ups%d" % i, "own") for i in range(2)]
                aT = sb(ph, "aT", [128, 32, 512], BF16)
                BaT = [S.buf("aT%d" % j) for j in range(32)]
                h2 = sb(ph, "h2", [128, 8, 512], BF16)
                Bh2 = [S.buf("h2_%d" % i) for i in range(4)]
                rl = [sb(ph, "rl%d" % i, [128, 512]) for i in range(2)]
                Brl = [S.buf("rl%d" % i) for i in range(2)]
                lng = sb(ph, "lng", [128, D])
                lnb = sb(ph, "lnb", [128, D])
                Bln = S.buf("lnbc", "own")
                S.dma("sp", lng[:], ln_g[l, 1].partition_broadcast(128), writes=[Bln])
                S.dma("sp", lnb[:], ln_b[l, 1].partition_broadcast(128), writes=[Bln])
                gb_, Bgb_ = make_gbc(ph, 1)
                lt = [dict(v=sb(ph, "lnv%d" % i, [128, D]), Bv=S.buf("lnv%d" % i), st=sb(ph, "lnst%d" % i, [128, 16]), Bst=S.buf("lnst%d" % i),
                           lng=lng, lnb=lnb, Bln=Bln, gbc=gb_, Bgbc=Bgb_, peng="dve") for i in range(1)]
                if l == 0:
                    lt[0]["dbg"] = True
                wdv = w_dn[l].rearrange("(j p) n -> p j n", p=128)
                for q in range(4):
                    S.dma("pool", wd[:, q * 8:(q + 1) * 8, :], wdv[:, q * 8:(q + 1) * 8, :], writes=[Bwd[q]])
                wuv = w_up[l].rearrange("(k p) n -> p k n", p=128)
                nev = 0
                for g in range(4):
                    tiles = list(range(g * 4, g * 4 + 4))
                    build_hT(lambda i, k: h2[:, k, (i % 4) * 128:(i % 4 + 1) * 128], lambda i: Bh2[i % 4], tiles, sc2p, 24)
                    for jg in range(16):
                        sl = jg % 2
                        S.dma("pool", ups[sl][:], wuv[:, :, jg * 256:(jg + 1) * 256], writes=[Bups[sl]])
                        for jj in range(2):
                            j = jg * 2 + jj
                            pb = 4 + (nev % 2)
                            for k in range(8):
                                S.op("pe", lambda e, sl=sl, jj=jj, k=k, pb=pb: e.matmul(
                                    out=PS[pb][:, :], lhsT=ups[sl][:, k, jj * 128:(jj + 1) * 128], rhs=h2[:, k, :],
                                    start=(k == 0), stop=(k == 7)), reads=[Bups[sl]] + Bh2, writes=[BPS[pb]])
                            r = nev % 2
                            nev += 1
                            S.op("act", lambda e, pb=pb, r=r: e.activation(out=rl[r][:], in_=PS[pb][:, :], func=AF.Relu),
                                 reads=[BPS[pb]], writes=[Brl[r]])
                            if j % 2 == 0:
                                S.op("dve", lambda e, j=j, r=r: e.tensor_tensor(out=aT[:, j, :], in0=rl[r][:], in1=rl[r][:], op=ALU.mult),
                                     reads=[Brl[r]], writes=[BaT[j]])
                            else:
                                S.op("act", lambda e, j=j, r=r: e.activation(out=aT[:, j, :], in_=rl[r][:], func=AF.Square),
                                     reads=[Brl[r]], writes=[BaT[j]])
                    if l == 0 and g == 0:
                        dump(3, h2[:, :, 0:128].rearrange("p k t -> p k t"), Bh2, n=None) if False else None
                        for k in range(8):
                            if dbg:
                                S.dma("pool", dbgt[3, :, k * 128:(k + 1) * 128], h2[:, k, 0:128], reads=Bh2, writes=[Bdbg])
                        dump(4, aT[:, 0, :], [BaT[0]])
                    for ii, i in enumerate(tiles):
                        pa, pbk = 6, 7
                        for nh, pbank in ((0, pa), (1, pbk)):
                            for j in range(32):
                                S.op("pe", lambda e, j=j, ii=ii, nh=nh, pbank=pbank: e.matmul(
                                    out=PS[pbank][:, :], lhsT=aT[:, j, ii * 128:(ii + 1) * 128], rhs=wd[:, j, nh * 512:(nh + 1) * 512],
                                    start=(j == 0), stop=(j == 31)), reads=[BaT[j], Bwd[j // 8]], writes=[BPS[pbank]])
                        if l == 0 and i == 0 and dbg:
                            S.dma("pool", dbgt[9, :, 0:512], PS[pa][:, :], reads=[BPS[pa]], writes=[Bdbg]) if False else None
                        resid_ln(i, PS[pa], PS[pbk], BPS[pa], BPS[pbk], 1, lt[0])
                        if l == 0 and i == 0:
                            dump(7, X[:, 0, :], [BX[0]])
                S.barrier()

        def hgrn_layer(l, j, hT, BhT):
            hcols = lambda i, k: hT[:, k, i // 2, PADL + (i % 2) * 128: PADL + (i % 2) * 128 + 128]
            with contextlib.ExitStack() as ph:
                lbv = sb(ph, "lbv", [128, 16])
                oml = sb(ph, "oml", [128, 16])
                Blb = S.buf("lbv", "own")
                if j == 0:
                    S.op("dve", lambda e: e.memset(lbv[:], 0.0), writes=[Blb])
                else:
                    lb0 = sb(ph, "lb0", [128, 16])
                    S.dma("sp", lb0[:], h_lbT[0], writes=[Blb])
                    S.dma("sp", lbv[:], h_lbT[1], writes=[Blb])
                    S.op("dve", lambda e: e.tensor_tensor(out=lbv[:], in0=lbv[:], in1=lb0[:], op=ALU.subtract), reads=[Blb], writes=[Blb])
                    S.op("act", lambda e: e.activation(out=lbv[:], in_=lbv[:], func=AF.Sigmoid), reads=[Blb], writes=[Blb])
                S.op("dve", lambda e: e.tensor_scalar(out=oml[:], in0=lbv[:], scalar1=-1.0, scalar2=1.0, op0=ALU.mult, op1=ALU.add),
                     reads=[Blb], writes=[Blb])
                maskH = sb(ph, "maskH", [128, 128])
                cm4 = sb(ph, "cm4", [128, 4, 128], BF16)
                rm4 = sb(ph, "rm4", [128, 4])
                r32 = sb(ph, "r32", [128, 128])
                Bcm = S.buf("hconst", "own")
                S.dma("pool", cm4[:], cst["colmask4"], writes=[Bcm])
                S.dma("sp", rm4[:], cst["rowmask4"], writes=[Bcm])
                S.dma("sp", r32[:], cst["rmask32"], writes=[Bcm])
                wq = [sb(ph, "hw%d" % i, [128, 8, D], BF16) for i in range(3)]
                Bwq = [S.buf("hw%d" % i, "own") for i in range(3)]
                Tst = sb(ph, "Tst", [128, 8, 128])
                BT = [S.buf("T%d" % h) for h in range(8)]
                GT = S.group("GT")
                vbs = [sb(ph, "vb%d" % i, [128, D], BF16) for i in range(2)]
                Bvbs = [S.buf("vb%d" % i) for i in range(2)]
                oblk = [sb(ph, "oblk%d" % i, [128, D]) for i in range(2)]
                Bob = [S.buf("oblk%d" % i) for i in range(2)]
                NS_ = 4
                tl = []
                for s_ in range(NS_):
                    t = {}
                    for nm in ("q", "nz", "uu", "l1", "l2", "Lp", "Dd", "eD", "emD", "t1", "kf"):
                        t[nm] = sb(ph, "h_%s%d" % (nm, s_), [128, 128])
                    t["wc"] = sb(ph, "h_wc%d" % s_, [128, 4])
                    for nm in ("kout", "qpp", "At"):
                        t[nm] = sb(ph, "h_%s%d" % (nm, s_), [128, 128], BF16)
                    for nm in ("koe", "qe", "Tp"):
                        t[nm] = sb(ph, "h_%s%d" % (nm, s_), [128, 4, 128], BF16)
                    t["B"] = {nm: S.buf("h_%s%d" % (nm, s_)) for nm in
                              ("q", "nz", "uu", "l1", "l2", "Lp", "Dd", "eD", "emD", "t1", "kf", "wc", "kout", "qpp", "At", "koe", "qe", "Tp")}
                    tl.append(t)
                wv = h_win[j].rearrange("(k p) n -> p k n", p=128)
                un = 0
                for d in range(2):
                    S.dma("sp", maskH[:], cst["maskH"][d], reads=[], writes=[Bcm])
                    for w3 in range(3):
                        c0 = d * 3072 + w3 * 1024
                        S.dma("pool", wq[w3][:], wv[:, :, c0:c0 + 1024], writes=[Bwq[w3]])
                    S.dma("sp", Tst[:], st_h[j, d].rearrange("h k v -> k h v"), writes=BT, grp=GT)
                    order = list(range(NT)) if d == 0 else list(range(NT - 1, -1, -1))
                    def hgen(bi, i, h, d=d, order=order):
                        seg = i // 2
                        first_of_seg = (i % 2 == 0) if d == 0 else (i % 2 == 1)
                        last_of_seg = not first_of_seg
                        corder = [0, 1, 2, 3] if d == 0 else [3, 2, 1, 0]
                        vbt, Bvbt = vbs[bi % 2], Bvbs[bi % 2]
                        ob, Bo = oblk[bi % 2], Bob[bi % 2]
                        idx = bi * 8 + h
                        t = tl[idx % NS_]
                        B = t["B"]
                        pqz = pg = po = 2 * (idx % NS_)
                        pu = 2 * (idx % NS_) + 1
                        col = d * 8 + h
                        if h == 0:
                            for nh in range(2):
                                pb = 6 + nh
                                for k in range(8):
                                    S.op("pe", lambda e, k=k, nh=nh, pb=pb: e.matmul(
                                        out=PS[pb][:, :], lhsT=hcols(i, k), rhs=wq[2][:, k, nh * 512:(nh + 1) * 512],
                                        start=(k == 0), stop=(k == 7)), reads=[BhT[seg], Bwq[2]], writes=[BPS[pb]])
                                S.op("act", lambda e, nh=nh, pb=pb: e.activation(out=vbt[:, nh * 512:(nh + 1) * 512], in_=PS[pb][:, :], func=AF.Copy),
                                     reads=[BPS[pb]], writes=[Bvbt])
                            yield
                        if first_of_seg and bi > 0:
                            S.op("dve", lambda e: e.tensor_scalar(out=Tst[:, h, :], in0=Tst[:, h, :], scalar1=flg[:, 0:1], scalar2=None, op0=ALU.mult),
                                 reads=[BT[h], Bc], writes=[BT[h]])
                        for w3, off in ((0, 0), (1, 128)):
                            for k in range(8):
                                S.op("pe", lambda e, i=i, k=k, w3=w3, off=off, pqz=pqz, h=h: e.matmul(
                                    out=PS[pqz][:, off:off + 128], lhsT=wq[w3][:, k, h * 128:(h + 1) * 128], rhs=hcols(i, k),
                                    start=(k == 0), stop=(k == 7)), reads=[BhT[seg], Bwq[w3]], writes=[BPS[pqz]])
                        yield
                        S.op("act", lambda e, t=t, pqz=pqz: e.activation(out=t["q"][:], in_=PS[pqz][:, 0:128], func=AF.Silu),
                             reads=[BPS[pqz]], writes=[B["q"]])
                        yield
                        S.op("dve", lambda e, t=t, pqz=pqz: e.tensor_scalar(out=t["nz"][:], in0=PS[pqz][:, 128:256], scalar1=-1.0, scalar2=80.0,
                                                                             op0=ALU.mult, op1=ALU.min), reads=[BPS[pqz]], writes=[B["nz"]])
                        yield
                        S.op("act", lambda e, t=t: e.activation(out=t["uu"][:], in_=t["nz"][:], func=AF.Exp), reads=[B["nz"]], writes=[B["uu"]])
                        yield
                        S.op("act", lambda e, t=t, col=col: e.activation(out=t["l1"][:], in_=t["uu"][:], func=AF.Ln, bias=1.0,
                                                                          scale=lbv[:, col:col + 1]), reads=[B["uu"], Blb], writes=[B["l1"]])
                        yield
                        S.op("act", lambda e, t=t: e.activation(out=t["l2"][:], in_=t["uu"][:], func=AF.Ln, bias=1.0, scale=1.0),
                             reads=[B["uu"]], writes=[B["l2"]])
                        yield
                        S.op("dve", lambda e, t=t: e.tensor_tensor(out=t["l1"][:], in0=t["l1"][:], in1=t["l2"][:], op=ALU.subtract),
                             reads=[B["l1"], B["l2"]], writes=[B["l1"]])
                        yield
                        S.op("dve", lambda e, t=t: e.tensor_tensor_scan(out=t["Lp"][:], data0=r32[:], data1=t["l1"][:], initial=0.0,
                                                                       op0=ALU.mult, op1=ALU.add), reads=[B["l1"], Bcm], writes=[B["Lp"]])
                        yield
                        Lv = t["Lp"][:].rearrange("p (c t) -> p c t", t=32)
                        yield
                        Dv = t["Dd"][:].rearrange("p (c t) -> p c t", t=32)
                        yield
                        if d == 0:
                            S.op("dve", lambda e, Lv=Lv, Dv=Dv: e.tensor_tensor(out=Dv, in0=Lv[:, :, 31:32].to_broadcast([128, 4, 32]), in1=Lv,
                                                                                op=ALU.subtract), reads=[B["Lp"]], writes=[B["Dd"]])
                        else:
                            S.op("dve", lambda e, t=t: e.tensor_tensor(out=t["Dd"][:], in0=t["Lp"][:], in1=t["l1"][:], op=ALU.subtract),
                                 reads=[B["Lp"], B["l1"]], writes=[B["Dd"]])
                        yield
                        S.op("pool", lambda e, t=t: e.tensor_scalar_max(out=t["Dd"][:], in0=t["Dd"][:], scalar1=-80.0), reads=[B["Dd"]], writes=[B["Dd"]])
                        yield
                        S.op("act", lambda e, t=t: e.activation(out=t["eD"][:], in_=t["Dd"][:], func=AF.Exp), reads=[B["Dd"]], writes=[B["eD"]])
                        yield
                        S.op("act", lambda e, t=t: e.activation(out=t["emD"][:], in_=t["Dd"][:], func=AF.Exp, scale=-1.0), reads=[B["Dd"]], writes=[B["emD"]])
                        yield
                        S.op("act", lambda e, t=t, Lv=Lv: e.activation(out=t["wc"][:].unsqueeze(2), in_=Lv[:, :, 31:32], func=AF.Exp),
                             reads=[B["Lp"]], writes=[B["wc"]])
                        yield
                        S.op("dve", lambda e, t=t: e.tensor_scalar_add(out=t["t1"][:], in0=t["uu"][:], scalar1=1.0), reads=[B["uu"]], writes=[B["t1"]])
                        yield
                        S.op("dve", lambda e, t=t: e.reciprocal(out=t["t1"][:], in_=t["t1"][:]), reads=[B["t1"]], writes=[B["t1"]])
                        yield
                        S.op("dve", lambda e, t=t, col=col: e.scalar_tensor_tensor(out=t["kf"][:], in0=t["uu"][:], scalar=oml[:, col:col + 1], in1=t["t1"][:],
                                                                                  op0=ALU.mult, op1=ALU.mult), reads=[B["uu"], B["t1"], Blb], writes=[B["kf"]])
                        yield
                        S.op("pool", lambda e, t=t: e.tensor_tensor(out=t["kout"][:], in0=t["kf"][:], in1=t["eD"][:], op=ALU.mult),
                             reads=[B["kf"], B["eD"]], writes=[B["kout"]])
                        yield
                        S.op("pool", lambda e, t=t: e.tensor_tensor(out=t["qpp"][:], in0=t["q"][:], in1=t["emD"][:], op=ALU.mult),
                             reads=[B["q"], B["emD"]], writes=[B["qpp"]])
                        yield
                        pgb = PS[pg][:].bitcast(BF16)
                        yield
                        S.op("pe", lambda e, t=t, pg=pg: e.matmul(out=PS[pg][:, 256:384], lhsT=t["kout"][:], rhs=t["qpp"][:], start=True, stop=True),
                             reads=[B["kout"], B["qpp"]], writes=[BPS[pg]])
                        yield
                        S.op("pe", lambda e, t=t, pgb=pgb: e.transpose(out=pgb[:, 768:896], in_=t["kout"][:], identity=identb[:]),
                             reads=[B["kout"], Bc], writes=[BPS[pg]])
                        yield
                        S.op("dve", lambda e, t=t, pg=pg: e.tensor_tensor(out=t["At"][:], in0=PS[pg][:, 256:384], in1=maskH[:], op=ALU.mult),
                             reads=[BPS[pg], Bcm], writes=[B["At"]])
                        yield
                        S.op("dve", lambda e, t=t, pgb=pgb: e.tensor_tensor(out=t["koe"][:], in0=pgb[:, 768:896].unsqueeze(1).to_broadcast([128, 4, 128]),
                                                                           in1=rm4[:].unsqueeze(2).to_broadcast([128, 4, 128]), op=ALU.mult),
                             reads=[BPS[pg], Bcm], writes=[B["koe"]])
                        yield
                        S.op("pool", lambda e, t=t: e.tensor_tensor(out=t["qe"][:], in0=t["qpp"][:].unsqueeze(1).to_broadcast([128, 4, 128]),
                                                                   in1=cm4[:], op=ALU.mult), reads=[B["qpp"], Bcm], writes=[B["qe"]])
                        yield
                        for c in range(4):
                            S.op("pe", lambda e, t=t, c=c, pu=pu, h=h: e.matmul(out=PS[pu][:, c * 128:(c + 1) * 128], lhsT=t["koe"][:, c, :],
                                                                              rhs=vbt[:, h * 128:(h + 1) * 128], start=True, stop=True),
                                 reads=[B["koe"], Bvbt], writes=[BPS[pu]])
                        yield
                        for c in corder:
                            S.op("dve", lambda e, t=t, c=c, h=h: e.tensor_scalar(out=t["Tp"][:, c, :], in0=Tst[:, h, :], scalar1=t["wc"][:, c:c + 1],
                                                                                scalar2=None, op0=ALU.mult), reads=[BT[h], B["wc"]], writes=[B["Tp"]])
                            S.op("dve", lambda e, t=t, c=c, h=h, pu=pu: e.scalar_tensor_tensor(out=Tst[:, h, :], in0=Tst[:, h, :], scalar=t["wc"][:, c:c + 1],
                                                                                              in1=PS[pu][:, c * 128:(c + 1) * 128], op0=ALU.mult, op1=ALU.add),
                                 reads=[BT[h], B["wc"], BPS[pu]], writes=[BT[h]])
                        yield
                        S.op("pe", lambda e, t=t, po=po, h=h: e.matmul(out=PS[po][:, 0:128], lhsT=t["At"][:], rhs=vbt[:, h * 128:(h + 1) * 128],
                                                                       start=True, stop=False), reads=[B["At"], Bvbt], writes=[BPS[po]])
                        yield
                        for ci, c in enumerate(corder):
                            S.op("pe", lambda e, t=t, po=po, c=c, ci=ci: e.matmul(out=PS[po][:, 0:128], lhsT=t["qe"][:, c, :], rhs=t["Tp"][:, c, :],
                                                                                  start=False, stop=(ci == 3)), reads=[B["qe"], B["Tp"]], writes=[BPS[po]])
                        yield
                        S.op("act", lambda e, ob=ob, po=po, h=h: e.activation(out=ob[:, h * 128:(h + 1) * 128], in_=PS[po][:, 0:128], func=AF.Copy),
                             reads=[BPS[po]], writes=[Bo])
                        yield
                        if last_of_seg:
                            S.dma("act", ns_h[j, seg, d, h], Tst[:, h, :], reads=[BT[h]], writes=[By], grp=S.group("st_T%d" % h))
                        if h == 7:
                            S.dma("sp", oscr[d, i * 128:(i + 1) * 128, :], ob[:], reads=[Bo], writes=[Bscr[d]], grp=S.group("st_ob%d" % (bi % 2)))
                    run_interleaved((hgen(bi, i, h) for bi, i in enumerate(order) for h in range(8)), NS_)
                S.barrier()

        def post_mixer(l, j, kind, hT, BhT):
            hcols = lambda i, k: hT[:, k, i // 2, PADL + (i % 2) * 128: PADL + (i % 2) * 128 + 128]
            with contextlib.ExitStack() as ph:
                wo = sb(ph, "wo", [128, 8, D], BF16)
                Bwo = S.buf("wo", "own")
                src_wo = h_wo[j] if kind == "h" else r_wo[j]
                S.dma("pool", wo[:], src_wo.rearrange("(k p) n -> p k n", p=128), writes=[Bwo])
                ot = [[sb(ph, "ot%d_%d" % (dd, s_), [128, D]) for s_ in range(2)] for dd in range(2)]
                Bot = [[S.buf("ot%d_%d" % (dd, s_), "own") for s_ in range(2)] for dd in range(2)]
                zb = sb(ph, "zb", [128, D], BF16)
                Bzb = S.buf("zb")
                zT = sb(ph, "zT", [128, 8, 128], BF16)
                BzT = S.buf("zT")
                lng = sb(ph, "lng", [128, D])
                lnb = sb(ph, "lnb", [128, D])
                Bln = S.buf("lnbc", "own")
                S.dma("sp", lng[:], ln_g[l, 0].partition_broadcast(128), writes=[Bln])
                S.dma("sp", lnb[:], ln_b[l, 0].partition_broadcast(128), writes=[Bln])
                gb_, Bgb_ = make_gbc(ph, 0)
                lt = [dict(v=sb(ph, "lnv%d" % i, [128, D]), Bv=S.buf("lnv%d" % i), st=sb(ph, "lnst%d" % i, [128, 16]), Bst=S.buf("lnst%d" % i),
                           lng=lng, lnb=lnb, Bln=Bln, gbc=gb_, Bgbc=Bgb_) for i in range(2)]
                if kind == "h":
                    wg = sb(ph, "wg", [128, 8, D], BF16)
                    Bwg = S.buf("wg", "own")
                    S.dma("pool", wg[:], h_win[j].rearrange("(k p) n -> p k n", p=128)[:, :, 6144:7168], writes=[Bwg])
                    ngbc = sb(ph, "ngbc", [128, D])
                    Bng = S.buf("ngbc", "own")
                    S.dma("sp", ngbc[:], h_ng[j].partition_broadcast(128), writes=[Bng])
                    sq = sb(ph, "sq", [128, D])
                    Bsq = S.buf("sq")
                    sgt = sb(ph, "sgt", [128, D])
                    Bsg = S.buf("sgt")
                    ss = sb(ph, "ss", [128, 8])
                    Bss = S.buf("ss")
                else:
                    gla = sb(ph, "gla", [128, 8, 160], BF16)
                    glb1 = sb(ph, "glb1", [128, D], BF16)
                    glb2 = sb(ph, "glb2", [32, D], BF16)
                    Bgl = S.buf("gl", "own")
                    S.dma("pool", gla[:], r_gla[j].rearrange("(k p) n -> p k n", p=128), writes=[Bgl])
                    S.dma("pool", glb1[:], r_glb[j, 0:128, :], writes=[Bgl])
                    S.dma("pool", glb2[:], r_glb[j, 128:160, :], writes=[Bgl])
                    muT = sb(ph, "muTg", [128, 8])
                    S.dma("sp", muT[:], r_vecT[j, 5], writes=[Bgl])
                    xt = sb(ph, "xg_t", [128, 8, 128])
                    xs = sb(ph, "xg_s", [128, 8, 128], BF16)
                    Bxt = S.buf("xg_t")
                    Bxs = S.buf("xg_s")
                    sg1 = sb(ph, "sg1", [128, 128], BF16)
                    sg2 = sb(ph, "sg2", [32, 128], BF16)
                    Bsgg = S.buf("sgg")
                for i in range(NT):
                    seg = i // 2
                    s_ = i % 2
                    for dd in range(2):
                        S.dma(hwq(), ot[dd][s_][:], oscr[dd, i * 128:(i + 1) * 128, :], reads=[Bscr[dd]], writes=[Bot[dd][s_]])
                    o0, o1 = ot[0][s_], ot[1][s_]
                    S.op("pool", lambda e, o0=o0, o1=o1: e.tensor_tensor(out=o0[:], in0=o0[:], in1=o1[:], op=ALU.add),
                         reads=[Bot[0][s_], Bot[1][s_]], writes=[Bot[0][s_]])
                    if kind == "h":
                        for nh in range(2):
                            for k in range(8):
                                S.op("pe", lambda e, i=i, k=k, nh=nh: e.matmul(out=PS[nh][:, :], lhsT=hcols(i, k), rhs=wg[:, k, nh * 512:(nh + 1) * 512],
                                                                               start=(k == 0), stop=(k == 7)), reads=[BhT[seg], Bwg], writes=[BPS[nh]])
                            S.op("act", lambda e, nh=nh: e.activation(out=sgt[:, nh * 512:(nh + 1) * 512], in_=PS[nh][:, :], func=AF.Silu),
                                 reads=[BPS[nh]], writes=[Bsg])
                        S.op("act", lambda e, o0=o0: e.activation(out=sq[:], in_=o0[:], func=AF.Square), reads=[Bot[0][s_]], writes=[Bsq])
                        S.op("dve", lambda e: e.tensor_reduce(out=ss[:], in_=sq[:].rearrange("p (h k) -> p h k", k=128), axis=AX.X, op=ALU.add),
                             reads=[Bsq], writes=[Bss])
                        S.op("dve", lambda e: e.tensor_scalar(out=ss[:], in0=ss[:], scalar1=1.0 / 128.0, scalar2=RMS_EPS, op0=ALU.mult, op1=ALU.add),
                             reads=[Bss], writes=[Bss])
                        S.op("act", lambda e: e.activation(out=ss[:], in_=ss[:], func=AF.Sqrt), reads=[Bss], writes=[Bss])
                        S.op("dve", lambda e: e.reciprocal(out=ss[:], in_=ss[:]), reads=[Bss], writes=[Bss])
                        S.op("dve", lambda e, o0=o0: e.tensor_tensor(out=o0[:].rearrange("p (h k) -> p h k", k=128), in0=o0[:].rearrange("p (h k) -> p h k", k=128),
                                                                    in1=ss[:].unsqueeze(2).to_broadcast([128, 8, 128]), op=ALU.mult),
                             reads=[Bot[0][s_], Bss], writes=[Bot[0][s_]])
                        S.op("pool", lambda e, o0=o0: e.tensor_tensor(out=o0[:], in0=o0[:], in1=ngbc[:], op=ALU.mult), reads=[Bot[0][s_], Bng], writes=[Bot[0][s_]])
                        S.op("dve", lambda e, o0=o0: e.tensor_tensor(out=zb[:], in0=o0[:], in1=sgt[:], op=ALU.mult), reads=[Bot[0][s_], Bsg], writes=[Bzb])
                    else:
                        b_ = i % 2
                        hL = hT[:, :, seg, PADL - 1 + b_ * 128: PADL - 1 + b_ * 128 + 128]
                        hR = hT[:, :, seg, PADL + 1 + b_ * 128: PADL + 1 + b_ * 128 + 128]
                        hC = hT[:, :, seg, PADL + b_ * 128: PADL + b_ * 128 + 128]
                        S.op("pool", lambda e, hL=hL, hR=hR: e.tensor_tensor(out=xt[:], in0=hL, in1=hR, op=ALU.add), reads=[BhT[seg]], writes=[Bxt])
                        S.op("dve", lambda e, hC=hC: e.scalar_tensor_tensor(out=xt[:], in0=xt[:], scalar=0.5, in1=hC, op0=ALU.mult, op1=ALU.subtract),
                             reads=[Bxt, BhT[seg]], writes=[Bxt])
                        S.op("dve", lambda e: e.tensor_tensor(out=xt[:], in0=xt[:], in1=muT[:].unsqueeze(2).to_broadcast([128, 8, 128]), op=ALU.mult),
                             reads=[Bxt, Bgl], writes=[Bxt])
                        S.op("dve", lambda e, hC=hC: e.tensor_tensor(out=xs[:], in0=xt[:], in1=hC, op=ALU.add), reads=[Bxt, BhT[seg]], writes=[Bxs])
                        for k in range(8):
                            S.op("pe", lambda e, k=k: e.matmul(out=PS[2][:, 0:128], lhsT=gla[:, k, 0:128], rhs=xs[:, k, :], start=(k == 0), stop=(k == 7)),
                                 reads=[Bgl, Bxs], writes=[BPS[2]])
                        for k in range(8):
                            S.op("pe", lambda e, k=k: e.matmul(out=PS[2][0:32, 128:256], lhsT=gla[:, k, 128:160], rhs=xs[:, k, :], start=(k == 0), stop=(k == 7)),
                                 reads=[Bgl, Bxs], writes=[BPS[2]])
                        S.op("act", lambda e: e.activation(out=sg1[:], in_=PS[2][:, 0:128], func=AF.Sigmoid), reads=[BPS[2]], writes=[Bsgg])
                        S.op("act", lambda e: e.activation(out=sg2[:], in_=PS[2][0:32, 128:256], func=AF.Sigmoid), reads=[BPS[2]], writes=[Bsgg])
                        for nh in range(2):
                            S.op("pe", lambda e, nh=nh: e.matmul(out=PS[nh][:, :], lhsT=sg1[:], rhs=glb1[:, nh * 512:(nh + 1) * 512], start=True, stop=False),
                                 reads=[Bsgg, Bgl], writes=[BPS[nh]])
                            S.op("pe", lambda e, nh=nh: e.matmul(out=PS[nh][:, :], lhsT=sg2[:], rhs=glb2[:, nh * 512:(nh + 1) * 512], start=False, stop=True),
                                 reads=[Bsgg, Bgl], writes=[BPS[nh]])
                            S.op("dve", lambda e, nh=nh, o0=o0: e.tensor_tensor(out=zb[:, nh * 512:(nh + 1) * 512], in0=o0[:, nh * 512:(nh + 1) * 512],
                                                                               in1=PS[nh][:, :], op=ALU.mult), reads=[Bot[0][s_], BPS[nh]], writes=[Bzb])
                    pzb = PS[3][:].bitcast(BF16)
                    for k in range(8):
                        S.op("pe", lambda e, k=k, pzb=pzb: e.transpose(out=pzb[:, k * 128:(k + 1) * 128], in_=zb[:, k * 128:(k + 1) * 128], identity=identb[:]),
                             reads=[Bzb, Bc], writes=[BPS[3]])
                    S.op("act", lambda e, pzb=pzb: e.activation(out=zT[:].rearrange("p k t -> p (k t)"), in_=pzb[:, 0:1024], func=AF.Copy),
                         reads=[BPS[3]], writes=[BzT])
                    pa, pbk = 4 + 2 * (i % 2), 5 + 2 * (i % 2)
                    for nh, pbank in ((0, pa), (1, pbk)):
                        for k in range(8):
                            S.op("pe", lambda e, k=k, nh=nh, pbank=pbank: e.matmul(out=PS[pbank][:, :], lhsT=zT[:, k, :], rhs=wo[:, k, nh * 512:(nh + 1) * 512],
                                                                                  start=(k == 0), stop=(k == 7)), reads=[BzT, Bwo], writes=[BPS[pbank]])
                    resid_ln(i, PS[pa], PS[pbk], BPS[pa], BPS[pbk], 0, lt[i % 2])
                S.barrier()

        def rwkv_layer(l, j, hT, BhT):
            with contextlib.ExitStack() as ph:
                vec = sb(ph, "rvec", [128, 16, 8])
                Bvec = S.buf("rvec", "own")
                S.dma("sp", vec[:], r_vecT[j].rearrange("n p k -> p n k"), writes=[Bvec])
                cF = {}
                Bk = S.buf("rconst", "own")
                for nm, shp, dt in (("rmask64", [128, 128], BF16), ("colmask2", [128, 2, 128], BF16), ("headind", [128, 2], F32),
                                    ("blockones", [128, 128], F32), ("sel64", [128, 2, 64], F32)):
                    cF[nm] = sb(ph, "rc_" + nm, shp, dt)
                    S.dma("pool" if dt == BF16 else "sp", cF[nm][:], cst[nm], writes=[Bk])
                mk1 = sb(ph, "mk1", [128, 256], BF16)
                mk2 = sb(ph, "mk2", [128, 256], BF16)
                mkn = sb(ph, "mkn", [128, 128], BF16)
                gng = sb(ph, "gng", [128, D])
                gnb = sb(ph, "gnb", [128, D])
                Bdirc = S.buf("dirc", "own")
                W3 = [sb(ph, "rw%d" % i, [128, 8, D], BF16) for i in range(3)]
                BW3 = [S.buf("rw%d" % i, "own") for i in range(3)]
                wla = sb(ph, "wla", [128, 8, 64], BF16)
                ala = sb(ph, "ala", [128, 8, 64], BF16)
                wlb = sb(ph, "wlb", [64, D], BF16)
                alb = sb(ph, "alb", [64, D], BF16)
                Blo = S.buf("lora", "own")
                Tst = sb(ph, "rT", [64, 16, 64])
                BT = [S.buf("rT%d" % h) for h in range(16)]
                GT = S.group("GT")
                xx = sb(ph, "xx", [128, 8, 128])
                Bxx = S.buf("xx")
                xs_r = sb(ph, "xs_r", [128, 8, 128], BF16)
                xs_k = sb(ph, "xs_k", [128, 8, 128], BF16)
                xs_s = sb(ph, "xs_s", [128, 8, 128], BF16)
                xsn = [xs_r, xs_s, xs_k, xs_s, xs_s]
                Bxs_r, Bxs_k, Bxs_s = S.buf("xs_r"), S.buf("xs_k"), S.buf("xs_s")
                Bxs = [Bxs_r, Bxs_s, Bxs_k, Bxs_s, Bxs_s]
                vf = sb(ph, "vf", [128, D])
                vb = vf
                Bvf = S.buf("vf")
                Bvb = Bvf
                tw = sb(ph, "tw", [64, 128], BF16)
                ta = sb(ph, "ta", [64, 128], BF16)
                Btw = S.buf("tw")
                Bta = S.buf("ta")
                yblk = sb(ph, "yblk", [128, D])
                Byb = S.buf("yblk")
                bon = sb(ph, "bon", [128, 16])
                Bbon = S.buf("bon")
                Bg = {nm: S.buf("gn_" + nm) for nm in ("st",)}
                gst = sb(ph, "gn_st", [128, 48])
                ct = {}
                for nm in ("sg", "aa", "kk", "kk2", "t", "Ls", "Dsg", "emD"):
                    ct[nm] = sb(ph, "c_" + nm, [128, 128])
                ct["rs"] = ct["kk2"]
                ct["prod"] = ct["kk2"]
                ct["k2"] = ct["t"]
                ct["Ds2"] = ct["Ls"]
                ct["bb"] = ct["aa"]
                ct["emD2"] = ct["Ls"]
                ct["eD"] = ct["Dsg"]
                ct["wc"] = sb(ph, "c_wc", [128, 2])
                KB = sb(ph, "KB", [128, 2, 128])
                QR = sb(ph, "QR", [128, 2, 128])
                KoT = sb(ph, "KoTm", [128, 2, 128])
                BoTn = sb(ph, "BoTnm", [128, 2, 128])
                Bct = {nm: S.buf("c_" + nm) for nm in list(ct.keys()) + ["KB", "QR", "KoT", "BoTn"]}
                Bct["rs"] = Bct["kk2"]
                Bct["prod"] = Bct["kk2"]
                Bct["k2"] = Bct["t"]
                Bct["Ds2"] = Bct["Ls"]
                Bct["bb"] = Bct["aa"]
                Bct["emD2"] = Bct["Ls"]
                Bct["eD"] = Bct["Dsg"]
                HX = []
                for s_ in range(2):
                    hx = dict(A1=sb(ph, "A1", [128, 256]), A2=sb(ph, "A2", [128, 256]), Nm=sb(ph, "Nm", [128, 128]), Zm=sb(ph, "Zm", [128, 128]),
                              RH=sb(ph, "RH", [128, 128]), RpE=sb(ph, "RpE", [64, 2, 128]), P2T=sb(ph, "P2T", [64, 2, 64]), T0p=sb(ph, "T0p", [64, 2, 64]),
                              wch=sb(ph, "wch", [64, 2]))
                    hx["B"] = {nm: S.buf("h%d_%s" % (s_, nm)) for nm in ("A1", "A2", "Nm", "Zm", "RH", "RpE", "P2T", "T0p", "wch")}
                    HX.append(hx)
                for d in range(2):
                    S.dma("pool", mk1[:], cst["mk1"][d], writes=[Bdirc])
                    S.dma("pool", mk2[:], cst["mk2"][d], writes=[Bdirc])
                    S.dma("pool", mkn[:], cst["mkn"][d], writes=[Bdirc])
                    S.dma("sp", gng[:], r_gng[j, d].partition_broadcast(128), writes=[Bdirc])
                    S.dma("sp", gnb[:], r_gnb[j, d].partition_broadcast(128), writes=[Bdirc])
                    for n3 in range(3):
                        S.dma("pool", W3[n3][:], r_wrkv[j, n3].rearrange("(k p) n -> p k n", p=128)[:, :, d * D:(d + 1) * D], writes=[BW3[n3]])
                    S.dma("pool", wla[:], r_wla[j].rearrange("(k p) z r -> p k z r", p=128)[:, :, d, :], writes=[Blo])
                    S.dma("pool", ala[:], r_ala[j].rearrange("(k p) z r -> p k z r", p=128)[:, :, d, :], writes=[Blo])
                    S.dma("pool", wlb[:], r_wlb[j, d], writes=[Blo])
                    S.dma("pool", alb[:], r_alb[j, d], writes=[Blo])
                    S.dma("sp", Tst[:], st_r[j, d].rearrange("h k v -> k h v"), writes=BT, grp=GT)
                    vcol = lambda n: vec[:, 6 + n * 2 + d, :]
                    order = list(range(NT)) if d == 0 else list(range(NT - 1, -1, -1))
                    for bi, i in enumerate(order):
                        seg = i // 2
                        b_ = i % 2
                        first_of_seg = (b_ == 0) if d == 0 else (b_ == 1)
                        last_of_seg = not first_of_seg
                        corder = [0, 1] if d == 0 else [1, 0]
                        if first_of_seg and bi > 0:
                            S.op("dve", lambda e: e.tensor_scalar(out=Tst[:], in0=Tst[:], scalar1=flg[0:64, 0:1], scalar2=None, op0=ALU.mult),
                                 reads=BT + [Bc], writes=BT)
                        _rt[0] += 1
                        S.skip = (_LIM["rstep"] < 1) or (_rt[0] > _LIM["rtiles"])
                        hL = hT[:, :, seg, PADL - 1 + b_ * 128: PADL - 1 + b_ * 128 + 128]
                        hR = hT[:, :, seg, PADL + 1 + b_ * 128: PADL + 1 + b_ * 128 + 128]
                        hC = hT[:, :, seg, PADL + b_ * 128: PADL + b_ * 128 + 128]
                        S.op("pool", lambda e, hL=hL, hR=hR: e.tensor_tensor(out=xx[:], in0=hL, in1=hR, op=ALU.add), reads=[BhT[seg]], writes=[Bxx])
                        S.op("dve", lambda e, hC=hC: e.scalar_tensor_tensor(out=xx[:], in0=xx[:], scalar=0.5, in1=hC, op0=ALU.mult, op1=ALU.subtract),
                             reads=[Bxx, BhT[seg]], writes=[Bxx])
                        def mk_xs(n, eng):
                            S.op(eng, lambda e: e.tensor_tensor(out=xsn[n][:], in0=xx[:], in1=vec[:, n, :].unsqueeze(2).to_broadcast([128, 8, 128]), op=ALU.mult),
                                 reads=[Bxx, Bvec], writes=[Bxs[n]])
                            S.op(eng, lambda e: e.tensor_tensor(out=xsn[n][:], in0=xsn[n][:], in1=hC, op=ALU.add), reads=[Bxs[n], BhT[seg]], writes=[Bxs[n]])
                        mk_xs(3, "dve")
                        mk_xs(0, "pool")
                        for nh in range(2):
                            pb = 6 + nh
                            for k in range(8):
                                S.op("pe", lambda e, k=k, nh=nh, pb=pb: e.matmul(out=PS[pb][:, :], lhsT=xsn[3][:, k, :], rhs=W3[2][:, k, nh * 512:(nh + 1) * 512],
                                                                               start=(k == 0), stop=(k == 7)), reads=[Bxs[3], BW3[2]], writes=[BPS[pb]])
                            S.op("act", lambda e, nh=nh, pb=pb: e.activation(out=vf[:, nh * 512:(nh + 1) * 512], in_=PS[pb][:, :], func=AF.Copy),
                                 reads=[BPS[pb]], writes=[Bvf])
                        mk_xs(1, "dve")
                        mk_xs(2, "pool")
                        for k in range(8):
                            S.op("pe", lambda e, k=k: e.matmul(out=PS[6][0:64, 0:128], lhsT=wla[:, k, :], rhs=xsn[1][:, k, :], start=(k == 0), stop=(k == 7)),
                                 reads=[Blo, Bxs[1]], writes=[BPS[6]])
                        mk_xs(4, "dve")
                        for k in range(8):
                            S.op("pe", lambda e, k=k: e.matmul(out=PS[6][0:64, 128:256], lhsT=ala[:, k, :], rhs=xsn[4][:, k, :], start=(k == 0), stop=(k == 7)),
                                 reads=[Blo, Bxs[4]], writes=[BPS[6]])
                        S.op("act", lambda e: e.activation(out=tw[:], in_=PS[6][0:64, 0:128], func=AF.Tanh), reads=[BPS[6]], writes=[Btw])
                        S.op("act", lambda e: e.activation(out=ta[:], in_=PS[6][0:64, 128:256], func=AF.Copy), reads=[BPS[6]], writes=[Bta])
                        for c in range(8):
                            cs = slice(c * 128, (c + 1) * 128)
                            S.skip = (_LIM["rstep"] < 4) or (_rt[0] > _LIM["rtiles"])
                            pp = PS[0]
                            for k in range(8):
                                S.op("pe", lambda e, k=k, cs=cs: e.matmul(out=pp[:, 0:128], lhsT=W3[0][:, k, cs], rhs=xsn[0][:, k, :], start=(k == 0), stop=(k == 7)),
                                     reads=[BW3[0], Bxs[0]], writes=[BPS[0]])
                            for k in range(8):
                                S.op("pe", lambda e, k=k, cs=cs: e.matmul(out=pp[:, 128:256], lhsT=W3[1][:, k, cs], rhs=xsn[2][:, k, :], start=(k == 0), stop=(k == 7)),
                                     reads=[BW3[1], Bxs[2]], writes=[BPS[0]])
                            S.op("pe", lambda e, cs=cs: e.matmul(out=pp[:, 256:384], lhsT=wlb[:, cs], rhs=tw[:], start=True, stop=True), reads=[Blo, Btw], writes=[BPS[0]])
                            S.op("pe", lambda e, cs=cs: e.matmul(out=pp[:, 384:512], lhsT=alb[:, cs], rhs=ta[:], start=True, stop=True), reads=[Blo, Bta], writes=[BPS[0]])
                            pr, pk = pp[:, 0:128], pp[:, 128:256]
                            S.skip = (_LIM["rstep"] < 5) or (_rt[0] > _LIM["rtiles"])
                            S.op("act", lambda e, c=c: e.activation(out=ct["sg"][:], in_=pp[:, 256:384], func=AF.Sigmoid, bias=vcol(0)[:, c:c + 1], scale=1.0),
                                 reads=[BPS[0], Bvec], writes=[Bct["sg"]])
                            S.op("act", lambda e, c=c: e.activation(out=ct["aa"][:], in_=pp[:, 384:512], func=AF.Sigmoid, bias=vcol(1)[:, c:c + 1], scale=1.0),
                                 reads=[BPS[0], Bvec], writes=[Bct["aa"]])
                            S.op("dve", lambda e, c=c: e.tensor_scalar(out=ct["kk"][:], in0=pk, scalar1=vcol(2)[:, c:c + 1], scalar2=None, op0=ALU.mult),
                                 reads=[BPS[0], Bvec], writes=[Bct["kk"]])
                            S.op("pool", lambda e: e.tensor_tensor(out=ct["kk2"][:], in0=ct["kk"][:], in1=ct["kk"][:], op=ALU.mult), reads=[Bct["kk"]], writes=[Bct["kk2"]])
                            S.op("pe", lambda e: e.matmul(out=PS[1][:, 0:128], lhsT=cF["blockones"][:], rhs=ct["kk2"][:], start=True, stop=True),
                                 reads=[Bk, Bct["kk2"]], writes=[BPS[1]])
                            S.op("dve", lambda e: e.tensor_scalar_max(out=ct["rs"][:], in0=PS[1][:, 0:128], scalar1=1e-24), reads=[BPS[1]], writes=[Bct["rs"]])
                            S.op("act", lambda e: e.activation(out=ct["rs"][:], in_=ct["rs"][:], func=AF.Sqrt), reads=[Bct["rs"]], writes=[Bct["rs"]])
                            S.op("dve", lambda e: e.reciprocal(out=ct["rs"][:], in_=ct["rs"][:]), reads=[Bct["rs"]], writes=[Bct["rs"]])
                            S.op("dve", lambda e: e.tensor_tensor(out=ct["kk"][:], in0=ct["kk"][:], in1=ct["rs"][:], op=ALU.mult), reads=[Bct["kk"], Bct["rs"]], writes=[Bct["kk"]])
                            S.skip = (_LIM["rstep"] < 6) or (_rt[0] > _LIM["rtiles"])
                            S.op("dve", lambda e, c=c: e.tensor_scalar(out=ct["t"][:], in0=ct["aa"][:], scalar1=1.0, scalar2=vcol(3)[:, c:c + 1], op0=ALU.subtract, op1=ALU.mult),
                                 reads=[Bct["aa"], Bvec], writes=[Bct["t"]])
                            S.op("dve", lambda e: e.scalar_tensor_tensor(out=ct["k2"][:], in0=ct["t"][:], scalar=1.0, in1=pk, op0=ALU.add, op1=ALU.mult),
                                 reads=[Bct["t"], BPS[0]], writes=[Bct["k2"]])
                            S.op("pool", lambda e: e.tensor_tensor(out=ct["bb"][:], in0=ct["kk"][:], in1=ct["aa"][:], op=ALU.mult), reads=[Bct["kk"], Bct["aa"]], writes=[Bct["bb"]])
                            S.op("dve", lambda e, c=c: e.scalar_tensor_tensor(out=ct["prod"][:], in0=pr, scalar=vcol(4)[:, c:c + 1], in1=ct["k2"][:], op0=ALU.mult, op1=ALU.mult),
                                 reads=[BPS[0], Bvec, Bct["k2"]], writes=[Bct["prod"]])
                            S.op("pe", lambda e, c=c: e.matmul(out=PS[7][:, 2 * c:2 * c + 2], lhsT=ct["prod"][:], rhs=cF["headind"][:], start=True, stop=True),
                                 reads=[Bct["prod"], Bk], writes=[BPS[7]])
                            S.skip = (_LIM["rstep"] < 7) or (_rt[0] > _LIM["rtiles"])
                            S.op("dve", lambda e: e.tensor_tensor_scan(out=ct["Ls"][:], data0=cF["rmask64"][:], data1=ct["sg"][:], initial=0.0, op0=ALU.mult, op1=ALU.add),
                                 reads=[Bct["sg"], Bk], writes=[Bct["Ls"]])
                            Lv = ct["Ls"][:].rearrange("p (c t) -> p c t", t=64)
                            Dv = ct["Dsg"][:].rearrange("p (c t) -> p c t", t=64)
                            if d == 0:
                                S.op("dve", lambda e, Lv=Lv, Dv=Dv: e.tensor_tensor(out=Dv, in0=Lv[:, :, 63:64].to_broadcast([128, 2, 64]), in1=Lv, op=ALU.subtract),
                                     reads=[Bct["Ls"]], writes=[Bct["Dsg"]])
                            else:
                                S.op("dve", lambda e: e.tensor_tensor(out=ct["Dsg"][:], in0=ct["Ls"][:], in1=ct["sg"][:], op=ALU.subtract),
                                     reads=[Bct["Ls"], Bct["sg"]], writes=[Bct["Dsg"]])
                            S.op("act", lambda e, Lv=Lv: e.activation(out=ct["wc"][:].unsqueeze(2), in_=Lv[:, :, 63:64], func=AF.Exp, scale=-C0), reads=[Bct["Ls"]], writes=[Bct["wc"]])
                            S.op("pool", lambda e: e.tensor_tensor(out=ct["Ds2"][:], in0=ct["Dsg"][:], in1=ct["sg"][:], op=ALU.add), reads=[Bct["Dsg"], Bct["sg"]], writes=[Bct["Ds2"]])
                            S.op("act", lambda e: e.activation(out=ct["emD"][:], in_=ct["Dsg"][:], func=AF.Exp, scale=C0), reads=[Bct["Dsg"]], writes=[Bct["emD"]])
                            S.op("act", lambda e: e.activation(out=ct["eD"][:], in_=ct["Dsg"][:], func=AF.Exp, scale=-C0), reads=[Bct["Dsg"]], writes=[Bct["eD"]])
                            S.op("act", lambda e: e.activation(out=ct["emD2"][:], in_=ct["Ds2"][:], func=AF.Exp, scale=C0), reads=[Bct["Ds2"]], writes=[Bct["emD2"]])
                            S.skip = (_LIM["rstep"] < 8) or (_rt[0] > _LIM["rtiles"])
                            S.op("pool", lambda e: e.tensor_tensor(out=KB[:, 0, :], in0=ct["k2"][:], in1=ct["eD"][:], op=ALU.mult), reads=[Bct["k2"], Bct["eD"]], writes=[Bct["KB"]])
                            S.op("pool", lambda e: e.tensor_tensor(out=KB[:, 1, :], in0=ct["bb"][:], in1=ct["eD"][:], op=ALU.mult), reads=[Bct["bb"], Bct["eD"]], writes=[Bct["KB"]])
                            S.op("pool", lambda e: e.tensor_tensor(out=QR[:, 0, :], in0=ct["kk"][:], in1=ct["emD2"][:], op=ALU.mult), reads=[Bct["kk"], Bct["emD2"]], writes=[Bct["QR"]])
                            S.op("dve", lambda e: e.tensor_tensor(out=QR[:, 1, :], in0=pr, in1=ct["emD"][:], op=ALU.mult), reads=[BPS[0], Bct["emD"]], writes=[Bct["QR"]])
                            S.skip = (_LIM["rstep"] < 9) or (_rt[0] > _LIM["rtiles"])
                            ptb = PS[1]
                            S.op("pe", lambda e, ptb=ptb: e.transpose(out=ptb[:, 128:256], in_=KB[:, 0, :], identity=ident[:]), reads=[Bct["KB"], Bc], writes=[BPS[1]])
                            S.op("pe", lambda e, ptb=ptb: e.transpose(out=ptb[:, 256:384], in_=KB[:, 1, :], identity=ident[:]), reads=[Bct["KB"], Bc], writes=[BPS[1]])
                            S.op("pe", lambda e, ptb=ptb: e.transpose(out=ptb[:, 384:512], in_=QR[:, 0, :], identity=ident[:]), reads=[Bct["QR"], Bc], writes=[BPS[1]])
                            hib = cF["headind"][:].unsqueeze(2).to_broadcast([128, 2, 128])
                            S.op("dve", lambda e, ptb=ptb, hib=hib: e.tensor_tensor(out=KoT[:], in0=ptb[:, 128:256].unsqueeze(1).to_broadcast([128, 2, 128]), in1=hib, op=ALU.mult),
                                 reads=[BPS[1], Bk], writes=[Bct["KoT"]])
                            S.op("dve", lambda e, ptb=ptb, hib=hib: e.scalar_tensor_tensor(out=BoTn[:], in0=ptb[:, 256:384].unsqueeze(1).to_broadcast([128, 2, 128]), scalar=-1.0, in1=hib,
                                                                                          op0=ALU.mult, op1=ALU.mult), reads=[BPS[1], Bk], writes=[Bct["BoTn"]])
                            def head_gen(c=c, hh=None, ptb=ptb):
                                head = 2 * c + hh
                                prs = slice(64 * hh, 64 * hh + 64)
                                hc = slice(head * 64, head * 64 + 64)
                                hcl = slice(hh * 64, hh * 64 + 64)
                                hx = HX[hh]
                                A1, A2, Nmt, Zmt, RH, RpE, P2T, T0p, wch = hx["A1"], hx["A2"], hx["Nm"], hx["Zm"], hx["RH"], hx["RpE"], hx["P2T"], hx["T0p"], hx["wch"]
                                Bh = hx["B"]
                                PA, PB, BA, BB = PS[2 + 2 * hh], PS[3 + 2 * hh], BPS[2 + 2 * hh], BPS[3 + 2 * hh]
                                qr2 = QR[prs, :, :].rearrange("p a t -> p (a t)")
                                S.op("pe", lambda e: e.matmul(out=PA[:, 0:256], lhsT=KB[prs, 0, :], rhs=qr2, start=True, stop=True), reads=[Bct["KB"], Bct["QR"]], writes=[BA])
                                S.op("pe", lambda e: e.matmul(out=PA[:, 256:512], lhsT=KB[prs, 1, :], rhs=qr2, start=True, stop=True), reads=[Bct["KB"], Bct["QR"]], writes=[BA])
                                S.op("pe", lambda e: e.matmul(out=PB[:, 0:128], lhsT=QR[prs, 0, :], rhs=KB[prs, 1, :], start=True, stop=True), reads=[Bct["KB"], Bct["QR"]], writes=[BB])
                                yield
                                S.op("dve", lambda e: e.tensor_tensor(out=A1[:], in0=PA[:, 0:256], in1=mk1[:], op=ALU.mult), reads=[BA, Bdirc], writes=[Bh["A1"]])
                                S.op("dve", lambda e: e.tensor_tensor(out=A2[:], in0=PA[:, 256:512], in1=mk2[:], op=ALU.mult), reads=[BA, Bdirc], writes=[Bh["A2"]])
                                S.op("dve", lambda e: e.tensor_tensor(out=Nmt[:], in0=PB[:, 0:128], in1=mkn[:], op=ALU.mult), reads=[BB, Bdirc], writes=[Bh["Nm"]])
                                yield
                                S.op("pe", lambda e: e.matmul(out=PB[:, 128:192], lhsT=A1[:, 0:128], rhs=vb[:, hc], start=True, stop=True), reads=[Bh["A1"], Bvb], writes=[BB])
                                S.op("dve", lambda e: e.tensor_copy(out=RH[:, 0:64], in_=ptb[:, 384 + 64 * hh:448 + 64 * hh]), reads=[BPS[1]], writes=[Bh["RH"]])
                                yield
                                S.op("act", lambda e: e.activation(out=RH[:, 64:128], in_=PB[:, 128:192], func=AF.Copy), reads=[BB], writes=[Bh["RH"]])
                                yield
                                for lvl in range(6):
                                    if lvl == 0:
                                        Zc, BZc = A2[:, 0:128], Bh["A2"]
                                    else:
                                        Zc, BZc = Zmt[:], Bh["Zm"]
                                    Nc, BNc = Nmt[:], Bh["Nm"]
                                    S.op("pe", lambda e, Zc=Zc: e.matmul(out=PB[:, 256:384], lhsT=Zc, rhs=RH[:], start=True, stop=True), reads=[BZc, Bh["RH"]], writes=[BB])
                                    if lvl < 5:
                                        S.op("pe", lambda e, Zc=Zc, Nc=Nc: e.matmul(out=PA[:, 0:128], lhsT=Nc, rhs=Zc, start=True, stop=True), reads=[BZc, BNc], writes=[BA])
                                        if lvl < 4:
                                            S.op("pe", lambda e, Zc=Zc, Nc=Nc: e.matmul(out=PA[:, 128:256], lhsT=Zc, rhs=Nc, start=True, stop=True), reads=[BZc, BNc], writes=[BA])
                                    yield
                                    S.op("dve", lambda e, lvl=lvl: e.tensor_tensor(out=RH[:], in0=RH[:], in1=PB[:, 256:384], op=(ALU.subtract if lvl == 0 else ALU.add)),
                                         reads=[Bh["RH"], BB], writes=[Bh["RH"]])
                                    if lvl < 5:
                                        S.op("act", lambda e: e.activation(out=Zmt[:], in_=PA[:, 0:128], func=AF.Copy), reads=[BA], writes=[Bh["Zm"]])
                                        if lvl < 4:
                                            S.op("act", lambda e: e.activation(out=Nmt[:], in_=PA[:, 128:256], func=AF.Copy), reads=[BA], writes=[Bh["Nm"]])
                                    yield
                                S.op("pe", lambda e: e.matmul(out=PA[0:64, 256:384], lhsT=cF["sel64"][:, hh, :], rhs=QR[:, 1, :], start=True, stop=False), reads=[Bk, Bct["QR"]], writes=[BA])
                                S.op("pe", lambda e: e.matmul(out=PA[0:64, 256:384], lhsT=RH[:, 0:64], rhs=A2[:, 128:256], start=False, stop=True), reads=[Bh["RH"], Bh["A2"]], writes=[BA])
                                for cc in range(2):
                                    S.op("pe", lambda e, cc=cc: e.matmul(out=PA[0:64, 384 + 64 * cc:448 + 64 * cc], lhsT=RH[:, 0:64], rhs=BoTn[:, cc, hcl], start=True, stop=True),
                                         reads=[Bh["RH"], Bct["BoTn"]], writes=[BA])
                                S.op("pe", lambda e: e.matmul(out=PB[0:64, 192:194], lhsT=cF["sel64"][:, hh, :], rhs=ct["wc"][:], start=True, stop=True), reads=[Bk, Bct["wc"]], writes=[BB])
                                yield
                                S.op("dve", lambda e: e.tensor_tensor(out=RpE[:], in0=PA[0:64, 256:384].unsqueeze(1).to_broadcast([64, 2, 128]), in1=cF["colmask2"][0:64], op=ALU.mult),
                                     reads=[BA, Bk], writes=[Bh["RpE"]])
                                S.op("dve", lambda e: e.tensor_tensor(out=P2T[:], in0=PA[0:64, 384:512].rearrange("p (c k) -> p c k", c=2),
                                                                      in1=ident[0:64, 0:64].unsqueeze(1).to_broadcast([64, 2, 64]), op=ALU.add), reads=[BA, Bc], writes=[Bh["P2T"]])
                                S.op("act", lambda e: e.activation(out=wch[:], in_=PB[0:64, 192:194], func=AF.Copy), reads=[BB], writes=[Bh["wch"]])
                                yield
                                for cc in corder:
                                    S.op("dve", lambda e, cc=cc: e.tensor_scalar(out=T0p[:, cc, :], in0=Tst[:, head, :], scalar1=wch[:, cc:cc + 1], scalar2=None, op0=ALU.mult),
                                         reads=[BT[head], Bh["wch"]], writes=[Bh["T0p"]])
                                    S.op("pe", lambda e, cc=cc: e.matmul(out=PB[0:64, 384:448], lhsT=KoT[:, cc, hcl], rhs=vb[:, hc], start=True, stop=False), reads=[Bct["KoT"], Bvb], writes=[BB])
                                    S.op("pe", lambda e, cc=cc: e.matmul(out=PB[0:64, 384:448], lhsT=BoTn[:, cc, hcl], rhs=RH[:, 64:128], start=False, stop=False), reads=[Bct["BoTn"], Bh["RH"]], writes=[BB])
                                    S.op("pe", lambda e, cc=cc: e.matmul(out=PB[0:64, 384:448], lhsT=P2T[:, cc, :], rhs=T0p[:, cc, :], start=False, stop=True), reads=[Bh["P2T"], Bh["T0p"]], writes=[BB])
                                    yield
                                    S.op("dve", lambda e: e.tensor_copy(out=Tst[:, head, :], in_=PB[0:64, 384:448]), reads=[BB], writes=[BT[head]])
                                    yield
                                S.op("pe", lambda e: e.matmul(out=PB[:, 448:512], lhsT=A1[:, 128:256], rhs=vb[:, hc], start=True, stop=False), reads=[Bh["A1"], Bvb], writes=[BB])
                                S.op("pe", lambda e: e.matmul(out=PB[:, 448:512], lhsT=A2[:, 128:256], rhs=RH[:, 64:128], start=False, stop=False), reads=[Bh["A2"], Bh["RH"]], writes=[BB])
                                for cc in range(2):
                                    S.op("pe", lambda e, cc=cc: e.matmul(out=PB[:, 448:512], lhsT=RpE[:, cc, :], rhs=T0p[:, cc, :], start=False, stop=(cc == 1)), reads=[Bh["RpE"], Bh["T0p"]], writes=[BB])
                                yield
                                S.op("act", lambda e: e.activation(out=yblk[:, hc], in_=PB[:, 448:512], func=AF.Copy), reads=[BB], writes=[Byb])
                            run_interleaved([head_gen(hh=0), head_gen(hh=1)], 2)
                        S.skip = (_LIM["rstep"] < 17) or (_rt[0] > _LIM["rtiles"])
                        yv = yblk[:].rearrange("p (h n) -> p h n", n=64)
                        sqt = xx[:].rearrange("p k t -> p (k t)")
                        sv = sqt.rearrange("p (h n) -> p h n", n=64)
                        S.op("act", lambda e: e.activation(out=bon[:], in_=PS[7][:, 0:16], func=AF.Copy), reads=[BPS[7]], writes=[Bbon])
                        S.op("dve", lambda e: e.tensor_reduce(out=gst[:, 0:16], in_=yv, axis=AX.X, op=ALU.add), reads=[Byb], writes=[Bg["st"]])
                        S.op("dve", lambda e: e.tensor_scalar(out=gst[:, 0:16], in0=gst[:, 0:16], scalar1=1.0 / 64.0, scalar2=None, op0=ALU.mult), reads=[Bg["st"]], writes=[Bg["st"]])
                        S.op("dve", lambda e: e.tensor_tensor(out=yv, in0=yv, in1=gst[:, 0:16].unsqueeze(2).to_broadcast([128, 16, 64]), op=ALU.subtract),
                             reads=[Byb, Bg["st"]], writes=[Byb])
                        S.op("act", lambda e: e.activation(out=sqt, in_=yblk[:], func=AF.Square), reads=[Byb], writes=[Bxx])
                        S.op("dve", lambda e: e.tensor_reduce(out=gst[:, 16:32], in_=sv, axis=AX.X, op=ALU.add), reads=[Bxx], writes=[Bg["st"]])
                        S.op("dve", lambda e: e.tensor_scalar(out=gst[:, 16:32], in0=gst[:, 16:32], scalar1=1.0 / 64.0, scalar2=GN_EPS, op0=ALU.mult, op1=ALU.add),
                             reads=[Bg["st"]], writes=[Bg["st"]])
                        S.op("act", lambda e: e.activation(out=gst[:, 16:32], in_=gst[:, 16:32], func=AF.Sqrt), reads=[Bg["st"]], writes=[Bg["st"]])
                        S.op("dve", lambda e: e.reciprocal(out=gst[:, 16:32], in_=gst[:, 16:32]), reads=[Bg["st"]], writes=[Bg["st"]])
                        S.op("dve", lambda e: e.tensor_tensor(out=yv, in0=yv, in1=gst[:, 16:32].unsqueeze(2).to_broadcast([128, 16, 64]), op=ALU.mult),
                             reads=[Byb, Bg["st"]], writes=[Byb])
                        S.op("pool", lambda e: e.tensor_tensor(out=yblk[:], in0=yblk[:], in1=gng[:], op=ALU.mult), reads=[Byb, Bdirc], writes=[Byb])
                        S.op("pool", lambda e: e.tensor_tensor(out=yblk[:], in0=yblk[:], in1=gnb[:], op=ALU.add), reads=[Byb, Bdirc], writes=[Byb])
                        vfv = vf[:].rearrange("p (h n) -> p h n", n=64)
                        S.op("dve", lambda e: e.tensor_tensor(out=vfv, in0=vfv, in1=bon[:].unsqueeze(2).to_broadcast([128, 16, 64]), op=ALU.mult),
                             reads=[Bvf, Bbon], writes=[Bvf])
                        S.op("pool", lambda e: e.tensor_tensor(out=yblk[:], in0=yblk[:], in1=vf[:], op=ALU.add), reads=[Byb, Bvf], writes=[Byb])
                        S.skip = False
                        S.dma("sp", oscr[d, i * 128:(i + 1) * 128, :], yblk[:], reads=[Byb], writes=[Bscr[d]], grp=S.group("st_yb"))
                        if last_of_seg:
                            S.dma("act", ns_r[j, seg, d].rearrange("h k v -> k h v"), Tst[:], reads=BT, writes=[By], grp=S.group("st_rT"))
                S.barrier()

        for l in range(n_layers):
            j = l // 2
            S.phase = "L%d.ada" % l
            adaln(l)
            if mix:
                with contextlib.ExitStack() as lph:
                    hT = sb(lph, "hT", [128, 8, NSEG, HTW], BF16)
                    BhT = [S.buf("hT%d" % s_) for s_ in range(NSEG)]
                    S.op("pool", lambda e: e.memset(hT[:, :, :, PADL - 1:PADL], 0.0), writes=BhT)
                    S.op("pool", lambda e: e.memset(hT[:, :, :, PADL + 256:PADL + 257], 0.0), writes=BhT)
                    S.phase = "L%d.hT" % l
                    build_hT(lambda i, k: hT[:, k, i // 2, PADL + (i % 2) * 128: PADL + (i % 2) * 128 + 128], lambda i: BhT[i // 2], list(range(NT)), sc1p, 0)
                    if l % 2 == 1:
                        S.op("dve", lambda e: e.tensor_scalar(out=hT[:, :, 1:NSEG, PADL - 1], in0=hT[:, :, 0:NSEG - 1, PADL + 255], scalar1=flg[:, 0:1], scalar2=None, op0=ALU.mult),
                             reads=BhT + [Bc], writes=BhT)
                        S.op("dve", lambda e: e.tensor_scalar(out=hT[:, :, 0:NSEG - 1, PADL + 256], in0=hT[:, :, 1:NSEG, PADL], scalar1=flg[:, 0:1], scalar2=None, op0=ALU.mult),
                             reads=BhT + [Bc], writes=BhT)
                    S.barrier()
                    stopped = False
                    if stop == "hT" and l == n_layers - 1:
                        stopped = True
                    elif l % 2 == 0:
                        S.phase = "L%d.mix" % l
                        hgrn_layer(l, j, hT, BhT)
                        if stop == "mix" and l == n_layers - 1:
                            stopped = True
                        else:
                            S.phase = "L%d.post" % l
                            post_mixer(l, j, "h", hT, BhT)
                    else:
                        S.phase = "L%d.mix" % l
                        rwkv_layer(l, j, hT, BhT)
                        if stop == "mix" and l == n_layers - 1:
                            stopped = True
                        else:
                            S.phase = "L%d.post" % l
                            post_mixer(l, j, "r", hT, BhT)
                if stopped or (stop == "post" and l == n_layers - 1):
                    break
            S.phase = "L%d.ffn" % l
            ffn(l)

        for i in range(NT):
            S.dma(hwq(), y_out[i * 128:(i + 1) * 128, :], X[:, i, :], reads=[BX[i]], writes=[By])
        S.barrier()
        S.emit(block)
        globals()["_LAST_SCHED"] = S
    return nc


_PROMPT_SLOTS = [[(p, p // 6) for p in range(32) if p % 6 == cix] for cix in range(6)]


def _fm(v):
    return np.ascontiguousarray(np.asarray(v, np.float32).reshape(8, 128).T)


def make_in_maps(inp, n_layers=4):
    NLW = n_layers
    NH = max(1, (n_layers + 1) // 2)
    NR = max(1, n_layers // 2)
    f = lambda a: np.ascontiguousarray(np.asarray(a, dtype=np.float32))
    consts = _consts()
    pos = _pos_table()
    shared = {
        "pos": pos,
        "ada_w": f(inp["ada_w"]),
        "ada_bT": np.ascontiguousarray(f(inp["ada_b"]).reshape(4, 48, 128).transpose(0, 2, 1)),
        "ln_g": f(inp["ln_g"]), "ln_b": f(inp["ln_b"]),
        "ffn_w_up": f(inp["ffn_w_up"]), "ffn_w_down": f(inp["ffn_w_down"]),
        "hgrn_w_in": f(inp["hgrn_w_in"]),
        "hgrn_lbT": np.ascontiguousarray(f(inp["hgrn_lb"]).reshape(2, 16, 128).transpose(0, 2, 1)),
        "hgrn_norm_g": f(inp["hgrn_norm_g"]), "hgrn_w_o": f(inp["hgrn_w_o"]),
        "rwkv_w_rkv": f(inp["rwkv_w_rkv"]), "rwkv_w_la": f(inp["rwkv_w_la"]), "rwkv_w_lb": f(inp["rwkv_w_lb"]),
        "rwkv_a_la": f(inp["rwkv_a_la"]), "rwkv_a_lb": f(inp["rwkv_a_lb"]),
        "rwkv_g_la": f(inp["rwkv_g_la"]), "rwkv_g_lb": f(inp["rwkv_g_lb"]),
        "rwkv_gn_g": f(inp["rwkv_gn_g"]), "rwkv_gn_b": f(inp["rwkv_gn_b"]), "rwkv_w_o": f(inp["rwkv_w_o"]),
    }
    vec = np.zeros((2, 16, 128, 8), np.float32)
    for j in range(2):
        for n in range(6):
            vec[j, n] = _fm(inp["rwkv_mu"][j, n])
        for n, nm in enumerate(("rwkv_w0", "rwkv_a0", "rwkv_k_k", "rwkv_k_a", "rwkv_r_k")):
            for d in range(2):
                vec[j, 6 + n * 2 + d] = _fm(inp[nm][j, d])
    shared["rwkv_vecT"] = vec
    for k, v in consts.items():
        shared["c_" + k] = v
    xp = f(inp["x_prompt"])
    xs = f(inp["x_sample"])
    sth = f(inp["state_hgrn"])
    strw = f(inp["state_rwkv"])
    maps = []
    for core in range(8):
        m = dict(shared)
        if core < 2:
            m["x_in"] = np.ascontiguousarray(xs[core])
            m["condT"] = _fm(inp["c"][core])
            fl = np.ones((128, 2), np.float32)
            m["st_h"] = np.ascontiguousarray(sth[core])
            m["st_r"] = np.ascontiguousarray(strw[core].transpose(0, 1, 2, 4, 3))
        else:
            slots = _PROMPT_SLOTS[core - 2]
            xin = np.zeros((NSEG, 256, D), np.float32)
            for s_ in range(NSEG):
                xin[s_] = xp[slots[s_][0]] if s_ < len(slots) else xp[slots[0][0]]
            m["x_in"] = xin.reshape(2048, D)
            m["condT"] = _fm(inp["c_ctx"])
            fl = np.zeros((128, 2), np.float32)
            m["st_h"] = np.zeros((2, 2, 8, 128, 128), np.float32)
            m["st_r"] = np.zeros((2, 2, 16, 64, 64), np.float32)
        m["flags"] = fl
        maps.append(m)
    cut = {"ada_w": NLW, "ada_bT": NLW, "ln_g": NLW, "ln_b": NLW, "ffn_w_up": NLW, "ffn_w_down": NLW,
           "hgrn_w_in": NH, "hgrn_norm_g": NH, "hgrn_w_o": NH, "rwkv_w_rkv": NR, "rwkv_w_la": NR, "rwkv_w_lb": NR,
           "rwkv_a_la": NR, "rwkv_a_lb": NR, "rwkv_g_la": NR, "rwkv_g_lb": NR, "rwkv_gn_g": NR, "rwkv_gn_b": NR, "rwkv_w_o": NR}
    if n_layers < 4:
        for k, n in cut.items():
            sl = np.ascontiguousarray(shared[k][:n])
            for m in maps:
                m[k] = sl
    return maps


def assemble(results):
    y_prompt = np.zeros((32, 256, D), np.float32)
    y_sample = np.zeros((2, 2048, D), np.float32)
    nsh = np.zeros((32, 2, 2, 8, 128, 128), np.float32)
    nsr = np.zeros((32, 2, 2, 16, 64, 64), np.float32)
    for core in range(8):
        r = results[core]
        if core < 2:
            y_sample[core] = r["y_out"]
        else:
            slots = _PROMPT_SLOTS[core - 2]
            yo = r["y_out"].reshape(NSEG, 256, D)
            for s_, (p, _) in enumerate(slots):
                y_prompt[p] = yo[s_]
                nsh[p] = r["ns_h"][:, s_]
                nsr[p] = r["ns_r"][:, s_].transpose(0, 1, 2, 4, 3)
    return y_prompt, y_sample, nsh, nsr


_NC_CACHE = {}
_LIM = {"heads": 10 ** 9, "step": 99, "rstep": 99, "rtiles": 10 ** 9}
_rt = [0]


def kernel(**inputs):
    if "nc" not in _NC_CACHE:
        _NC_CACHE["nc"] = build_program()
    nc = _NC_CACHE["nc"]
    maps = make_in_maps(inputs)
    res = run_bass_kernel_spmd(nc, maps, core_ids=list(range(8)))
    return assemble(res.results)
```

```python
import contextlib
import numpy as np
import concourse.bass as bass
import concourse.mybir as mybir
from concourse.bass_utils import run_bass_kernel_spmd

F32 = mybir.dt.float32
BF16 = mybir.dt.bfloat16
ALU = mybir.AluOpType
AF = mybir.ActivationFunctionType
AX = mybir.AxisListType

D = 1024
NT = 16
NSEG = 8
DN_ALPHA = 8 ** 0.25
LN_EPS = 1e-5
RMS_EPS = 1e-6
GN_EPS = 64e-5
C0 = 0.606531
PADL = 2
HTW = 260


class SemGroup:
    def __init__(self, sem):
        self.sem = sem
        self.count = 0
        self.sw = None


class Buf:
    __slots__ = ("name", "w", "r", "grp", "excl")

    def __init__(self, name, grp=None):
        self.name = name
        self.w = []
        self.r = {}
        self.grp = grp
        self.excl = False


class _Rec:
    def __init__(self):
        self.call = None

    def __getattr__(self, name):
        def f(*a, **k):
            self.call = (name, a, k)
            return self
        return f


class Sched:
    ENG = ("pe", "act", "dve", "pool", "sp")

    def __init__(self, nc, stack):
        self.nc = nc
        self.stack = stack
        self.ops = {e: [] for e in self.ENG}
        self.esem = {e: stack.enter_context(nc.semaphore("es_" + e)) for e in self.ENG}
        self.targets = {e: set() for e in self.ENG}
        self.groups = []
        self.gcache = {}
        self.lastreal = {e: 0 for e in self.ENG}

    def group(self, key=None):
        if key is not None and key in self.gcache:
            return self.gcache[key]
        g = SemGroup(self.stack.enter_context(self.nc.semaphore("dg%d" % len(self.groups))))
        self.groups.append(g)
        if key is not None:
            self.gcache[key] = g
        return g

    def buf(self, name, grp=None):
        if grp == "own":
            grp = self.group(name)
        return Buf(name, grp)

    def _deps(self, eng, reads, writes):
        deps = []
        for b in reads:
            deps.extend(b.w)
        for b in writes:
            deps.extend(b.w)
            for k, v in b.r.items():
                if isinstance(k, str):
                    deps.append(("e", k, v))
                else:
                    deps.append(("d", k[1], v))
        waits = {}
        for d in deps:
            if d[0] == "e" and d[1] == eng and eng in ("pe", "sp"):
                continue
            key = (d[0], d[1])
            val = d[1].count if d[0] == "d" else d[2]
            waits[key] = max(waits.get(key, 0), val)
        for k, v in waits.items():
            if k[0] == "e":
                self.targets[k[1]].add(v)
        return waits

    skip = False
    phase = ""

    def op(self, eng, fn, reads=(), writes=()):
        if self.skip:
            return
        ex = [b for b in reads if b.excl]
        if ex:
            writes = list(writes) + ex
        waits = self._deps(eng, reads, writes)
        rec = _Rec()
        fn(rec)
        assert rec.call is not None
        self.ops[eng].append([rec.call, waits, "c", None, self.phase])
        idx = len(self.ops[eng])
        self.lastreal[eng] = idx
        for b in reads:
            b.r[eng] = idx
        for b in writes:
            b.w = [("e", eng, idx)]
            b.r = {}
        return idx

    def dma(self, eng, out, in_, reads=(), writes=(), grp=None, **kw):
        if self.skip:
            return
        waits = self._deps(eng, reads, writes)
        g = grp
        if g is None:
            for b in writes:
                if b.grp is not None:
                    g = b.grp
        assert g is not None
        if eng == "pool":
            if g.sw is None:
                g.sw = self.group()
            g = g.sw
        g.count += 1
        cnt = g.count

        kw2 = dict(kw)
        kw2["out"] = out
        kw2["in_"] = in_
        self.ops[eng].append([("dma_start", (), kw2), waits, "d", g])
        for b in reads:
            b.r[("d", g)] = cnt
        for b in writes:
            if b.w and all(x[0] == "d" for x in b.w):
                b.w = [x for x in b.w if x[1] is not g] + [("d", g, cnt)]
            else:
                b.w = [("d", g, cnt)]
            b.r = {}

    def barrier(self):
        last = dict(self.lastreal)
        for e in self.ENG:
            waits = {}
            for o in self.ENG:
                if o != e and last[o] > 0:
                    waits[("e", o)] = last[o]
                    self.targets[o].add(last[o])
            for g in self.groups:
                if g.count > 0:
                    waits[("d", g)] = g.count
            self.ops[e].append([None, waits, "w", None])

    def emit(self, block):
        tval = {}
        for e in self.ENG:
            c = 0
            m = {}
            for i in range(1, len(self.ops[e]) + 1):
                if i in self.targets[e]:
                    c += 1
                    m[i] = c
            tval[e] = m
        sched = self

        def run(e, engobj):
            seen = {}
            for i, rec in enumerate(sched.ops[e], start=1):
                fn, waits = rec[0], rec[1]
                for k, v in waits.items():
                    if k[0] == "e":
                        val = tval[k[1]][v]
                        sem = sched.esem[k[1]]
                    else:
                        val = 16 * v
                        sem = k[1].sem
                    if seen.get(id(sem), 0) >= val:
                        continue
                    seen[id(sem)] = val
                    engobj.wait_ge(sem, val)
                if fn is None:
                    assert i not in tval[e]
                    continue
                ins = getattr(engobj, fn[0])(*fn[1], **fn[2])
                if rec[2] == "d":
                    ins.then_inc(rec[3].sem, 16)
                elif i in tval[e]:
                    ins.then_inc(sched.esem[e], 1)

        @block.tensor
        def _(pe):
            run("pe", pe)

        @block.scalar
        def _(act):
            run("act", act)

        @block.vector
        def _(dve):
            run("dve", dve)

        @block.gpsimd
        def _(pool):
            run("pool", pool)

        @block.sync
        def _(sp):
            run("sp", sp)


def _consts():
    c = {}
    idx = np.arange(128)
    c["ident"] = np.eye(128, dtype=np.float32)
    same32 = (idx[:, None] // 32) == (idx[None, :] // 32)
    mh = np.zeros((2, 128, 128), np.float32)
    mh[0] = same32 & (idx[:, None] <= idx[None, :])
    mh[1] = same32 & (idx[:, None] >= idx[None, :])
    c["maskH"] = mh
    cm4 = np.zeros((128, 4, 128), np.float32)
    for cc in range(4):
        cm4[:, cc, cc * 32:(cc + 1) * 32] = 1.0
    c["colmask4"] = cm4
    rm4 = np.zeros((128, 4), np.float32)
    rm4[idx, idx // 32] = 1.0
    c["rowmask4"] = rm4
    r32 = np.ones((128, 128), np.float32)
    r32[:, ::32] = 0.0
    c["rmask32"] = r32
    r64 = np.ones((128, 128), np.float32)
    r64[:, ::64] = 0.0
    c["rmask64"] = r64
    same64 = (idx[:, None] // 64) == (idx[None, :] // 64)
    st = [same64 & (idx[:, None] < idx[None, :]), same64 & (idx[:, None] > idx[None, :])]
    inc = [same64 & (idx[:, None] <= idx[None, :]), same64 & (idx[:, None] >= idx[None, :])]
    mk1 = np.zeros((2, 128, 256), np.float32)
    mk2 = np.zeros((2, 128, 256), np.float32)
    mkn = np.zeros((2, 128, 128), np.float32)
    for d in range(2):
        mk1[d, :, :128] = st[d]
        mk1[d, :, 128:] = inc[d]
        mk2[d, :, :128] = st[d]
        mk2[d, :, 128:] = -inc[d].astype(np.float32)
        mkn[d] = st[d].T
    c["mk1"] = mk1
    c["mk2"] = mk2
    c["mkn"] = mkn
    cm2 = np.zeros((128, 2, 128), np.float32)
    cm2[:, 0, :64] = 1.0
    cm2[:, 1, 64:] = 1.0
    c["colmask2"] = cm2
    hi = np.zeros((128, 2), np.float32)
    hi[idx, idx // 64] = 1.0
    c["headind"] = hi
    c["blockones"] = same64.astype(np.float32)
    sel = np.zeros((128, 2, 64), np.float32)
    for hh in range(2):
        sel[64 * hh + np.arange(64), hh, np.arange(64)] = 1.0
    c["sel64"] = sel
    return c


def _pos_table():
    rows, gw, quarter, half = 2048 // 64, 64, 256, 512
    omega = (1.0 / (10000.0 ** (np.arange(quarter, dtype=np.float32) / np.float32(quarter)))).astype(np.float32)
    r = np.arange(rows, dtype=np.float32)[:, None] * omega
    cc = np.arange(gw, dtype=np.float32)[:, None] * omega
    row_emb = np.concatenate([np.sin(r), np.cos(r)], -1)
    col_emb = np.concatenate([np.sin(cc), np.cos(cc)], -1)
    emb = np.concatenate([np.broadcast_to(row_emb[:, None, :], (rows, gw, half)),
                          np.broadcast_to(col_emb[None, :, :], (rows, gw, half))], -1)
    return np.ascontiguousarray(emb.reshape(rows * gw, D).astype(np.float32))


CONST_SHAPES = {
    "ident": [128, 128], "maskH": [2, 128, 128], "colmask4": [128, 4, 128], "rowmask4": [128, 4],
    "rmask32": [128, 128], "rmask64": [128, 128], "mk1": [2, 128, 256], "mk2": [2, 128, 256],
    "mkn": [2, 128, 128], "colmask2": [128, 2, 128], "headind": [128, 2], "blockones": [128, 128],
    "sel64": [128, 2, 64],
}


def build_program(n_layers=4, mix=True, dbg=False, stop=None):
    nc = bass.Bass("TRN2", target_bir_lowering=False)

    def din(name, shape):
        return nc.dram_tensor(name, list(shape), F32, kind="ExternalInput").ap()

    def dout(name, shape):
        return nc.dram_tensor(name, list(shape), F32, kind="ExternalOutput").ap()

    NLW = n_layers
    NH = max(1, (n_layers + 1) // 2)
    NR = max(1, n_layers // 2)
    x_in = din("x_in", [2048, D])
    pos = din("pos", [2048, D])
    condT = din("condT", [128, 8])
    flags = din("flags", [128, 2])
    ada_w = din("ada_w", [NLW, D, 6 * D])
    ada_bT = din("ada_bT", [NLW, 128, 48])
    ln_g = din("ln_g", [NLW, 2, D])
    ln_b = din("ln_b", [NLW, 2, D])
    w_up = din("ffn_w_up", [NLW, D, 4 * D])
    w_dn = din("ffn_w_down", [NLW, 4 * D, D])
    h_win = din("hgrn_w_in", [NH, D, 7 * D])
    h_lbT = din("hgrn_lbT", [2, 128, 16])
    h_ng = din("hgrn_norm_g", [NH, D])
    h_wo = din("hgrn_w_o", [NH, D, D])
    st_h = din("st_h", [2, 2, 8, 128, 128])
    st_r = din("st_r", [2, 2, 16, 64, 64])
    r_vecT = din("rwkv_vecT", [2, 16, 128, 8])
    r_wrkv = din("rwkv_w_rkv", [NR, 3, D, 2 * D])
    r_wla = din("rwkv_w_la", [NR, D, 2, 64])
    r_wlb = din("rwkv_w_lb", [NR, 2, 64, D])
    r_ala = din("rwkv_a_la", [NR, D, 2, 64])
    r_alb = din("rwkv_a_lb", [NR, 2, 64, D])
    r_gla = din("rwkv_g_la", [NR, D, 160])
    r_glb = din("rwkv_g_lb", [NR, 160, D])
    r_gng = din("rwkv_gn_g", [NR, 2, D])
    r_gnb = din("rwkv_gn_b", [NR, 2, D])
    r_wo = din("rwkv_w_o", [NR, D, D])
    cst = {k: din("c_" + k, v) for k, v in CONST_SHAPES.items()}

    y_out = dout("y_out", [2048, D])
    ns_h = dout("ns_h", [2, NSEG, 2, 8, 128, 128])
    ns_r = dout("ns_r", [2, NSEG, 2, 16, 64, 64])
    oscr = nc.dram_tensor("oscr", [2, 2048, D], F32).ap()
    dbgt = dout("dbg", [16, 128, D]) if dbg else None

    with contextlib.ExitStack() as st:
        S = Sched(nc, st)

        _uid = [0]

        def sb(stack, name, shape, dt=F32):
            _uid[0] += 1
            return stack.enter_context(nc.sbuf_tensor("%s_u%d" % (name, _uid[0]), list(shape), dt))

        X = sb(st, "X", [128, NT, D])
        BX = [S.buf("X%d" % i) for i in range(NT)]
        GX = S.group("GX")
        ident = sb(st, "ident", [128, 128])
        identb = sb(st, "identb", [128, 128], BF16)
        Bc = S.buf("consts", "own")
        flg = sb(st, "flg", [128, 2])
        scond = sb(st, "scond", [128, 8])
        modT = sb(st, "modT", [128, 48])
        sc1p = sb(st, "sc1p", [128, 8])
        sc2p = sb(st, "sc2p", [128, 8])
        Bmod = S.buf("mod")
        PS = [st.enter_context(nc.psum_tensor("ps%d" % i, [128, 512], F32)) for i in range(8)]
        BPS = [S.buf("ps%d" % i) for i in range(8)]
        for b_ in BPS:
            b_.excl = True
        Gout = S.group("Gout")
        By = S.buf("y_out", Gout)
        Gscr = S.group("Gscr")
        Bscr = [S.buf("oscr0", Gscr), S.buf("oscr1", Gscr)]

        block = st.enter_context(nc.Block())
        _dq = [0]
        Bdbg = S.buf("dbg", S.group("dbg"))

        def dump(slot, ap, bufs, p=128, n=None):
            if not dbg:
                return
            n = n if n is not None else ap.shape[-1]
            S.dma("pool", dbgt[slot, 0:p, 0:n], ap, reads=bufs, writes=[Bdbg])

        def hwq():
            _dq[0] += 1
            return "sp" if _dq[0] % 2 == 0 else "act"

        S.dma("sp", ident[:], cst["ident"], writes=[Bc])
        S.dma("pool", identb[:], cst["ident"], writes=[Bc])
        S.dma("sp", flg[:], flags, writes=[Bc])
        S.dma("sp", scond[:], condT, writes=[Bc])
        S.op("act", lambda e: e.activation(out=scond[:], in_=scond[:], func=AF.Silu), reads=[Bc], writes=[Bc])
        for i in range(NT):
            S.dma(hwq(), X[:, i, :], x_in[i * 128:(i + 1) * 128, :], writes=[BX[i]], grp=GX)
        with contextlib.ExitStack() as ph:
            pt = [sb(ph, "pos%d" % i, [128, D]) for i in range(2)]
            Bpt = [S.buf("pos%d" % i, "own") for i in range(2)]
            for i in range(NT):
                S.dma(hwq(), pt[i % 2][:], pos[i * 128:(i + 1) * 128, :], writes=[Bpt[i % 2]])
                eng = "dve"
                S.op(eng, lambda e, i=i: e.scalar_tensor_tensor(out=X[:, i, :], in0=pt[i % 2][:], scalar=flg[:, 1:2], in1=X[:, i, :],
                                                                op0=ALU.mult, op1=ALU.add),
                     reads=[Bpt[i % 2], Bc, BX[i]], writes=[BX[i]])
            dump(1, X[:, 0, :], [BX[0]])
            S.barrier()

        def run_interleaved(gens, width):
            it = iter(gens)
            active = []
            while True:
                while len(active) < width:
                    g = next(it, None)
                    if g is None:
                        break
                    active.append(g)
                if not active:
                    break
                for g in list(active):
                    try:
                        next(g)
                    except StopIteration:
                        active.remove(g)

        def adaln(l):
            with contextlib.ExitStack() as ph:
                slab = [sb(ph, "adas%d" % i, [128, 8, 256]) for i in range(2)]
                Bsl = [S.buf("adas%d" % i, "own") for i in range(2)]
                abT = sb(ph, "abT", [128, 48])
                Bab = S.buf("abT", "own")
                S.dma("sp", abT[:], ada_bT[l], writes=[Bab])
                wv = ada_w[l].rearrange("(k p) n -> p k n", p=128)
                acc = PS[0]
                for jg in range(24):
                    sl = jg % 2
                    S.dma(hwq(), slab[sl][:], wv[:, :, jg * 256:(jg + 1) * 256], writes=[Bsl[sl]])
                    for jj in range(2):
                        j = jg * 2 + jj
                        for k in range(8):
                            S.op("pe", lambda e, sl=sl, jj=jj, j=j, k=k: e.matmul(
                                out=acc[:, j:j + 1], lhsT=slab[sl][:, k, jj * 128:(jj + 1) * 128], rhs=scond[:, k:k + 1],
                                start=(k == 0), stop=(k == 7)), reads=[Bsl[sl], Bc], writes=[BPS[0]])
                S.op("dve", lambda e: e.tensor_tensor(out=modT[:], in0=acc[:, 0:48], in1=abT[:], op=ALU.add),
                     reads=[BPS[0], Bab], writes=[Bmod])
                S.op("dve", lambda e: e.tensor_scalar_add(out=sc1p[:], in0=modT[:, 8:16], scalar1=1.0), reads=[Bmod], writes=[Bmod])
                S.op("dve", lambda e: e.tensor_scalar_add(out=sc2p[:], in0=modT[:, 32:40], scalar1=1.0), reads=[Bmod], writes=[Bmod])
                if l == 0:
                    dump(0, modT[:], [Bmod])
                S.barrier()

        def make_gbc(ph, which):
            base = (16, 40)[which]
            gb = sb(ph, "gbc", [128, D])
            Bgb = S.buf("gbc%d" % which)
            gtmp = sb(ph, "gtmp", [128, 128])
            Bgt = S.buf("gtmp")
            for k in range(8):
                S.op("dve", lambda e, k=k: e.tensor_scalar(
                    out=gtmp[:], in0=modT[:, base + k:base + k + 1].to_broadcast([128, 128]),
                    scalar1=1.0 / DN_ALPHA, scalar2=None, op0=ALU.mult), reads=[Bmod], writes=[Bgt])
                S.op("pe", lambda e: e.transpose(out=PS[1][:, 0:128], in_=gtmp[:], identity=ident[:]),
                     reads=[Bgt, Bc], writes=[BPS[1]])
                S.op("act", lambda e, k=k: e.activation(out=gb[:, k * 128:(k + 1) * 128], in_=PS[1][:, 0:128], func=AF.Copy),
                     reads=[BPS[1]], writes=[Bgb])
            return gb, Bgb

        def build_hT(dst_fn, Bdst_fn, tiles, scp, shcol):
            n = 0
            for i in tiles:
                for kk in range(2):
                    pb = 2 + (n % 2)
                    n += 1
                    for k4 in range(4):
                        k = kk * 4 + k4
                        S.op("pe", lambda e, i=i, k=k, k4=k4, pb=pb: e.transpose(
                            out=PS[pb][:, k4 * 128:(k4 + 1) * 128], in_=X[:, i, k * 128:(k + 1) * 128], identity=ident[:]),
                            reads=[BX[i], Bc], writes=[BPS[pb]])
                    for k4 in range(4):
                        k = kk * 4 + k4
                        if k % 2 == 0:
                            S.op("act", lambda e, i=i, k=k, k4=k4, pb=pb: e.activation(
                                out=dst_fn(i, k), in_=PS[pb][:, k4 * 128:(k4 + 1) * 128], func=AF.Identity,
                                bias=modT[:, shcol + k:shcol + k + 1], scale=scp[:, k:k + 1]),
                                reads=[BPS[pb], Bmod], writes=[Bdst_fn(i)])
                        else:
                            S.op("dve", lambda e, i=i, k=k, k4=k4, pb=pb: e.tensor_scalar(
                                out=dst_fn(i, k), in0=PS[pb][:, k4 * 128:(k4 + 1) * 128],
                                scalar1=scp[:, k:k + 1], scalar2=modT[:, shcol + k:shcol + k + 1], op0=ALU.mult, op1=ALU.add),
                                reads=[BPS[pb], Bmod], writes=[Bdst_fn(i)])

        def resid_ln(i, pa, pb, Bpa, Bpb, which, lt):
            v, Bv = lt["v"], lt["Bv"]
            stt, Bst = lt["st"], lt["Bst"]
            S.op("dve", lambda e: e.tensor_tensor(out=v[:, 0:512], in0=pa[:, :], in1=lt["gbc"][:, 0:512], op=ALU.mult),
                 reads=[Bpa, lt["Bgbc"]], writes=[Bv])
            S.op("dve", lambda e: e.tensor_tensor(out=v[:, 512:1024], in0=pb[:, :], in1=lt["gbc"][:, 512:1024], op=ALU.mult),
                 reads=[Bpb, lt["Bgbc"]], writes=[Bv])
            peng = lt.get("peng", "pool")
            S.op(peng, lambda e: e.tensor_tensor(out=v[:], in0=v[:], in1=X[:, i, :], op=ALU.add), reads=[Bv, BX[i]], writes=[Bv])
            if i == 0 and lt.get("dbg"):
                dump(5, v[:], [Bv])
            S.op("dve", lambda e: e.bn_stats(out=stt[:, 0:6], in_=v[:, 0:512]), reads=[Bv], writes=[Bst])
            S.op("dve", lambda e: e.bn_stats(out=stt[:, 6:12], in_=v[:, 512:1024]), reads=[Bv], writes=[Bst])
            S.op("dve", lambda e: e.bn_aggr(out=stt[:, 12:14], in_=stt[:, 0:12]), reads=[Bst], writes=[Bst])
            S.op("dve", lambda e: e.tensor_scalar_add(out=stt[:, 14:15], in0=stt[:, 13:14], scalar1=LN_EPS / (DN_ALPHA ** 2)), reads=[Bst], writes=[Bst])
            S.op("act", lambda e: e.activation(out=stt[:, 14:15], in_=stt[:, 14:15], func=AF.Sqrt), reads=[Bst], writes=[Bst])
            S.op("dve", lambda e: e.reciprocal(out=stt[:, 14:15], in_=stt[:, 14:15]), reads=[Bst], writes=[Bst])
            S.op("dve", lambda e: e.tensor_scalar(out=v[:], in0=v[:], scalar1=stt[:, 12:13], scalar2=stt[:, 14:15],
                                                  op0=ALU.subtract, op1=ALU.mult), reads=[Bv, Bst], writes=[Bv])
            if i == 0 and lt.get("dbg"):
                dump(6, stt[:], [Bst])
                dump(8, v[:], [Bv])
            S.op(peng, lambda e: e.tensor_tensor(out=v[:], in0=v[:], in1=lt["lng"][:], op=ALU.mult),
                 reads=[Bv, lt["Bln"]], writes=[Bv])
            S.op(peng, lambda e: e.tensor_tensor(out=X[:, i, :], in0=v[:], in1=lt["lnb"][:], op=ALU.add),
                 reads=[Bv, lt["Bln"]], writes=[BX[i]])

        def ffn(l):
            with contextlib.ExitStack() as ph:
                wd = sb(ph, "wd", [128, 32, D], BF16)
                Bwd = [S.buf("wd%d" % i, "own") for i in range(4)]
                ups = [sb(ph, "ups%d" % i, [128, 8, 256], BF16) for i in range(2)]
                Bups = [S.buf("ups%d" % i, "own") for i in range(2)]
                aT = sb(ph, "aT", [128, 32, 512], BF16)
                BaT = [S.buf("aT%d" % j) for j in range(32)]
                h2 = sb(ph, "h2", [128, 8, 512], BF16)
                Bh2 = [S.buf("h2_%d" % i) for i in range(4)]
                rl = [sb(ph, "rl%d" % i, [128, 512]) for i in range(2)]
                Brl = [S.buf("rl%d" % i) for i in range(2)]
                lng = sb(ph, "lng", [128, D])
                lnb = sb(ph, "lnb", [128, D])
                Bln = S.buf("lnbc", "own")
                S.dma("sp", lng[:], ln_g[l, 1].partition_broadcast(128), writes=[Bln])
                S.dma("sp", lnb[:], ln_b[l, 1].partition_broadcast(128), writes=[Bln])
                gb_, Bgb_ = make_gbc(ph, 1)
                lt = [dict(v=sb(ph, "lnv%d" % i, [128, D]), Bv=S.buf("lnv%d" % i), st=sb(ph, "lnst%d" % i, [128, 16]), Bst=S.buf("lnst%d" % i),
                           lng=lng, lnb=lnb, Bln=Bln, gbc=gb_, Bgbc=Bgb_, peng="dve") for i in range(1)]
                if l == 0:
                    lt[0]["dbg"] = True
                wdv = w_dn[l].rearrange("(j p) n -> p j n", p=128)
                for q in range(4):
                    S.dma("pool", wd[:, q * 8:(q + 1) * 8, :], wdv[:, q * 8:(q + 1) * 8, :], writes=[Bwd[q]])
                wuv = w_up[l].rearrange("(k p) n -> p k n", p=128)
                nev = 0
                for g in range(4):
                    tiles = list(range(g * 4, g * 4 + 4))
                    build_hT(lambda i, k: h2[:, k, (i % 4) * 128:(i % 4 + 1) * 128], lambda i: Bh2[i % 4], tiles, sc2p, 24)
                    for jg in range(16):
                        sl = jg % 2
                        S.dma("pool", ups[sl][:], wuv[:, :, jg * 256:(jg + 1) * 256], writes=[Bups[sl]])
                        for jj in range(2):
                            j = jg * 2 + jj
                            pb = 4 + (nev % 2)
                            for k in range(8):
                                S.op("pe", lambda e, sl=sl, jj=jj, k=k, pb=pb: e.matmul(
                                    out=PS[pb][:, :], lhsT=ups[sl][:, k, jj * 128:(jj + 1) * 128], rhs=h2[:, k, :],
                                    start=(k == 0), stop=(k == 7)), reads=[Bups[sl]] + Bh2, writes=[BPS[pb]])
                            r = nev % 2
                            nev += 1
                            S.op("act", lambda e, pb=pb, r=r: e.activation(out=rl[r][:], in_=PS[pb][:, :], func=AF.Relu),
                                 reads=[BPS[pb]], writes=[Brl[r]])
                            if j % 2 == 0:
                                S.op("dve", lambda e, j=j, r=r: e.tensor_tensor(out=aT[:, j, :], in0=rl[r][:], in1=rl[r][:], op=ALU.mult),
                                     reads=[Brl[r]], writes=[BaT[j]])
                            else:
                                S.op("act", lambda e, j=j, r=r: e.activation(out=aT[:, j, :], in_=rl[r][:], func=AF.Square),
                                     reads=[Brl[r]], writes=[BaT[j]])
                    if l == 0 and g == 0:
                        dump(3, h2[:, :, 0:128].rearrange("p k t -> p k t"), Bh2, n=None) if False else None
                        for k in range(8):
                            if dbg:
                                S.dma("pool", dbgt[3, :, k * 128:(k + 1) * 128], h2[:, k, 0:128], reads=Bh2, writes=[Bdbg])
                        dump(4, aT[:, 0, :], [BaT[0]])
                    for ii, i in enumerate(tiles):
                        pa, pbk = 6, 7
                        for nh, pbank in ((0, pa), (1, pbk)):
                            for j in range(32):
                                S.op("pe", lambda e, j=j, ii=ii, nh=nh, pbank=pbank: e.matmul(
                                    out=PS[pbank][:, :], lhsT=aT[:, j, ii * 128:(ii + 1) * 128], rhs=wd[:, j, nh * 512:(nh + 1) * 512],
                                    start=(j == 0), stop=(j == 31)), reads=[BaT[j], Bwd[j // 8]], writes=[BPS[pbank]])
                        if l == 0 and i == 0 and dbg:
                            S.dma("pool", dbgt[9, :, 0:512], PS[pa][:, :], reads=[BPS[pa]], writes=[Bdbg]) if False else None
                        resid_ln(i, PS[pa], PS[pbk], BPS[pa], BPS[pbk], 1, lt[0])
                        if l == 0 and i == 0:
                            dump(7, X[:, 0, :], [BX[0]])
                S.barrier()

        def hgrn_layer(l, j, hT, BhT):
            hcols = lambda i, k: hT[:, k, i // 2, PADL + (i % 2) * 128: PADL + (i % 2) * 128 + 128]
            with contextlib.ExitStack() as ph:
                lbv = sb(ph, "lbv", [128, 16])
                oml = sb(ph, "oml", [128, 16])
                Blb = S.buf("lbv", "own")
                if j == 0:
                    S.op("dve", lambda e: e.memset(lbv[:], 0.0), writes=[Blb])
                else:
                    lb0 = sb(ph, "lb0", [128, 16])
                    S.dma("sp", lb0[:], h_lbT[0], writes=[Blb])
                    S.dma("sp", lbv[:], h_lbT[1], writes=[Blb])
                    S.op("dve", lambda e: e.tensor_tensor(out=lbv[:], in0=lbv[:], in1=lb0[:], op=ALU.subtract), reads=[Blb], writes=[Blb])
                    S.op("act", lambda e: e.activation(out=lbv[:], in_=lbv[:], func=AF.Sigmoid), reads=[Blb], writes=[Blb])
                S.op("dve", lambda e: e.tensor_scalar(out=oml[:], in0=lbv[:], scalar1=-1.0, scalar2=1.0, op0=ALU.mult, op1=ALU.add),
                     reads=[Blb], writes=[Blb])
                maskH = sb(ph, "maskH", [128, 128])
                cm4 = sb(ph, "cm4", [128, 4, 128], BF16)
                rm4 = sb(ph, "rm4", [128, 4])
                r32 = sb(ph, "r32", [128, 128])
                Bcm = S.buf("hconst", "own")
                S.dma("pool", cm4[:], cst["colmask4"], writes=[Bcm])
                S.dma("sp", rm4[:], cst["rowmask4"], writes=[Bcm])
                S.dma("sp", r32[:], cst["rmask32"], writes=[Bcm])
                wq = [sb(ph, "hw%d" % i, [128, 8, D], BF16) for i in range(3)]
                Bwq = [S.buf("hw%d" % i, "own") for i in range(3)]
                Tst = sb(ph, "Tst", [128, 8, 128])
                BT = [S.buf("T%d" % h) for h in range(8)]
                GT = S.group("GT")
                vbs = [sb(ph, "vb%d" % i, [128, D], BF16) for i in range(2)]
                Bvbs = [S.buf("vb%d" % i) for i in range(2)]
                oblk = [sb(ph, "oblk%d" % i, [128, D]) for i in range(2)]
                Bob = [S.buf("oblk%d" % i) for i in range(2)]
                NS_ = 4
                tl = []
                for s_ in range(NS_):
                    t = {}
                    for nm in ("q", "nz", "uu", "l1", "l2", "Lp", "Dd", "eD", "emD", "t1", "kf"):
                        t[nm] = sb(ph, "h_%s%d" % (nm, s_), [128, 128])
                    t["wc"] = sb(ph, "h_wc%d" % s_, [128, 4])
                    for nm in ("kout", "qpp", "At"):
                        t[nm] = sb(ph, "h_%s%d" % (nm, s_), [128, 128], BF16)
                    for nm in ("koe", "qe", "Tp"):
                        t[nm] = sb(ph, "h_%s%d" % (nm, s_), [128, 4, 128], BF16)
                    t["B"] = {nm: S.buf("h_%s%d" % (nm, s_)) for nm in
                              ("q", "nz", "uu", "l1", "l2", "Lp", "Dd", "eD", "emD", "t1", "kf", "wc", "kout", "qpp", "At", "koe", "qe", "Tp")}
                    tl.append(t)
                wv = h_win[j].rearrange("(k p) n -> p k n", p=128)
                un = 0
                for d in range(2):
                    S.dma("sp", maskH[:], cst["maskH"][d], reads=[], writes=[Bcm])
                    for w3 in range(3):
                        c0 = d * 3072 + w3 * 1024
                        S.dma("pool", wq[w3][:], wv[:, :, c0:c0 + 1024], writes=[Bwq[w3]])
                    S.dma("sp", Tst[:], st_h[j, d].rearrange("h k v -> k h v"), writes=BT, grp=GT)
                    order = list(range(NT)) if d == 0 else list(range(NT - 1, -1, -1))
                    def hgen(bi, i, h, d=d, order=order):
                        seg = i // 2
                        first_of_seg = (i % 2 == 0) if d == 0 else (i % 2 == 1)
                        last_of_seg = not first_of_seg
                        corder = [0, 1, 2, 3] if d == 0 else [3, 2, 1, 0]
                        vbt, Bvbt = vbs[bi % 2], Bvbs[bi % 2]
                        ob, Bo = oblk[bi % 2], Bob[bi % 2]
                        idx = bi * 8 + h
                        t = tl[idx % NS_]
                        B = t["B"]
                        pqz = pg = po = 2 * (idx % NS_)
                        pu = 2 * (idx % NS_) + 1
                        col = d * 8 + h
                        if h == 0:
                            for nh in range(2):
                                pb = 6 + nh
                                for k in range(8):
                                    S.op("pe", lambda e, k=k, nh=nh, pb=pb: e.matmul(
                                        out=PS[pb][:, :], lhsT=hcols(i, k), rhs=wq[2][:, k, nh * 512:(nh + 1) * 512],
                                        start=(k == 0), stop=(k == 7)), reads=[BhT[seg], Bwq[2]], writes=[BPS[pb]])
                                S.op("act", lambda e, nh=nh, pb=pb: e.activation(out=vbt[:, nh * 512:(nh + 1) * 512], in_=PS[pb][:, :], func=AF.Copy),
                                     reads=[BPS[pb]], writes=[Bvbt])
                            yield
                        if first_of_seg and bi > 0:
                            S.op("dve", lambda e: e.tensor_scalar(out=Tst[:, h, :], in0=Tst[:, h, :], scalar1=flg[:, 0:1], scalar2=None, op0=ALU.mult),
                                 reads=[BT[h], Bc], writes=[BT[h]])
                        for w3, off in ((0, 0), (1, 128)):
                            for k in range(8):
                                S.op("pe", lambda e, i=i, k=k, w3=w3, off=off, pqz=pqz, h=h: e.matmul(
                                    out=PS[pqz][:, off:off + 128], lhsT=wq[w3][:, k, h * 128:(h + 1) * 128], rhs=hcols(i, k),
                                    start=(k == 0), stop=(k == 7)), reads=[BhT[seg], Bwq[w3]], writes=[BPS[pqz]])
                        yield
                        S.op("act", lambda e, t=t, pqz=pqz: e.activation(out=t["q"][:], in_=PS[pqz][:, 0:128], func=AF.Silu),
                             reads=[BPS[pqz]], writes=[B["q"]])
                        yield
                        S.op("dve", lambda e, t=t, pqz=pqz: e.tensor_scalar(out=t["nz"][:], in0=PS[pqz][:, 128:256], scalar1=-1.0, scalar2=80.0,
                                                                             op0=ALU.mult, op1=ALU.min), reads=[BPS[pqz]], writes=[B["nz"]])
                        yield
                        S.op("act", lambda e, t=t: e.activation(out=t["uu"][:], in_=t["nz"][:], func=AF.Exp), reads=[B["nz"]], writes=[B["uu"]])
                        yield
                        S.op("act", lambda e, t=t, col=col: e.activation(out=t["l1"][:], in_=t["uu"][:], func=AF.Ln, bias=1.0,
                                                                          scale=lbv[:, col:col + 1]), reads=[B["uu"], Blb], writes=[B["l1"]])
                        yield
                        S.op("act", lambda e, t=t: e.activation(out=t["l2"][:], in_=t["uu"][:], func=AF.Ln, bias=1.0, scale=1.0),
                             reads=[B["uu"]], writes=[B["l2"]])
                        yield
                        S.op("dve", lambda e, t=t: e.tensor_tensor(out=t["l1"][:], in0=t["l1"][:], in1=t["l2"][:], op=ALU.subtract),
                             reads=[B["l1"], B["l2"]], writes=[B["l1"]])
                        yield
                        S.op("dve", lambda e, t=t: e.tensor_tensor_scan(out=t["Lp"][:], data0=r32[:], data1=t["l1"][:], initial=0.0,
                                                                       op0=ALU.mult, op1=ALU.add), reads=[B["l1"], Bcm], writes=[B["Lp"]])
                        yield
                        Lv = t["Lp"][:].rearrange("p (c t) -> p c t", t=32)
                        yield
                        Dv = t["Dd"][:].rearrange("p (c t) -> p c t", t=32)
                        yield
                        if d == 0:
                            S.op("dve", lambda e, Lv=Lv, Dv=Dv: e.tensor_tensor(out=Dv, in0=Lv[:, :, 31:32].to_broadcast([128, 4, 32]), in1=Lv,
                                                                                op=ALU.subtract), reads=[B["Lp"]], writes=[B["Dd"]])
                        else:
                            S.op("dve", lambda e, t=t: e.tensor_tensor(out=t["Dd"][:], in0=t["Lp"][:], in1=t["l1"][:], op=ALU.subtract),
                                 reads=[B["Lp"], B["l1"]], writes=[B["Dd"]])
                        yield
                        S.op("pool", lambda e, t=t: e.tensor_scalar_max(out=t["Dd"][:], in0=t["Dd"][:], scalar1=-80.0), reads=[B["Dd"]], writes=[B["Dd"]])
                        yield
                        S.op("act", lambda e, t=t: e.activation(out=t["eD"][:], in_=t["Dd"][:], func=AF.Exp), reads=[B["Dd"]], writes=[B["eD"]])
                        yield
                        S.op("act", lambda e, t=t: e.activation(out=t["emD"][:], in_=t["Dd"][:], func=AF.Exp, scale=-1.0), reads=[B["Dd"]], writes=[B["emD"]])
                        yield
                        S.op("act", lambda e, t=t, Lv=Lv: e.activation(out=t["wc"][:].unsqueeze(2), in_=Lv[:, :, 31:32], func=AF.Exp),
                             reads=[B["Lp"]], writes=[B["wc"]])
                        yield
                        S.op("dve", lambda e, t=t: e.tensor_scalar_add(out=t["t1"][:], in0=t["uu"][:], scalar1=1.0), reads=[B["uu"]], writes=[B["t1"]])
                        yield
                        S.op("dve", lambda e, t=t: e.reciprocal(out=t["t1"][:], in_=t["t1"][:]), reads=[B["t1"]], writes=[B["t1"]])
                        yield
                        S.op("dve", lambda e, t=t, col=col: e.scalar_tensor_tensor(out=t["kf"][:], in0=t["uu"][:], scalar=oml[:, col:col + 1], in1=t["t1"][:],
                                                                                  op0=ALU.mult, op1=ALU.mult), reads=[B["uu"], B["t1"], Blb], writes=[B["kf"]])
                        yield
                        S.op("pool", lambda e, t=t: e.tensor_tensor(out=t["kout"][:], in0=t["kf"][:], in1=t["eD"][:], op=ALU.mult),
                             reads=[B["kf"], B["eD"]], writes=[B["kout"]])
                        yield
                        S.op("pool", lambda e, t=t: e.tensor_tensor(out=t["qpp"][:], in0=t["q"][:], in1=t["emD"][:], op=ALU.mult),
                             reads=[B["q"], B["emD"]], writes=[B["qpp"]])
                        yield
                        pgb = PS[pg][:].bitcast(BF16)
                        yield
                        S.op("pe", lambda e, t=t, pg=pg: e.matmul(out=PS[pg][:, 256:384], lhsT=t["kout"][:], rhs=t["qpp"][:], start=True, stop=True),
                             reads=[B["kout"], B["qpp"]], writes=[BPS[pg]])
                        yield
                        S.op("pe", lambda e, t=t, pgb=pgb: e.transpose(out=pgb[:, 768:896], in_=t["kout"][:], identity=identb[:]),
                             reads=[B["kout"], Bc], writes=[BPS[pg]])
                        yield
                        S.op("dve", lambda e, t=t, pg=pg: e.tensor_tensor(out=t["At"][:], in0=PS[pg][:, 256:384], in1=maskH[:], op=ALU.mult),
                             reads=[BPS[pg], Bcm], writes=[B["At"]])
                        yield
                        S.op("dve", lambda e, t=t, pgb=pgb: e.tensor_tensor(out=t["koe"][:], in0=pgb[:, 768:896].unsqueeze(1).to_broadcast([128, 4, 128]),
                                                                           in1=rm4[:].unsqueeze(2).to_broadcast([128, 4, 128]), op=ALU.mult),
                             reads=[BPS[pg], Bcm], writes=[B["koe"]])
                        yield
                        S.op("pool", lambda e, t=t: e.tensor_tensor(out=t["qe"][:], in0=t["qpp"][:].unsqueeze(1).to_broadcast([128, 4, 128]),
                                                                   in1=cm4[:], op=ALU.mult), reads=[B["qpp"], Bcm], writes=[B["qe"]])
                        yield
                        for c in range(4):
                            S.op("pe", lambda e, t=t, c=c, pu=pu, h=h: e.matmul(out=PS[pu][:, c * 128:(c + 1) * 128], lhsT=t["koe"][:, c, :],
                                                                              rhs=vbt[:, h * 128:(h + 1) * 128], start=True, stop=True),
                                 reads=[B["koe"], Bvbt], writes=[BPS[pu]])
                        yield
                        for c in corder:
                            S.op("dve", lambda e, t=t, c=c, h=h: e.tensor_scalar(out=t["Tp"][:, c, :], in0=Tst[:, h, :], scalar1=t["wc"][:, c:c + 1],
                                                                                scalar2=None, op0=ALU.mult), reads=[BT[h], B["wc"]], writes=[B["Tp"]])
                            S.op("dve", lambda e, t=t, c=c, h=h, pu=pu: e.scalar_tensor_tensor(out=Tst[:, h, :], in0=Tst[:, h, :], scalar=t["wc"][:, c:c + 1],
                                                                                              in1=PS[pu][:, c * 128:(c + 1) * 128], op0=ALU.mult, op1=ALU.add),
                                 reads=[BT[h], B["wc"], BPS[pu]], writes=[BT[h]])
                        yield
                        S.op("pe", lambda e, t=t, po=po, h=h: e.matmul(out=PS[po][:, 0:128], lhsT=t["At"][:], rhs=vbt[:, h * 128:(h + 1) * 128],
                                                                       start=True, stop=False), reads=[B["At"], Bvbt], writes=[BPS[po]])
                        yield
                        for ci, c in enumerate(corder):
                            S.op("pe", lambda e, t=t, po=po, c=c, ci=ci: e.matmul(out=PS[po][:, 0:128], lhsT=t["qe"][:, c, :], rhs=t["Tp"][:, c, :],
                                                                                  start=False, stop=(ci == 3)), reads=[B["qe"], B["Tp"]], writes=[BPS[po]])
                        yield
                        S.op("act", lambda e, ob=ob, po=po, h=h: e.activation(out=ob[:, h * 128:(h + 1) * 128], in_=PS[po][:, 0:128], func=AF.Copy),
                             reads=[BPS[po]], writes=[Bo])
                        yield
                        if last_of_seg:
                            S.dma("act", ns_h[j, seg, d, h], Tst[:, h, :], reads=[BT[h]], writes=[By], grp=S.group("st_T%d" % h))
                        if h == 7:
                            S.dma("sp", oscr[d, i * 128:(i + 1) * 128, :], ob[:], reads=[Bo], writes=[Bscr[d]], grp=S.group("st_ob%d" % (bi % 2)))
                    run_interleaved((hgen(bi, i, h) for bi, i in enumerate(order) for h in range(8)), NS_)
                S.barrier()

        def post_mixer(l, j, kind, hT, BhT):
            hcols = lambda i, k: hT[:, k, i // 2, PADL + (i % 2) * 128: PADL + (i % 2) * 128 + 128]
            with contextlib.ExitStack() as ph:
                wo = sb(ph, "wo", [128, 8, D], BF16)
                Bwo = S.buf("wo", "own")
                src_wo = h_wo[j] if kind == "h" else r_wo[j]
                S.dma("pool", wo[:], src_wo.rearrange("(k p) n -> p k n", p=128), writes=[Bwo])
                ot = [[sb(ph, "ot%d_%d" % (dd, s_), [128, D]) for s_ in range(2)] for dd in range(2)]
                Bot = [[S.buf("ot%d_%d" % (dd, s_), "own") for s_ in range(2)] for dd in range(2)]
                zb = sb(ph, "zb", [128, D], BF16)
                Bzb = S.buf("zb")
                zT = sb(ph, "zT", [128, 8, 128], BF16)
                BzT = S.buf("zT")
                lng = sb(ph, "lng", [128, D])
                lnb = sb(ph, "lnb", [128, D])
                Bln = S.buf("lnbc", "own")
                S.dma("sp", lng[:], ln_g[l, 0].partition_broadcast(128), writes=[Bln])
                S.dma("sp", lnb[:], ln_b[l, 0].partition_broadcast(128), writes=[Bln])
                gb_, Bgb_ = make_gbc(ph, 0)
                lt = [dict(v=sb(ph, "lnv%d" % i, [128, D]), Bv=S.buf("lnv%d" % i), st=sb(ph, "lnst%d" % i, [128, 16]), Bst=S.buf("lnst%d" % i),
                           lng=lng, lnb=lnb, Bln=Bln, gbc=gb_, Bgbc=Bgb_) for i in range(2)]
                if kind == "h":
                    wg = sb(ph, "wg", [128, 8, D], BF16)
                    Bwg = S.buf("wg", "own")
                    S.dma("pool", wg[:], h_win[j].rearrange("(k p) n -> p k n", p=128)[:, :, 6144:7168], writes=[Bwg])
                    ngbc = sb(ph, "ngbc", [128, D])
                    Bng = S.buf("ngbc", "own")
                    S.dma("sp", ngbc[:], h_ng[j].partition_broadcast(128), writes=[Bng])
                    sq = sb(ph, "sq", [128, D])
                    Bsq = S.buf("sq")
                    sgt = sb(ph, "sgt", [128, D])
                    Bsg = S.buf("sgt")
                    ss = sb(ph, "ss", [128, 8])
                    Bss = S.buf("ss")
                else:
                    gla = sb(ph, "gla", [128, 8, 160], BF16)
                    glb1 = sb(ph, "glb1", [128, D], BF16)
                    glb2 = sb(ph, "glb2", [32, D], BF16)
                    Bgl = S.buf("gl", "own")
                    S.dma("pool", gla[:], r_gla[j].rearrange("(k p) n -> p k n", p=128), writes=[Bgl])
                    S.dma("pool", glb1[:], r_glb[j, 0:128, :], writes=[Bgl])
                    S.dma("pool", glb2[:], r_glb[j, 128:160, :], writes=[Bgl])
                    muT = sb(ph, "muTg", [128, 8])
                    S.dma("sp", muT[:], r_vecT[j, 5], writes=[Bgl])
                    xt = sb(ph, "xg_t", [128, 8, 128])
                    xs = sb(ph, "xg_s", [128, 8, 128], BF16)
                    Bxt = S.buf("xg_t")
                    Bxs = S.buf("xg_s")
                    sg1 = sb(ph, "sg1", [128, 128], BF16)
                    sg2 = sb(ph, "sg2", [32, 128], BF16)
                    Bsgg = S.buf("sgg")
                for i in range(NT):
                    seg = i // 2
                    s_ = i % 2
                    for dd in range(2):
                        S.dma(hwq(), ot[dd][s_][:], oscr[dd, i * 128:(i + 1) * 128, :], reads=[Bscr[dd]], writes=[Bot[dd][s_]])
                    o0, o1 = ot[0][s_], ot[1][s_]
                    S.op("pool", lambda e, o0=o0, o1=o1: e.tensor_tensor(out=o0[:], in0=o0[:], in1=o1[:], op=ALU.add),
                         reads=[Bot[0][s_], Bot[1][s_]], writes=[Bot[0][s_]])
                    if kind == "h":
                        for nh in range(2):
                            for k in range(8):
                                S.op("pe", lambda e, i=i, k=k, nh=nh: e.matmul(out=PS[nh][:, :], lhsT=hcols(i, k), rhs=wg[:, k, nh * 512:(nh + 1) * 512],
                                                                               start=(k == 0), stop=(k == 7)), reads=[BhT[seg], Bwg], writes=[BPS[nh]])
                            S.op("act", lambda e, nh=nh: e.activation(out=sgt[:, nh * 512:(nh + 1) * 512], in_=PS[nh][:, :], func=AF.Silu),
                                 reads=[BPS[nh]], writes=[Bsg])
                        S.op("act", lambda e, o0=o0: e.activation(out=sq[:], in_=o0[:], func=AF.Square), reads=[Bot[0][s_]], writes=[Bsq])
                        S.op("dve", lambda e: e.tensor_reduce(out=ss[:], in_=sq[:].rearrange("p (h k) -> p h k", k=128), axis=AX.X, op=ALU.add),
                             reads=[Bsq], writes=[Bss])
                        S.op("dve", lambda e: e.tensor_scalar(out=ss[:], in0=ss[:], scalar1=1.0 / 128.0, scalar2=RMS_EPS, op0=ALU.mult, op1=ALU.add),
                             reads=[Bss], writes=[Bss])
                        S.op("act", lambda e: e.activation(out=ss[:], in_=ss[:], func=AF.Sqrt), reads=[Bss], writes=[Bss])
                        S.op("dve", lambda e: e.reciprocal(out=ss[:], in_=ss[:]), reads=[Bss], writes=[Bss])
                        S.op("dve", lambda e, o0=o0: e.tensor_tensor(out=o0[:].rearrange("p (h k) -> p h k", k=128), in0=o0[:].rearrange("p (h k) -> p h k", k=128),
                                                                    in1=ss[:].unsqueeze(2).to_broadcast([128, 8, 128]), op=ALU.mult),
                             reads=[Bot[0][s_], Bss], writes=[Bot[0][s_]])
                        S.op("pool", lambda e, o0=o0: e.tensor_tensor(out=o0[:], in0=o0[:], in1=ngbc[:], op=ALU.mult), reads=[Bot[0][s_], Bng], writes=[Bot[0][s_]])
                        S.op("dve", lambda e, o0=o0: e.tensor_tensor(out=zb[:], in0=o0[:], in1=sgt[:], op=ALU.mult), reads=[Bot[0][s_], Bsg], writes=[Bzb])
                    else:
                        b_ = i % 2
                        hL = hT[:, :, seg, PADL - 1 + b_ * 128: PADL - 1 + b_ * 128 + 128]
                        hR = hT[:, :, seg, PADL + 1 + b_ * 128: PADL + 1 + b_ * 128 + 128]
                        hC = hT[:, :, seg, PADL + b_ * 128: PADL + b_ * 128 + 128]
                        S.op("pool", lambda e, hL=hL, hR=hR: e.tensor_tensor(out=xt[:], in0=hL, in1=hR, op=ALU.add), reads=[BhT[seg]], writes=[Bxt])
                        S.op("dve", lambda e, hC=hC: e.scalar_tensor_tensor(out=xt[:], in0=xt[:], scalar=0.5, in1=hC, op0=ALU.mult, op1=ALU.subtract),
                             reads=[Bxt, BhT[seg]], writes=[Bxt])
                        S.op("dve", lambda e: e.tensor_tensor(out=xt[:], in0=xt[:], in1=muT[:].unsqueeze(2).to_broadcast([128, 8, 128]), op=ALU.mult),
                             reads=[Bxt, Bgl], writes=[Bxt])
                        S.op("dve", lambda e, hC=hC: e.tensor_tensor(out=xs[:], in0=xt[:], in1=hC, op=ALU.add), reads=[Bxt, BhT[seg]], writes=[Bxs])
                        for k in range(8):
                            S.op("pe", lambda e, k=k: e.matmul(out=PS[2][:, 0:128], lhsT=gla[:, k, 0:128], rhs=xs[:, k, :], start=(k == 0), stop=(k == 7)),
                                 reads=[Bgl, Bxs], writes=[BPS[2]])
                        for k in range(8):
                            S.op("pe", lambda e, k=k: e.matmul(out=PS[2][0:32, 128:256], lhsT=gla[:, k, 128:160], rhs=xs[:, k, :], start=(k == 0), stop=(k == 7)),
                                 reads=[Bgl, Bxs], writes=[BPS[2]])
                        S.op("act", lambda e: e.activation(out=sg1[:], in_=PS[2][:, 0:128], func=AF.Sigmoid), reads=[BPS[2]], writes=[Bsgg])
                        S.op("act", lambda e: e.activation(out=sg2[:], in_=PS[2][0:32, 128:256], func=AF.Sigmoid), reads=[BPS[2]], writes=[Bsgg])
                        for nh in range(2):
                            S.op("pe", lambda e, nh=nh: e.matmul(out=PS[nh][:, :], lhsT=sg1[:], rhs=glb1[:, nh * 512:(nh + 1) * 512], start=True, stop=False),
                                 reads=[Bsgg, Bgl], writes=[BPS[nh]])
                            S.op("pe", lambda e, nh=nh: e.matmul(out=PS[nh][:, :], lhsT=sg2[:], rhs=glb2[:, nh * 512:(nh + 1) * 512], start=False, stop=True),
                                 reads=[Bsgg, Bgl], writes=[BPS[nh]])
                            S.op("dve", lambda e, nh=nh, o0=o0: e.tensor_tensor(out=zb[:, nh * 512:(nh + 1) * 512], in0=o0[:, nh * 512:(nh + 1) * 512],
                                                                               in1=PS[nh][:, :], op=ALU.mult), reads=[Bot[0][s_], BPS[nh]], writes=[Bzb])
                    pzb = PS[3][:].bitcast(BF16)
                    for k in range(8):
                        S.op("pe", lambda e, k=k, pzb=pzb: e.transpose(out=pzb[:, k * 128:(k + 1) * 128], in_=zb[:, k * 128:(k + 1) * 128], identity=identb[:]),
                             reads=[Bzb, Bc], writes=[BPS[3]])
                    S.op("act", lambda e, pzb=pzb: e.activation(out=zT[:].rearrange("p k t -> p (k t)"), in_=pzb[:, 0:1024], func=AF.Copy),
                         reads=[BPS[3]], writes=[BzT])
                    pa, pbk = 4 + 2 * (i % 2), 5 + 2 * (i % 2)
                    for nh, pbank in ((0, pa), (1, pbk)):
                        for k in range(8):
                            S.op("pe", lambda e, k=k, nh=nh, pbank=pbank: e.matmul(out=PS[pbank][:, :], lhsT=zT[:, k, :], rhs=wo[:, k, nh * 512:(nh + 1) * 512],
                                                                                  start=(k == 0), stop=(k == 7)), reads=[BzT, Bwo], writes=[BPS[pbank]])
                    resid_ln(i, PS[pa], PS[pbk], BPS[pa], BPS[pbk], 0, lt[i % 2])
                S.barrier()

        def rwkv_layer(l, j, hT, BhT):
            with contextlib.ExitStack() as ph:
                vec = sb(ph, "rvec", [128, 16, 8])
                Bvec = S.buf("rvec", "own")
                S.dma("sp", vec[:], r_vecT[j].rearrange("n p k -> p n k"), writes=[Bvec])
                cF = {}
                Bk = S.buf("rconst", "own")
                for nm, shp, dt in (("rmask64", [128, 128], BF16), ("colmask2", [128, 2, 128], BF16), ("headind", [128, 2], F32),
                                    ("blockones", [128, 128], F32), ("sel64", [128, 2, 64], F32)):
                    cF[nm] = sb(ph, "rc_" + nm, shp, dt)
                    S.dma("pool" if dt == BF16 else "sp", cF[nm][:], cst[nm], writes=[Bk])
                mk1 = sb(ph, "mk1", [128, 256], BF16)
                mk2 = sb(ph, "mk2", [128, 256], BF16)
                mkn = sb(ph, "mkn", [128, 128], BF16)
                gng = sb(ph, "gng", [128, D])
                gnb = sb(ph, "gnb", [128, D])
                Bdirc = S.buf("dirc", "own")
                W3 = [sb(ph, "rw%d" % i, [128, 8, D], BF16) for i in range(3)]
                BW3 = [S.buf("rw%d" % i, "own") for i in range(3)]
                wla = sb(ph, "wla", [128, 8, 64], BF16)
                ala = sb(ph, "ala", [128, 8, 64], BF16)
                wlb = sb(ph, "wlb", [64, D], BF16)
                alb = sb(ph, "alb", [64, D], BF16)
                Blo = S.buf("lora", "own")
                Tst = sb(ph, "rT", [64, 16, 64])
                BT = [S.buf("rT%d" % h) for h in range(16)]
                GT = S.group("GT")
                xx = sb(ph, "xx", [128, 8, 128])
                Bxx = S.buf("xx")
                xs_r = sb(ph, "xs_r", [128, 8, 128], BF16)
                xs_k = sb(ph, "xs_k", [128, 8, 128], BF16)
                xs_s = sb(ph, "xs_s", [128, 8, 128], BF16)
                xsn = [xs_r, xs_s, xs_k, xs_s, xs_s]
                Bxs_r, Bxs_k, Bxs_s = S.buf("xs_r"), S.buf("xs_k"), S.buf("xs_s")
                Bxs = [Bxs_r, Bxs_s, Bxs_k, Bxs_s, Bxs_s]
                vf = sb(ph, "vf", [128, D])
                vb = vf
                Bvf = S.buf("vf")
                Bvb = Bvf
                tw = sb(ph, "tw", [64, 128], BF16)
                ta = sb(ph, "ta", [64, 128], BF16)
                Btw = S.buf("tw")
                Bta = S.buf("ta")
                yblk = sb(ph, "yblk", [128, D])
                Byb = S.buf("yblk")
                bon = sb(ph, "bon", [128, 16])
                Bbon = S.buf("bon")
                Bg = {nm: S.buf("gn_" + nm) for nm in ("st",)}
                gst = sb(ph, "gn_st", [128, 48])
                ct = {}
                for nm in ("sg", "aa", "kk", "kk2", "t", "Ls", "Dsg", "emD"):
                    ct[nm] = sb(ph, "c_" + nm, [128, 128])
                ct["rs"] = ct["kk2"]
                ct["prod"] = ct["kk2"]
                ct["k2"] = ct["t"]
                ct["Ds2"] = ct["Ls"]
                ct["bb"] = ct["aa"]
                ct["emD2"] = ct["Ls"]
                ct["eD"] = ct["Dsg"]
                wcs = [sb(ph, "c_wc%d" % i_, [128, 2]) for i_ in range(2)]
                Bwcs = [S.buf("c_wc%d" % i_) for i_ in range(2)]
                KB = sb(ph, "KB", [128, 2, 128])
                QR = sb(ph, "QR", [128, 2, 128])
                KoT = sb(ph, "KoTm", [128, 2, 128])
                BoTn = sb(ph, "BoTnm", [128, 2, 128])
                Bct = {nm: S.buf("c_" + nm) for nm in list(ct.keys()) + ["KB", "QR", "KoT", "BoTn"]}
                Bct["rs"] = Bct["kk2"]
                Bct["prod"] = Bct["kk2"]
                Bct["k2"] = Bct["t"]
                Bct["Ds2"] = Bct["Ls"]
                Bct["bb"] = Bct["aa"]
                Bct["emD2"] = Bct["Ls"]
                Bct["eD"] = Bct["Dsg"]
                HX = []
                for s_ in range(2):
                    hx = dict(A1=sb(ph, "A1", [128, 256]), A2=sb(ph, "A2", [128, 256]), Nm=sb(ph, "Nm", [128, 128]), Zm=sb(ph, "Zm", [128, 128]),
                              RH=sb(ph, "RH", [128, 128]), RpE=sb(ph, "RpE", [64, 2, 128]), P2T=sb(ph, "P2T", [64, 2, 64]), T0p=sb(ph, "T0p", [64, 2, 64]),
                              wch=sb(ph, "wch", [64, 2]))
                    hx["B"] = {nm: S.buf("h%d_%s" % (s_, nm)) for nm in ("A1", "A2", "Nm", "Zm", "RH", "RpE", "P2T", "T0p", "wch")}
                    HX.append(hx)
                for d in range(2):
                    S.dma("pool", mk1[:], cst["mk1"][d], writes=[Bdirc])
                    S.dma("pool", mk2[:], cst["mk2"][d], writes=[Bdirc])
                    S.dma("pool", mkn[:], cst["mkn"][d], writes=[Bdirc])
                    S.dma("sp", gng[:], r_gng[j, d].partition_broadcast(128), writes=[Bdirc])
                    S.dma("sp", gnb[:], r_gnb[j, d].partition_broadcast(128), writes=[Bdirc])
                    for n3 in range(3):
                        S.dma("pool", W3[n3][:], r_wrkv[j, n3].rearrange("(k p) n -> p k n", p=128)[:, :, d * D:(d + 1) * D], writes=[BW3[n3]])
                    S.dma("pool", wla[:], r_wla[j].rearrange("(k p) z r -> p k z r", p=128)[:, :, d, :], writes=[Blo])
                    S.dma("pool", ala[:], r_ala[j].rearrange("(k p) z r -> p k z r", p=128)[:, :, d, :], writes=[Blo])
                    S.dma("pool", wlb[:], r_wlb[j, d], writes=[Blo])
                    S.dma("pool", alb[:], r_alb[j, d], writes=[Blo])
                    S.dma("sp", Tst[:], st_r[j, d].rearrange("h k v -> k h v"), writes=BT, grp=GT)
                    vcol = lambda n: vec[:, 6 + n * 2 + d, :]
                    order = list(range(NT)) if d == 0 else list(range(NT - 1, -1, -1))
                    for bi, i in enumerate(order):
                        seg = i // 2
                        b_ = i % 2
                        first_of_seg = (b_ == 0) if d == 0 else (b_ == 1)
                        last_of_seg = not first_of_seg
                        corder = [0, 1] if d == 0 else [1, 0]
                        if first_of_seg and bi > 0:
                            S.op("dve", lambda e: e.tensor_scalar(out=Tst[:], in0=Tst[:], scalar1=flg[0:64, 0:1], scalar2=None, op0=ALU.mult),
                                 reads=BT + [Bc], writes=BT)
                        _rt[0] += 1
                        S.skip = (_LIM["rstep"] < 1) or (_rt[0] > _LIM["rtiles"])
                        hL = hT[:, :, seg, PADL - 1 + b_ * 128: PADL - 1 + b_ * 128 + 128]
                        hR = hT[:, :, seg, PADL + 1 + b_ * 128: PADL + 1 + b_ * 128 + 128]
                        hC = hT[:, :, seg, PADL + b_ * 128: PADL + b_ * 128 + 128]
                        S.op("pool", lambda e, hL=hL, hR=hR: e.tensor_tensor(out=xx[:], in0=hL, in1=hR, op=ALU.add), reads=[BhT[seg]], writes=[Bxx])
                        S.op("dve", lambda e, hC=hC: e.scalar_tensor_tensor(out=xx[:], in0=xx[:], scalar=0.5, in1=hC, op0=ALU.mult, op1=ALU.subtract),
                             reads=[Bxx, BhT[seg]], writes=[Bxx])
                        def mk_xs(n, eng):
                            S.op(eng, lambda e: e.tensor_tensor(out=xsn[n][:], in0=xx[:], in1=vec[:, n, :].unsqueeze(2).to_broadcast([128, 8, 128]), op=ALU.mult),
                                 reads=[Bxx, Bvec], writes=[Bxs[n]])
                            S.op(eng, lambda e: e.tensor_tensor(out=xsn[n][:], in0=xsn[n][:], in1=hC, op=ALU.add), reads=[Bxs[n], BhT[seg]], writes=[Bxs[n]])
                        mk_xs(3, "dve")
                        mk_xs(0, "pool")
                        for nh in range(2):
                            pb = 6 + nh
                            for k in range(8):
                                S.op("pe", lambda e, k=k, nh=nh, pb=pb: e.matmul(out=PS[pb][:, :], lhsT=xsn[3][:, k, :], rhs=W3[2][:, k, nh * 512:(nh + 1) * 512],
                                                                               start=(k == 0), stop=(k == 7)), reads=[Bxs[3], BW3[2]], writes=[BPS[pb]])
                            S.op("act", lambda e, nh=nh, pb=pb: e.activation(out=vf[:, nh * 512:(nh + 1) * 512], in_=PS[pb][:, :], func=AF.Copy),
                                 reads=[BPS[pb]], writes=[Bvf])
                        mk_xs(1, "dve")
                        mk_xs(2, "pool")
                        for k in range(8):
                            S.op("pe", lambda e, k=k: e.matmul(out=PS[6][0:64, 0:128], lhsT=wla[:, k, :], rhs=xsn[1][:, k, :], start=(k == 0), stop=(k == 7)),
                                 reads=[Blo, Bxs[1]], writes=[BPS[6]])
                        mk_xs(4, "dve")
                        for k in range(8):
                            S.op("pe", lambda e, k=k: e.matmul(out=PS[6][0:64, 128:256], lhsT=ala[:, k, :], rhs=xsn[4][:, k, :], start=(k == 0), stop=(k == 7)),
                                 reads=[Blo, Bxs[4]], writes=[BPS[6]])
                        S.op("act", lambda e: e.activation(out=tw[:], in_=PS[6][0:64, 0:128], func=AF.Tanh), reads=[BPS[6]], writes=[Btw])
                        S.op("act", lambda e: e.activation(out=ta[:], in_=PS[6][0:64, 128:256], func=AF.Copy), reads=[BPS[6]], writes=[Bta])
                        ptb = PS[1]
                        def proA(c):
                            cs = slice(c * 128, (c + 1) * 128)
                            yield
                            pp = PS[0]
                            yield
                            for k in range(8):
                                S.op("pe", lambda e, k=k, cs=cs: e.matmul(out=pp[:, 0:128], lhsT=W3[0][:, k, cs], rhs=xsn[0][:, k, :], start=(k == 0), stop=(k == 7)),
                                     reads=[BW3[0], Bxs[0]], writes=[BPS[0]])
                            yield
                            for k in range(8):
                                S.op("pe", lambda e, k=k, cs=cs: e.matmul(out=pp[:, 128:256], lhsT=W3[1][:, k, cs], rhs=xsn[2][:, k, :], start=(k == 0), stop=(k == 7)),
                                     reads=[BW3[1], Bxs[2]], writes=[BPS[0]])
                            yield
                            S.op("pe", lambda e, cs=cs: e.matmul(out=pp[:, 256:384], lhsT=wlb[:, cs], rhs=tw[:], start=True, stop=True), reads=[Blo, Btw], writes=[BPS[0]])
                            yield
                            S.op("pe", lambda e, cs=cs: e.matmul(out=pp[:, 384:512], lhsT=alb[:, cs], rhs=ta[:], start=True, stop=True), reads=[Blo, Bta], writes=[BPS[0]])
                            yield
                            pr, pk = pp[:, 0:128], pp[:, 128:256]
                            yield
                            S.op("act", lambda e, c=c: e.activation(out=ct["sg"][:], in_=pp[:, 256:384], func=AF.Sigmoid, bias=vcol(0)[:, c:c + 1], scale=1.0),
                                 reads=[BPS[0], Bvec], writes=[Bct["sg"]])
                            yield
                            S.op("act", lambda e, c=c: e.activation(out=ct["aa"][:], in_=pp[:, 384:512], func=AF.Sigmoid, bias=vcol(1)[:, c:c + 1], scale=1.0),
                                 reads=[BPS[0], Bvec], writes=[Bct["aa"]])
                            yield
                            S.op("dve", lambda e, c=c: e.tensor_scalar(out=ct["kk"][:], in0=pk, scalar1=vcol(2)[:, c:c + 1], scalar2=None, op0=ALU.mult),
                                 reads=[BPS[0], Bvec], writes=[Bct["kk"]])
                            yield
                            S.op("pool", lambda e: e.tensor_tensor(out=ct["kk2"][:], in0=ct["kk"][:], in1=ct["kk"][:], op=ALU.mult), reads=[Bct["kk"]], writes=[Bct["kk2"]])
                            yield
                            S.op("pe", lambda e: e.matmul(out=PS[1][:, 0:128], lhsT=cF["blockones"][:], rhs=ct["kk2"][:], start=True, stop=True),
                                 reads=[Bk, Bct["kk2"]], writes=[BPS[1]])
                            yield
                            S.op("dve", lambda e: e.tensor_scalar_max(out=ct["rs"][:], in0=PS[1][:, 0:128], scalar1=1e-24), reads=[BPS[1]], writes=[Bct["rs"]])
                            yield
                            S.op("act", lambda e: e.activation(out=ct["rs"][:], in_=ct["rs"][:], func=AF.Sqrt), reads=[Bct["rs"]], writes=[Bct["rs"]])
                            yield
                            S.op("dve", lambda e: e.reciprocal(out=ct["rs"][:], in_=ct["rs"][:]), reads=[Bct["rs"]], writes=[Bct["rs"]])
                            yield
                            S.op("dve", lambda e: e.tensor_tensor(out=ct["kk"][:], in0=ct["kk"][:], in1=ct["rs"][:], op=ALU.mult), reads=[Bct["kk"], Bct["rs"]], writes=[Bct["kk"]])
                            yield
                            S.op("dve", lambda e, c=c: e.tensor_scalar(out=ct["t"][:], in0=ct["aa"][:], scalar1=1.0, scalar2=vcol(3)[:, c:c + 1], op0=ALU.subtract, op1=ALU.mult),
                                 reads=[Bct["aa"], Bvec], writes=[Bct["t"]])
                            yield
                            S.op("dve", lambda e: e.scalar_tensor_tensor(out=ct["k2"][:], in0=ct["t"][:], scalar=1.0, in1=pk, op0=ALU.add, op1=ALU.mult),
                                 reads=[Bct["t"], BPS[0]], writes=[Bct["k2"]])
                            yield
                            S.op("pool", lambda e: e.tensor_tensor(out=ct["bb"][:], in0=ct["kk"][:], in1=ct["aa"][:], op=ALU.mult), reads=[Bct["kk"], Bct["aa"]], writes=[Bct["bb"]])
                            yield
                            S.op("dve", lambda e, c=c: e.scalar_tensor_tensor(out=ct["prod"][:], in0=pr, scalar=vcol(4)[:, c:c + 1], in1=ct["k2"][:], op0=ALU.mult, op1=ALU.mult),
                                 reads=[BPS[0], Bvec, Bct["k2"]], writes=[Bct["prod"]])
                            yield
                            S.op("pe", lambda e, c=c: e.matmul(out=PS[7][:, 2 * c:2 * c + 2], lhsT=ct["prod"][:], rhs=cF["headind"][:], start=True, stop=True),
                                 reads=[Bct["prod"], Bk], writes=[BPS[7]])
                            yield
                            S.op("dve", lambda e: e.tensor_tensor_scan(out=ct["Ls"][:], data0=cF["rmask64"][:], data1=ct["sg"][:], initial=0.0, op0=ALU.mult, op1=ALU.add),
                                 reads=[Bct["sg"], Bk], writes=[Bct["Ls"]])
                            yield
                            Lv = ct["Ls"][:].rearrange("p (c t) -> p c t", t=64)
                            yield
                            Dv = ct["Dsg"][:].rearrange("p (c t) -> p c t", t=64)
                            yield
                            if d == 0:
                                S.op("dve", lambda e, Lv=Lv, Dv=Dv: e.tensor_tensor(out=Dv, in0=Lv[:, :, 63:64].to_broadcast([128, 2, 64]), in1=Lv, op=ALU.subtract),
                                     reads=[Bct["Ls"]], writes=[Bct["Dsg"]])
                            else:
                                S.op("dve", lambda e: e.tensor_tensor(out=ct["Dsg"][:], in0=ct["Ls"][:], in1=ct["sg"][:], op=ALU.subtract),
                                     reads=[Bct["Ls"], Bct["sg"]], writes=[Bct["Dsg"]])
                            yield
                            S.op("act", lambda e, Lv=Lv: e.activation(out=wcs[c % 2][:].unsqueeze(2), in_=Lv[:, :, 63:64], func=AF.Exp, scale=-C0), reads=[Bct["Ls"]], writes=[Bwcs[c % 2]])
                            yield
                            S.op("pool", lambda e: e.tensor_tensor(out=ct["Ds2"][:], in0=ct["Dsg"][:], in1=ct["sg"][:], op=ALU.add), reads=[Bct["Dsg"], Bct["sg"]], writes=[Bct["Ds2"]])
                            yield
                            S.op("act", lambda e: e.activation(out=ct["emD"][:], in_=ct["Dsg"][:], func=AF.Exp, scale=C0), reads=[Bct["Dsg"]], writes=[Bct["emD"]])
                            yield
                            S.op("act", lambda e: e.activation(out=ct["eD"][:], in_=ct["Dsg"][:], func=AF.Exp, scale=-C0), reads=[Bct["Dsg"]], writes=[Bct["eD"]])
                            yield
                            S.op("act", lambda e: e.activation(out=ct["emD2"][:], in_=ct["Ds2"][:], func=AF.Exp, scale=C0), reads=[Bct["Ds2"]], writes=[Bct["emD2"]])
                            yield
                        def proB(c):
                            pp = PS[0]
                            pr = pp[:, 0:128]
                            S.op("pool", lambda e: e.tensor_tensor(out=KB[:, 0, :], in0=ct["k2"][:], in1=ct["eD"][:], op=ALU.mult), reads=[Bct["k2"], Bct["eD"]], writes=[Bct["KB"]])
                            S.op("pool", lambda e: e.tensor_tensor(out=KB[:, 1, :], in0=ct["bb"][:], in1=ct["eD"][:], op=ALU.mult), reads=[Bct["bb"], Bct["eD"]], writes=[Bct["KB"]])
                            S.op("pool", lambda e: e.tensor_tensor(out=QR[:, 0, :], in0=ct["kk"][:], in1=ct["emD2"][:], op=ALU.mult), reads=[Bct["kk"], Bct["emD2"]], writes=[Bct["QR"]])
                            S.op("dve", lambda e: e.tensor_tensor(out=QR[:, 1, :], in0=pr, in1=ct["emD"][:], op=ALU.mult), reads=[BPS[0], Bct["emD"]], writes=[Bct["QR"]])
                            ptb = PS[1]
                            S.op("pe", lambda e, ptb=ptb: e.transpose(out=ptb[:, 128:256], in_=KB[:, 0, :], identity=ident[:]), reads=[Bct["KB"], Bc], writes=[BPS[1]])
                            S.op("pe", lambda e, ptb=ptb: e.transpose(out=ptb[:, 256:384], in_=KB[:, 1, :], identity=ident[:]), reads=[Bct["KB"], Bc], writes=[BPS[1]])
                            S.op("pe", lambda e, ptb=ptb: e.transpose(out=ptb[:, 384:512], in_=QR[:, 0, :], identity=ident[:]), reads=[Bct["QR"], Bc], writes=[BPS[1]])
                            hib = cF["headind"][:].unsqueeze(2).to_broadcast([128, 2, 128])
                            S.op("dve", lambda e, ptb=ptb, hib=hib: e.tensor_tensor(out=KoT[:], in0=ptb[:, 128:256].unsqueeze(1).to_broadcast([128, 2, 128]), in1=hib, op=ALU.mult),
                                 reads=[BPS[1], Bk], writes=[Bct["KoT"]])
                            S.op("dve", lambda e, ptb=ptb, hib=hib: e.scalar_tensor_tensor(out=BoTn[:], in0=ptb[:, 256:384].unsqueeze(1).to_broadcast([128, 2, 128]), scalar=-1.0, in1=hib,
                                                                                          op0=ALU.mult, op1=ALU.mult), reads=[BPS[1], Bk], writes=[Bct["BoTn"]])
                        def head_gen(c, hh):
                            head = 2 * c + hh
                            prs = slice(64 * hh, 64 * hh + 64)
                            hc = slice(head * 64, head * 64 + 64)
                            hcl = slice(hh * 64, hh * 64 + 64)
                            hx = HX[hh]
                            A1, A2, Nmt, Zmt, RH, RpE, P2T, T0p, wch = hx["A1"], hx["A2"], hx["Nm"], hx["Zm"], hx["RH"], hx["RpE"], hx["P2T"], hx["T0p"], hx["wch"]
                            Bh = hx["B"]
                            PA, PB, BA, BB = PS[2 + 2 * hh], PS[3 + 2 * hh], BPS[2 + 2 * hh], BPS[3 + 2 * hh]
                            qr2 = QR[prs, :, :].rearrange("p a t -> p (a t)")
                            S.op("pe", lambda e: e.matmul(out=PA[:, 0:256], lhsT=KB[prs, 0, :], rhs=qr2, start=True, stop=True), reads=[Bct["KB"], Bct["QR"]], writes=[BA])
                            S.op("pe", lambda e: e.matmul(out=PA[:, 256:512], lhsT=KB[prs, 1, :], rhs=qr2, start=True, stop=True), reads=[Bct["KB"], Bct["QR"]], writes=[BA])
                            S.op("pe", lambda e: e.matmul(out=PB[:, 0:128], lhsT=QR[prs, 0, :], rhs=KB[prs, 1, :], start=True, stop=True), reads=[Bct["KB"], Bct["QR"]], writes=[BB])
                            yield
                            S.op("dve", lambda e: e.tensor_tensor(out=A1[:], in0=PA[:, 0:256], in1=mk1[:], op=ALU.mult), reads=[BA, Bdirc], writes=[Bh["A1"]])
                            S.op("dve", lambda e: e.tensor_tensor(out=A2[:], in0=PA[:, 256:512], in1=mk2[:], op=ALU.mult), reads=[BA, Bdirc], writes=[Bh["A2"]])
                            S.op("dve", lambda e: e.tensor_tensor(out=Nmt[:], in0=PB[:, 0:128], in1=mkn[:], op=ALU.mult), reads=[BB, Bdirc], writes=[Bh["Nm"]])
                            yield
                            S.op("pe", lambda e: e.matmul(out=PB[:, 128:192], lhsT=A1[:, 0:128], rhs=vb[:, hc], start=True, stop=True), reads=[Bh["A1"], Bvb], writes=[BB])
                            S.op("dve", lambda e: e.tensor_copy(out=RH[:, 0:64], in_=ptb[:, 384 + 64 * hh:448 + 64 * hh]), reads=[BPS[1]], writes=[Bh["RH"]])
                            yield
                            S.op("act", lambda e: e.activation(out=RH[:, 64:128], in_=PB[:, 128:192], func=AF.Copy), reads=[BB], writes=[Bh["RH"]])
                            yield
                            for lvl in range(6):
                                if lvl == 0:
                                    Zc, BZc = A2[:, 0:128], Bh["A2"]
                                else:
                                    Zc, BZc = Zmt[:], Bh["Zm"]
                                Nc, BNc = Nmt[:], Bh["Nm"]
                                S.op("pe", lambda e, Zc=Zc: e.matmul(out=PB[:, 256:384], lhsT=Zc, rhs=RH[:], start=True, stop=True), reads=[BZc, Bh["RH"]], writes=[BB])
                                if lvl < 5:
                                    S.op("pe", lambda e, Zc=Zc, Nc=Nc: e.matmul(out=PA[:, 0:128], lhsT=Nc, rhs=Zc, start=True, stop=True), reads=[BZc, BNc], writes=[BA])
                                    if lvl < 4:
                                        S.op("pe", lambda e, Zc=Zc, Nc=Nc: e.matmul(out=PA[:, 128:256], lhsT=Zc, rhs=Nc, start=True, stop=True), reads=[BZc, BNc], writes=[BA])
                                yield
                                S.op("dve", lambda e, lvl=lvl: e.tensor_tensor(out=RH[:], in0=RH[:], in1=PB[:, 256:384], op=(ALU.subtract if lvl == 0 else ALU.add)),
                                     reads=[Bh["RH"], BB], writes=[Bh["RH"]])
                                if lvl < 5:
                                    S.op("act", lambda e: e.activation(out=Zmt[:], in_=PA[:, 0:128], func=AF.Copy), reads=[BA], writes=[Bh["Zm"]])
                                    if lvl < 4:
                                        S.op("act", lambda e: e.activation(out=Nmt[:], in_=PA[:, 128:256], func=AF.Copy), reads=[BA], writes=[Bh["Nm"]])
                                yield
                            S.op("pe", lambda e: e.matmul(out=PA[0:64, 256:384], lhsT=cF["sel64"][:, hh, :], rhs=QR[:, 1, :], start=True, stop=False), reads=[Bk, Bct["QR"]], writes=[BA])
                            S.op("pe", lambda e: e.matmul(out=PA[0:64, 256:384], lhsT=RH[:, 0:64], rhs=A2[:, 128:256], start=False, stop=True), reads=[Bh["RH"], Bh["A2"]], writes=[BA])
                            for cc in range(2):
                                S.op("pe", lambda e, cc=cc: e.matmul(out=PA[0:64, 384 + 64 * cc:448 + 64 * cc], lhsT=RH[:, 0:64], rhs=BoTn[:, cc, hcl], start=True, stop=True),
                                     reads=[Bh["RH"], Bct["BoTn"]], writes=[BA])
                            S.op("pe", lambda e: e.matmul(out=PB[0:64, 192:194], lhsT=cF["sel64"][:, hh, :], rhs=wcs[c % 2][:], start=True, stop=True), reads=[Bk, Bwcs[c % 2]], writes=[BB])
                            yield
                            S.op("dve", lambda e: e.tensor_tensor(out=RpE[:], in0=PA[0:64, 256:384].unsqueeze(1).to_broadcast([64, 2, 128]), in1=cF["colmask2"][0:64], op=ALU.mult),
                                 reads=[BA, Bk], writes=[Bh["RpE"]])
                            S.op("dve", lambda e: e.tensor_tensor(out=P2T[:], in0=PA[0:64, 384:512].rearrange("p (c k) -> p c k", c=2),
                                                                  in1=ident[0:64, 0:64].unsqueeze(1).to_broadcast([64, 2, 64]), op=ALU.add), reads=[BA, Bc], writes=[Bh["P2T"]])
                            S.op("act", lambda e: e.activation(out=wch[:], in_=PB[0:64, 192:194], func=AF.Copy), reads=[BB], writes=[Bh["wch"]])
                            yield
                            for cc in corder:
                                S.op("dve", lambda e, cc=cc: e.tensor_scalar(out=T0p[:, cc, :], in0=Tst[:, head, :], scalar1=wch[:, cc:cc + 1], scalar2=None, op0=ALU.mult),
                                     reads=[BT[head], Bh["wch"]], writes=[Bh["T0p"]])
                                S.op("pe", lambda e, cc=cc: e.matmul(out=PB[0:64, 384:448], lhsT=KoT[:, cc, hcl], rhs=vb[:, hc], start=True, stop=False), reads=[Bct["KoT"], Bvb], writes=[BB])
                                S.op("pe", lambda e, cc=cc: e.matmul(out=PB[0:64, 384:448], lhsT=BoTn[:, cc, hcl], rhs=RH[:, 64:128], start=False, stop=False), reads=[Bct["BoTn"], Bh["RH"]], writes=[BB])
                                S.op("pe", lambda e, cc=cc: e.matmul(out=PB[0:64, 384:448], lhsT=P2T[:, cc, :], rhs=T0p[:, cc, :], start=False, stop=True), reads=[Bh["P2T"], Bh["T0p"]], writes=[BB])
                                yield
                                S.op("dve", lambda e: e.tensor_copy(out=Tst[:, head, :], in_=PB[0:64, 384:448]), reads=[BB], writes=[BT[head]])
                                yield
                            S.op("pe", lambda e: e.matmul(out=PB[:, 448:512], lhsT=A1[:, 128:256], rhs=vb[:, hc], start=True, stop=False), reads=[Bh["A1"], Bvb], writes=[BB])
                            S.op("pe", lambda e: e.matmul(out=PB[:, 448:512], lhsT=A2[:, 128:256], rhs=RH[:, 64:128], start=False, stop=False), reads=[Bh["A2"], Bh["RH"]], writes=[BB])
                            for cc in range(2):
                                S.op("pe", lambda e, cc=cc: e.matmul(out=PB[:, 448:512], lhsT=RpE[:, cc, :], rhs=T0p[:, cc, :], start=False, stop=(cc == 1)), reads=[Bh["RpE"], Bh["T0p"]], writes=[BB])
                            yield
                            S.op("act", lambda e: e.activation(out=yblk[:, hc], in_=PB[:, 448:512], func=AF.Copy), reads=[BB], writes=[Byb])
                        for _ in proA(0):
                            pass
                        proB(0)
                        for c in range(8):
                            gens = [head_gen(c, 0), head_gen(c, 1)] + ([proA(c + 1)] if c < 7 else [])
                            run_interleaved(gens, 3)
                            if c < 7:
                                proB(c + 1)
                        S.skip = (_LIM["rstep"] < 17) or (_rt[0] > _LIM["rtiles"])
                        yv = yblk[:].rearrange("p (h n) -> p h n", n=64)
                        sqt = xx[:].rearrange("p k t -> p (k t)")
                        sv = sqt.rearrange("p (h n) -> p h n", n=64)
                        S.op("act", lambda e: e.activation(out=bon[:], in_=PS[7][:, 0:16], func=AF.Copy), reads=[BPS[7]], writes=[Bbon])
                        S.op("dve", lambda e: e.tensor_reduce(out=gst[:, 0:16], in_=yv, axis=AX.X, op=ALU.add), reads=[Byb], writes=[Bg["st"]])
                        S.op("dve", lambda e: e.tensor_scalar(out=gst[:, 0:16], in0=gst[:, 0:16], scalar1=1.0 / 64.0, scalar2=None, op0=ALU.mult), reads=[Bg["st"]], writes=[Bg["st"]])
                        S.op("dve", lambda e: e.tensor_tensor(out=yv, in0=yv, in1=gst[:, 0:16].unsqueeze(2).to_broadcast([128, 16, 64]), op=ALU.subtract),
                             reads=[Byb, Bg["st"]], writes=[Byb])
                        S.op("act", lambda e: e.activation(out=sqt, in_=yblk[:], func=AF.Square), reads=[Byb], writes=[Bxx])
                        S.op("dve", lambda e: e.tensor_reduce(out=gst[:, 16:32], in_=sv, axis=AX.X, op=ALU.add), reads=[Bxx], writes=[Bg["st"]])
                        S.op("dve", lambda e: e.tensor_scalar(out=gst[:, 16:32], in0=gst[:, 16:32], scalar1=1.0 / 64.0, scalar2=GN_EPS, op0=ALU.mult, op1=ALU.add),
                             reads=[Bg["st"]], writes=[Bg["st"]])
                        S.op("act", lambda e: e.activation(out=gst[:, 16:32], in_=gst[:, 16:32], func=AF.Sqrt), reads=[Bg["st"]], writes=[Bg["st"]])
                        S.op("dve", lambda e: e.reciprocal(out=gst[:, 16:32], in_=gst[:, 16:32]), reads=[Bg["st"]], writes=[Bg["st"]])
                        S.op("dve", lambda e: e.tensor_tensor(out=yv, in0=yv, in1=gst[:, 16:32].unsqueeze(2).to_broadcast([128, 16, 64]), op=ALU.mult),
                             reads=[Byb, Bg["st"]], writes=[Byb])
                        S.op("pool", lambda e: e.tensor_tensor(out=yblk[:], in0=yblk[:], in1=gng[:], op=ALU.mult), reads=[Byb, Bdirc], writes=[Byb])
                        S.op("pool", lambda e: e.tensor_tensor(out=yblk[:], in0=yblk[:], in1=gnb[:], op=ALU.add), reads=[Byb, Bdirc], writes=[Byb])
                        vfv = vf[:].rearrange("p (h n) -> p h n", n=64)
                        S.op("dve", lambda e: e.tensor_tensor(out=vfv, in0=vfv, in1=bon[:].unsqueeze(2).to_broadcast([128, 16, 64]), op=ALU.mult),
                             reads=[Bvf, Bbon], writes=[Bvf])
                        S.op("pool", lambda e: e.tensor_tensor(out=yblk[:], in0=yblk[:], in1=vf[:], op=ALU.add), reads=[Byb, Bvf], writes=[Byb])
                        S.skip = False
                        S.dma("sp", oscr[d, i * 128:(i + 1) * 128, :], yblk[:], reads=[Byb], writes=[Bscr[d]], grp=S.group("st_yb"))
                        if last_of_seg:
                            S.dma("act", ns_r[j, seg, d].rearrange("h k v -> k h v"), Tst[:], reads=BT, writes=[By], grp=S.group("st_rT"))
                S.barrier()

        for l in range(n_layers):
            j = l // 2
            S.phase = "L%d.ada" % l
            adaln(l)
            if mix:
                with contextlib.ExitStack() as lph:
                    hT = sb(lph, "hT", [128, 8, NSEG, HTW], BF16)
                    BhT = [S.buf("hT%d" % s_) for s_ in range(NSEG)]
                    S.op("pool", lambda e: e.memset(hT[:, :, :, PADL - 1:PADL], 0.0), writes=BhT)
                    S.op("pool", lambda e: e.memset(hT[:, :, :, PADL + 256:PADL + 257], 0.0), writes=BhT)
                    S.phase = "L%d.hT" % l
                    build_hT(lambda i, k: hT[:, k, i // 2, PADL + (i % 2) * 128: PADL + (i % 2) * 128 + 128], lambda i: BhT[i // 2], list(range(NT)), sc1p, 0)
                    if l % 2 == 1:
                        S.op("dve", lambda e: e.tensor_scalar(out=hT[:, :, 1:NSEG, PADL - 1], in0=hT[:, :, 0:NSEG - 1, PADL + 255], scalar1=flg[:, 0:1], scalar2=None, op0=ALU.mult),
                             reads=BhT + [Bc], writes=BhT)
                        S.op("dve", lambda e: e.tensor_scalar(out=hT[:, :, 0:NSEG - 1, PADL + 256], in0=hT[:, :, 1:NSEG, PADL], scalar1=flg[:, 0:1], scalar2=None, op0=ALU.mult),
                             reads=BhT + [Bc], writes=BhT)
                    S.barrier()
                    stopped = False
                    if stop == "hT" and l == n_layers - 1:
                        stopped = True
                    elif l % 2 == 0:
                        S.phase = "L%d.mix" % l
                        hgrn_layer(l, j, hT, BhT)
                        if stop == "mix" and l == n_layers - 1:
                            stopped = True
                        else:
                            S.phase = "L%d.post" % l
                            post_mixer(l, j, "h", hT, BhT)
                    else:
                        S.phase = "L%d.mix" % l
                        rwkv_layer(l, j, hT, BhT)
                        if stop == "mix" and l == n_layers - 1:
                            stopped = True
                        else:
                            S.phase = "L%d.post" % l
                            post_mixer(l, j, "r", hT, BhT)
                if stopped or (stop == "post" and l == n_layers - 1):
                    break
            S.phase = "L%d.ffn" % l
            ffn(l)

        for i in range(NT):
            S.dma(hwq(), y_out[i * 128:(i + 1) * 128, :], X[:, i, :], reads=[BX[i]], writes=[By])
        S.barrier()
        S.emit(block)
        globals()["_LAST_SCHED"] = S
    return nc


_PROMPT_SLOTS = [[(p, p // 6) for p in range(32) if p % 6 == cix] for cix in range(6)]


def _fm(v):
    return np.ascontiguousarray(np.asarray(v, np.float32).reshape(8, 128).T)


def make_in_maps(inp, n_layers=4):
    NLW = n_layers
    NH = max(1, (n_layers + 1) // 2)
    NR = max(1, n_layers // 2)
    f = lambda a: np.ascontiguousarray(np.asarray(a, dtype=np.float32))
    consts = _consts()
    pos = _pos_table()
    shared = {
        "pos": pos,
        "ada_w": f(inp["ada_w"]),
        "ada_bT": np.ascontiguousarray(f(inp["ada_b"]).reshape(4, 48, 128).transpose(0, 2, 1)),
        "ln_g": f(inp["ln_g"]), "ln_b": f(inp["ln_b"]),
        "ffn_w_up": f(inp["ffn_w_up"]), "ffn_w_down": f(inp["ffn_w_down"]),
        "hgrn_w_in": f(inp["hgrn_w_in"]),
        "hgrn_lbT": np.ascontiguousarray(f(inp["hgrn_lb"]).reshape(2, 16, 128).transpose(0, 2, 1)),
        "hgrn_norm_g": f(inp["hgrn_norm_g"]), "hgrn_w_o": f(inp["hgrn_w_o"]),
        "rwkv_w_rkv": f(inp["rwkv_w_rkv"]), "rwkv_w_la": f(inp["rwkv_w_la"]), "rwkv_w_lb": f(inp["rwkv_w_lb"]),
        "rwkv_a_la": f(inp["rwkv_a_la"]), "rwkv_a_lb": f(inp["rwkv_a_lb"]),
        "rwkv_g_la": f(inp["rwkv_g_la"]), "rwkv_g_lb": f(inp["rwkv_g_lb"]),
        "rwkv_gn_g": f(inp["rwkv_gn_g"]), "rwkv_gn_b": f(inp["rwkv_gn_b"]), "rwkv_w_o": f(inp["rwkv_w_o"]),
    }
    vec = np.zeros((2, 16, 128, 8), np.float32)
    for j in range(2):
        for n in range(6):
            vec[j, n] = _fm(inp["rwkv_mu"][j, n])
        for n, nm in enumerate(("rwkv_w0", "rwkv_a0", "rwkv_k_k", "rwkv_k_a", "rwkv_r_k")):
            for d in range(2):
                vec[j, 6 + n * 2 + d] = _fm(inp[nm][j, d])
    shared["rwkv_vecT"] = vec
    for k, v in consts.items():
        shared["c_" + k] = v
    xp = f(inp["x_prompt"])
    xs = f(inp["x_sample"])
    sth = f(inp["state_hgrn"])
    strw = f(inp["state_rwkv"])
    maps = []
    for core in range(8):
        m = dict(shared)
        if core < 2:
            m["x_in"] = np.ascontiguousarray(xs[core])
            m["condT"] = _fm(inp["c"][core])
            fl = np.ones((128, 2), np.float32)
            m["st_h"] = np.ascontiguousarray(sth[core])
            m["st_r"] = np.ascontiguousarray(strw[core].transpose(0, 1, 2, 4, 3))
        else:
            slots = _PROMPT_SLOTS[core - 2]
            xin = np.zeros((NSEG, 256, D), np.float32)
            for s_ in range(NSEG):
                xin[s_] = xp[slots[s_][0]] if s_ < len(slots) else xp[slots[0][0]]
            m["x_in"] = xin.reshape(2048, D)
            m["condT"] = _fm(inp["c_ctx"])
            fl = np.zeros((128, 2), np.float32)
            m["st_h"] = np.zeros((2, 2, 8, 128, 128), np.float32)
            m["st_r"] = np.zeros((2, 2, 16, 64, 64), np.float32)
        m["flags"] = fl
        maps.append(m)
    cut = {"ada_w": NLW, "ada_bT": NLW, "ln_g": NLW, "ln_b": NLW, "ffn_w_up": NLW, "ffn_w_down": NLW,
           "hgrn_w_in": NH, "hgrn_norm_g": NH, "hgrn_w_o": NH, "rwkv_w_rkv": NR, "rwkv_w_la": NR, "rwkv_w_lb": NR,
           "rwkv_a_la": NR, "rwkv_a_lb": NR, "rwkv_g_la": NR, "rwkv_g_lb": NR, "rwkv_gn_g": NR, "rwkv_gn_b": NR, "rwkv_w_o": NR}
    if n_layers < 4:
        for k, n in cut.items():
            sl = np.ascontiguousarray(shared[k][:n])
            for m in maps:
                m[k] = sl
    return maps


def assemble(results):
    y_prompt = np.zeros((32, 256, D), np.float32)
    y_sample = np.zeros((2, 2048, D), np.float32)
    nsh = np.zeros((32, 2, 2, 8, 128, 128), np.float32)
    nsr = np.zeros((32, 2, 2, 16, 64, 64), np.float32)
    for core in range(8):
        r = results[core]
        if core < 2:
            y_sample[core] = r["y_out"]
        else:
            slots = _PROMPT_SLOTS[core - 2]
            yo = r["y_out"].reshape(NSEG, 256, D)
            for s_, (p, _) in enumerate(slots):
                y_prompt[p] = yo[s_]
                nsh[p] = r["ns_h"][:, s_]
                nsr[p] = r["ns_r"][:, s_].transpose(0, 1, 2, 4, 3)
    return y_prompt, y_sample, nsh, nsr


_NC_CACHE = {}
_LIM = {"heads": 10 ** 9, "step": 99, "rstep": 99, "rtiles": 10 ** 9}
_rt = [0]


def kernel(**inputs):
    if "nc" not in _NC_CACHE:
        _NC_CACHE["nc"] = build_program()
    nc = _NC_CACHE["nc"]
    maps = make_in_maps(inputs)
    res = run_bass_kernel_spmd(nc, maps, core_ids=list(range(8)))
    return assemble(res.results)
```
